# Optimizing a Trainium2 kernel written in Bass

```python
import jax, jax.numpy as jnp
from jax import lax
import numpy as np

D_MODEL = 1024
BATCH = 8
SEQ = 4096
DEPTH = 1

GRID_W = 64
ATTN_HEADS = 8
ATTN_KV_HEADS = 2
ATTN_HEAD_DIM = 64
RET_HEADS = 4
RET_KEY_DIM = 64
RET_VALUE_DIM = 128
Q_BLOCK = 128
RET_CHUNK = 128
ROPE_THETA = 10000.0
EPS = 1e-6

ATTN_WIDTH = ATTN_HEADS * ATTN_HEAD_DIM
ATTN_KV_WIDTH = ATTN_KV_HEADS * ATTN_HEAD_DIM
RET_QK_WIDTH = RET_HEADS * RET_KEY_DIM
RET_V_WIDTH = RET_HEADS * RET_VALUE_DIM
MERGE_WIDTH = 2 * D_MODEL
IN_SIZES = (ATTN_WIDTH, ATTN_KV_WIDTH, ATTN_KV_WIDTH, ATTN_WIDTH,
            RET_QK_WIDTH, RET_QK_WIDTH, RET_V_WIDTH, RET_V_WIDTH, MERGE_WIDTH)
IN_WIDTH = int(sum(IN_SIZES))
IN_SPLITS = tuple(int(s) for s in np.cumsum(IN_SIZES)[:-1])

kernel_name = "hybrid_gqa_axialrope_bidir_retention_gated_merge"


def rmsnorm(x, g):
    xf = x.astype(jnp.float32)
    xf = xf * lax.rsqrt(jnp.mean(xf * xf, axis=-1, keepdims=True) + EPS)
    return xf.astype(x.dtype) * g


def axial_rope_tables(seq_len, head_dim):
    rows = seq_len // GRID_W
    r, cidx = jnp.meshgrid(jnp.arange(rows), jnp.arange(GRID_W), indexing='ij')
    row = r.reshape(-1).astype(jnp.float32)
    col = cidx.reshape(-1).astype(jnp.float32)
    half = head_dim // 2
    inv_freq = ROPE_THETA ** (-jnp.arange(0, half, 2, dtype=jnp.float32) / half)
    ang_r = row[:, None] * inv_freq[None, :]
    ang_c = col[:, None] * inv_freq[None, :]
    return (jnp.cos(ang_r), jnp.sin(ang_r), jnp.cos(ang_c), jnp.sin(ang_c))


def _rotate(xp, cos, sin):
    x1, x2 = jnp.split(xp, 2, axis=-1)
    cos = cos[None, :, None, :].astype(xp.dtype)
    sin = sin[None, :, None, :].astype(xp.dtype)
    return jnp.concatenate([x1 * cos - x2 * sin, x2 * cos + x1 * sin], axis=-1)


def apply_axial_rope(x, tables):
    cos_r, sin_r, cos_c, sin_c = tables
    xr, xc = jnp.split(x, 2, axis=-1)
    return jnp.concatenate([_rotate(xr, cos_r, sin_r), _rotate(xc, cos_c, sin_c)], axis=-1)


def gqa_attention(q, k, v, qn_g, kn_g, tables):
    B, S = q.shape[0], q.shape[1]
    G = ATTN_HEADS // ATTN_KV_HEADS
    q = apply_axial_rope(rmsnorm(q, qn_g), tables) * (ATTN_HEAD_DIM ** -0.5)
    k = apply_axial_rope(rmsnorm(k, kn_g), tables)
    nblk = S // Q_BLOCK
    qb = q.reshape(B, nblk, Q_BLOCK, ATTN_KV_HEADS, G, ATTN_HEAD_DIM).transpose(1, 0, 2, 3, 4, 5)

    def block(qi):
        s = jnp.einsum('bqkgd,bskd->bkgqs', qi, k).astype(jnp.float32)
        p = jax.nn.softmax(s, axis=-1).astype(v.dtype)
        return jnp.einsum('bkgqs,bskd->bqkgd', p, v)

    o = lax.map(block, qb)
    return o.transpose(1, 0, 2, 3, 4, 5).reshape(B, S, ATTN_WIDTH)


def retention_one_direction(q, k, v, log_gamma, strict):
    B, S, H, dk = q.shape
    dv = v.shape[-1]
    C = RET_CHUNK
    N = S // C
    dt = q.dtype
    qc = q.reshape(B, N, C, H, dk)
    kc = k.reshape(B, N, C, H, dk)
    vc = v.reshape(B, N, C, H, dv)
    idx = jnp.arange(C, dtype=jnp.float32)
    diff = idx[:, None] - idx[None, :]
    mask = (diff > 0) if strict else (diff >= 0)
    decay_intra = jnp.where(mask[None], jnp.exp(log_gamma[:, None, None] * jnp.maximum(diff, 0.0)[None]), 0.0)
    scores = jnp.einsum('bnihd,bnjhd->bnhij', qc, kc) * decay_intra.astype(dt)[None, None]
    o_intra = jnp.einsum('bnhij,bnjhe->bnihe', scores, vc)
    k_dec = jnp.exp(log_gamma[None, :] * (C - 1 - idx)[:, None]).astype(dt)
    kv = jnp.einsum('bnjhd,bnjhe->nbhde', kc * k_dec[:, :, None], vc)
    chunk_decay = jnp.exp(log_gamma * C).astype(kv.dtype)[None, :, None, None]

    def step(R, kv_n):
        return chunk_decay * R + kv_n, R

    _, R_prev = lax.scan(step, jnp.zeros_like(kv[0]), kv)
    q_dec = jnp.exp(log_gamma[None, :] * (idx + 1.0)[:, None]).astype(dt)
    o_inter = jnp.einsum('bnihd,nbhde->bnihe', qc * q_dec[:, :, None], R_prev)
    return (o_intra + o_inter).reshape(B, S, H, dv)


def bidirectional_retention(q, k, v, w_dec_f, w_dec_b, gn_g, tables):
    B, S = q.shape[0], q.shape[1]
    q = apply_axial_rope(q, tables)
    k = apply_axial_rope(k, tables) * (RET_KEY_DIM ** -0.5)
    lg_f = jax.nn.log_sigmoid(w_dec_f.astype(jnp.float32))
    lg_b = jax.nn.log_sigmoid(w_dec_b.astype(jnp.float32))
    o_f = retention_one_direction(q, k, v, lg_f, False)
    o_b = jnp.flip(retention_one_direction(jnp.flip(q, 1), jnp.flip(k, 1), jnp.flip(v, 1), lg_b, True), 1)
    o = (o_f + o_b).astype(jnp.float32)
    mu = jnp.mean(o, axis=-1, keepdims=True)
    var = jnp.mean(jnp.square(o - mu), axis=-1, keepdims=True)
    o = ((o - mu) * lax.rsqrt(var + EPS)).astype(v.dtype)
    return o.reshape(B, S, RET_V_WIDTH) * gn_g


def setup_inputs(seed: int = 0) -> dict:
    key = jax.random.key(seed)
    ks = jax.random.split(key, 20)
    f32 = jnp.float32

    def w(k, shape, fan_in):
        return jax.random.normal(k, shape, f32) * (fan_in ** -0.5)

    def gain(k, shape):
        return 1.0 + 0.05 * jax.random.normal(k, shape, f32)

    base = jnp.log(2.0 ** (5.0 + jnp.arange(RET_HEADS, dtype=f32)) - 1.0)
    return {
        "x": jax.random.normal(ks[0], (BATCH, SEQ, D_MODEL), f32),
        "c": jax.random.normal(ks[1], (BATCH, D_MODEL), f32),
        "w_ada": w(ks[2], (DEPTH, D_MODEL, 3 * D_MODEL), D_MODEL) * 0.5,
        "b_ada": 0.02 * jax.random.normal(ks[3], (DEPTH, 3 * D_MODEL), f32),
        "g_pre": gain(ks[4], (DEPTH, D_MODEL)),
        "w_in": w(ks[5], (DEPTH, D_MODEL, IN_WIDTH), D_MODEL),
        "qn_g": gain(ks[6], (DEPTH, ATTN_HEAD_DIM)),
        "kn_g": gain(ks[7], (DEPTH, ATTN_HEAD_DIM)),
        "w_dec_f": base[None] + 0.1 * jax.random.normal(ks[8], (DEPTH, RET_HEADS), f32),
        "w_dec_b": base[None] + 0.1 * jax.random.normal(ks[9], (DEPTH, RET_HEADS), f32),
        "gn_g": gain(ks[10], (DEPTH, RET_V_WIDTH)),
        "w_pa": w(ks[11], (DEPTH, ATTN_WIDTH, D_MODEL), ATTN_WIDTH),
        "w_pr": w(ks[12], (DEPTH, RET_V_WIDTH, D_MODEL), RET_V_WIDTH),
        "w_out": w(ks[13], (DEPTH, D_MODEL, D_MODEL), D_MODEL),
        "g_post": gain(ks[14], (DEPTH, D_MODEL)),
    }


def reference(x, c, w_ada, b_ada, g_pre, w_in, qn_g, kn_g, w_dec_f, w_dec_b, gn_g,
              w_pa, w_pr, w_out, g_post):
    B, S, D = x.shape
    tables = axial_rope_tables(S, ATTN_HEAD_DIM)
    c_act = jax.nn.silu(c)
    for l in range(DEPTH):
        mod = c_act @ w_ada[l] + b_ada[l]
        shift, scale, gate = jnp.split(mod, 3, axis=-1)
        h = rmsnorm(x, g_pre[l]) * (1.0 + scale[:, None, :]) + shift[:, None, :]
        p = h @ w_in[l]
        qa, ka, va, za, qr, kr, vr, zr, gl = jnp.split(p, IN_SPLITS, axis=-1)
        ya = gqa_attention(qa.reshape(B, S, ATTN_HEADS, ATTN_HEAD_DIM),
                           ka.reshape(B, S, ATTN_KV_HEADS, ATTN_HEAD_DIM),
                           va.reshape(B, S, ATTN_KV_HEADS, ATTN_HEAD_DIM),
                           qn_g[l], kn_g[l], tables) * jax.nn.silu(za)
        yr = bidirectional_retention(qr.reshape(B, S, RET_HEADS, RET_KEY_DIM),
                                     kr.reshape(B, S, RET_HEADS, RET_KEY_DIM),
                                     vr.reshape(B, S, RET_HEADS, RET_VALUE_DIM),
                                     w_dec_f[l], w_dec_b[l], gn_g[l], tables) * jax.nn.silu(zr)
        g_att, g_ret = jnp.split(jax.nn.sigmoid(gl), 2, axis=-1)
        merged = g_att * (ya @ w_pa[l]) + g_ret * (yr @ w_pr[l])
        y = rmsnorm(merged @ w_out[l], g_post[l])
        x = x + gate[:, None, :] * y
    return x
```

```python
import numpy as np
import concourse.bass as bass
import concourse.mybir as mybir
from concourse.bass_utils import run_bass_kernel_spmd

F32 = mybir.dt.float32
BF16 = mybir.dt.bfloat16
AF = mybir.ActivationFunctionType
ALU = mybir.AluOpType
AX = mybir.AxisListType


class T:
    __slots__ = ("name", "last_w", "readers", "excl")

    def __init__(self, name="", excl=False):
        self.name = name
        self.last_w = None
        self.readers = []
        self.excl = excl


class Op:
    __slots__ = ("eng", "fn", "deps", "signal", "token", "is_dma", "idx", "gidx")

    def __init__(self, eng, fn, is_dma):
        self.eng = eng
        self.fn = fn
        self.deps = []
        self.signal = False
        self.token = None
        self.is_dma = is_dma


class Sched:
    ENGS = ("pe", "act", "dve", "pool", "sp")

    def __init__(self, n_dma_sems=8):
        self.ops = {e: [] for e in self.ENGS}
        self.n_dma_sems = n_dma_sems
        self.dma_rr = {e: 0 for e in self.ENGS}
        self.dma_last = {}
        self.gcount = 0

    def add(self, eng, fn, reads=(), writes=(), dma=False):
        op = Op(eng, fn, dma)
        op.gidx = self.gcount
        self.gcount += 1
        writes = list(writes) + [t for t in reads if t.excl]
        reads = [t for t in reads if not t.excl]
        raw = []
        war = []
        for t in reads:
            if t.last_w is not None:
                raw.append(t.last_w)
        for t in writes:
            if t.last_w is not None:
                raw.append(t.last_w)
            war.extend(t.readers)
        for t in reads:
            t.readers.append(op)
        for t in writes:
            t.last_w = op
            t.readers = []
        if dma:
            k = self.dma_rr[eng]
            self.dma_rr[eng] = (k + 1) % self.n_dma_sems
            prev = self.dma_last.get((eng, k))
            if prev is not None:
                raw.append(prev)
            self.dma_last[(eng, k)] = op
            op.token = (eng, k)
        seen = set()
        for lst, is_war in ((raw, False), (war, True)):
            for d in lst:
                if d is op or id(d) in seen:
                    continue
                if (not d.is_dma) and (not dma) and d.eng == eng:
                    if eng == "pe" or is_war:
                        continue
                seen.add(id(d))
                op.deps.append(d)
        op.idx = len(self.ops[eng])
        self.ops[eng].append(op)
        return op

    def emit(self, nc, block_ctx_extra=None):
        for e in self.ENGS:
            for op in self.ops[e]:
                for d in op.deps:
                    d.signal = True
        import contextlib
        with contextlib.ExitStack() as st:
            sems = {e: st.enter_context(nc.semaphore("s_" + e)) for e in ("pe", "act", "dve", "pool")}
            dsems = {}
            for e in self.ENGS:
                if any(o.is_dma for o in self.ops[e]):
                    for k in range(self.n_dma_sems):
                        dsems[(e, k)] = st.enter_context(nc.semaphore("d_%s_%d" % (e, k)))
            for e in self.ENGS:
                cnt = 0
                dcnt = {}
                for op in self.ops[e]:
                    if op.is_dma:
                        key = op.token
                        dcnt[key] = dcnt.get(key, 0) + 16
                        op.token = (dsems[key], dcnt[key], key)
                    elif op.signal:
                        cnt += 1
                        op.token = (sems[e], cnt, e)
            block = st.enter_context(nc.Block())
            handles = {"pe": block.tensor, "act": block.scalar, "dve": block.vector, "pool": block.gpsimd,
                       "sp": block.sync}
            nwaits = {e: 0 for e in self.ENGS}

            def make(e):
                ops = self.ops[e]

                def body(eng):
                    known = {}
                    for op in ops:
                        need = {}
                        for d in op.deps:
                            sem, val, key = d.token
                            if known.get(key, 0) >= val:
                                continue
                            if key not in need or need[key][1] < val:
                                need[key] = (sem, val)
                        for key, (sem, val) in need.items():
                            eng.wait_ge(sem, val)
                            known[key] = val
                            nwaits[e] += 1
                        inst = op.fn(eng)
                        if op.is_dma:
                            inst.then_inc(op.token[0], 16)
                        elif op.signal:
                            inst.then_inc(op.token[0], 1)
                    fin = {}
                    for op in ops:
                        if op.is_dma:
                            fin[op.token[2]] = (op.token[0], op.token[1])
                    for key, (sem, val) in fin.items():
                        if known.get(key, 0) < val:
                            eng.wait_ge(sem, val)
                return body

            for e in self.ENGS:
                if self.ops[e]:
                    handles[e](make(e))
            self.nwaits = nwaits

import contextlib

SEQ = 4096
DM = 1024
NG = 8
EPS = 1e-6

BLK_ADA = 0
BLK_P2KV = 6
BLK_RK = 7
BLK_RV = 8
BLK_Q = 9
BLK_ZA = 10
BLK_RQ = 11
BLK_ZR = 12
BLK_M = 13
BLK_WO = 21
NBLK = 23
NSMALL = 40


def _host_blocks(w_ada, w_in, w_pa, w_pr, w_out):
    blocks = np.zeros((NBLK, 128, 4096), np.float32)

    def kmaj(w):
        nc_ = w.shape[1]
        t = np.zeros((128, 8, 512), np.float32)
        t[:, :, :nc_] = w.reshape(8, 128, nc_).transpose(1, 0, 2)
        return t.reshape(128, 4096)

    wa = w_ada
    for j in range(6):
        blocks[BLK_ADA + j] = kmaj(wa[:, j * 512:(j + 1) * 512])
    o_q, o_k, o_v, o_za, o_rq, o_rk, o_rv, o_zr, o_gl = 0, 512, 640, 768, 1280, 1536, 1792, 2304, 2816
    k0 = w_in[:, o_k:o_k + 64]
    k1 = w_in[:, o_k + 64:o_k + 128]
    blocks[BLK_P2KV] = kmaj(np.concatenate([k0, k0, k1, k1, w_in[:, o_v:o_v + 128]], axis=1))
    blocks[BLK_RK] = kmaj(w_in[:, o_rk:o_rk + 256])
    blocks[BLK_RV] = kmaj(w_in[:, o_rv:o_rv + 512])
    blocks[BLK_Q] = kmaj(w_in[:, o_q:o_q + 512])
    blocks[BLK_ZA] = kmaj(w_in[:, o_za:o_za + 512])
    rq = [w_in[:, o_rq + 64 * h:o_rq + 64 * (h + 1)] for h in range(4)]
    blocks[BLK_RQ] = kmaj(np.concatenate([rq[0], rq[0], rq[1], rq[1], rq[2], rq[2], rq[3], rq[3]], axis=1))
    blocks[BLK_ZR] = kmaj(w_in[:, o_zr:o_zr + 512])
    for n in range(8):
        t = np.zeros((128, 4096), np.float32)
        gl = np.concatenate([w_in[:, o_gl + n * 128:o_gl + (n + 1) * 128],
                             w_in[:, o_gl + 1024 + n * 128:o_gl + 1024 + (n + 1) * 128]], axis=1)
        t[:, 0:2048] = gl.reshape(8, 128, 256).transpose(1, 0, 2).reshape(128, 2048)
        t[:, 2048:2560] = w_pa[:, n * 128:(n + 1) * 128].reshape(4, 128, 128).transpose(1, 0, 2).reshape(128, 512)
        t[:, 2560:3072] = w_pr[:, n * 128:(n + 1) * 128].reshape(4, 128, 128).transpose(1, 0, 2).reshape(128, 512)
        blocks[BLK_M + n] = t
    for hf in range(2):
        blocks[BLK_WO + hf] = kmaj(w_out[:, hf * 512:(hf + 1) * 512])
    return blocks


def _host_consts():
    t = np.arange(SEQ)
    row = (t // 64).astype(np.float32)
    col = (t % 64).astype(np.float32)
    inv = (10000.0 ** (-np.arange(0, 32, 2, dtype=np.float32) / 32.0)).astype(np.float32)
    ang_r = row[None, :] * inv[:, None]
    ang_c = col[None, :] * inv[:, None]
    cos64 = np.concatenate([np.cos(ang_r), np.cos(ang_r), np.cos(ang_c), np.cos(ang_c)], 0)
    sin64 = np.concatenate([np.sin(ang_r), np.sin(ang_r), np.sin(ang_c), np.sin(ang_c)], 0)
    rope = np.stack([np.concatenate([cos64, cos64], 0), np.concatenate([sin64, sin64], 0)], 0).astype(np.float32)
    ident = np.eye(128, dtype=np.float32)
    perm = np.zeros((128, 128), np.float32)
    for hb in range(2):
        for d in range(64):
            blk = d // 16
            if blk % 2 == 0:
                p, s = d + 16, -1.0
            else:
                p, s = d - 16, 1.0
            perm[hb * 64 + p, hb * 64 + d] = s
    ones = np.zeros((128, 128), np.float32)
    ones[:64, :64] = 1.0
    ones[64:, 64:] = 1.0
    return rope, np.stack([ident, perm, ones], 0)


class _Stop(Exception):
    pass


def build_program(stop=None, dumps=()):
    nc = bass.Bass("TRN2", target_bir_lowering=False)
    x_d = nc.dram_tensor("x", [SEQ, DM], F32, kind="ExternalInput").ap()
    wsrc = nc.dram_tensor("wsrc", [NBLK, 128, 4096], F32, kind="ExternalInput").ap()
    rope_d = nc.dram_tensor("rope", [2, 128, SEQ], F32, kind="ExternalInput").ap()
    cst_d = nc.dram_tensor("cst", [3, 128, 128], F32, kind="ExternalInput").ap()
    small_d = nc.dram_tensor("small", [128, NSMALL], F32, kind="ExternalInput").ap()
    rows_d = nc.dram_tensor("rows", [2, DM], F32, kind="ExternalInput").ap()
    wdec_d = nc.dram_tensor("wdec", [1, 8], F32, kind="ExternalInput").ap()
    out_d = nc.dram_tensor("out", [SEQ, DM], F32, kind="ExternalOutput").ap()
    wbf_d = nc.dram_tensor("wbf", [NBLK, 128, 4096], BF16, kind="Internal").ap()

    S = Sched(n_dma_sems=12)
    tdict = {}

    def tt(name):
        if name not in tdict:
            tdict[name] = T(name)
        return tdict[name]

    with contextlib.ExitStack() as st:
        def sb(name, shape, dt):
            return st.enter_context(nc.sbuf_tensor(name, shape, dt))

        PS = st.enter_context(nc.psum_tensor("ps", [128, 4096], F32))
        TB = [T("bank%d" % i, excl=True) for i in range(8)]

        def bank(i, n=1):
            return PS[:, i * 512:(i + n) * 512]

        K2 = sb("K2", [128, 2, SEQ], BF16)
        VX = sb("VX", [128, 32, 320], BF16)
        RF = sb("RF", [128, 16, 512], BF16)
        NWB = 3
        WB = sb("WB", [128, NWB, 4096], BF16)
        XT = sb("XT", [128, 2, DM], F32)
        JUNK = sb("JUNK", [128, DM], BF16)
        XN = sb("XN", [128, 2, DM], BF16)
        HT = sb("HT", [128, 8, 512], BF16)
        CS = sb("CS", [128, 2, 512], F32)
        QBF = sb("QBF", [128, 512], BF16)
        SQ = sb("SQ", [128, 512], BF16)
        T1 = sb("T1", [128, 512], F32)
        T2 = sb("T2", [128, 512], F32)
        RRS = sb("RRS", [128, 512], F32)
        QT = sb("QT", [128, 4, 512], BF16)
        ZA = sb("ZA", [128, 4, 512], BF16)
        NPT = 3
        PT = sb("PT", [128, NPT, 1024], BF16)
        RC = sb("RC", [128, 512], F32)
        TMPN = sb("TMPN", [128, 512], F32)
        YA = sb("YA", [128, 4, 512], BF16)
        YR = sb("YR", [128, 4, 512], BF16)
        RQ = sb("RQ", [128, 4, 512], BF16)
        QD = sb("QD", [128, 4, 512], BF16)
        ZR = sb("ZR", [128, 4, 512], BF16)
        RKT = sb("RKT", [128, 2, 512], BF16)
        RVG = sb("RVG", [128, 4, 512], BF16)
        KK = sb("KK", [128, 2, 512], BF16)
        SM = sb("SM", [128, 2, 512], BF16)
        RRO = sb("RRO", [128, 2, 512], BF16)
        ON = sb("ON", [128, 2, 512], BF16)
        SST = sb("SST", [128, 512], F32)
        GATE = sb("GATE", [128, 2, 2, 512], BF16)
        MT = sb("MT", [128, 2, 512], F32)
        MG = sb("MG", [128, 8, 512], BF16)
        YT = sb("YT", [128, DM], F32)
        IDENT = sb("IDENT", [128, 128], BF16)
        PERM = sb("PERM", [128, 128], BF16)
        ONESB = sb("ONESB", [128, 128], BF16)
        DMK = sb("DMK", [128, 512], F32)
        QDEC = sb("QDEC", [128, 512], F32)
        KD = sb("KD", [128, 512], F32)
        DEC = sb("DEC", [128, 512], F32)
        GG = sb("GG", [128, DM], F32)
        SMALL = sb("SMALL", [128, NSMALL], F32)
        MISC = sb("MISC", [128, 128], F32)
        IO = sb("IO", [128, 3, 128], F32)
        TMPD = sb("TMPD", [128, 128], F32)
        CBC = sb("CBC", [128, 8, 128], BF16)
        CBF = sb("CBF", [128, 8], BF16)
        STAT = sb("STAT", [128, 2, 4, 8], F32)

        M_ACOL, M_SHCOL, M_LG, M_DC, M_WD, M_E, M_KD8, M_KDV, M_IOP, M_IOPR, M_GK8 = 0, 8, 16, 24, 32, 40, 48, 56, 64, 65, 66
        M_SS, M_RSTD, M_CACT, M_TMP8 = 70, 74, 80, 88
        M_GNR, M_GNN = 96, 104
        S_C, S_GPRE, S_BSC, S_BSH, S_QG, S_KG, S_GN = 0, 8, 16, 24, 32, 33, 34

        def mm(o, l, r, start, stop, rd, wr):
            S.add("pe", lambda e: e.matmul(o, l, r, start=start, stop=stop), reads=rd, writes=wr)

        def trp(o, i, rd, wr):
            S.add("pe", lambda e: e.transpose(o, i, IDENT[:]), reads=rd + [tt("ident")], writes=wr)

        def act(o, i, f, rd, wr, **kw):
            S.add("act", lambda e: e.activation(out=o, in_=i, func=f, **kw), reads=rd, writes=wr)

        def tten(eng, o, a, b, op, rd, wr):
            S.add(eng, lambda e: e.tensor_tensor(out=o, in0=a, in1=b, op=op), reads=rd, writes=wr)

        def tsc(eng, o, a, s1, s2, op0, op1, rd, wr):
            if op1 is None:
                S.add(eng, lambda e: e.tensor_scalar(out=o, in0=a, scalar1=s1, scalar2=None, op0=op0), reads=rd, writes=wr)
            else:
                S.add(eng, lambda e: e.tensor_scalar(out=o, in0=a, scalar1=s1, scalar2=s2, op0=op0, op1=op1), reads=rd, writes=wr)

        def stt(eng, o, a, s, b, op0, op1, rd, wr):
            S.add(eng, lambda e: e.scalar_tensor_tensor(out=o, in0=a, scalar=s, in1=b, op0=op0, op1=op1), reads=rd, writes=wr)

        def cpy(eng, o, i, rd, wr):
            S.add(eng, lambda e: e.tensor_copy(out=o, in_=i), reads=rd, writes=wr)

        def dma(eng, o, i, rd, wr):
            S.add(eng, lambda e: e.dma_start(out=o, in_=i), reads=rd, writes=wr, dma=True)

        MUL, ADD, MAX = ALU.mult, ALU.add, ALU.max

        def ckpt(name):
            if stop == name:
                raise _Stop()

        def bnstats(o, i, rd, wr):
            S.add("dve", lambda e: e.bn_stats(out=o, in_=i), reads=rd, writes=wr)

        def bnaggr(o, i, rd, wr):
            S.add("dve", lambda e: e.bn_aggr(out=o, in_=i), reads=rd, writes=wr)

        def recip(o, i, rd, wr):
            S.add("dve", lambda e: e.reciprocal(out=o, in_=i), reads=rd, writes=wr)

        order = [BLK_ADA + j for j in range(6)]
        for tg in range(NG):
            order += [BLK_P2KV, BLK_RK, BLK_RV]
        for tg in range(NG):
            order += [BLK_Q, BLK_ZA, BLK_RQ, BLK_RK, BLK_RV, BLK_ZR] + [BLK_M + n for n in range(8)] + [BLK_WO, BLK_WO + 1]
        conv_done = set()
        ws_state = {"issued": 0, "cur": 0}

        def conv(blk):
            if blk in conv_done:
                return
            conv_done.add(blk)
            dma("pool", wbf_d[blk], wsrc[blk], [], [tt("wbf%d" % blk)])

        def ws_issue_upto(n):
            while ws_state["issued"] < min(n, len(order)):
                i = ws_state["issued"]
                blk = order[i]
                conv(blk)
                dma("sp", WB[:, i % NWB, :], wbf_d[blk], [tt("wbf%d" % blk)], [tt("wb%d" % (i % NWB))])
                ws_state["issued"] += 1

        def ws_get(expect, la=None):
            i = ws_state["cur"]
            assert order[i] == expect, (i, order[i], expect)
            ws_issue_upto(i + (NWB if la is None else la))
            ws_state["cur"] += 1
            return WB[:, i % NWB, :], tt("wb%d" % (i % NWB))

        def run_all():
            for blk in [BLK_ADA + j for j in range(6)] + [BLK_P2KV, BLK_RK, BLK_RV]:
                conv(blk)
            dma("sp", SMALL[:], small_d, [], [tt("small")])
            dma("sp", GG[:], rows_d[0:1, :].partition_broadcast(128), [], [tt("gg")])
            dma("sp", YT[:], rows_d[1:2, :].partition_broadcast(128), [], [tt("yt")])
            dma("sp", MISC[:, M_WD:M_WD + 8], wdec_d.partition_broadcast(128), [], [tt("wd")])
            dma("pool", IDENT[:], cst_d[0], [], [tt("ident")])
            dma("pool", PERM[:], cst_d[1], [], [tt("perm")])
            dma("pool", ONESB[:], cst_d[2], [], [tt("onesb")])
            ws_issue_upto(NWB)

            act(MISC[:, M_CACT:M_CACT + 8], SMALL[:, S_C:S_C + 8], AF.Silu, [tt("small")], [tt("cact")])
            cpy("dve", CBF[:], MISC[:, M_CACT:M_CACT + 8], [tt("cact")], [tt("cbf")])
            cpy("dve", CBC[:], MISC[:, M_CACT:M_CACT + 8].unsqueeze(2).broadcast_to([128, 8, 128]), [tt("cact")], [tt("cbc")])
            act(MISC[:, M_E:M_E + 8], MISC[:, M_WD:M_WD + 8], AF.Exp, [tt("wd")], [tt("e")], scale=-1.0)
            act(MISC[:, M_E:M_E + 8], MISC[:, M_E:M_E + 8], AF.Ln, [tt("e")], [tt("e")], bias=1.0)
            tsc("dve", MISC[:, M_LG:M_LG + 8], MISC[:, M_E:M_E + 8], -1.0, None, MUL, None, [tt("e")], [tt("lg")])
            LGF = lambda h, lo=0, hi=128: MISC[lo:hi, M_LG + h:M_LG + h + 1]
            LGB = lambda h, lo=0, hi=128: MISC[lo:hi, M_LG + 4 + h:M_LG + 5 + h]
            S.add("pool", lambda e: e.iota(IO[:, 0, :], pattern=[[1, 128]], base=0, channel_multiplier=-1,
                                           allow_small_or_imprecise_dtypes=True), writes=[tt("io0")])
            S.add("pool", lambda e: e.iota(IO[:, 1, :], pattern=[[1, 128]], base=1, channel_multiplier=0,
                                           allow_small_or_imprecise_dtypes=True), writes=[tt("io1")])
            S.add("pool", lambda e: e.iota(IO[:, 2, :], pattern=[[-1, 128]], base=128, channel_multiplier=0,
                                           allow_small_or_imprecise_dtypes=True), writes=[tt("io2")])
            S.add("pool", lambda e: e.iota(MISC[:, M_IOP:M_IOP + 1], pattern=[[1, 1]], base=0, channel_multiplier=1,
                                           allow_small_or_imprecise_dtypes=True), writes=[tt("iop")])
            S.add("pool", lambda e: e.iota(MISC[:, M_IOPR:M_IOPR + 1], pattern=[[1, 1]], base=127, channel_multiplier=-1,
                                           allow_small_or_imprecise_dtypes=True), writes=[tt("iopr")])
            NEG = RRS[:, 0:128]
            POS = RRS[:, 128:256]
            tsc("dve", POS, IO[:, 0, :], 0.0, None, MAX, None, [tt("io0")], [tt("pos")])
            tten("dve", NEG, POS, IO[:, 0, :], ALU.subtract, [tt("pos"), tt("io0")], [tt("neg")])
            for h in range(4):
                tsc("dve", TMPD[:], POS, LGF(h), None, MUL, None, [tt("pos"), tt("lg")], [tt("tmpd")])
                stt("dve", TMPD[:], NEG, LGB(h), TMPD[:], MUL, ADD, [tt("neg"), tt("lg"), tt("tmpd")], [tt("tmpd")])
                act(DMK[:, h * 128:(h + 1) * 128], TMPD[:], AF.Exp, [tt("tmpd")], [tt("dmk")])
                tsc("dve", TMPD[0:64, :], IO[0:64, 1, :], LGF(h, 0, 64), None, MUL, None, [tt("io1"), tt("lg"), tt("dmk")], [tt("tmpd")])
                tsc("dve", TMPD[64:128, :], IO[64:128, 2, :], LGB(h, 64, 128), None, MUL, None, [tt("io2"), tt("lg"), tt("tmpd")], [tt("tmpd")])
                act(QDEC[:, h * 128:(h + 1) * 128], TMPD[:], AF.Exp, [tt("tmpd")], [tt("qdec")])
            KDV = MISC[:, M_KDV:M_KDV + 8].rearrange("p (h d) -> p h d", d=2)
            tsc("dve", MISC[:, M_KD8:M_KD8 + 4], MISC[:, M_LG:M_LG + 4], MISC[:, M_IOPR:M_IOPR + 1], None, MUL, None,
                [tt("lg"), tt("iopr")], [tt("kd8")])
            tsc("dve", MISC[:, M_KD8 + 4:M_KD8 + 8], MISC[:, M_LG + 4:M_LG + 8], MISC[:, M_IOP:M_IOP + 1], None, MUL, None,
                [tt("lg"), tt("iop"), tt("kd8")], [tt("kd8")])
            act(KDV[:, :, 0], MISC[:, M_KD8:M_KD8 + 4], AF.Exp, [tt("kd8")], [tt("kdv")])
            act(KDV[:, :, 1], MISC[:, M_KD8 + 4:M_KD8 + 8], AF.Exp, [tt("kd8"), tt("kdv")], [tt("kdv")])
            cpy("dve", KD[:].rearrange("p (a c) -> p a c", c=64),
                MISC[:, M_KDV:M_KDV + 8].unsqueeze(2).broadcast_to([128, 8, 64]), [tt("kdv")], [tt("kd")])
            act(MISC[:, M_DC:M_DC + 8], MISC[:, M_LG:M_LG + 8], AF.Exp, [tt("lg")], [tt("dc")], scale=128.0)
            cpy("dve", DEC[0:64, :].rearrange("p (h c) -> p h c", c=128),
                MISC[0:64, M_DC:M_DC + 4].unsqueeze(2).broadcast_to([64, 4, 128]), [tt("dc")], [tt("dec")])
            cpy("dve", DEC[64:128, :].rearrange("p (h c) -> p h c", c=128),
                MISC[64:128, M_DC + 4:M_DC + 8].unsqueeze(2).broadcast_to([64, 4, 128]), [tt("dc"), tt("dec")], [tt("dec")])
            tsc("dve", MISC[:, M_GK8:M_GK8 + 1], SMALL[:, S_KG:S_KG + 1], 8.0, None, MUL, None, [tt("small")], [tt("gk8")])
            S.add("pool", lambda e: e.memset(SST[:], 0.0), writes=[tt("sst")])
            S.add("pool", lambda e: e.memset(VX[:], 1.0), writes=[tt("vx%d" % g) for g in range(NG)])

            ckpt('setup0')
            pmod = bank(6)[:, 0:16]
            for blk in range(4):
                wb, twb = ws_get(BLK_ADA + blk)
                wbv = wb.rearrange("p (k c) -> p k c", k=8)
                for c in range(4):
                    for kc in range(8):
                        mm(pmod[:, blk * 4 + c:blk * 4 + c + 1], wbv[:, kc, c * 128:(c + 1) * 128], CBF[:, kc:kc + 1],
                           kc == 0, kc == 7, [twb, tt("cbf")], [TB[6]])
            tten("dve", MISC[:, M_TMP8:M_TMP8 + 8], pmod[:, 8:16], SMALL[:, S_BSC:S_BSC + 8], ADD, [TB[6], tt("small")], [tt("tmp8")])
            stt("dve", MISC[:, M_ACOL:M_ACOL + 8], MISC[:, M_TMP8:M_TMP8 + 8], 1.0, SMALL[:, S_GPRE:S_GPRE + 8], ADD, MUL,
                [tt("tmp8"), tt("small")], [tt("acol")])
            tten("dve", MISC[:, M_SHCOL:M_SHCOL + 8], pmod[:, 0:8], SMALL[:, S_BSH:S_BSH + 8], ADD, [TB[6], tt("small")], [tt("shcol")])
            for hf in range(2):
                wb, twb = ws_get(BLK_ADA + 4 + hf)
                wbv = wb.rearrange("p (k c) -> p k c", k=8)
                for kc in range(8):
                    mm(bank(4 + hf), CBC[:, kc, :], wbv[:, kc, :], kc == 0, kc == 7, [twb, tt("cbc")], [TB[4 + hf]])
                tten("dve", GG[:, hf * 512:(hf + 1) * 512], bank(4 + hf), GG[:, hf * 512:(hf + 1) * 512], ADD,
                     [TB[4 + hf], tt("gg")], [tt("gg")])
                tten("dve", GG[:, hf * 512:(hf + 1) * 512], GG[:, hf * 512:(hf + 1) * 512], YT[:, hf * 512:(hf + 1) * 512], MUL,
                     [tt("gg"), tt("yt")], [tt("gg")])

            ckpt('ada')
            xt_rr = {"i": 0}

            def load_x(tile):
                i = xt_rr["i"] % 2
                xt_rr["i"] += 1
                dma("sp", XT[:, i, :], x_d[tile * 128:(tile + 1) * 128, :], [], [tt("xt%d" % i)])
                return i

            def make_hT(tg):
                HTv = HT
                for j in range(4):
                    tile = tg * 4 + j
                    xi = load_x(tile)
                    xb = j % 2
                    act(JUNK[:], XT[:, xi, :], AF.Square, [tt("xt%d" % xi)], [tt("junk"), tt("ss")], accum_out=MISC[:, M_SS:M_SS + 1])
                    act(MISC[:, M_RSTD:M_RSTD + 1], MISC[:, M_SS:M_SS + 1], AF.Ln, [tt("ss")], [tt("rstd")], scale=1.0 / DM, bias=EPS)
                    act(MISC[:, M_RSTD:M_RSTD + 1], MISC[:, M_RSTD:M_RSTD + 1], AF.Exp, [tt("rstd")], [tt("rstd")], scale=-0.5)
                    tsc("dve", XN[:, xb, :], XT[:, xi, :], MISC[:, M_RSTD:M_RSTD + 1], None, MUL, None,
                        [tt("xt%d" % xi), tt("rstd")], [tt("xn%d" % xb)])
                    for kc in range(8):
                        pb = bank(kc // 2).bitcast(BF16)[:, (kc % 2) * 512 + j * 128:(kc % 2) * 512 + (j + 1) * 128]
                        trp(pb, XN[:, xb, kc * 128:(kc + 1) * 128], [tt("xn%d" % xb)], [TB[kc // 2]])
                for kc in range(8):
                    pb = bank(kc // 2).bitcast(BF16)[:, (kc % 2) * 512:(kc % 2 + 1) * 512]
                    if kc < 4:
                        tsc("dve", HTv[:, kc, :], pb, MISC[:, M_ACOL + kc:M_ACOL + kc + 1], MISC[:, M_SHCOL + kc:M_SHCOL + kc + 1],
                            MUL, ADD, [TB[kc // 2], tt("acol"), tt("shcol")], [tt("ht")])
                    else:
                        act(HTv[:, kc, :], pb, AF.Identity, [TB[kc // 2], tt("acol"), tt("shcol")], [tt("ht")],
                            scale=MISC[:, M_ACOL + kc:M_ACOL + kc + 1], bias=MISC[:, M_SHCOL + kc:M_SHCOL + kc + 1])

            def load_cs(tg):
                dma("sp", CS[:, 0, :], rope_d[0][:, tg * 512:(tg + 1) * 512], [], [tt("cs")])
                dma("sp", CS[:, 1, :], rope_d[1][:, tg * 512:(tg + 1) * 512], [], [tt("cs")])

            def proj_fm(wbv, twb, c0, bk):
                for kc in range(8):
                    mm(bank(bk), wbv[:, kc, c0:c0 + 128], HT[:, kc, :], kc == 0, kc == 7, [twb, tt("ht")], [TB[bk]])

            def proj_tm(wbv, twb, j, c0, ncol, o, tbk):
                for kc in range(8):
                    mm(o, HT[:, kc, j * 128:(j + 1) * 128], wbv[:, kc, c0:c0 + ncol], kc == 0, kc == 7, [twb, tt("ht")], [tbk])

            def rope(bkA, gcol, gT, rms, out_ap, out_T):
                import os
                cut = int(os.environ.get("ROPE_CUT", "99"))
                A = bank(bkA)
                rdg = [gT] if gT is not None else []
                gsc = gcol if gT is not None else float(gcol)
                if cut < 1: return
                act(QBF[:], A, AF.Copy, [TB[bkA]] + rdg, [tt("qbf")], scale=gsc)
                if cut < 2: return
                if rms:
                    act(SQ[:], A, AF.Square, [TB[bkA]], [tt("sq")])
                if cut < 3: return
                mm(bank(4), PERM[:], QBF[:], True, True, [tt("perm"), tt("qbf")], [TB[4]])
                if cut < 4: return
                if rms:
                    mm(bank(5), ONESB[:], SQ[:], True, True, [tt("onesb"), tt("sq")], [TB[5]])
                if cut < 5: return
                stt("dve", T1[:], A, gsc, CS[:, 0, :], MUL, MUL, [TB[bkA], tt("cs")] + rdg, [tt("t1")])
                if cut < 6: return
                tten("dve", T2[:], bank(4), CS[:, 1, :], MUL, [TB[4], tt("cs")], [tt("t2")])
                if cut < 7: return
                if rms:
                    act(RRS[:], bank(5), AF.Ln, [TB[5]], [tt("rrs")], bias=64.0 * EPS)
                    if cut < 8: return
                    act(RRS[:], RRS[:], AF.Exp, [tt("rrs")], [tt("rrs")], scale=-0.5)
                    if cut < 9: return
                    tten("pool", T1[:], T1[:], T2[:], ADD, [tt("t1"), tt("t2")], [tt("t1")])
                    if cut < 10: return
                    tten("pool", out_ap, T1[:], RRS[:], MUL, [tt("t1"), tt("rrs")], [out_T])
                else:
                    tten("pool", out_ap, T1[:], T2[:], ADD, [tt("t1"), tt("t2")], [out_T])

            def ret_kv_inputs(tg):
                wb, twb = ws_get(BLK_RK)
                wbv = wb.rearrange("p (k c) -> p k c", k=8)
                for pr in range(2):
                    proj_fm(wbv, twb, pr * 128, 6 + pr)
                    rope(6 + pr, 0.125, None, False, RKT[:, pr, :], tt("rkt"))
                wb, twb = ws_get(BLK_RV)
                wbv = wb.rearrange("p (k c) -> p k c", k=8)
                for j in range(4):
                    bk = 6 + (j % 2)
                    proj_tm(wbv, twb, j, 0, 512, bank(bk), TB[bk])
                    act(RVG[:, j, :], bank(bk), AF.Copy, [TB[bk]], [tt("rvg")])

            def kv_matmul(j):
                ptk = bank(6).bitcast(BF16)[:, 0:256]
                for pr in range(2):
                    trp(ptk[:, pr * 128:(pr + 1) * 128], RKT[:, pr, j * 128:(j + 1) * 128], [tt("rkt")], [TB[6]])
                kb = j % 2
                tten("dve", KK[:, kb, :].rearrange("p (h d c) -> p h d c", h=4, d=2),
                     ptk.rearrange("p (h c) -> p h c", h=4).unsqueeze(2).broadcast_to([128, 4, 2, 64]),
                     KD[:].rearrange("p (h d c) -> p h d c", h=4, d=2), MUL, [TB[6], tt("kd")], [tt("kk%d" % kb)])
                for h in range(4):
                    mm(bank(7)[:, h * 128:(h + 1) * 128], KK[:, kb, h * 128:(h + 1) * 128], RVG[:, j, h * 128:(h + 1) * 128],
                       True, True, [tt("kk%d" % kb), tt("rvg")], [TB[7]])

            for tg in range(NG):
                make_hT(tg)
                ckpt('ht%d' % tg)
                load_cs(tg)
                wb, twb = ws_get(BLK_P2KV)
                wbv = wb.rearrange("p (k c) -> p k c", k=8)
                for g in range(2):
                    proj_fm(wbv, twb, g * 128, 6 + g)
                    rope(6 + g, MISC[:, M_GK8:M_GK8 + 1], tt("gk8"), True, K2[:, g, tg * 512:(tg + 1) * 512], tt("k2_%d_%d" % (g, tg)))
                ckpt('k%d' % tg)
                for j in range(4):
                    proj_tm(wbv, twb, j, 256, 128, bank(6)[:, j * 128:(j + 1) * 128], TB[6])
                cpy("dve", VX[:, tg * 4:(tg + 1) * 4, 64:320].rearrange("p j (g c) -> p j g c", g=2)[:, :, :, 0:64],
                    bank(6).rearrange("p (j g c) -> p j g c", j=4, g=2), [TB[6]], [tt("vx%d" % tg)])
                ckpt('v%d' % tg)
                ret_kv_inputs(tg)
                ckpt('rkv%d' % tg)
                for j in range(4):
                    ci = tg * 4 + j
                    kv_matmul(j)
                    cpy("pool", RF[(ci % 2) * 64:(ci % 2) * 64 + 64, ci // 2, :], SST[0:64, :], [tt("sst")], [tt("rf")])
                    tten("dve", SST[0:64, :], SST[0:64, :], DEC[0:64, :], MUL, [tt("sst"), tt("dec")], [tt("sst")])
                    tten("dve", SST[0:64, :], SST[0:64, :], bank(7)[0:64, :], ADD, [tt("sst"), TB[7]], [tt("sst")])

            ckpt('p2')
            for tg in range(NG - 1, -1, -1):
                make_hT(tg)
                load_cs(tg)
                wq, twq = ws_get(BLK_Q)
                wqv = wq.rearrange("p (k c) -> p k c", k=8)
                for p in range(4):
                    proj_fm(wqv, twq, p * 128, 6 + (p % 2))
                    rope(6 + (p % 2), SMALL[:, S_QG:S_QG + 1], tt("small"), True, QT[:, p, :], tt("qt%d" % p))
                wz, twz = ws_get(BLK_ZA)
                wzv = wz.rearrange("p (k c) -> p k c", k=8)
                for p in range(4):
                    bk = 6 + (p % 2)
                    proj_fm(wzv, twz, p * 128, bk)
                    act(ZA[:, p, :], bank(bk), AF.Silu, [TB[bk]], [tt("za%d" % p)])
                ckpt('q%d' % tg)
                for p in range(4):
                    g = p // 2
                    pend = None
                    for kc in range(32):
                        si = kc % 2
                        mm(bank(2 * si), K2[0:64, g, kc * 128:(kc + 1) * 128], QT[0:64, p, :], True, True,
                           [tt("k2_%d_%d" % (g, kc // 4)), tt("qt%d" % p)], [TB[2 * si]])
                        mm(bank(2 * si + 1), K2[64:128, g, kc * 128:(kc + 1) * 128], QT[64:128, p, :], True, True,
                           [tt("k2_%d_%d" % (g, kc // 4)), tt("qt%d" % p)], [TB[2 * si + 1]])
                        pi = kc % NPT
                        act(PT[:, pi, :], bank(2 * si, 2), AF.Exp, [TB[2 * si], TB[2 * si + 1]], [tt("pt%d" % pi)])
                        if pend is not None:
                            pkc, ppi = pend
                            mm(bank(4), VX[:, pkc, 64 + g * 128:192 + g * 128], PT[:, ppi, 0:512], pkc == 0, False,
                               [tt("vx%d" % (pkc // 4)), tt("pt%d" % ppi)], [TB[4]])
                            mm(bank(5), VX[:, pkc, g * 128:g * 128 + 128], PT[:, ppi, 512:1024], pkc == 0, False,
                               [tt("vx%d" % (pkc // 4)), tt("pt%d" % ppi)], [TB[5]])
                        pend = (kc, pi)
                    pkc, ppi = pend
                    mm(bank(4), VX[:, pkc, 64 + g * 128:192 + g * 128], PT[:, ppi, 0:512], False, True,
                       [tt("vx%d" % (pkc // 4)), tt("pt%d" % ppi)], [TB[4]])
                    mm(bank(5), VX[:, pkc, g * 128:g * 128 + 128], PT[:, ppi, 512:1024], False, True,
                       [tt("vx%d" % (pkc // 4)), tt("pt%d" % ppi)], [TB[5]])
                    recip(RC[0:64, :], bank(4)[64:128, :], [TB[4]], [tt("rc")])
                    recip(RC[64:128, :], bank(5)[0:64, :], [TB[5]], [tt("rc")])
                    tten("dve", TMPN[0:64, :], bank(4)[0:64, :], RC[0:64, :], MUL, [TB[4], tt("rc")], [tt("tmpn")])
                    tten("dve", TMPN[64:128, :], bank(5)[64:128, :], RC[64:128, :], MUL, [TB[5], tt("rc")], [tt("tmpn")])
                    tten("pool", YA[:, p, :], TMPN[:], ZA[:, p, :], MUL, [tt("tmpn"), tt("za%d" % p)], [tt("ya")])
                ckpt('att%d' % tg)
                wr, twr = ws_get(BLK_RQ)
                wrv = wr.rearrange("p (k c) -> p k c", k=8)
                for h in range(4):
                    proj_fm(wrv, twr, h * 128, 6 + (h % 2))
                    rope(6 + (h % 2), 1.0, None, False, RQ[:, h, :], tt("rq"))
                    tten("pool", QD[:, h, :].rearrange("p (j c) -> p j c", j=4), RQ[:, h, :].rearrange("p (j c) -> p j c", j=4),
                         QDEC[:, h * 128:(h + 1) * 128].unsqueeze(1).broadcast_to([128, 4, 128]), MUL, [tt("rq"), tt("qdec")], [tt("qd")])
                ckpt('rqd%d' % tg)
                ret_kv_inputs(tg)
                wz, twz = ws_get(BLK_ZR)
                wzv = wz.rearrange("p (k c) -> p k c", k=8)
                for h in range(4):
                    bk = 6 + (h % 2)
                    proj_fm(wzv, twz, h * 128, bk)
                    act(ZR[:, h, :], bank(bk), AF.Silu, [TB[bk]], [tt("zr")])
                ckpt('rin%d' % tg)
                for j in range(3, -1, -1):
                    ci = tg * 4 + j
                    sb_i = j % 2
                    for h in range(4):
                        r0 = (h % 2) * 64
                        mm(bank(h % 2)[:, (h // 2) * 128:(h // 2 + 1) * 128], RKT[r0:r0 + 64, h // 2, j * 128:(j + 1) * 128],
                           RQ[r0:r0 + 64, h, j * 128:(j + 1) * 128], True, True, [tt("rkt"), tt("rq")], [TB[h % 2]])
                    for hb in range(2):
                        tten("dve", SM[:, sb_i, :].rearrange("p (a b c) -> p a b c", a=2, b=2)[:, :, hb, :],
                             bank(hb)[:, 0:256].rearrange("p (a c) -> p a c", a=2),
                             DMK[:].rearrange("p (a b c) -> p a b c", a=2, b=2)[:, :, hb, :], MUL, [TB[hb], tt("dmk")], [tt("sm%d" % sb_i)])
                    ckpt('rs%d' % ci)
                    cpy("pool", RRO[0:64, sb_i, :], RF[(ci % 2) * 64:(ci % 2) * 64 + 64, ci // 2, :], [tt("rf")], [tt("rro%d" % sb_i)])
                    cpy("pool", RRO[64:128, sb_i, :], SST[64:128, :], [tt("sst")], [tt("rro%d" % sb_i)])
                    for h in range(4):
                        mm(bank(2)[:, h * 128:(h + 1) * 128], SM[:, sb_i, h * 128:(h + 1) * 128], RVG[:, j, h * 128:(h + 1) * 128],
                           True, False, [tt("sm%d" % sb_i), tt("rvg")], [TB[2]])
                        mm(bank(2)[:, h * 128:(h + 1) * 128], QD[:, h, j * 128:(j + 1) * 128], RRO[:, sb_i, h * 128:(h + 1) * 128],
                           False, True, [tt("qd"), tt("rro%d" % sb_i)], [TB[2]])
                    ckpt('ro%d' % ci)
                    for h in range(4):
                        bnstats(STAT[:, sb_i, h, 0:6], bank(2)[:, h * 128:(h + 1) * 128], [TB[2]], [tt("stat%d" % sb_i)])
                        bnaggr(STAT[:, sb_i, h, 6:8], STAT[:, sb_i, h, 0:6], [tt("stat%d" % sb_i)], [tt("stat%d" % sb_i)])
                    gr = MISC[:, M_GNR + sb_i * 4:M_GNR + sb_i * 4 + 4]
                    gn = MISC[:, M_GNN + sb_i * 4:M_GNN + sb_i * 4 + 4]
                    act(gr, STAT[:, sb_i, :, 7], AF.Ln, [tt("stat%d" % sb_i)], [tt("gr%d" % sb_i)], bias=EPS)
                    act(gr, gr, AF.Exp, [tt("gr%d" % sb_i)], [tt("gr%d" % sb_i)], scale=-0.5)
                    stt("dve", gn, STAT[:, sb_i, :, 6], -1.0, gr, MUL, MUL, [tt("stat%d" % sb_i), tt("gr%d" % sb_i)], [tt("gn%d" % sb_i)])
                    for h in range(4):
                        tsc("dve", ON[:, sb_i, h * 128:(h + 1) * 128], bank(2)[:, h * 128:(h + 1) * 128],
                            MISC[:, M_GNR + sb_i * 4 + h:M_GNR + sb_i * 4 + h + 1], MISC[:, M_GNN + sb_i * 4 + h:M_GNN + sb_i * 4 + h + 1],
                            MUL, ADD, [TB[2], tt("gr%d" % sb_i), tt("gn%d" % sb_i)], [tt("on%d" % sb_i)])
                    ckpt('rgn%d' % ci)
                    ptn = bank(3).bitcast(BF16)[:, 0:512]
                    for h in range(4):
                        trp(ptn[:, h * 128:(h + 1) * 128], ON[:, sb_i, h * 128:(h + 1) * 128], [tt("on%d" % sb_i)], [TB[3]])
                    for h in range(4):
                        stt("dve", YR[:, h, j * 128:(j + 1) * 128], ptn[:, h * 128:(h + 1) * 128], SMALL[:, S_GN + h:S_GN + h + 1],
                            ZR[:, h, j * 128:(j + 1) * 128], MUL, MUL, [TB[3], tt("small"), tt("zr")], [tt("yr")])
                    ckpt('ryr%d' % ci)
                    kv_matmul(j)
                    tten("dve", SST[64:128, :], SST[64:128, :], DEC[64:128, :], MUL, [tt("sst"), tt("dec")], [tt("sst")])
                    tten("dve", SST[64:128, :], SST[64:128, :], bank(7)[64:128, :], ADD, [tt("sst"), TB[7]], [tt("sst")])
                ckpt('ret%d' % tg)
                for n in range(8):
                    wm, twm = ws_get(BLK_M + n)
                    glv = wm[:, 0:2048].rearrange("p (k c) -> p k c", k=8)
                    pav = wm[:, 2048:2560].rearrange("p (k c) -> p k c", k=4)
                    prv = wm[:, 2560:3072].rearrange("p (k c) -> p k c", k=4)
                    gi = n % 2
                    for br in range(2):
                        bk = 6 + br
                        for kc in range(8):
                            mm(bank(bk), glv[:, kc, br * 128:(br + 1) * 128], HT[:, kc, :], kc == 0, kc == 7, [twm, tt("ht")], [TB[bk]])
                        act(GATE[:, gi, br, :], bank(bk), AF.Sigmoid, [TB[bk]], [tt("gate%d_%d" % (gi, br))])
                    for cc in range(4):
                        mm(bank(4), pav[:, cc, :], YA[:, cc, :], cc == 0, cc == 3, [twm, tt("ya")], [TB[4]])
                    for cc in range(4):
                        mm(bank(5), prv[:, cc, :], YR[:, cc, :], cc == 0, cc == 3, [twm, tt("yr")], [TB[5]])
                    tten("dve", MT[:, 0, :], bank(4), GATE[:, gi, 0, :], MUL, [TB[4], tt("gate%d_0" % gi)], [tt("mt0")])
                    tten("dve", MT[:, 1, :], bank(5), GATE[:, gi, 1, :], MUL, [TB[5], tt("gate%d_1" % gi)], [tt("mt1")])
                    tten("pool", MG[:, n, :], MT[:, 0, :], MT[:, 1, :], ADD, [tt("mt0"), tt("mt1")], [tt("mg")])
                ckpt('mrg%d' % tg)
                wo0, two0 = ws_get(BLK_WO)
                wo1, two1 = ws_get(BLK_WO + 1, la=NWB - 1)
                wov = [wo0.rearrange("p (k c) -> p k c", k=8), wo1.rearrange("p (k c) -> p k c", k=8)]
                twos = [two0, two1]
                for j in range(4):
                    tile = tg * 4 + j
                    pb0 = 0 if j % 2 == 0 else 2
                    for hf in range(2):
                        for kc in range(8):
                            mm(bank(pb0 + hf), MG[:, kc, j * 128:(j + 1) * 128], wov[hf][:, kc, :], kc == 0, kc == 7,
                               [tt("mg"), twos[hf]], [TB[pb0 + hf]])
                    xi = load_x(tile)
                    act(JUNK[:], bank(pb0, 2), AF.Square, [TB[pb0], TB[pb0 + 1]], [tt("junk"), tt("ss2")], accum_out=MISC[:, M_SS + 1:M_SS + 2])
                    act(MISC[:, M_RSTD + 1:M_RSTD + 2], MISC[:, M_SS + 1:M_SS + 2], AF.Ln, [tt("ss2")], [tt("rstd2")], scale=1.0 / DM, bias=EPS)
                    act(MISC[:, M_RSTD + 1:M_RSTD + 2], MISC[:, M_RSTD + 1:M_RSTD + 2], AF.Exp, [tt("rstd2")], [tt("rstd2")], scale=-0.5)
                    stt("dve", YT[:], bank(pb0, 2), MISC[:, M_RSTD + 1:M_RSTD + 2], GG[:], MUL, MUL,
                        [TB[pb0], TB[pb0 + 1], tt("rstd2"), tt("gg")], [tt("yt")])
                    tten("pool", XT[:, xi, :], XT[:, xi, :], YT[:], ADD, [tt("xt%d" % xi), tt("yt")], [tt("xt%d" % xi)])
                    dma("pool", out_d[tile * 128:(tile + 1) * 128, :], XT[:, xi, :], [tt("xt%d" % xi)], [])
        try:
            run_all()
        except _Stop:
            pass
        dump_aps = {"HT": HT, "K2": K2, "VX": VX, "RF": RF, "MISC": MISC, "GG": GG, "DMK": DMK, "QDEC": QDEC, "KD": KD,
                    "DEC": DEC, "YA": YA, "YR": YR, "MG": MG, "QT": QT, "ZA": ZA, "RKT": RKT, "RVG": RVG, "SST": SST,
                    "RQ": RQ, "QD": QD, "ZR": ZR, "XN": XN, "CS": CS}
        for nm in dumps:
            src = dump_aps[nm]
            shp = list(src.shape)
            flat = int(np.prod(shp[1:]))
            dd = nc.dram_tensor("dbg_" + nm, [128, flat], src.dtype, kind="ExternalOutput").ap()
            allT = list(tdict.values()) + TB
            sv = src[:] if len(shp) == 2 else src[:].rearrange("p a b -> p (a b)") if len(shp) == 3 else src[:].rearrange("p a b c -> p (a b c)")
            dma("sp", dd, sv, allT, [])
        S.emit(nc)
    return nc, S


_CACHE = {}


def kernel(x, c, w_ada, b_ada, g_pre, w_in, qn_g, kn_g, w_dec_f, w_dec_b, gn_g, w_pa, w_pr, w_out, g_post):
    x = np.asarray(x, np.float32)
    c = np.asarray(c, np.float32)
    f = lambda a: np.asarray(a, np.float32)[0]
    w_ada, b_ada, g_pre, w_in, qn_g, kn_g = f(w_ada), f(b_ada), f(g_pre), f(w_in), f(qn_g), f(kn_g)
    w_dec_f, w_dec_b, gn_g, w_pa, w_pr, w_out, g_post = f(w_dec_f), f(w_dec_b), f(gn_g), f(w_pa), f(w_pr), f(w_out), f(g_post)
    if "nc" not in _CACHE:
        _CACHE["nc"] = build_program()[0]
        _CACHE["consts"] = _host_consts()
    nc = _CACHE["nc"]
    rope, cst = _CACHE["consts"]
    blocks = _host_blocks(w_ada, w_in, w_pa, w_pr, w_out)
    col8 = lambda v: np.ascontiguousarray(v.reshape(8, 128).T)
    rows = np.ascontiguousarray(np.stack([b_ada[2048:3072], g_post], 0))
    wdec = np.concatenate([w_dec_f, w_dec_b])[None, :].astype(np.float32)
    in_maps = []
    for b in range(8):
        small = np.zeros((128, NSMALL), np.float32)
        small[:, 0:8] = col8(c[b])
        small[:, 8:16] = col8(g_pre)
        small[:, 16:24] = col8(b_ada[1024:2048])
        small[:, 24:32] = col8(b_ada[0:1024])
        small[:, 32] = np.concatenate([qn_g, qn_g])
        small[:, 33] = np.concatenate([kn_g, kn_g])
        small[:, 34:38] = gn_g.reshape(4, 128).T
        in_maps.append({"x": np.ascontiguousarray(x[b]), "wsrc": blocks, "rope": rope, "cst": cst,
                        "small": small, "rows": rows, "wdec": wdec})
    res = run_bass_kernel_spmd(nc, in_maps, core_ids=list(range(8)))
    return np.stack([np.asarray(r["out"], np.float32) for r in res.results], 0)
```

```python
import numpy as np
import concourse.bass as bass
import concourse.mybir as mybir
from concourse.bass_utils import run_bass_kernel_spmd

F32 = mybir.dt.float32
BF16 = mybir.dt.bfloat16
AF = mybir.ActivationFunctionType
ALU = mybir.AluOpType
AX = mybir.AxisListType


class T:
    __slots__ = ("name", "last_w", "readers", "excl")

    def __init__(self, name="", excl=False):
        self.name = name
        self.last_w = None
        self.readers = []
        self.excl = excl


class Op:
    __slots__ = ("eng", "fn", "deps", "signal", "token", "is_dma", "idx", "gidx")

    def __init__(self, eng, fn, is_dma):
        self.eng = eng
        self.fn = fn
        self.deps = []
        self.signal = False
        self.token = None
        self.is_dma = is_dma


class Sched:
    ENGS = ("pe", "act", "dve", "pool", "sp")

    def __init__(self, n_dma_sems=8):
        self.ops = {e: [] for e in self.ENGS}
        self.n_dma_sems = n_dma_sems
        self.dma_rr = {e: 0 for e in self.ENGS}
        self.dma_last = {}
        self.gcount = 0

    def add(self, eng, fn, reads=(), writes=(), dma=False):
        op = Op(eng, fn, dma)
        op.gidx = self.gcount
        self.gcount += 1
        writes = list(writes) + [t for t in reads if t.excl]
        reads = [t for t in reads if not t.excl]
        raw = []
        war = []
        for t in reads:
            if t.last_w is not None:
                raw.append(t.last_w)
        for t in writes:
            if t.last_w is not None:
                raw.append(t.last_w)
            war.extend(t.readers)
        for t in reads:
            t.readers.append(op)
        for t in writes:
            t.last_w = op
            t.readers = []
        if dma:
            k = self.dma_rr[eng]
            self.dma_rr[eng] = (k + 1) % self.n_dma_sems
            prev = self.dma_last.get((eng, k))
            if prev is not None:
                raw.append(prev)
            self.dma_last[(eng, k)] = op
            op.token = (eng, k)
        seen = set()
        for lst, is_war in ((raw, False), (war, True)):
            for d in lst:
                if d is op or id(d) in seen:
                    continue
                if (not d.is_dma) and (not dma) and d.eng == eng:
                    if eng == "pe" or is_war:
                        continue
                seen.add(id(d))
                op.deps.append(d)
        op.idx = len(self.ops[eng])
        self.ops[eng].append(op)
        return op

    def emit(self, nc, block_ctx_extra=None):
        for e in self.ENGS:
            for op in self.ops[e]:
                for d in op.deps:
                    d.signal = True
        import contextlib
        with contextlib.ExitStack() as st:
            sems = {e: st.enter_context(nc.semaphore("s_" + e)) for e in ("pe", "act", "dve", "pool")}
            dsems = {}
            for e in self.ENGS:
                if any(o.is_dma for o in self.ops[e]):
                    for k in range(self.n_dma_sems):
                        dsems[(e, k)] = st.enter_context(nc.semaphore("d_%s_%d" % (e, k)))
            for e in self.ENGS:
                cnt = 0
                dcnt = {}
                for op in self.ops[e]:
                    if op.is_dma:
                        key = op.token
                        dcnt[key] = dcnt.get(key, 0) + 16
                        op.token = (dsems[key], dcnt[key], key)
                    elif op.signal:
                        cnt += 1
                        op.token = (sems[e], cnt, e)
            block = st.enter_context(nc.Block())
            handles = {"pe": block.tensor, "act": block.scalar, "dve": block.vector, "pool": block.gpsimd,
                       "sp": block.sync}
            nwaits = {e: 0 for e in self.ENGS}

            def make(e):
                ops = self.ops[e]

                def body(eng):
                    known = {}
                    for op in ops:
                        need = {}
                        for d in op.deps:
                            sem, val, key = d.token
                            if known.get(key, 0) >= val:
                                continue
                            if key not in need or need[key][1] < val:
                                need[key] = (sem, val)
                        for key, (sem, val) in need.items():
                            eng.wait_ge(sem, val)
                            known[key] = val
                            nwaits[e] += 1
                        inst = op.fn(eng)
                        if op.is_dma:
                            inst.then_inc(op.token[0], 16)
                        elif op.signal:
                            inst.then_inc(op.token[0], 1)
                    fin = {}
                    for op in ops:
                        if op.is_dma:
                            fin[op.token[2]] = (op.token[0], op.token[1])
                    for key, (sem, val) in fin.items():
                        if known.get(key, 0) < val:
                            eng.wait_ge(sem, val)
                return body

            for e in self.ENGS:
                if self.ops[e]:
                    handles[e](make(e))
            self.nwaits = nwaits

import contextlib

SEQ = 4096
DM = 1024
NG = 8
EPS = 1e-6

BLK_ADA = 0
BLK_P2KV = 6
BLK_RK = 7
BLK_RV = 8
BLK_Q = 9
BLK_ZA = 10
BLK_RQ = 11
BLK_ZR = 12
BLK_M = 13
BLK_WO = 21
NBLK = 23
NSMALL = 40


def _host_blocks(w_ada, w_in, w_pa, w_pr, w_out):
    blocks = np.zeros((NBLK, 128, 4096), np.float32)

    def kmaj(w):
        nc_ = w.shape[1]
        t = np.zeros((128, 8, 512), np.float32)
        t[:, :, :nc_] = w.reshape(8, 128, nc_).transpose(1, 0, 2)
        return t.reshape(128, 4096)

    wa = w_ada
    for j in range(6):
        blocks[BLK_ADA + j] = kmaj(wa[:, j * 512:(j + 1) * 512])
    o_q, o_k, o_v, o_za, o_rq, o_rk, o_rv, o_zr, o_gl = 0, 512, 640, 768, 1280, 1536, 1792, 2304, 2816
    k0 = w_in[:, o_k:o_k + 64]
    k1 = w_in[:, o_k + 64:o_k + 128]
    blocks[BLK_P2KV] = kmaj(np.concatenate([k0, k0, k1, k1, w_in[:, o_v:o_v + 128]], axis=1))
    blocks[BLK_RK] = kmaj(w_in[:, o_rk:o_rk + 256])
    blocks[BLK_RV] = kmaj(w_in[:, o_rv:o_rv + 512])
    blocks[BLK_Q] = kmaj(w_in[:, o_q:o_q + 512])
    blocks[BLK_ZA] = kmaj(w_in[:, o_za:o_za + 512])
    rq = [w_in[:, o_rq + 64 * h:o_rq + 64 * (h + 1)] for h in range(4)]
    blocks[BLK_RQ] = kmaj(np.concatenate([rq[0], rq[0], rq[1], rq[1], rq[2], rq[2], rq[3], rq[3]], axis=1))
    blocks[BLK_ZR] = kmaj(w_in[:, o_zr:o_zr + 512])
    for n in range(8):
        t = np.zeros((128, 4096), np.float32)
        gl = np.concatenate([w_in[:, o_gl + n * 128:o_gl + (n + 1) * 128],
                             w_in[:, o_gl + 1024 + n * 128:o_gl + 1024 + (n + 1) * 128]], axis=1)
        t[:, 0:2048] = gl.reshape(8, 128, 256).transpose(1, 0, 2).reshape(128, 2048)
        t[:, 2048:2560] = w_pa[:, n * 128:(n + 1) * 128].reshape(4, 128, 128).transpose(1, 0, 2).reshape(128, 512)
        t[:, 2560:3072] = w_pr[:, n * 128:(n + 1) * 128].reshape(4, 128, 128).transpose(1, 0, 2).reshape(128, 512)
        blocks[BLK_M + n] = t
    for hf in range(2):
        blocks[BLK_WO + hf] = kmaj(w_out[:, hf * 512:(hf + 1) * 512])
    return blocks


def _host_consts():
    t = np.arange(SEQ)
    row = (t // 64).astype(np.float32)
    col = (t % 64).astype(np.float32)
    inv = (10000.0 ** (-np.arange(0, 32, 2, dtype=np.float32) / 32.0)).astype(np.float32)
    ang_r = row[None, :] * inv[:, None]
    ang_c = col[None, :] * inv[:, None]
    cos64 = np.concatenate([np.cos(ang_r), np.cos(ang_r), np.cos(ang_c), np.cos(ang_c)], 0)
    sin64 = np.concatenate([np.sin(ang_r), np.sin(ang_r), np.sin(ang_c), np.sin(ang_c)], 0)
    rope = np.stack([np.concatenate([cos64, cos64], 0), np.concatenate([sin64, sin64], 0)], 0).astype(np.float32)
    ident = np.eye(128, dtype=np.float32)
    perm = np.zeros((128, 128), np.float32)
    for hb in range(2):
        for d in range(64):
            blk = d // 16
            if blk % 2 == 0:
                p, s = d + 16, -1.0
            else:
                p, s = d - 16, 1.0
            perm[hb * 64 + p, hb * 64 + d] = s
    ones = np.zeros((128, 128), np.float32)
    ones[:64, :64] = 1.0
    ones[64:, 64:] = 1.0
    return rope, np.stack([ident, perm, ones], 0)


class _Stop(Exception):
    pass


def build_program(stop=None, dumps=()):
    nc = bass.Bass("TRN2", target_bir_lowering=False)
    x_d = nc.dram_tensor("x", [SEQ, DM], F32, kind="ExternalInput").ap()
    wsrc = nc.dram_tensor("wsrc", [NBLK, 128, 4096], F32, kind="ExternalInput").ap()
    rope_d = nc.dram_tensor("rope", [2, 128, SEQ], F32, kind="ExternalInput").ap()
    cst_d = nc.dram_tensor("cst", [3, 128, 128], F32, kind="ExternalInput").ap()
    small_d = nc.dram_tensor("small", [128, NSMALL], F32, kind="ExternalInput").ap()
    rows_d = nc.dram_tensor("rows", [2, DM], F32, kind="ExternalInput").ap()
    wdec_d = nc.dram_tensor("wdec", [1, 8], F32, kind="ExternalInput").ap()
    out_d = nc.dram_tensor("out", [SEQ, DM], F32, kind="ExternalOutput").ap()
    wbf_d = nc.dram_tensor("wbf", [NBLK, 128, 4096], BF16, kind="Internal").ap()

    S = Sched(n_dma_sems=12)
    tdict = {}

    _alias = {"io0": "sa", "io1": "sa", "io2": "sa", "tmpd": "sa", "cbc": "mt1", "junk": "mt0"}

    def tt(name):
        name = _alias.get(name, name)
        if name not in tdict:
            tdict[name] = T(name)
        return tdict[name]

    with contextlib.ExitStack() as st:
        def sb(name, shape, dt):
            return st.enter_context(nc.sbuf_tensor(name, shape, dt))

        PS = st.enter_context(nc.psum_tensor("ps", [128, 4096], F32))
        TB = [T("bank%d" % i, excl=True) for i in range(8)]

        def bank(i, n=1):
            return PS[:, i * 512:(i + n) * 512]

        K2 = sb("K2", [128, 2, SEQ], BF16)
        VX = sb("VX", [128, 32, 320], BF16)
        RF = sb("RF", [128, 16, 512], BF16)
        NWB = 3
        WB = sb("WB", [128, NWB, 4096], BF16)
        XT = sb("XT", [128, 2, DM], F32)
        XN = sb("XN", [128, 2, DM], BF16)
        HT = sb("HT", [128, 8, 512], BF16)
        CS = sb("CS", [128, 2, 512], F32)
        QBF = sb("QBF", [128, 2, 512], BF16)
        SQ = sb("SQ", [128, 2, 512], BF16)
        T1 = sb("T1", [128, 2, 512], F32)
        T2 = sb("T2", [128, 2, 512], F32)
        RRS = sb("RRS", [128, 2, 512], F32)
        QT = sb("QT", [128, 4, 512], BF16)
        ZA = sb("ZA", [128, 4, 512], BF16)
        NPT = 3
        PT = sb("PT", [128, NPT, 1024], BF16)
        SA = sb("SA", [128, 512], F32)
        SB_ = sb("SB_", [128, 512], F32)
        DEN = sb("DEN", [128, 512], F32)
        TMPN = sb("TMPN", [128, 512], F32)
        YA = sb("YA", [128, 4, 512], BF16)
        YR = sb("YR", [128, 4, 512], BF16)
        RQ = sb("RQ", [128, 4, 512], BF16)
        QD = sb("QD", [128, 4, 512], BF16)
        ZR = sb("ZR", [128, 4, 512], BF16)
        RKT = sb("RKT", [128, 2, 512], BF16)
        RVG = sb("RVG", [128, 4, 512], BF16)
        KK = sb("KK", [128, 2, 512], BF16)
        SM = sb("SM", [128, 2, 512], BF16)
        RRO = sb("RRO", [128, 2, 512], BF16)
        ON = sb("ON", [128, 2, 512], BF16)
        SST = sb("SST", [128, 512], F32)
        GATE = sb("GATE", [128, 2, 2, 512], BF16)
        MT = sb("MT", [128, 2, 512], F32)
        MG = sb("MG", [128, 8, 512], BF16)
        YT = sb("YT", [128, DM], F32)
        IDENT = sb("IDENT", [128, 128], BF16)
        PERM = sb("PERM", [128, 128], BF16)
        ONESB = sb("ONESB", [128, 128], BF16)
        DMK = sb("DMK", [128, 512], F32)
        QDEC = sb("QDEC", [128, 512], F32)
        KD = sb("KD", [128, 512], F32)
        DEC = sb("DEC", [128, 512], F32)
        GG = sb("GG", [128, DM], F32)
        SMALL = sb("SMALL", [128, NSMALL], F32)
        MISC = sb("MISC", [128, 128], F32)
        IO = SA[:, 0:384].rearrange("p (a b) -> p a b", a=3)
        TMPD = SA[:, 384:512]
        CBC = MT[:, 1, :].bitcast(BF16).rearrange("p (a b) -> p a b", a=8)
        JUNK = MT[:, 0, :].bitcast(BF16)
        CBF = sb("CBF", [128, 8], BF16)
        STAT = sb("STAT", [128, 2, 4, 8], F32)

        M_ACOL, M_SHCOL, M_LG, M_DC, M_WD, M_E, M_KD8, M_KDV, M_IOP, M_IOPR, M_GK8 = 0, 8, 16, 24, 32, 40, 48, 56, 64, 65, 66
        M_SS, M_RSTD, M_CACT, M_TMP8 = 70, 74, 80, 88
        M_GNR, M_GNN = 96, 104
        S_C, S_GPRE, S_BSC, S_BSH, S_QG, S_KG, S_GN = 0, 8, 16, 24, 32, 33, 34

        def mm(o, l, r, start, stop, rd, wr):
            S.add("pe", lambda e: e.matmul(o, l, r, start=start, stop=stop), reads=rd, writes=wr)

        def trp(o, i, rd, wr):
            S.add("pe", lambda e: e.transpose(o, i, IDENT[:]), reads=rd + [tt("ident")], writes=wr)

        def act(o, i, f, rd, wr, **kw):
            S.add("act", lambda e: e.activation(out=o, in_=i, func=f, **kw), reads=rd, writes=wr)

        def tten(eng, o, a, b, op, rd, wr):
            S.add(eng, lambda e: e.tensor_tensor(out=o, in0=a, in1=b, op=op), reads=rd, writes=wr)

        def tsc(eng, o, a, s1, s2, op0, op1, rd, wr):
            if op1 is None:
                S.add(eng, lambda e: e.tensor_scalar(out=o, in0=a, scalar1=s1, scalar2=None, op0=op0), reads=rd, writes=wr)
            else:
                S.add(eng, lambda e: e.tensor_scalar(out=o, in0=a, scalar1=s1, scalar2=s2, op0=op0, op1=op1), reads=rd, writes=wr)

        def stt(eng, o, a, s, b, op0, op1, rd, wr):
            S.add(eng, lambda e: e.scalar_tensor_tensor(out=o, in0=a, scalar=s, in1=b, op0=op0, op1=op1), reads=rd, writes=wr)

        def cpy(eng, o, i, rd, wr):
            S.add(eng, lambda e: e.tensor_copy(out=o, in_=i), reads=rd, writes=wr)

        def dma(eng, o, i, rd, wr):
            S.add(eng, lambda e: e.dma_start(out=o, in_=i), reads=rd, writes=wr, dma=True)

        MUL, ADD, MAX = ALU.mult, ALU.add, ALU.max

        def ckpt(name):
            if stop == name:
                raise _Stop()

        def bnstats(o, i, rd, wr):
            S.add("dve", lambda e: e.bn_stats(out=o, in_=i), reads=rd, writes=wr)

        def bnaggr(o, i, rd, wr):
            S.add("dve", lambda e: e.bn_aggr(out=o, in_=i), reads=rd, writes=wr)

        def recip(o, i, rd, wr):
            S.add("dve", lambda e: e.reciprocal(out=o, in_=i), reads=rd, writes=wr)

        order = [BLK_ADA + j for j in range(6)]
        for tg in range(NG):
            order += [BLK_P2KV, BLK_RK, BLK_RV]
        for tg in range(NG):
            order += [BLK_Q, BLK_ZA, BLK_RQ, BLK_RK, BLK_RV, BLK_ZR] + [BLK_M + n for n in range(8)] + [BLK_WO, BLK_WO + 1]
        conv_done = set()
        ws_state = {"issued": 0, "cur": 0}

        def conv(blk):
            if blk in conv_done:
                return
            conv_done.add(blk)
            dma("pool", wbf_d[blk], wsrc[blk], [], [tt("wbf%d" % blk)])

        def ws_issue_upto(n):
            while ws_state["issued"] < min(n, len(order)):
                i = ws_state["issued"]
                blk = order[i]
                conv(blk)
                dma("sp", WB[:, i % NWB, :], wbf_d[blk], [tt("wbf%d" % blk)], [tt("wb%d" % (i % NWB))])
                ws_state["issued"] += 1

        def ws_get(expect, la=None):
            i = ws_state["cur"]
            assert order[i] == expect, (i, order[i], expect)
            ws_issue_upto(i + (NWB if la is None else la))
            ws_state["cur"] += 1
            return WB[:, i % NWB, :], tt("wb%d" % (i % NWB))

        def run_all():
            for blk in [BLK_ADA + j for j in range(6)] + [BLK_P2KV, BLK_RK, BLK_RV]:
                conv(blk)
            dma("sp", SMALL[:], small_d, [], [tt("small")])
            dma("sp", GG[:], rows_d[0:1, :].partition_broadcast(128), [], [tt("gg")])
            dma("sp", YT[:], rows_d[1:2, :].partition_broadcast(128), [], [tt("yt")])
            dma("sp", MISC[:, M_WD:M_WD + 8], wdec_d.partition_broadcast(128), [], [tt("wd")])
            dma("pool", IDENT[:], cst_d[0], [], [tt("ident")])
            dma("pool", PERM[:], cst_d[1], [], [tt("perm")])
            dma("pool", ONESB[:], cst_d[2], [], [tt("onesb")])
            ws_issue_upto(NWB)

            act(MISC[:, M_CACT:M_CACT + 8], SMALL[:, S_C:S_C + 8], AF.Silu, [tt("small")], [tt("cact")])
            cpy("dve", CBF[:], MISC[:, M_CACT:M_CACT + 8], [tt("cact")], [tt("cbf")])
            cpy("dve", CBC, MISC[:, M_CACT:M_CACT + 8].unsqueeze(2).broadcast_to([128, 8, 128]), [tt("cact")], [tt("cbc")])
            act(MISC[:, M_E:M_E + 8], MISC[:, M_WD:M_WD + 8], AF.Exp, [tt("wd")], [tt("e")], scale=-1.0)
            act(MISC[:, M_E:M_E + 8], MISC[:, M_E:M_E + 8], AF.Ln, [tt("e")], [tt("e")], bias=1.0)
            tsc("dve", MISC[:, M_LG:M_LG + 8], MISC[:, M_E:M_E + 8], -1.0, None, MUL, None, [tt("e")], [tt("lg")])
            LGF = lambda h, lo=0, hi=128: MISC[lo:hi, M_LG + h:M_LG + h + 1]
            LGB = lambda h, lo=0, hi=128: MISC[lo:hi, M_LG + 4 + h:M_LG + 5 + h]
            S.add("pool", lambda e: e.iota(IO[:, 0, :], pattern=[[1, 128]], base=0, channel_multiplier=-1,
                                           allow_small_or_imprecise_dtypes=True), writes=[tt("io0")])
            S.add("pool", lambda e: e.iota(IO[:, 1, :], pattern=[[1, 128]], base=1, channel_multiplier=0,
                                           allow_small_or_imprecise_dtypes=True), writes=[tt("io1")])
            S.add("pool", lambda e: e.iota(IO[:, 2, :], pattern=[[-1, 128]], base=128, channel_multiplier=0,
                                           allow_small_or_imprecise_dtypes=True), writes=[tt("io2")])
            S.add("pool", lambda e: e.iota(MISC[:, M_IOP:M_IOP + 1], pattern=[[1, 1]], base=0, channel_multiplier=1,
                                           allow_small_or_imprecise_dtypes=True), writes=[tt("iop")])
            S.add("pool", lambda e: e.iota(MISC[:, M_IOPR:M_IOPR + 1], pattern=[[1, 1]], base=127, channel_multiplier=-1,
                                           allow_small_or_imprecise_dtypes=True), writes=[tt("iopr")])
            NEG = RRS[:, 0, 0:128]
            POS = RRS[:, 0, 128:256]
            tsc("dve", POS, IO[:, 0, :], 0.0, None, MAX, None, [tt("io0")], [tt("pos")])
            tten("dve", NEG, POS, IO[:, 0, :], ALU.subtract, [tt("pos"), tt("io0")], [tt("neg")])
            for h in range(4):
                tsc("dve", TMPD, POS, LGF(h), None, MUL, None, [tt("pos"), tt("lg")], [tt("tmpd")])
                stt("dve", TMPD, NEG, LGB(h), TMPD, MUL, ADD, [tt("neg"), tt("lg"), tt("tmpd")], [tt("tmpd")])
                act(DMK[:, h * 128:(h + 1) * 128], TMPD, AF.Exp, [tt("tmpd")], [tt("dmk")])
                tsc("dve", TMPD[0:64, :], IO[0:64, 1, :], LGF(h, 0, 64), None, MUL, None, [tt("io1"), tt("lg"), tt("dmk")], [tt("tmpd")])
                tsc("dve", TMPD[64:128, :], IO[64:128, 2, :], LGB(h, 64, 128), None, MUL, None, [tt("io2"), tt("lg"), tt("tmpd")], [tt("tmpd")])
                act(QDEC[:, h * 128:(h + 1) * 128], TMPD, AF.Exp, [tt("tmpd")], [tt("qdec")])
            KDV = MISC[:, M_KDV:M_KDV + 8].rearrange("p (h d) -> p h d", d=2)
            tsc("dve", MISC[:, M_KD8:M_KD8 + 4], MISC[:, M_LG:M_LG + 4], MISC[:, M_IOPR:M_IOPR + 1], None, MUL, None,
                [tt("lg"), tt("iopr")], [tt("kd8")])
            tsc("dve", MISC[:, M_KD8 + 4:M_KD8 + 8], MISC[:, M_LG + 4:M_LG + 8], MISC[:, M_IOP:M_IOP + 1], None, MUL, None,
                [tt("lg"), tt("iop"), tt("kd8")], [tt("kd8")])
            act(KDV[:, :, 0], MISC[:, M_KD8:M_KD8 + 4], AF.Exp, [tt("kd8")], [tt("kdv")])
            act(KDV[:, :, 1], MISC[:, M_KD8 + 4:M_KD8 + 8], AF.Exp, [tt("kd8"), tt("kdv")], [tt("kdv")])
            cpy("dve", KD[:].rearrange("p (a c) -> p a c", c=64),
                MISC[:, M_KDV:M_KDV + 8].unsqueeze(2).broadcast_to([128, 8, 64]), [tt("kdv")], [tt("kd")])
            act(MISC[:, M_DC:M_DC + 8], MISC[:, M_LG:M_LG + 8], AF.Exp, [tt("lg")], [tt("dc")], scale=128.0)
            cpy("dve", DEC[0:64, :].rearrange("p (h c) -> p h c", c=128),
                MISC[0:64, M_DC:M_DC + 4].unsqueeze(2).broadcast_to([64, 4, 128]), [tt("dc")], [tt("dec")])
            cpy("dve", DEC[64:128, :].rearrange("p (h c) -> p h c", c=128),
                MISC[64:128, M_DC + 4:M_DC + 8].unsqueeze(2).broadcast_to([64, 4, 128]), [tt("dc"), tt("dec")], [tt("dec")])
            tsc("dve", MISC[:, M_GK8:M_GK8 + 1], SMALL[:, S_KG:S_KG + 1], 8.0, None, MUL, None, [tt("small")], [tt("gk8")])
            S.add("pool", lambda e: e.memset(SST[:], 0.0), writes=[tt("sst")])
            S.add("pool", lambda e: e.memset(VX[:], 1.0), writes=[tt("vx%d" % g) for g in range(NG)])

            ckpt('setup0')
            pmod = bank(6)[:, 0:16]
            for blk in range(4):
                wb, twb = ws_get(BLK_ADA + blk)
                wbv = wb.rearrange("p (k c) -> p k c", k=8)
                for c in range(4):
                    for kc in range(8):
                        mm(pmod[:, blk * 4 + c:blk * 4 + c + 1], wbv[:, kc, c * 128:(c + 1) * 128], CBF[:, kc:kc + 1],
                           kc == 0, kc == 7, [twb, tt("cbf")], [TB[6]])
            tten("dve", MISC[:, M_TMP8:M_TMP8 + 8], pmod[:, 8:16], SMALL[:, S_BSC:S_BSC + 8], ADD, [TB[6], tt("small")], [tt("tmp8")])
            stt("dve", MISC[:, M_ACOL:M_ACOL + 8], MISC[:, M_TMP8:M_TMP8 + 8], 1.0, SMALL[:, S_GPRE:S_GPRE + 8], ADD, MUL,
                [tt("tmp8"), tt("small")], [tt("acol")])
            tten("dve", MISC[:, M_SHCOL:M_SHCOL + 8], pmod[:, 0:8], SMALL[:, S_BSH:S_BSH + 8], ADD, [TB[6], tt("small")], [tt("shcol")])
            for hf in range(2):
                wb, twb = ws_get(BLK_ADA + 4 + hf)
                wbv = wb.rearrange("p (k c) -> p k c", k=8)
                for kc in range(8):
                    mm(bank(4 + hf), CBC[:, kc, :], wbv[:, kc, :], kc == 0, kc == 7, [twb, tt("cbc")], [TB[4 + hf]])
                tten("dve", GG[:, hf * 512:(hf + 1) * 512], bank(4 + hf), GG[:, hf * 512:(hf + 1) * 512], ADD,
                     [TB[4 + hf], tt("gg")], [tt("gg")])
                tten("dve", GG[:, hf * 512:(hf + 1) * 512], GG[:, hf * 512:(hf + 1) * 512], YT[:, hf * 512:(hf + 1) * 512], MUL,
                     [tt("gg"), tt("yt")], [tt("gg")])

            ckpt('ada')
            xt_rr = {"i": 0}

            def load_x(tile):
                i = xt_rr["i"] % 2
                xt_rr["i"] += 1
                dma("sp", XT[:, i, :], x_d[tile * 128:(tile + 1) * 128, :], [], [tt("xt%d" % i)])
                return i

            def make_hT(tg):
                HTv = HT
                for j in range(4):
                    tile = tg * 4 + j
                    xi = load_x(tile)
                    xb = j % 2
                    act(JUNK, XT[:, xi, :], AF.Square, [tt("xt%d" % xi)], [tt("junk"), tt("ss")], accum_out=MISC[:, M_SS:M_SS + 1])
                    act(MISC[:, M_RSTD:M_RSTD + 1], MISC[:, M_SS:M_SS + 1], AF.Ln, [tt("ss")], [tt("rstd")], scale=1.0 / DM, bias=EPS)
                    act(MISC[:, M_RSTD:M_RSTD + 1], MISC[:, M_RSTD:M_RSTD + 1], AF.Exp, [tt("rstd")], [tt("rstd")], scale=-0.5)
                    tsc("dve", XN[:, xb, :], XT[:, xi, :], MISC[:, M_RSTD:M_RSTD + 1], None, MUL, None,
                        [tt("xt%d" % xi), tt("rstd")], [tt("xn%d" % xb)])
                    for kc in range(8):
                        pb = bank(kc // 2).bitcast(BF16)[:, (kc % 2) * 512 + j * 128:(kc % 2) * 512 + (j + 1) * 128]
                        trp(pb, XN[:, xb, kc * 128:(kc + 1) * 128], [tt("xn%d" % xb)], [TB[kc // 2]])
                for kc in range(8):
                    pb = bank(kc // 2).bitcast(BF16)[:, (kc % 2) * 512:(kc % 2 + 1) * 512]
                    if kc < 4:
                        tsc("dve", HTv[:, kc, :], pb, MISC[:, M_ACOL + kc:M_ACOL + kc + 1], MISC[:, M_SHCOL + kc:M_SHCOL + kc + 1],
                            MUL, ADD, [TB[kc // 2], tt("acol"), tt("shcol")], [tt("ht")])
                    else:
                        act(HTv[:, kc, :], pb, AF.Identity, [TB[kc // 2], tt("acol"), tt("shcol")], [tt("ht")],
                            scale=MISC[:, M_ACOL + kc:M_ACOL + kc + 1], bias=MISC[:, M_SHCOL + kc:M_SHCOL + kc + 1])

            def load_cs(tg):
                dma("sp", CS[:, 0, :], rope_d[0][:, tg * 512:(tg + 1) * 512], [], [tt("cs")])
                dma("sp", CS[:, 1, :], rope_d[1][:, tg * 512:(tg + 1) * 512], [], [tt("cs")])

            def proj_fm(wbv, twb, c0, bk):
                for kc in range(8):
                    mm(bank(bk), wbv[:, kc, c0:c0 + 128], HT[:, kc, :], kc == 0, kc == 7, [twb, tt("ht")], [TB[bk]])

            def proj_tm(wbv, twb, j, c0, ncol, o, tbk):
                for kc in range(8):
                    mm(o, HT[:, kc, j * 128:(j + 1) * 128], wbv[:, kc, c0:c0 + ncol], kc == 0, kc == 7, [twb, tt("ht")], [tbk])

            rope_ctr = {"i": 0}

            def rope(bkA, gcol, gT, rms, out_ap, out_T):
                i = rope_ctr["i"] % 2
                rope_ctr["i"] += 1
                bS, bC = (4, 5) if i == 0 else (2, 3)
                A = bank(bkA)
                rdg = [gT] if gT is not None else []
                gsc = gcol if gT is not None else float(gcol)
                tsc("dve", QBF[:, i, :], A, gsc, None, MUL, None, [TB[bkA]] + rdg, [tt("qbf%d" % i)])
                if rms:
                    act(SQ[:, i, :], A, AF.Square, [TB[bkA]], [tt("sq%d" % i)])
                mm(bank(bS), PERM[:], QBF[:, i, :], True, True, [tt("perm"), tt("qbf%d" % i)], [TB[bS]])
                if rms:
                    mm(bank(bC), ONESB[:], SQ[:, i, :], True, True, [tt("onesb"), tt("sq%d" % i)], [TB[bC]])
                stt("dve", T1[:, i, :], A, gsc, CS[:, 0, :], MUL, MUL, [TB[bkA], tt("cs")] + rdg, [tt("t1_%d" % i)])
                tten("dve", T2[:, i, :], bank(bS), CS[:, 1, :], MUL, [TB[bS], tt("cs")], [tt("t2_%d" % i)])
                if rms:
                    act(RRS[:, i, :], bank(bC), AF.Ln, [TB[bC]], [tt("rrs%d" % i)], bias=64.0 * EPS)
                    act(RRS[:, i, :], RRS[:, i, :], AF.Exp, [tt("rrs%d" % i)], [tt("rrs%d" % i)], scale=-0.5)
                    tten("pool", T1[:, i, :], T1[:, i, :], T2[:, i, :], ADD, [tt("t1_%d" % i), tt("t2_%d" % i)], [tt("t1_%d" % i)])
                    tten("pool", out_ap, T1[:, i, :], RRS[:, i, :], MUL, [tt("t1_%d" % i), tt("rrs%d" % i)], [out_T])
                else:
                    tten("pool", out_ap, T1[:, i, :], T2[:, i, :], ADD, [tt("t1_%d" % i), tt("t2_%d" % i)], [out_T])

            def ret_kv_inputs(tg):
                wb, twb = ws_get(BLK_RK)
                wbv = wb.rearrange("p (k c) -> p k c", k=8)
                for pr in range(2):
                    proj_fm(wbv, twb, pr * 128, 6 + pr)
                    rope(6 + pr, 0.125, None, False, RKT[:, pr, :], tt("rkt"))
                wb, twb = ws_get(BLK_RV)
                wbv = wb.rearrange("p (k c) -> p k c", k=8)
                for j in range(4):
                    bk = 6 + (j % 2)
                    proj_tm(wbv, twb, j, 0, 512, bank(bk), TB[bk])
                    act(RVG[:, j, :], bank(bk), AF.Copy, [TB[bk]], [tt("rvg")])

            def kv_matmul(j):
                ptk = bank(6).bitcast(BF16)[:, 0:256]
                for pr in range(2):
                    trp(ptk[:, pr * 128:(pr + 1) * 128], RKT[:, pr, j * 128:(j + 1) * 128], [tt("rkt")], [TB[6]])
                kb = j % 2
                tten("dve", KK[:, kb, :].rearrange("p (h d c) -> p h d c", h=4, d=2),
                     ptk.rearrange("p (h c) -> p h c", h=4).unsqueeze(2).broadcast_to([128, 4, 2, 64]),
                     KD[:].rearrange("p (h d c) -> p h d c", h=4, d=2), MUL, [TB[6], tt("kd")], [tt("kk%d" % kb)])
                for h in range(4):
                    mm(bank(7)[:, h * 128:(h + 1) * 128], KK[:, kb, h * 128:(h + 1) * 128], RVG[:, j, h * 128:(h + 1) * 128],
                       True, True, [tt("kk%d" % kb), tt("rvg")], [TB[7]])

            for tg in range(NG):
                make_hT(tg)
                ckpt('ht%d' % tg)
                load_cs(tg)
                wb, twb = ws_get(BLK_P2KV)
                wbv = wb.rearrange("p (k c) -> p k c", k=8)
                for g in range(2):
                    proj_fm(wbv, twb, g * 128, 6 + g)
                    rope(6 + g, MISC[:, M_GK8:M_GK8 + 1], tt("gk8"), True, K2[:, g, tg * 512:(tg + 1) * 512], tt("k2_%d_%d" % (g, tg)))
                ckpt('k%d' % tg)
                for j in range(4):
                    proj_tm(wbv, twb, j, 256, 128, bank(6)[:, j * 128:(j + 1) * 128], TB[6])
                cpy("dve", VX[:, tg * 4:(tg + 1) * 4, 64:320].rearrange("p j (g c) -> p j g c", g=2)[:, :, :, 0:64],
                    bank(6).rearrange("p (j g c) -> p j g c", j=4, g=2), [TB[6]], [tt("vx%d" % tg)])
                ckpt('v%d' % tg)
                ret_kv_inputs(tg)
                ckpt('rkv%d' % tg)
                for j in range(4):
                    ci = tg * 4 + j
                    kv_matmul(j)
                    cpy("pool", RF[(ci % 2) * 64:(ci % 2) * 64 + 64, ci // 2, :], SST[0:64, :], [tt("sst")], [tt("rf")])
                    tten("dve", SST[0:64, :], SST[0:64, :], DEC[0:64, :], MUL, [tt("sst"), tt("dec")], [tt("sst")])
                    tten("dve", SST[0:64, :], SST[0:64, :], bank(7)[0:64, :], ADD, [tt("sst"), TB[7]], [tt("sst")])

            ckpt('p2')
            for tg in range(NG - 1, -1, -1):
                make_hT(tg)
                load_cs(tg)
                wq, twq = ws_get(BLK_Q)
                wqv = wq.rearrange("p (k c) -> p k c", k=8)
                for p in range(4):
                    proj_fm(wqv, twq, p * 128, 6 + (p % 2))
                    rope(6 + (p % 2), SMALL[:, S_QG:S_QG + 1], tt("small"), True, QT[:, p, :], tt("qt%d" % p))
                wz, twz = ws_get(BLK_ZA)
                wzv = wz.rearrange("p (k c) -> p k c", k=8)
                for p in range(4):
                    bk = 6 + (p % 2)
                    proj_fm(wzv, twz, p * 128, bk)
                    act(ZA[:, p, :], bank(bk), AF.Silu, [TB[bk]], [tt("za%d" % p)])
                ckpt('q%d' % tg)
                for p in range(4):
                    g = p // 2

                    def qk(kc):
                        si = kc % 2
                        mm(bank(2 * si), K2[0:64, g, kc * 128:(kc + 1) * 128], QT[0:64, p, :], True, True,
                           [tt("k2_%d_%d" % (g, kc // 4)), tt("qt%d" % p)], [TB[2 * si]])
                        mm(bank(2 * si + 1), K2[64:128, g, kc * 128:(kc + 1) * 128], QT[64:128, p, :], True, True,
                           [tt("k2_%d_%d" % (g, kc // 4)), tt("qt%d" % p)], [TB[2 * si + 1]])

                    def pv(kc):
                        pi = kc % NPT
                        mm(bank(4), VX[:, kc, 64 + g * 128:192 + g * 128], PT[:, pi, 0:512], kc == 0, kc == 31,
                           [tt("vx%d" % (kc // 4)), tt("pt%d" % pi)], [TB[4]])
                        mm(bank(5), VX[:, kc, g * 128:g * 128 + 128], PT[:, pi, 512:1024], kc == 0, kc == 31,
                           [tt("vx%d" % (kc // 4)), tt("pt%d" % pi)], [TB[5]])

                    qk(0)
                    qk(1)
                    for kc in range(32):
                        si = kc % 2
                        act(PT[:, kc % NPT, :], bank(2 * si, 2), AF.Exp, [TB[2 * si], TB[2 * si + 1]], [tt("pt%d" % (kc % NPT))])
                        if kc + 2 < 32:
                            qk(kc + 2)
                        pv(kc)
                    cpy("dve", SA[:], bank(4), [TB[4]], [tt("sa")])
                    cpy("dve", SB_[:], bank(5), [TB[5]], [tt("sb")])
                    recip(DEN[0:64, :], SA[64:128, :], [tt("sa")], [tt("den")])
                    recip(DEN[64:128, :], SB_[0:64, :], [tt("sb")], [tt("den")])
                    tten("pool", TMPN[0:64, :], SA[0:64, :], DEN[0:64, :], MUL, [tt("sa"), tt("den")], [tt("tmpn")])
                    tten("pool", TMPN[64:128, :], SB_[64:128, :], DEN[64:128, :], MUL, [tt("sb"), tt("den")], [tt("tmpn")])
                    tten("pool", YA[:, p, :], TMPN[:], ZA[:, p, :], MUL, [tt("tmpn"), tt("za%d" % p)], [tt("ya")])
                ckpt('att%d' % tg)
                wr, twr = ws_get(BLK_RQ)
                wrv = wr.rearrange("p (k c) -> p k c", k=8)
                for h in range(4):
                    proj_fm(wrv, twr, h * 128, 6 + (h % 2))
                    rope(6 + (h % 2), 1.0, None, False, RQ[:, h, :], tt("rq"))
                    tten("pool", QD[:, h, :].rearrange("p (j c) -> p j c", j=4), RQ[:, h, :].rearrange("p (j c) -> p j c", j=4),
                         QDEC[:, h * 128:(h + 1) * 128].unsqueeze(1).broadcast_to([128, 4, 128]), MUL, [tt("rq"), tt("qdec")], [tt("qd")])
                ckpt('rqd%d' % tg)
                ret_kv_inputs(tg)
                wz, twz = ws_get(BLK_ZR)
                wzv = wz.rearrange("p (k c) -> p k c", k=8)
                for h in range(4):
                    bk = 6 + (h % 2)
                    proj_fm(wzv, twz, h * 128, bk)
                    act(ZR[:, h, :], bank(bk), AF.Silu, [TB[bk]], [tt("zr")])
                ckpt('rin%d' % tg)
                for j in range(3, -1, -1):
                    ci = tg * 4 + j
                    sb_i = j % 2
                    for h in range(4):
                        r0 = (h % 2) * 64
                        mm(bank(h % 2)[:, (h // 2) * 128:(h // 2 + 1) * 128], RKT[r0:r0 + 64, h // 2, j * 128:(j + 1) * 128],
                           RQ[r0:r0 + 64, h, j * 128:(j + 1) * 128], True, True, [tt("rkt"), tt("rq")], [TB[h % 2]])
                    for hb in range(2):
                        tten("dve", SM[:, sb_i, :].rearrange("p (a b c) -> p a b c", a=2, b=2)[:, :, hb, :],
                             bank(hb)[:, 0:256].rearrange("p (a c) -> p a c", a=2),
                             DMK[:].rearrange("p (a b c) -> p a b c", a=2, b=2)[:, :, hb, :], MUL, [TB[hb], tt("dmk")], [tt("sm%d" % sb_i)])
                    ckpt('rs%d' % ci)
                    cpy("pool", RRO[0:64, sb_i, :], RF[(ci % 2) * 64:(ci % 2) * 64 + 64, ci // 2, :], [tt("rf")], [tt("rro%d" % sb_i)])
                    cpy("pool", RRO[64:128, sb_i, :], SST[64:128, :], [tt("sst")], [tt("rro%d" % sb_i)])
                    for h in range(4):
                        mm(bank(2)[:, h * 128:(h + 1) * 128], SM[:, sb_i, h * 128:(h + 1) * 128], RVG[:, j, h * 128:(h + 1) * 128],
                           True, False, [tt("sm%d" % sb_i), tt("rvg")], [TB[2]])
                        mm(bank(2)[:, h * 128:(h + 1) * 128], QD[:, h, j * 128:(j + 1) * 128], RRO[:, sb_i, h * 128:(h + 1) * 128],
                           False, True, [tt("qd"), tt("rro%d" % sb_i)], [TB[2]])
                    ckpt('ro%d' % ci)
                    for h in range(4):
                        bnstats(STAT[:, sb_i, h, 0:6], bank(2)[:, h * 128:(h + 1) * 128], [TB[2]], [tt("stat%d" % sb_i)])
                        bnaggr(STAT[:, sb_i, h, 6:8], STAT[:, sb_i, h, 0:6], [tt("stat%d" % sb_i)], [tt("stat%d" % sb_i)])
                    gr = MISC[:, M_GNR + sb_i * 4:M_GNR + sb_i * 4 + 4]
                    gn = MISC[:, M_GNN + sb_i * 4:M_GNN + sb_i * 4 + 4]
                    act(gr, STAT[:, sb_i, :, 7], AF.Ln, [tt("stat%d" % sb_i)], [tt("gr%d" % sb_i)], bias=EPS)
                    act(gr, gr, AF.Exp, [tt("gr%d" % sb_i)], [tt("gr%d" % sb_i)], scale=-0.5)
                    stt("dve", gn, STAT[:, sb_i, :, 6], -1.0, gr, MUL, MUL, [tt("stat%d" % sb_i), tt("gr%d" % sb_i)], [tt("gn%d" % sb_i)])
                    for h in range(4):
                        tsc("dve", ON[:, sb_i, h * 128:(h + 1) * 128], bank(2)[:, h * 128:(h + 1) * 128],
                            MISC[:, M_GNR + sb_i * 4 + h:M_GNR + sb_i * 4 + h + 1], MISC[:, M_GNN + sb_i * 4 + h:M_GNN + sb_i * 4 + h + 1],
                            MUL, ADD, [TB[2], tt("gr%d" % sb_i), tt("gn%d" % sb_i)], [tt("on%d" % sb_i)])
                    ckpt('rgn%d' % ci)
                    ptn = bank(3).bitcast(BF16)[:, 0:512]
                    for h in range(4):
                        trp(ptn[:, h * 128:(h + 1) * 128], ON[:, sb_i, h * 128:(h + 1) * 128], [tt("on%d" % sb_i)], [TB[3]])
                    for h in range(4):
                        stt("dve", YR[:, h, j * 128:(j + 1) * 128], ptn[:, h * 128:(h + 1) * 128], SMALL[:, S_GN + h:S_GN + h + 1],
                            ZR[:, h, j * 128:(j + 1) * 128], MUL, MUL, [TB[3], tt("small"), tt("zr")], [tt("yr")])
                    ckpt('ryr%d' % ci)
                    kv_matmul(j)
                    tten("dve", SST[64:128, :], SST[64:128, :], DEC[64:128, :], MUL, [tt("sst"), tt("dec")], [tt("sst")])
                    tten("dve", SST[64:128, :], SST[64:128, :], bank(7)[64:128, :], ADD, [tt("sst"), TB[7]], [tt("sst")])
                ckpt('ret%d' % tg)
                for n in range(8):
                    wm, twm = ws_get(BLK_M + n)
                    glv = wm[:, 0:2048].rearrange("p (k c) -> p k c", k=8)
                    pav = wm[:, 2048:2560].rearrange("p (k c) -> p k c", k=4)
                    prv = wm[:, 2560:3072].rearrange("p (k c) -> p k c", k=4)
                    gi = n % 2
                    b0 = 4 * (n % 2)
                    for br in range(2):
                        bk = b0 + 2 + br
                        for kc in range(8):
                            mm(bank(bk), glv[:, kc, br * 128:(br + 1) * 128], HT[:, kc, :], kc == 0, kc == 7, [twm, tt("ht")], [TB[bk]])
                        act(GATE[:, gi, br, :], bank(bk), AF.Sigmoid, [TB[bk]], [tt("gate%d_%d" % (gi, br))])
                    for cc in range(4):
                        mm(bank(b0), pav[:, cc, :], YA[:, cc, :], cc == 0, cc == 3, [twm, tt("ya")], [TB[b0]])
                    for cc in range(4):
                        mm(bank(b0 + 1), prv[:, cc, :], YR[:, cc, :], cc == 0, cc == 3, [twm, tt("yr")], [TB[b0 + 1]])
                    tten("dve", MT[:, 0, :], bank(b0), GATE[:, gi, 0, :], MUL, [TB[b0], tt("gate%d_0" % gi)], [tt("mt0")])
                    tten("dve", MT[:, 1, :], bank(b0 + 1), GATE[:, gi, 1, :], MUL, [TB[b0 + 1], tt("gate%d_1" % gi)], [tt("mt1")])
                    tten("pool", MG[:, n, :], MT[:, 0, :], MT[:, 1, :], ADD, [tt("mt0"), tt("mt1")], [tt("mg")])
                ckpt('mrg%d' % tg)
                wo0, two0 = ws_get(BLK_WO)
                wo1, two1 = ws_get(BLK_WO + 1, la=NWB - 1)
                wov = [wo0.rearrange("p (k c) -> p k c", k=8), wo1.rearrange("p (k c) -> p k c", k=8)]
                twos = [two0, two1]
                for j in range(4):
                    tile = tg * 4 + j
                    pb0 = 0 if j % 2 == 0 else 2
                    for hf in range(2):
                        for kc in range(8):
                            mm(bank(pb0 + hf), MG[:, kc, j * 128:(j + 1) * 128], wov[hf][:, kc, :], kc == 0, kc == 7,
                               [tt("mg"), twos[hf]], [TB[pb0 + hf]])
                    xi = load_x(tile)
                    act(JUNK, bank(pb0, 2), AF.Square, [TB[pb0], TB[pb0 + 1]], [tt("junk"), tt("ss2")], accum_out=MISC[:, M_SS + 1:M_SS + 2])
                    act(MISC[:, M_RSTD + 1:M_RSTD + 2], MISC[:, M_SS + 1:M_SS + 2], AF.Ln, [tt("ss2")], [tt("rstd2")], scale=1.0 / DM, bias=EPS)
                    act(MISC[:, M_RSTD + 1:M_RSTD + 2], MISC[:, M_RSTD + 1:M_RSTD + 2], AF.Exp, [tt("rstd2")], [tt("rstd2")], scale=-0.5)
                    stt("dve", YT[:], bank(pb0, 2), MISC[:, M_RSTD + 1:M_RSTD + 2], GG[:], MUL, MUL,
                        [TB[pb0], TB[pb0 + 1], tt("rstd2"), tt("gg")], [tt("yt")])
                    tten("pool", XT[:, xi, :], XT[:, xi, :], YT[:], ADD, [tt("xt%d" % xi), tt("yt")], [tt("xt%d" % xi)])
                    dma("pool", out_d[tile * 128:(tile + 1) * 128, :], XT[:, xi, :], [tt("xt%d" % xi)], [])
        try:
            run_all()
        except _Stop:
            pass
        dump_aps = {"HT": HT, "K2": K2, "VX": VX, "RF": RF, "MISC": MISC, "GG": GG, "DMK": DMK, "QDEC": QDEC, "KD": KD,
                    "DEC": DEC, "YA": YA, "YR": YR, "MG": MG, "QT": QT, "ZA": ZA, "RKT": RKT, "RVG": RVG, "SST": SST,
                    "RQ": RQ, "QD": QD, "ZR": ZR, "XN": XN, "CS": CS}
        for nm in dumps:
            src = dump_aps[nm]
            shp = list(src.shape)
            flat = int(np.prod(shp[1:]))
            dd = nc.dram_tensor("dbg_" + nm, [128, flat], src.dtype, kind="ExternalOutput").ap()
            allT = list(tdict.values()) + TB
            sv = src[:] if len(shp) == 2 else src[:].rearrange("p a b -> p (a b)") if len(shp) == 3 else src[:].rearrange("p a b c -> p (a b c)")
            dma("sp", dd, sv, allT, [])
        S.emit(nc)
    return nc, S


_CACHE = {}


def kernel(x, c, w_ada, b_ada, g_pre, w_in, qn_g, kn_g, w_dec_f, w_dec_b, gn_g, w_pa, w_pr, w_out, g_post):
    x = np.asarray(x, np.float32)
    c = np.asarray(c, np.float32)
    f = lambda a: np.asarray(a, np.float32)[0]
    w_ada, b_ada, g_pre, w_in, qn_g, kn_g = f(w_ada), f(b_ada), f(g_pre), f(w_in), f(qn_g), f(kn_g)
    w_dec_f, w_dec_b, gn_g, w_pa, w_pr, w_out, g_post = f(w_dec_f), f(w_dec_b), f(gn_g), f(w_pa), f(w_pr), f(w_out), f(g_post)
    if "nc" not in _CACHE:
        _CACHE["nc"] = build_program()[0]
        _CACHE["consts"] = _host_consts()
    nc = _CACHE["nc"]
    rope, cst = _CACHE["consts"]
    blocks = _host_blocks(w_ada, w_in, w_pa, w_pr, w_out)
    col8 = lambda v: np.ascontiguousarray(v.reshape(8, 128).T)
    rows = np.ascontiguousarray(np.stack([b_ada[2048:3072], g_post], 0))
    wdec = np.concatenate([w_dec_f, w_dec_b])[None, :].astype(np.float32)
    in_maps = []
    for b in range(8):
        small = np.zeros((128, NSMALL), np.float32)
        small[:, 0:8] = col8(c[b])
        small[:, 8:16] = col8(g_pre)
        small[:, 16:24] = col8(b_ada[1024:2048])
        small[:, 24:32] = col8(b_ada[0:1024])
        small[:, 32] = np.concatenate([qn_g, qn_g])
        small[:, 33] = np.concatenate([kn_g, kn_g])
        small[:, 34:38] = gn_g.reshape(4, 128).T
        in_maps.append({"x": np.ascontiguousarray(x[b]), "wsrc": blocks, "rope": rope, "cst": cst,
                        "small": small, "rows": rows, "wdec": wdec})
    res = run_bass_kernel_spmd(nc, in_maps, core_ids=list(range(8)))
    return np.stack([np.asarray(r["out"], np.float32) for r in res.results], 0)
```

```python
import numpy as np
import concourse.bass as bass
import concourse.mybir as mybir
from concourse.bass_utils import run_bass_kernel_spmd

F32 = mybir.dt.float32
BF16 = mybir.dt.bfloat16
AF = mybir.ActivationFunctionType
ALU = mybir.AluOpType
AX = mybir.AxisListType


class T:
    __slots__ = ("name", "last_w", "readers", "excl")

    def __init__(self, name="", excl=False):
        self.name = name
        self.last_w = None
        self.readers = []
        self.excl = excl


class Op:
    __slots__ = ("eng", "fn", "deps", "signal", "token", "is_dma", "idx", "gidx")

    def __init__(self, eng, fn, is_dma):
        self.eng = eng
        self.fn = fn
        self.deps = []
        self.signal = False
        self.token = None
        self.is_dma = is_dma


class Sched:
    ENGS = ("pe", "act", "dve", "pool", "sp")

    def __init__(self, n_dma_sems=8):
        self.ops = {e: [] for e in self.ENGS}
        self.n_dma_sems = n_dma_sems
        self.dma_rr = {e: 0 for e in self.ENGS}
        self.dma_last = {}
        self.gcount = 0

    def add(self, eng, fn, reads=(), writes=(), dma=False):
        op = Op(eng, fn, dma)
        op.gidx = self.gcount
        self.gcount += 1
        writes = list(writes) + [t for t in reads if t.excl]
        reads = [t for t in reads if not t.excl]
        raw = []
        war = []
        for t in reads:
            if t.last_w is not None:
                raw.append(t.last_w)
        for t in writes:
            if t.last_w is not None:
                raw.append(t.last_w)
            war.extend(t.readers)
        for t in reads:
            t.readers.append(op)
        for t in writes:
            t.last_w = op
            t.readers = []
        if dma:
            k = self.dma_rr[eng]
            self.dma_rr[eng] = (k + 1) % self.n_dma_sems
            prev = self.dma_last.get((eng, k))
            if prev is not None:
                raw.append(prev)
            self.dma_last[(eng, k)] = op
            op.token = (eng, k)
        seen = set()
        for lst, is_war in ((raw, False), (war, True)):
            for d in lst:
                if d is op or id(d) in seen:
                    continue
                if (not d.is_dma) and (not dma) and d.eng == eng:
                    if eng == "pe" or is_war:
                        continue
                seen.add(id(d))
                op.deps.append(d)
        op.idx = len(self.ops[eng])
        self.ops[eng].append(op)
        return op

    def emit(self, nc, block_ctx_extra=None):
        for e in self.ENGS:
            for op in self.ops[e]:
                for d in op.deps:
                    d.signal = True
        import contextlib
        with contextlib.ExitStack() as st:
            sems = {e: st.enter_context(nc.semaphore("s_" + e)) for e in ("pe", "act", "dve", "pool")}
            dsems = {}
            for e in self.ENGS:
                if any(o.is_dma for o in self.ops[e]):
                    for k in range(self.n_dma_sems):
                        dsems[(e, k)] = st.enter_context(nc.semaphore("d_%s_%d" % (e, k)))
            for e in self.ENGS:
                cnt = 0
                dcnt = {}
                for op in self.ops[e]:
                    if op.is_dma:
                        key = op.token
                        dcnt[key] = dcnt.get(key, 0) + 16
                        op.token = (dsems[key], dcnt[key], key)
                    elif op.signal:
                        cnt += 1
                        op.token = (sems[e], cnt, e)
            block = st.enter_context(nc.Block())
            handles = {"pe": block.tensor, "act": block.scalar, "dve": block.vector, "pool": block.gpsimd,
                       "sp": block.sync}
            nwaits = {e: 0 for e in self.ENGS}

            def make(e):
                ops = self.ops[e]

                def body(eng):
                    known = {}
                    for op in ops:
                        need = {}
                        for d in op.deps:
                            sem, val, key = d.token
                            if known.get(key, 0) >= val:
                                continue
                            if key not in need or need[key][1] < val:
                                need[key] = (sem, val)
                        for key, (sem, val) in need.items():
                            eng.wait_ge(sem, val)
                            known[key] = val
                            nwaits[e] += 1
                        inst = op.fn(eng)
                        if op.is_dma:
                            inst.then_inc(op.token[0], 16)
                        elif op.signal:
                            inst.then_inc(op.token[0], 1)
                    fin = {}
                    for op in ops:
                        if op.is_dma:
                            fin[op.token[2]] = (op.token[0], op.token[1])
                    for key, (sem, val) in fin.items():
                        if known.get(key, 0) < val:
                            eng.wait_ge(sem, val)
                return body

            for e in self.ENGS:
                if self.ops[e]:
                    handles[e](make(e))
            self.nwaits = nwaits

import contextlib

SEQ = 4096
DM = 1024
NG = 8
EPS = 1e-6

BLK_ADA = 0
BLK_P2KV = 6
BLK_RK = 7
BLK_RV = 8
BLK_Q = 9
BLK_ZA = 10
BLK_RQ = 11
BLK_ZR = 12
BLK_M = 13
BLK_WO = 21
NBLK = 23
NSMALL = 40


def _host_blocks(w_ada, w_in, w_pa, w_pr, w_out):
    blocks = np.zeros((NBLK, 128, 4096), np.float32)

    def kmaj(w):
        nc_ = w.shape[1]
        t = np.zeros((128, 8, 512), np.float32)
        t[:, :, :nc_] = w.reshape(8, 128, nc_).transpose(1, 0, 2)
        return t.reshape(128, 4096)

    wa = w_ada
    for j in range(6):
        blocks[BLK_ADA + j] = kmaj(wa[:, j * 512:(j + 1) * 512])
    o_q, o_k, o_v, o_za, o_rq, o_rk, o_rv, o_zr, o_gl = 0, 512, 640, 768, 1280, 1536, 1792, 2304, 2816
    k0 = w_in[:, o_k:o_k + 64]
    k1 = w_in[:, o_k + 64:o_k + 128]
    blocks[BLK_P2KV] = kmaj(np.concatenate([k0, k0, k1, k1, w_in[:, o_v:o_v + 128]], axis=1))
    blocks[BLK_RK] = kmaj(w_in[:, o_rk:o_rk + 256])
    blocks[BLK_RV] = kmaj(w_in[:, o_rv:o_rv + 512])
    blocks[BLK_Q] = kmaj(w_in[:, o_q:o_q + 512])
    blocks[BLK_ZA] = kmaj(w_in[:, o_za:o_za + 512])
    rq = [w_in[:, o_rq + 64 * h:o_rq + 64 * (h + 1)] for h in range(4)]
    blocks[BLK_RQ] = kmaj(np.concatenate([rq[0], rq[0], rq[1], rq[1], rq[2], rq[2], rq[3], rq[3]], axis=1))
    blocks[BLK_ZR] = kmaj(w_in[:, o_zr:o_zr + 512])
    for n in range(8):
        t = np.zeros((128, 4096), np.float32)
        gl = np.concatenate([w_in[:, o_gl + n * 128:o_gl + (n + 1) * 128],
                             w_in[:, o_gl + 1024 + n * 128:o_gl + 1024 + (n + 1) * 128]], axis=1)
        t[:, 0:2048] = gl.reshape(8, 128, 256).transpose(1, 0, 2).reshape(128, 2048)
        t[:, 2048:2560] = w_pa[:, n * 128:(n + 1) * 128].reshape(4, 128, 128).transpose(1, 0, 2).reshape(128, 512)
        t[:, 2560:3072] = w_pr[:, n * 128:(n + 1) * 128].reshape(4, 128, 128).transpose(1, 0, 2).reshape(128, 512)
        blocks[BLK_M + n] = t
    for hf in range(2):
        blocks[BLK_WO + hf] = kmaj(w_out[:, hf * 512:(hf + 1) * 512])
    return blocks


def _host_consts():
    t = np.arange(SEQ)
    row = (t // 64).astype(np.float32)
    col = (t % 64).astype(np.float32)
    inv = (10000.0 ** (-np.arange(0, 32, 2, dtype=np.float32) / 32.0)).astype(np.float32)
    ang_r = row[None, :] * inv[:, None]
    ang_c = col[None, :] * inv[:, None]
    cos64 = np.concatenate([np.cos(ang_r), np.cos(ang_r), np.cos(ang_c), np.cos(ang_c)], 0)
    sin64 = np.concatenate([np.sin(ang_r), np.sin(ang_r), np.sin(ang_c), np.sin(ang_c)], 0)
    rope = np.stack([np.concatenate([cos64, cos64], 0), np.concatenate([sin64, sin64], 0)], 0).astype(np.float32)
    ident = np.eye(128, dtype=np.float32)
    perm = np.zeros((128, 128), np.float32)
    for hb in range(2):
        for d in range(64):
            blk = d // 16
            if blk % 2 == 0:
                p, s = d + 16, -1.0
            else:
                p, s = d - 16, 1.0
            perm[hb * 64 + p, hb * 64 + d] = s
    ones = np.zeros((128, 128), np.float32)
    ones[:64, :64] = 1.0
    ones[64:, 64:] = 1.0
    return rope, np.stack([ident, perm, ones], 0)


class _Stop(Exception):
    pass


def build_program(stop=None, dumps=()):
    nc = bass.Bass("TRN2", target_bir_lowering=False)
    x_d = nc.dram_tensor("x", [SEQ, DM], F32, kind="ExternalInput").ap()
    wsrc = nc.dram_tensor("wsrc", [NBLK, 128, 4096], F32, kind="ExternalInput").ap()
    rope_d = nc.dram_tensor("rope", [2, 128, SEQ], F32, kind="ExternalInput").ap()
    cst_d = nc.dram_tensor("cst", [3, 128, 128], F32, kind="ExternalInput").ap()
    small_d = nc.dram_tensor("small", [128, NSMALL], F32, kind="ExternalInput").ap()
    rows_d = nc.dram_tensor("rows", [2, DM], F32, kind="ExternalInput").ap()
    wdec_d = nc.dram_tensor("wdec", [1, 8], F32, kind="ExternalInput").ap()
    out_d = nc.dram_tensor("out", [SEQ, DM], F32, kind="ExternalOutput").ap()
    wbf_d = nc.dram_tensor("wbf", [NBLK, 128, 4096], BF16, kind="Internal").ap()

    S = Sched(n_dma_sems=12)
    tdict = {}

    _alias = {"io0": "sa", "io1": "sa", "io2": "sa", "tmpd": "sa", "cbc": "mt1", "junk": "mt0"}

    def tt(name):
        name = _alias.get(name, name)
        if name not in tdict:
            tdict[name] = T(name)
        return tdict[name]

    with contextlib.ExitStack() as st:
        def sb(name, shape, dt):
            return st.enter_context(nc.sbuf_tensor(name, shape, dt))

        PS = st.enter_context(nc.psum_tensor("ps", [128, 4096], F32))
        TB = [T("bank%d" % i, excl=True) for i in range(8)]

        def bank(i, n=1):
            return PS[:, i * 512:(i + n) * 512]

        K2 = sb("K2", [128, 2, SEQ], BF16)
        VX = sb("VX", [128, 32, 320], BF16)
        RF = sb("RF", [128, 16, 512], BF16)
        NWB = 3
        WB = sb("WB", [128, NWB, 4096], BF16)
        XT = sb("XT", [128, 2, DM], F32)
        XN = sb("XN", [128, 2, DM], BF16)
        HT = sb("HT", [128, 8, 512], BF16)
        CS = sb("CS", [128, 2, 512], F32)
        QBF = sb("QBF", [128, 2, 512], BF16)
        SQ = sb("SQ", [128, 2, 512], BF16)
        T1 = sb("T1", [128, 2, 512], F32)
        T2 = sb("T2", [128, 2, 512], F32)
        RRS = sb("RRS", [128, 2, 512], F32)
        QT = sb("QT", [128, 4, 512], BF16)
        ZA = sb("ZA", [128, 4, 512], BF16)
        NPT = 3
        PT = sb("PT", [128, NPT, 1024], BF16)
        SA = sb("SA", [128, 512], F32)
        SB_ = sb("SB_", [128, 512], F32)
        DEN = sb("DEN", [128, 512], F32)
        TMPN = sb("TMPN", [128, 512], F32)
        YA = sb("YA", [128, 4, 512], BF16)
        YR = sb("YR", [128, 4, 512], BF16)
        RQ = sb("RQ", [128, 4, 512], BF16)
        QD = sb("QD", [128, 4, 512], BF16)
        ZR = sb("ZR", [128, 4, 512], BF16)
        RKT = sb("RKT", [128, 2, 512], BF16)
        RVG = sb("RVG", [128, 4, 512], BF16)
        KK = sb("KK", [128, 2, 512], BF16)
        SM = sb("SM", [128, 2, 512], BF16)
        RRO = sb("RRO", [128, 2, 512], BF16)
        ON = sb("ON", [128, 2, 512], BF16)
        SST = sb("SST", [128, 512], F32)
        GATE = sb("GATE", [128, 2, 2, 512], BF16)
        MT = sb("MT", [128, 2, 512], F32)
        MG = sb("MG", [128, 8, 512], BF16)
        YT = sb("YT", [128, DM], F32)
        IDENT = sb("IDENT", [128, 128], BF16)
        PERM = sb("PERM", [128, 128], BF16)
        ONESB = sb("ONESB", [128, 128], BF16)
        DMK = sb("DMK", [128, 512], F32)
        QDEC = sb("QDEC", [128, 512], F32)
        KD = sb("KD", [128, 512], F32)
        DEC = sb("DEC", [128, 512], F32)
        GG = sb("GG", [128, DM], F32)
        SMALL = sb("SMALL", [128, NSMALL], F32)
        MISC = sb("MISC", [128, 128], F32)
        IO = SA[:, 0:384].rearrange("p (a b) -> p a b", a=3)
        TMPD = SA[:, 384:512]
        CBC = MT[:, 1, :].bitcast(BF16).rearrange("p (a b) -> p a b", a=8)
        JUNK = MT[:, 0, :].bitcast(BF16)
        CBF = sb("CBF", [128, 8], BF16)
        STAT = sb("STAT", [128, 2, 4, 8], F32)

        M_ACOL, M_SHCOL, M_LG, M_DC, M_WD, M_E, M_KD8, M_KDV, M_IOP, M_IOPR, M_GK8 = 0, 8, 16, 24, 32, 40, 48, 56, 64, 65, 66
        M_SS, M_RSTD, M_CACT, M_TMP8 = 70, 74, 80, 88
        M_GNR, M_GNN = 96, 104
        S_C, S_GPRE, S_BSC, S_BSH, S_QG, S_KG, S_GN = 0, 8, 16, 24, 32, 33, 34

        def mm(o, l, r, start, stop, rd, wr):
            S.add("pe", lambda e: e.matmul(o, l, r, start=start, stop=stop), reads=rd, writes=wr)

        def trp(o, i, rd, wr):
            S.add("pe", lambda e: e.transpose(o, i, IDENT[:]), reads=rd + [tt("ident")], writes=wr)

        def act(o, i, f, rd, wr, **kw):
            S.add("act", lambda e: e.activation(out=o, in_=i, func=f, **kw), reads=rd, writes=wr)

        def tten(eng, o, a, b, op, rd, wr):
            S.add(eng, lambda e: e.tensor_tensor(out=o, in0=a, in1=b, op=op), reads=rd, writes=wr)

        def tsc(eng, o, a, s1, s2, op0, op1, rd, wr):
            if op1 is None:
                S.add(eng, lambda e: e.tensor_scalar(out=o, in0=a, scalar1=s1, scalar2=None, op0=op0), reads=rd, writes=wr)
            else:
                S.add(eng, lambda e: e.tensor_scalar(out=o, in0=a, scalar1=s1, scalar2=s2, op0=op0, op1=op1), reads=rd, writes=wr)

        def stt(eng, o, a, s, b, op0, op1, rd, wr):
            S.add(eng, lambda e: e.scalar_tensor_tensor(out=o, in0=a, scalar=s, in1=b, op0=op0, op1=op1), reads=rd, writes=wr)

        def cpy(eng, o, i, rd, wr):
            S.add(eng, lambda e: e.tensor_copy(out=o, in_=i), reads=rd, writes=wr)

        def dma(eng, o, i, rd, wr):
            S.add(eng, lambda e: e.dma_start(out=o, in_=i), reads=rd, writes=wr, dma=True)

        MUL, ADD, MAX = ALU.mult, ALU.add, ALU.max

        def ckpt(name):
            if stop == name:
                raise _Stop()

        def bnstats(o, i, rd, wr):
            S.add("dve", lambda e: e.bn_stats(out=o, in_=i), reads=rd, writes=wr)

        def bnaggr(o, i, rd, wr):
            S.add("dve", lambda e: e.bn_aggr(out=o, in_=i), reads=rd, writes=wr)

        def recip(o, i, rd, wr):
            S.add("dve", lambda e: e.reciprocal(out=o, in_=i), reads=rd, writes=wr)

        order = [BLK_ADA + j for j in range(6)]
        for tg in range(NG):
            order += [BLK_P2KV, BLK_RK, BLK_RV]
        for tg in range(NG):
            order += [BLK_Q, BLK_ZA, BLK_RQ, BLK_RK, BLK_RV, BLK_ZR] + [BLK_M + n for n in range(8)] + [BLK_WO, BLK_WO + 1]
        conv_done = set()
        ws_state = {"issued": 0, "cur": 0}

        def conv(blk):
            if blk in conv_done:
                return
            conv_done.add(blk)
            dma("pool", wbf_d[blk], wsrc[blk], [], [tt("wbf%d" % blk)])

        def ws_issue_upto(n):
            while ws_state["issued"] < min(n, len(order)):
                i = ws_state["issued"]
                blk = order[i]
                conv(blk)
                dma("sp", WB[:, i % NWB, :], wbf_d[blk], [tt("wbf%d" % blk)], [tt("wb%d" % (i % NWB))])
                ws_state["issued"] += 1

        def ws_get(expect, la=None):
            i = ws_state["cur"]
            assert order[i] == expect, (i, order[i], expect)
            ws_issue_upto(i + (NWB if la is None else la))
            ws_state["cur"] += 1
            return WB[:, i % NWB, :], tt("wb%d" % (i % NWB))

        def run_all():
            for blk in [BLK_ADA + j for j in range(6)] + [BLK_P2KV, BLK_RK, BLK_RV]:
                conv(blk)
            dma("sp", SMALL[:], small_d, [], [tt("small")])
            dma("sp", GG[:], rows_d[0:1, :].partition_broadcast(128), [], [tt("gg")])
            dma("sp", YT[:], rows_d[1:2, :].partition_broadcast(128), [], [tt("yt")])
            dma("sp", MISC[:, M_WD:M_WD + 8], wdec_d.partition_broadcast(128), [], [tt("wd")])
            dma("pool", IDENT[:], cst_d[0], [], [tt("ident")])
            dma("pool", PERM[:], cst_d[1], [], [tt("perm")])
            dma("pool", ONESB[:], cst_d[2], [], [tt("onesb")])
            ws_issue_upto(NWB)

            act(MISC[:, M_CACT:M_CACT + 8], SMALL[:, S_C:S_C + 8], AF.Silu, [tt("small")], [tt("cact")])
            cpy("dve", CBF[:], MISC[:, M_CACT:M_CACT + 8], [tt("cact")], [tt("cbf")])
            cpy("dve", CBC, MISC[:, M_CACT:M_CACT + 8].unsqueeze(2).broadcast_to([128, 8, 128]), [tt("cact")], [tt("cbc")])
            act(MISC[:, M_E:M_E + 8], MISC[:, M_WD:M_WD + 8], AF.Exp, [tt("wd")], [tt("e")], scale=-1.0)
            act(MISC[:, M_E:M_E + 8], MISC[:, M_E:M_E + 8], AF.Ln, [tt("e")], [tt("e")], bias=1.0)
            tsc("dve", MISC[:, M_LG:M_LG + 8], MISC[:, M_E:M_E + 8], -1.0, None, MUL, None, [tt("e")], [tt("lg")])
            LGF = lambda h, lo=0, hi=128: MISC[lo:hi, M_LG + h:M_LG + h + 1]
            LGB = lambda h, lo=0, hi=128: MISC[lo:hi, M_LG + 4 + h:M_LG + 5 + h]
            S.add("pool", lambda e: e.iota(IO[:, 0, :], pattern=[[1, 128]], base=0, channel_multiplier=-1,
                                           allow_small_or_imprecise_dtypes=True), writes=[tt("io0")])
            S.add("pool", lambda e: e.iota(IO[:, 1, :], pattern=[[1, 128]], base=1, channel_multiplier=0,
                                           allow_small_or_imprecise_dtypes=True), writes=[tt("io1")])
            S.add("pool", lambda e: e.iota(IO[:, 2, :], pattern=[[-1, 128]], base=128, channel_multiplier=0,
                                           allow_small_or_imprecise_dtypes=True), writes=[tt("io2")])
            S.add("pool", lambda e: e.iota(MISC[:, M_IOP:M_IOP + 1], pattern=[[1, 1]], base=0, channel_multiplier=1,
                                           allow_small_or_imprecise_dtypes=True), writes=[tt("iop")])
            S.add("pool", lambda e: e.iota(MISC[:, M_IOPR:M_IOPR + 1], pattern=[[1, 1]], base=127, channel_multiplier=-1,
                                           allow_small_or_imprecise_dtypes=True), writes=[tt("iopr")])
            NEG = RRS[:, 0, 0:128]
            POS = RRS[:, 0, 128:256]
            tsc("dve", POS, IO[:, 0, :], 0.0, None, MAX, None, [tt("io0")], [tt("pos")])
            tten("dve", NEG, POS, IO[:, 0, :], ALU.subtract, [tt("pos"), tt("io0")], [tt("neg")])
            for h in range(4):
                tsc("dve", TMPD, POS, LGF(h), None, MUL, None, [tt("pos"), tt("lg")], [tt("tmpd")])
                stt("dve", TMPD, NEG, LGB(h), TMPD, MUL, ADD, [tt("neg"), tt("lg"), tt("tmpd")], [tt("tmpd")])
                act(DMK[:, h * 128:(h + 1) * 128], TMPD, AF.Exp, [tt("tmpd")], [tt("dmk")])
                tsc("dve", TMPD[0:64, :], IO[0:64, 1, :], LGF(h, 0, 64), None, MUL, None, [tt("io1"), tt("lg"), tt("dmk")], [tt("tmpd")])
                tsc("dve", TMPD[64:128, :], IO[64:128, 2, :], LGB(h, 64, 128), None, MUL, None, [tt("io2"), tt("lg"), tt("tmpd")], [tt("tmpd")])
                act(QDEC[:, h * 128:(h + 1) * 128], TMPD, AF.Exp, [tt("tmpd")], [tt("qdec")])
            KDV = MISC[:, M_KDV:M_KDV + 8].rearrange("p (h d) -> p h d", d=2)
            tsc("dve", MISC[:, M_KD8:M_KD8 + 4], MISC[:, M_LG:M_LG + 4], MISC[:, M_IOPR:M_IOPR + 1], None, MUL, None,
                [tt("lg"), tt("iopr")], [tt("kd8")])
            tsc("dve", MISC[:, M_KD8 + 4:M_KD8 + 8], MISC[:, M_LG + 4:M_LG + 8], MISC[:, M_IOP:M_IOP + 1], None, MUL, None,
                [tt("lg"), tt("iop"), tt("kd8")], [tt("kd8")])
            act(KDV[:, :, 0], MISC[:, M_KD8:M_KD8 + 4], AF.Exp, [tt("kd8")], [tt("kdv")])
            act(KDV[:, :, 1], MISC[:, M_KD8 + 4:M_KD8 + 8], AF.Exp, [tt("kd8"), tt("kdv")], [tt("kdv")])
            cpy("dve", KD[:].rearrange("p (a c) -> p a c", c=64),
                MISC[:, M_KDV:M_KDV + 8].unsqueeze(2).broadcast_to([128, 8, 64]), [tt("kdv")], [tt("kd")])
            act(MISC[:, M_DC:M_DC + 8], MISC[:, M_LG:M_LG + 8], AF.Exp, [tt("lg")], [tt("dc")], scale=128.0)
            cpy("dve", DEC[0:64, :].rearrange("p (h c) -> p h c", c=128),
                MISC[0:64, M_DC:M_DC + 4].unsqueeze(2).broadcast_to([64, 4, 128]), [tt("dc")], [tt("dec")])
            cpy("dve", DEC[64:128, :].rearrange("p (h c) -> p h c", c=128),
                MISC[64:128, M_DC + 4:M_DC + 8].unsqueeze(2).broadcast_to([64, 4, 128]), [tt("dc"), tt("dec")], [tt("dec")])
            tsc("dve", MISC[:, M_GK8:M_GK8 + 1], SMALL[:, S_KG:S_KG + 1], 8.0, None, MUL, None, [tt("small")], [tt("gk8")])
            S.add("pool", lambda e: e.memset(SST[:], 0.0), writes=[tt("sst")])
            S.add("pool", lambda e: e.memset(VX[:], 1.0), writes=[tt("vx%d" % g) for g in range(NG)])

            ckpt('setup0')
            pmod = bank(6)[:, 0:16]
            for blk in range(4):
                wb, twb = ws_get(BLK_ADA + blk)
                wbv = wb.rearrange("p (k c) -> p k c", k=8)
                for c in range(4):
                    for kc in range(8):
                        mm(pmod[:, blk * 4 + c:blk * 4 + c + 1], wbv[:, kc, c * 128:(c + 1) * 128], CBF[:, kc:kc + 1],
                           kc == 0, kc == 7, [twb, tt("cbf")], [TB[6]])
            tten("dve", MISC[:, M_TMP8:M_TMP8 + 8], pmod[:, 8:16], SMALL[:, S_BSC:S_BSC + 8], ADD, [TB[6], tt("small")], [tt("tmp8")])
            stt("dve", MISC[:, M_ACOL:M_ACOL + 8], MISC[:, M_TMP8:M_TMP8 + 8], 1.0, SMALL[:, S_GPRE:S_GPRE + 8], ADD, MUL,
                [tt("tmp8"), tt("small")], [tt("acol")])
            tten("dve", MISC[:, M_SHCOL:M_SHCOL + 8], pmod[:, 0:8], SMALL[:, S_BSH:S_BSH + 8], ADD, [TB[6], tt("small")], [tt("shcol")])
            for hf in range(2):
                wb, twb = ws_get(BLK_ADA + 4 + hf)
                wbv = wb.rearrange("p (k c) -> p k c", k=8)
                for kc in range(8):
                    mm(bank(4 + hf), CBC[:, kc, :], wbv[:, kc, :], kc == 0, kc == 7, [twb, tt("cbc")], [TB[4 + hf]])
                tten("dve", GG[:, hf * 512:(hf + 1) * 512], bank(4 + hf), GG[:, hf * 512:(hf + 1) * 512], ADD,
                     [TB[4 + hf], tt("gg")], [tt("gg")])
                tten("dve", GG[:, hf * 512:(hf + 1) * 512], GG[:, hf * 512:(hf + 1) * 512], YT[:, hf * 512:(hf + 1) * 512], MUL,
                     [tt("gg"), tt("yt")], [tt("gg")])

            ckpt('ada')
            xt_rr = {"i": 0}

            def load_x(tile):
                i = xt_rr["i"] % 2
                xt_rr["i"] += 1
                dma("sp", XT[:, i, :], x_d[tile * 128:(tile + 1) * 128, :], [], [tt("xt%d" % i)])
                return i

            cur = {"ht": HT, "htT": tt("ht")}

            def hT_tile(tg, j):
                tile = tg * 4 + j
                xi = load_x(tile)
                xb = j % 2
                act(JUNK, XT[:, xi, :], AF.Square, [tt("xt%d" % xi)], [tt("junk"), tt("ss")], accum_out=MISC[:, M_SS:M_SS + 1])
                act(MISC[:, M_RSTD:M_RSTD + 1], MISC[:, M_SS:M_SS + 1], AF.Ln, [tt("ss")], [tt("rstd")], scale=1.0 / DM, bias=EPS)
                act(MISC[:, M_RSTD:M_RSTD + 1], MISC[:, M_RSTD:M_RSTD + 1], AF.Exp, [tt("rstd")], [tt("rstd")], scale=-0.5)
                tsc("dve", XN[:, xb, :], XT[:, xi, :], MISC[:, M_RSTD:M_RSTD + 1], None, MUL, None,
                    [tt("xt%d" % xi), tt("rstd")], [tt("xn%d" % xb)])
                for kc in range(8):
                    pb = bank(kc // 2).bitcast(BF16)[:, (kc % 2) * 512 + j * 128:(kc % 2) * 512 + (j + 1) * 128]
                    trp(pb, XN[:, xb, kc * 128:(kc + 1) * 128], [tt("xn%d" % xb)], [TB[kc // 2]])

            def hT_evac(dst, dstT):
                for kc in range(8):
                    pb = bank(kc // 2).bitcast(BF16)[:, (kc % 2) * 512:(kc % 2 + 1) * 512]
                    if kc < 4:
                        tsc("dve", dst[:, kc, :], pb, MISC[:, M_ACOL + kc:M_ACOL + kc + 1], MISC[:, M_SHCOL + kc:M_SHCOL + kc + 1],
                            MUL, ADD, [TB[kc // 2], tt("acol"), tt("shcol")], [dstT])
                    else:
                        act(dst[:, kc, :], pb, AF.Identity, [TB[kc // 2], tt("acol"), tt("shcol")], [dstT],
                            scale=MISC[:, M_ACOL + kc:M_ACOL + kc + 1], bias=MISC[:, M_SHCOL + kc:M_SHCOL + kc + 1])

            def make_hT(tg):
                cur["ht"], cur["htT"] = HT, tt("ht")
                for j in range(4):
                    hT_tile(tg, j)
                hT_evac(HT, tt("ht"))

            def load_cs(tg):
                dma("sp", CS[:, 0, :], rope_d[0][:, tg * 512:(tg + 1) * 512], [], [tt("cs")])
                dma("sp", CS[:, 1, :], rope_d[1][:, tg * 512:(tg + 1) * 512], [], [tt("cs")])

            def proj_fm(wbv, twb, c0, bk):
                for kc in range(8):
                    mm(bank(bk), wbv[:, kc, c0:c0 + 128], cur["ht"][:, kc, :], kc == 0, kc == 7, [twb, cur["htT"]], [TB[bk]])

            def proj_tm(wbv, twb, j, c0, ncol, o, tbk):
                for kc in range(8):
                    mm(o, cur["ht"][:, kc, j * 128:(j + 1) * 128], wbv[:, kc, c0:c0 + ncol], kc == 0, kc == 7, [twb, cur["htT"]], [tbk])

            rope_ctr = {"i": 0, "alt": [(4, 5)]}

            def rope(bkA, gcol, gT, rms, out_ap, out_T):
                i = rope_ctr["i"] % 2
                rope_ctr["i"] += 1
                bS, bC = rope_ctr["alt"][i % len(rope_ctr["alt"])]
                A = bank(bkA)
                rdg = [gT] if gT is not None else []
                gsc = gcol if gT is not None else float(gcol)
                tsc("dve", QBF[:, i, :], A, gsc, None, MUL, None, [TB[bkA]] + rdg, [tt("qbf%d" % i)])
                if rms:
                    act(SQ[:, i, :], A, AF.Square, [TB[bkA]], [tt("sq%d" % i)])
                mm(bank(bS), PERM[:], QBF[:, i, :], True, True, [tt("perm"), tt("qbf%d" % i)], [TB[bS]])
                if rms:
                    mm(bank(bC), ONESB[:], SQ[:, i, :], True, True, [tt("onesb"), tt("sq%d" % i)], [TB[bC]])
                stt("dve", T1[:, i, :], A, gsc, CS[:, 0, :], MUL, MUL, [TB[bkA], tt("cs")] + rdg, [tt("t1_%d" % i)])
                tten("dve", T2[:, i, :], bank(bS), CS[:, 1, :], MUL, [TB[bS], tt("cs")], [tt("t2_%d" % i)])
                if rms:
                    act(RRS[:, i, :], bank(bC), AF.Ln, [TB[bC]], [tt("rrs%d" % i)], bias=64.0 * EPS)
                    act(RRS[:, i, :], RRS[:, i, :], AF.Exp, [tt("rrs%d" % i)], [tt("rrs%d" % i)], scale=-0.5)
                    tten("pool", T1[:, i, :], T1[:, i, :], T2[:, i, :], ADD, [tt("t1_%d" % i), tt("t2_%d" % i)], [tt("t1_%d" % i)])
                    tten("pool", out_ap, T1[:, i, :], RRS[:, i, :], MUL, [tt("t1_%d" % i), tt("rrs%d" % i)], [out_T])
                else:
                    tten("pool", out_ap, T1[:, i, :], T2[:, i, :], ADD, [tt("t1_%d" % i), tt("t2_%d" % i)], [out_T])

            def ret_rk(tg):
                wb, twb = ws_get(BLK_RK)
                wbv = wb.rearrange("p (k c) -> p k c", k=8)
                for pr in range(2):
                    proj_fm(wbv, twb, pr * 128, 6 + pr)
                    rope(6 + pr, 0.125, None, False, RKT[:, pr, :], tt("rkt"))

            def ret_rv(tg):
                wb, twb = ws_get(BLK_RV)
                wbv = wb.rearrange("p (k c) -> p k c", k=8)
                for j in range(4):
                    bk = 6 + (j % 2)
                    proj_tm(wbv, twb, j, 0, 512, bank(bk), TB[bk])
                    act(RVG[:, j, :], bank(bk), AF.Copy, [TB[bk]], [tt("rvg")])

            def ret_kv_inputs(tg):
                ret_rk(tg)
                ret_rv(tg)

            def kv_matmul(j):
                ptk = bank(6).bitcast(BF16)[:, 0:256]
                for pr in range(2):
                    trp(ptk[:, pr * 128:(pr + 1) * 128], RKT[:, pr, j * 128:(j + 1) * 128], [tt("rkt")], [TB[6]])
                kb = j % 2
                tten("dve", KK[:, kb, :].rearrange("p (h d c) -> p h d c", h=4, d=2),
                     ptk.rearrange("p (h c) -> p h c", h=4).unsqueeze(2).broadcast_to([128, 4, 2, 64]),
                     KD[:].rearrange("p (h d c) -> p h d c", h=4, d=2), MUL, [TB[6], tt("kd")], [tt("kk%d" % kb)])
                for h in range(4):
                    mm(bank(7)[:, h * 128:(h + 1) * 128], KK[:, kb, h * 128:(h + 1) * 128], RVG[:, j, h * 128:(h + 1) * 128],
                       True, True, [tt("kk%d" % kb), tt("rvg")], [TB[7]])

            bufs = [(HT, tt("ht")), (MG, tt("mg"))]
            for j in range(4):
                hT_tile(0, j)
            hT_evac(*bufs[0])
            for tg in range(NG):
                cur["ht"], cur["htT"] = bufs[tg % 2]
                nxt = bufs[(tg + 1) % 2] if tg + 1 < NG else None
                ckpt('ht%d' % tg)
                load_cs(tg)
                wb, twb = ws_get(BLK_P2KV)
                wbv = wb.rearrange("p (k c) -> p k c", k=8)
                for g in range(2):
                    proj_fm(wbv, twb, g * 128, 6 + g)
                    rope(6 + g, MISC[:, M_GK8:M_GK8 + 1], tt("gk8"), True, K2[:, g, tg * 512:(tg + 1) * 512], tt("k2_%d_%d" % (g, tg)))
                ckpt('k%d' % tg)
                if nxt:
                    hT_tile(tg + 1, 0)
                for j in range(4):
                    proj_tm(wbv, twb, j, 256, 128, bank(6)[:, j * 128:(j + 1) * 128], TB[6])
                cpy("dve", VX[:, tg * 4:(tg + 1) * 4, 64:320].rearrange("p j (g c) -> p j g c", g=2)[:, :, :, 0:64],
                    bank(6).rearrange("p (j g c) -> p j g c", j=4, g=2), [TB[6]], [tt("vx%d" % tg)])
                ckpt('v%d' % tg)
                if nxt:
                    hT_tile(tg + 1, 1)
                ret_rk(tg)
                if nxt:
                    hT_tile(tg + 1, 2)
                ret_rv(tg)
                ckpt('rkv%d' % tg)
                if nxt:
                    hT_tile(tg + 1, 3)
                    hT_evac(*nxt)
                for j in range(4):
                    ci = tg * 4 + j
                    kv_matmul(j)
                    act(RF[(ci % 2) * 64:(ci % 2) * 64 + 64, ci // 2, :], SST[0:64, :], AF.Copy, [tt("sst")], [tt("rf")])
                    tten("dve", SST[0:64, :], SST[0:64, :], DEC[0:64, :], MUL, [tt("sst"), tt("dec")], [tt("sst")])
                    tten("dve", SST[0:64, :], SST[0:64, :], bank(7)[0:64, :], ADD, [tt("sst"), TB[7]], [tt("sst")])

            rope_ctr["alt"] = [(4, 5), (2, 3)]
            ckpt('p2')
            for tg in range(NG - 1, -1, -1):
                make_hT(tg)
                load_cs(tg)
                wq, twq = ws_get(BLK_Q)
                wqv = wq.rearrange("p (k c) -> p k c", k=8)
                for p in range(4):
                    proj_fm(wqv, twq, p * 128, 6 + (p % 2))
                    rope(6 + (p % 2), SMALL[:, S_QG:S_QG + 1], tt("small"), True, QT[:, p, :], tt("qt%d" % p))
                wz, twz = ws_get(BLK_ZA)
                wzv = wz.rearrange("p (k c) -> p k c", k=8)
                for p in range(4):
                    bk = 6 + (p % 2)
                    proj_fm(wzv, twz, p * 128, bk)
                    act(ZA[:, p, :], bank(bk), AF.Silu, [TB[bk]], [tt("za%d" % p)])
                ckpt('q%d' % tg)
                for p in range(4):
                    g = p // 2

                    def qk(kc):
                        si = kc % 2
                        mm(bank(2 * si), K2[0:64, g, kc * 128:(kc + 1) * 128], QT[0:64, p, :], True, True,
                           [tt("k2_%d_%d" % (g, kc // 4)), tt("qt%d" % p)], [TB[2 * si]])
                        mm(bank(2 * si + 1), K2[64:128, g, kc * 128:(kc + 1) * 128], QT[64:128, p, :], True, True,
                           [tt("k2_%d_%d" % (g, kc // 4)), tt("qt%d" % p)], [TB[2 * si + 1]])

                    def pv(kc):
                        pi = kc % NPT
                        mm(bank(4), VX[:, kc, 64 + g * 128:192 + g * 128], PT[:, pi, 0:512], kc == 0, kc == 31,
                           [tt("vx%d" % (kc // 4)), tt("pt%d" % pi)], [TB[4]])
                        mm(bank(5), VX[:, kc, g * 128:g * 128 + 128], PT[:, pi, 512:1024], kc == 0, kc == 31,
                           [tt("vx%d" % (kc // 4)), tt("pt%d" % pi)], [TB[5]])

                    qk(0)
                    qk(1)
                    for kc in range(32):
                        si = kc % 2
                        act(PT[:, kc % NPT, :], bank(2 * si, 2), AF.Exp, [TB[2 * si], TB[2 * si + 1]], [tt("pt%d" % (kc % NPT))])
                        if kc + 2 < 32:
                            qk(kc + 2)
                        pv(kc)
                    cpy("dve", SA[:], bank(4), [TB[4]], [tt("sa")])
                    cpy("dve", SB_[:], bank(5), [TB[5]], [tt("sb")])
                    recip(DEN[0:64, :], SA[64:128, :], [tt("sa")], [tt("den")])
                    recip(DEN[64:128, :], SB_[0:64, :], [tt("sb")], [tt("den")])
                    tten("pool", TMPN[0:64, :], SA[0:64, :], DEN[0:64, :], MUL, [tt("sa"), tt("den")], [tt("tmpn")])
                    tten("pool", TMPN[64:128, :], SB_[64:128, :], DEN[64:128, :], MUL, [tt("sb"), tt("den")], [tt("tmpn")])
                    tten("pool", YA[:, p, :], TMPN[:], ZA[:, p, :], MUL, [tt("tmpn"), tt("za%d" % p)], [tt("ya")])
                ckpt('att%d' % tg)
                wr, twr = ws_get(BLK_RQ)
                wrv = wr.rearrange("p (k c) -> p k c", k=8)
                for h in range(4):
                    proj_fm(wrv, twr, h * 128, 6 + (h % 2))
                    rope(6 + (h % 2), 1.0, None, False, RQ[:, h, :], tt("rq"))
                    tten("pool", QD[:, h, :].rearrange("p (j c) -> p j c", j=4), RQ[:, h, :].rearrange("p (j c) -> p j c", j=4),
                         QDEC[:, h * 128:(h + 1) * 128].unsqueeze(1).broadcast_to([128, 4, 128]), MUL, [tt("rq"), tt("qdec")], [tt("qd")])
                ckpt('rqd%d' % tg)
                ret_kv_inputs(tg)
                wz, twz = ws_get(BLK_ZR)
                wzv = wz.rearrange("p (k c) -> p k c", k=8)
                for h in range(4):
                    bk = 6 + (h % 2)
                    proj_fm(wzv, twz, h * 128, bk)
                    act(ZR[:, h, :], bank(bk), AF.Silu, [TB[bk]], [tt("zr")])
                ckpt('rin%d' % tg)
                for j in range(3, -1, -1):
                    ci = tg * 4 + j
                    sb_i = j % 2
                    for h in range(4):
                        r0 = (h % 2) * 64
                        mm(bank(h % 2)[:, (h // 2) * 128:(h // 2 + 1) * 128], RKT[r0:r0 + 64, h // 2, j * 128:(j + 1) * 128],
                           RQ[r0:r0 + 64, h, j * 128:(j + 1) * 128], True, True, [tt("rkt"), tt("rq")], [TB[h % 2]])
                    for hb in range(2):
                        tten("dve", SM[:, sb_i, :].rearrange("p (a b c) -> p a b c", a=2, b=2)[:, :, hb, :],
                             bank(hb)[:, 0:256].rearrange("p (a c) -> p a c", a=2),
                             DMK[:].rearrange("p (a b c) -> p a b c", a=2, b=2)[:, :, hb, :], MUL, [TB[hb], tt("dmk")], [tt("sm%d" % sb_i)])
                    ckpt('rs%d' % ci)
                    act(RRO[0:64, sb_i, :], RF[(ci % 2) * 64:(ci % 2) * 64 + 64, ci // 2, :], AF.Copy, [tt("rf")], [tt("rro%d" % sb_i)])
                    cpy("pool", RRO[64:128, sb_i, :], SST[64:128, :], [tt("sst")], [tt("rro%d" % sb_i)])
                    for h in range(4):
                        mm(bank(2)[:, h * 128:(h + 1) * 128], SM[:, sb_i, h * 128:(h + 1) * 128], RVG[:, j, h * 128:(h + 1) * 128],
                           True, False, [tt("sm%d" % sb_i), tt("rvg")], [TB[2]])
                        mm(bank(2)[:, h * 128:(h + 1) * 128], QD[:, h, j * 128:(j + 1) * 128], RRO[:, sb_i, h * 128:(h + 1) * 128],
                           False, True, [tt("qd"), tt("rro%d" % sb_i)], [TB[2]])
                    ckpt('ro%d' % ci)
                    for h in range(4):
                        bnstats(STAT[:, sb_i, h, 0:6], bank(2)[:, h * 128:(h + 1) * 128], [TB[2]], [tt("stat%d" % sb_i)])
                        bnaggr(STAT[:, sb_i, h, 6:8], STAT[:, sb_i, h, 0:6], [tt("stat%d" % sb_i)], [tt("stat%d" % sb_i)])
                    gr = MISC[:, M_GNR + sb_i * 4:M_GNR + sb_i * 4 + 4]
                    gn = MISC[:, M_GNN + sb_i * 4:M_GNN + sb_i * 4 + 4]
                    act(gr, STAT[:, sb_i, :, 7], AF.Ln, [tt("stat%d" % sb_i)], [tt("gr%d" % sb_i)], bias=EPS)
                    act(gr, gr, AF.Exp, [tt("gr%d" % sb_i)], [tt("gr%d" % sb_i)], scale=-0.5)
                    stt("dve", gn, STAT[:, sb_i, :, 6], -1.0, gr, MUL, MUL, [tt("stat%d" % sb_i), tt("gr%d" % sb_i)], [tt("gn%d" % sb_i)])
                    for h in range(4):
                        tsc("dve", ON[:, sb_i, h * 128:(h + 1) * 128], bank(2)[:, h * 128:(h + 1) * 128],
                            MISC[:, M_GNR + sb_i * 4 + h:M_GNR + sb_i * 4 + h + 1], MISC[:, M_GNN + sb_i * 4 + h:M_GNN + sb_i * 4 + h + 1],
                            MUL, ADD, [TB[2], tt("gr%d" % sb_i), tt("gn%d" % sb_i)], [tt("on%d" % sb_i)])
                    ckpt('rgn%d' % ci)
                    ptn = bank(3).bitcast(BF16)[:, 0:512]
                    for h in range(4):
                        trp(ptn[:, h * 128:(h + 1) * 128], ON[:, sb_i, h * 128:(h + 1) * 128], [tt("on%d" % sb_i)], [TB[3]])
                    for h in range(4):
                        stt("dve", YR[:, h, j * 128:(j + 1) * 128], ptn[:, h * 128:(h + 1) * 128], SMALL[:, S_GN + h:S_GN + h + 1],
                            ZR[:, h, j * 128:(j + 1) * 128], MUL, MUL, [TB[3], tt("small"), tt("zr")], [tt("yr")])
                    ckpt('ryr%d' % ci)
                    kv_matmul(j)
                    tten("dve", SST[64:128, :], SST[64:128, :], DEC[64:128, :], MUL, [tt("sst"), tt("dec")], [tt("sst")])
                    tten("dve", SST[64:128, :], SST[64:128, :], bank(7)[64:128, :], ADD, [tt("sst"), TB[7]], [tt("sst")])
                ckpt('ret%d' % tg)
                for n in range(8):
                    wm, twm = ws_get(BLK_M + n)
                    glv = wm[:, 0:2048].rearrange("p (k c) -> p k c", k=8)
                    pav = wm[:, 2048:2560].rearrange("p (k c) -> p k c", k=4)
                    prv = wm[:, 2560:3072].rearrange("p (k c) -> p k c", k=4)
                    gi = n % 2
                    b0 = 4 * (n % 2)
                    for br in range(2):
                        bk = b0 + 2 + br
                        for kc in range(8):
                            mm(bank(bk), glv[:, kc, br * 128:(br + 1) * 128], HT[:, kc, :], kc == 0, kc == 7, [twm, tt("ht")], [TB[bk]])
                        act(GATE[:, gi, br, :], bank(bk), AF.Sigmoid, [TB[bk]], [tt("gate%d_%d" % (gi, br))])
                    for cc in range(4):
                        mm(bank(b0), pav[:, cc, :], YA[:, cc, :], cc == 0, cc == 3, [twm, tt("ya")], [TB[b0]])
                    for cc in range(4):
                        mm(bank(b0 + 1), prv[:, cc, :], YR[:, cc, :], cc == 0, cc == 3, [twm, tt("yr")], [TB[b0 + 1]])
                    tten("dve", MT[:, 0, :], bank(b0), GATE[:, gi, 0, :], MUL, [TB[b0], tt("gate%d_0" % gi)], [tt("mt0")])
                    tten("dve", MT[:, 1, :], bank(b0 + 1), GATE[:, gi, 1, :], MUL, [TB[b0 + 1], tt("gate%d_1" % gi)], [tt("mt1")])
                    tten("pool", MG[:, n, :], MT[:, 0, :], MT[:, 1, :], ADD, [tt("mt0"), tt("mt1")], [tt("mg")])
                ckpt('mrg%d' % tg)
                wo0, two0 = ws_get(BLK_WO)
                wo1, two1 = ws_get(BLK_WO + 1, la=NWB - 1)
                wov = [wo0.rearrange("p (k c) -> p k c", k=8), wo1.rearrange("p (k c) -> p k c", k=8)]
                twos = [two0, two1]
                for j in range(4):
                    tile = tg * 4 + j
                    pb0 = 2 * j
                    for hf in range(2):
                        for kc in range(8):
                            mm(bank(pb0 + hf), MG[:, kc, j * 128:(j + 1) * 128], wov[hf][:, kc, :], kc == 0, kc == 7,
                               [tt("mg"), twos[hf]], [TB[pb0 + hf]])
                    xi = load_x(tile)
                    act(JUNK, bank(pb0, 2), AF.Square, [TB[pb0], TB[pb0 + 1]], [tt("junk"), tt("ss2")], accum_out=MISC[:, M_SS + 1:M_SS + 2])
                    act(MISC[:, M_RSTD + 1:M_RSTD + 2], MISC[:, M_SS + 1:M_SS + 2], AF.Ln, [tt("ss2")], [tt("rstd2")], scale=1.0 / DM, bias=EPS)
                    act(MISC[:, M_RSTD + 1:M_RSTD + 2], MISC[:, M_RSTD + 1:M_RSTD + 2], AF.Exp, [tt("rstd2")], [tt("rstd2")], scale=-0.5)
                    stt("dve", YT[:], bank(pb0, 2), MISC[:, M_RSTD + 1:M_RSTD + 2], GG[:], MUL, MUL,
                        [TB[pb0], TB[pb0 + 1], tt("rstd2"), tt("gg")], [tt("yt")])
                    tten("pool", XT[:, xi, :], XT[:, xi, :], YT[:], ADD, [tt("xt%d" % xi), tt("yt")], [tt("xt%d" % xi)])
                    dma("pool", out_d[tile * 128:(tile + 1) * 128, :], XT[:, xi, :], [tt("xt%d" % xi)], [])
        try:
            run_all()
        except _Stop:
            pass
        dump_aps = {"HT": HT, "K2": K2, "VX": VX, "RF": RF, "MISC": MISC, "GG": GG, "DMK": DMK, "QDEC": QDEC, "KD": KD,
                    "DEC": DEC, "YA": YA, "YR": YR, "MG": MG, "QT": QT, "ZA": ZA, "RKT": RKT, "RVG": RVG, "SST": SST,
                    "RQ": RQ, "QD": QD, "ZR": ZR, "XN": XN, "CS": CS}
        for nm in dumps:
            src = dump_aps[nm]
            shp = list(src.shape)
            flat = int(np.prod(shp[1:]))
            dd = nc.dram_tensor("dbg_" + nm, [128, flat], src.dtype, kind="ExternalOutput").ap()
            allT = list(tdict.values()) + TB
            sv = src[:] if len(shp) == 2 else src[:].rearrange("p a b -> p (a b)") if len(shp) == 3 else src[:].rearrange("p a b c -> p (a b c)")
            dma("sp", dd, sv, allT, [])
        S.emit(nc)
    return nc, S


_CACHE = {}


def kernel(x, c, w_ada, b_ada, g_pre, w_in, qn_g, kn_g, w_dec_f, w_dec_b, gn_g, w_pa, w_pr, w_out, g_post):
    x = np.asarray(x, np.float32)
    c = np.asarray(c, np.float32)
    f = lambda a: np.asarray(a, np.float32)[0]
    w_ada, b_ada, g_pre, w_in, qn_g, kn_g = f(w_ada), f(b_ada), f(g_pre), f(w_in), f(qn_g), f(kn_g)
    w_dec_f, w_dec_b, gn_g, w_pa, w_pr, w_out, g_post = f(w_dec_f), f(w_dec_b), f(gn_g), f(w_pa), f(w_pr), f(w_out), f(g_post)
    if "nc" not in _CACHE:
        _CACHE["nc"] = build_program()[0]
        _CACHE["consts"] = _host_consts()
    nc = _CACHE["nc"]
    rope, cst = _CACHE["consts"]
    blocks = _host_blocks(w_ada, w_in, w_pa, w_pr, w_out)
    col8 = lambda v: np.ascontiguousarray(v.reshape(8, 128).T)
    rows = np.ascontiguousarray(np.stack([b_ada[2048:3072], g_post], 0))
    wdec = np.concatenate([w_dec_f, w_dec_b])[None, :].astype(np.float32)
    in_maps = []
    for b in range(8):
        small = np.zeros((128, NSMALL), np.float32)
        small[:, 0:8] = col8(c[b])
        small[:, 8:16] = col8(g_pre)
        small[:, 16:24] = col8(b_ada[1024:2048])
        small[:, 24:32] = col8(b_ada[0:1024])
        small[:, 32] = np.concatenate([qn_g, qn_g])
        small[:, 33] = np.concatenate([kn_g, kn_g])
        small[:, 34:38] = gn_g.reshape(4, 128).T
        in_maps.append({"x": np.ascontiguousarray(x[b]), "wsrc": blocks, "rope": rope, "cst": cst,
                        "small": small, "rows": rows, "wdec": wdec})
    res = run_bass_kernel_spmd(nc, in_maps, core_ids=list(range(8)))
    return np.stack([np.asarray(r["out"], np.float32) for r in res.results], 0)
```

```python
import numpy as np
import concourse.bass as bass
import concourse.mybir as mybir
from concourse.bass_utils import run_bass_kernel_spmd

F32 = mybir.dt.float32
BF16 = mybir.dt.bfloat16
AF = mybir.ActivationFunctionType
ALU = mybir.AluOpType
AX = mybir.AxisListType


class T:
    __slots__ = ("name", "last_w", "readers", "excl")

    def __init__(self, name="", excl=False):
        self.name = name
        self.last_w = None
        self.readers = []
        self.excl = excl


class Op:
    __slots__ = ("eng", "fn", "deps", "signal", "token", "is_dma", "idx", "gidx", "preds", "succs", "dur", "lat", "cls", "prio", "t_start", "t_done", "npred")

    def __init__(self, eng, fn, is_dma):
        self.eng = eng
        self.fn = fn
        self.deps = []
        self.signal = False
        self.token = None
        self.is_dma = is_dma
        self.preds = []
        self.succs = []
        self.dur = 0.2
        self.lat = 0.0
        self.cls = None


class Sched:
    ENGS = ("pe", "act", "dve", "pool", "sp")

    def __init__(self, n_dma_sems=8):
        self.ops = {e: [] for e in self.ENGS}
        self.n_dma_sems = n_dma_sems
        self.dma_rr = {e: 0 for e in self.ENGS}
        self.dma_last = {}
        self.gcount = 0

    def add(self, eng, fn, reads=(), writes=(), dma=False, dur=0.2, lat=0.0, cls=None):
        op = Op(eng, fn, dma)
        op.dur, op.lat, op.cls = dur, lat, cls
        op.gidx = self.gcount
        self.gcount += 1
        writes = list(writes) + [t for t in reads if t.excl]
        reads = [t for t in reads if not t.excl]
        raw = []
        war = []
        for t in reads:
            if t.last_w is not None:
                raw.append(t.last_w)
        for t in writes:
            if t.last_w is not None:
                raw.append(t.last_w)
            war.extend(t.readers)
        for t in reads:
            t.readers.append(op)
        for t in writes:
            t.last_w = op
            t.readers = []
        if dma:
            k = self.dma_rr[eng]
            self.dma_rr[eng] = (k + 1) % self.n_dma_sems
            prev = self.dma_last.get((eng, k))
            if prev is not None:
                raw.append(prev)
            self.dma_last[(eng, k)] = op
            op.token = (eng, k)
        seen = set()
        pseen = set()
        for lst, is_war in ((raw, False), (war, True)):
            for d in lst:
                if d is op or id(d) in seen:
                    continue
                if id(d) not in pseen:
                    pseen.add(id(d))
                    op.preds.append(d)
                if (not d.is_dma) and (not dma) and d.eng == eng:
                    if eng == "pe" or is_war:
                        continue
                seen.add(id(d))
                op.deps.append(d)
        op.idx = len(self.ops[eng])
        self.ops[eng].append(op)
        return op

    def reschedule(self, hop=0.15, act_switch=1.4):
        import heapq
        allops = []
        for e in self.ENGS:
            allops.extend(self.ops[e])
        allops.sort(key=lambda o: o.gidx)
        for o in allops:
            o.succs = []
        for o in allops:
            o.npred = len(o.preds)
            for p in o.preds:
                p.succs.append(o)
        for o in reversed(allops):
            m = 0.0
            for s_ in o.succs:
                if s_.prio > m:
                    m = s_.prio
            o.prio = m + o.dur + o.lat
        ready = {e: [] for e in self.ENGS}
        for o in allops:
            o.t_done = None
            if o.npred == 0:
                ready[o.eng].append(o)
        free_at = {e: 0.0 for e in self.ENGS}
        cur_cls = {e: None for e in self.ENGS}
        new_order = {e: [] for e in self.ENGS}
        nleft = len(allops)

        def rtime(o):
            t = 0.0
            for p in o.preds:
                tp = p.t_done + (hop if (p.eng != o.eng or p.is_dma) else 0.0)
                if tp > t:
                    t = tp
            return t

        rt_cache = {}
        while nleft:
            best = None
            for e in self.ENGS:
                lst = ready[e]
                if not lst:
                    continue
                fa = free_at[e]
                for o in lst:
                    r = rt_cache.get(id(o))
                    if r is None:
                        r = rtime(o)
                        rt_cache[id(o)] = r
                    st = r if r > fa else fa
                    if e == "act" and o.cls is not None and cur_cls[e] is not None and o.cls != cur_cls[e]:
                        st += act_switch
                    key = (st, -o.prio)
                    if best is None or key < best[0]:
                        best = (key, o)
            (st, _), o = best
            e = o.eng
            ready[e].remove(o)
            o.t_start = st
            free_at[e] = st + o.dur
            o.t_done = st + o.dur + o.lat
            if e == "act" and o.cls is not None:
                cur_cls[e] = o.cls
            new_order[e].append(o)
            nleft -= 1
            for s_ in o.succs:
                s_.npred -= 1
                if s_.npred == 0:
                    ready[s_.eng].append(s_)
        for e in self.ENGS:
            self.ops[e] = new_order[e]
        self.makespan = max(free_at.values())

    def emit(self, nc, block_ctx_extra=None):
        for e in self.ENGS:
            for op in self.ops[e]:
                for d in op.deps:
                    d.signal = True
        import contextlib
        with contextlib.ExitStack() as st:
            sems = {e: st.enter_context(nc.semaphore("s_" + e)) for e in ("pe", "act", "dve", "pool")}
            dsems = {}
            for e in self.ENGS:
                if any(o.is_dma for o in self.ops[e]):
                    for k in range(self.n_dma_sems):
                        dsems[(e, k)] = st.enter_context(nc.semaphore("d_%s_%d" % (e, k)))
            for e in self.ENGS:
                cnt = 0
                dcnt = {}
                for op in self.ops[e]:
                    if op.is_dma:
                        key = op.token
                        dcnt[key] = dcnt.get(key, 0) + 16
                        op.token = (dsems[key], dcnt[key], key)
                    elif op.signal:
                        cnt += 1
                        op.token = (sems[e], cnt, e)
            block = st.enter_context(nc.Block())
            handles = {"pe": block.tensor, "act": block.scalar, "dve": block.vector, "pool": block.gpsimd,
                       "sp": block.sync}
            nwaits = {e: 0 for e in self.ENGS}

            def make(e):
                ops = self.ops[e]

                def body(eng):
                    known = {}
                    for op in ops:
                        need = {}
                        for d in op.deps:
                            sem, val, key = d.token
                            if known.get(key, 0) >= val:
                                continue
                            if key not in need or need[key][1] < val:
                                need[key] = (sem, val)
                        for key, (sem, val) in need.items():
                            eng.wait_ge(sem, val)
                            known[key] = val
                            nwaits[e] += 1
                        inst = op.fn(eng)
                        if op.is_dma:
                            inst.then_inc(op.token[0], 16)
                        elif op.signal:
                            inst.then_inc(op.token[0], 1)
                    fin = {}
                    for op in ops:
                        if op.is_dma:
                            fin[op.token[2]] = (op.token[0], op.token[1])
                    for key, (sem, val) in fin.items():
                        if known.get(key, 0) < val:
                            eng.wait_ge(sem, val)
                return body

            for e in self.ENGS:
                if self.ops[e]:
                    handles[e](make(e))
            self.nwaits = nwaits

import contextlib

SEQ = 4096
DM = 1024
NG = 8
EPS = 1e-6

BLK_ADA = 0
BLK_P2KV = 6
BLK_RK = 7
BLK_RV = 8
BLK_Q = 9
BLK_ZA = 10
BLK_RQ = 11
BLK_ZR = 12
BLK_M = 13
BLK_WO = 21
NBLK = 23
NSMALL = 40


def _host_blocks(w_ada, w_in, w_pa, w_pr, w_out):
    blocks = np.zeros((NBLK, 128, 4096), np.float32)

    def kmaj(w):
        nc_ = w.shape[1]
        t = np.zeros((128, 8, 512), np.float32)
        t[:, :, :nc_] = w.reshape(8, 128, nc_).transpose(1, 0, 2)
        return t.reshape(128, 4096)

    wa = w_ada
    for j in range(6):
        blocks[BLK_ADA + j] = kmaj(wa[:, j * 512:(j + 1) * 512])
    o_q, o_k, o_v, o_za, o_rq, o_rk, o_rv, o_zr, o_gl = 0, 512, 640, 768, 1280, 1536, 1792, 2304, 2816
    k0 = w_in[:, o_k:o_k + 64]
    k1 = w_in[:, o_k + 64:o_k + 128]
    blocks[BLK_P2KV] = kmaj(np.concatenate([k0, k0, k1, k1, w_in[:, o_v:o_v + 128]], axis=1))
    blocks[BLK_RK] = kmaj(w_in[:, o_rk:o_rk + 256])
    blocks[BLK_RV] = kmaj(w_in[:, o_rv:o_rv + 512])
    blocks[BLK_Q] = kmaj(w_in[:, o_q:o_q + 512])
    blocks[BLK_ZA] = kmaj(w_in[:, o_za:o_za + 512])
    rq = [w_in[:, o_rq + 64 * h:o_rq + 64 * (h + 1)] for h in range(4)]
    blocks[BLK_RQ] = kmaj(np.concatenate([rq[0], rq[0], rq[1], rq[1], rq[2], rq[2], rq[3], rq[3]], axis=1))
    blocks[BLK_ZR] = kmaj(w_in[:, o_zr:o_zr + 512])
    for n in range(8):
        t = np.zeros((128, 4096), np.float32)
        gl = np.concatenate([w_in[:, o_gl + n * 128:o_gl + (n + 1) * 128],
                             w_in[:, o_gl + 1024 + n * 128:o_gl + 1024 + (n + 1) * 128]], axis=1)
        t[:, 0:2048] = gl.reshape(8, 128, 256).transpose(1, 0, 2).reshape(128, 2048)
        t[:, 2048:2560] = w_pa[:, n * 128:(n + 1) * 128].reshape(4, 128, 128).transpose(1, 0, 2).reshape(128, 512)
        t[:, 2560:3072] = w_pr[:, n * 128:(n + 1) * 128].reshape(4, 128, 128).transpose(1, 0, 2).reshape(128, 512)
        blocks[BLK_M + n] = t
    for hf in range(2):
        blocks[BLK_WO + hf] = kmaj(w_out[:, hf * 512:(hf + 1) * 512])
    return blocks


def _host_consts():
    t = np.arange(SEQ)
    row = (t // 64).astype(np.float32)
    col = (t % 64).astype(np.float32)
    inv = (10000.0 ** (-np.arange(0, 32, 2, dtype=np.float32) / 32.0)).astype(np.float32)
    ang_r = row[None, :] * inv[:, None]
    ang_c = col[None, :] * inv[:, None]
    cos64 = np.concatenate([np.cos(ang_r), np.cos(ang_r), np.cos(ang_c), np.cos(ang_c)], 0)
    sin64 = np.concatenate([np.sin(ang_r), np.sin(ang_r), np.sin(ang_c), np.sin(ang_c)], 0)
    rope = np.stack([np.concatenate([cos64, cos64], 0), np.concatenate([sin64, sin64], 0)], 0).astype(np.float32)
    ident = np.eye(128, dtype=np.float32)
    perm = np.zeros((128, 128), np.float32)
    for hb in range(2):
        for d in range(64):
            blk = d // 16
            if blk % 2 == 0:
                p, s = d + 16, -1.0
            else:
                p, s = d - 16, 1.0
            perm[hb * 64 + p, hb * 64 + d] = s
    ones = np.zeros((128, 128), np.float32)
    ones[:64, :64] = 1.0
    ones[64:, 64:] = 1.0
    return rope, np.stack([ident, perm, ones], 0)


class _Stop(Exception):
    pass


def build_program(stop=None, dumps=()):
    nc = bass.Bass("TRN2", target_bir_lowering=False)
    x_d = nc.dram_tensor("x", [SEQ, DM], F32, kind="ExternalInput").ap()
    wsrc = nc.dram_tensor("wsrc", [NBLK, 128, 4096], F32, kind="ExternalInput").ap()
    rope_d = nc.dram_tensor("rope", [2, 128, SEQ], F32, kind="ExternalInput").ap()
    cst_d = nc.dram_tensor("cst", [3, 128, 128], F32, kind="ExternalInput").ap()
    small_d = nc.dram_tensor("small", [128, NSMALL], F32, kind="ExternalInput").ap()
    rows_d = nc.dram_tensor("rows", [2, DM], F32, kind="ExternalInput").ap()
    wdec_d = nc.dram_tensor("wdec", [1, 8], F32, kind="ExternalInput").ap()
    out_d = nc.dram_tensor("out", [SEQ, DM], F32, kind="ExternalOutput").ap()
    wbf_d = nc.dram_tensor("wbf", [NBLK, 128, 4096], BF16, kind="Internal").ap()

    S = Sched(n_dma_sems=12)
    tdict = {}

    _alias = {"io0": "sa", "io1": "sa", "io2": "sa", "tmpd": "sa", "cbc": "mt1", "junk": "mt0"}

    def tt(name):
        name = _alias.get(name, name)
        if name not in tdict:
            tdict[name] = T(name)
        return tdict[name]

    with contextlib.ExitStack() as st:
        def sb(name, shape, dt):
            return st.enter_context(nc.sbuf_tensor(name, shape, dt))

        PS = st.enter_context(nc.psum_tensor("ps", [128, 4096], F32))
        TB = [T("bank%d" % i, excl=True) for i in range(8)]

        def bank(i, n=1):
            return PS[:, i * 512:(i + n) * 512]

        K2 = sb("K2", [128, 2, SEQ], BF16)
        VX = sb("VX", [128, 32, 320], BF16)
        RF = sb("RF", [128, 16, 512], BF16)
        NWB = 3
        WB = sb("WB", [128, NWB, 4096], BF16)
        XT = sb("XT", [128, 2, DM], F32)
        XN = sb("XN", [128, 2, DM], BF16)
        HT = sb("HT", [128, 8, 512], BF16)
        CS = sb("CS", [128, 2, 512], F32)
        QBF = sb("QBF", [128, 2, 512], BF16)
        SQ = sb("SQ", [128, 2, 512], BF16)
        T1 = sb("T1", [128, 2, 512], F32)
        T2 = sb("T2", [128, 2, 512], F32)
        RRS = sb("RRS", [128, 2, 512], F32)
        QT = sb("QT", [128, 4, 512], BF16)
        ZA = sb("ZA", [128, 4, 512], BF16)
        NPT = 3
        PT = sb("PT", [128, NPT, 1024], BF16)
        SA = sb("SA", [128, 512], F32)
        SB_ = sb("SB_", [128, 512], F32)
        DEN = sb("DEN", [128, 512], F32)
        TMPN = sb("TMPN", [128, 512], F32)
        YA = sb("YA", [128, 4, 512], BF16)
        YR = sb("YR", [128, 4, 512], BF16)
        RQ = sb("RQ", [128, 4, 512], BF16)
        QD = sb("QD", [128, 4, 512], BF16)
        ZR = sb("ZR", [128, 4, 512], BF16)
        RKT = sb("RKT", [128, 2, 512], BF16)
        RVG = sb("RVG", [128, 4, 512], BF16)
        KK = sb("KK", [128, 2, 512], BF16)
        SM = sb("SM", [128, 2, 512], BF16)
        RRO = sb("RRO", [128, 2, 512], BF16)
        ON = sb("ON", [128, 2, 512], BF16)
        SST = sb("SST", [128, 512], F32)
        GATE = sb("GATE", [128, 2, 2, 512], BF16)
        MT = sb("MT", [128, 2, 512], F32)
        MG = sb("MG", [128, 8, 512], BF16)
        YT = sb("YT", [128, DM], F32)
        IDENT = sb("IDENT", [128, 128], BF16)
        PERM = sb("PERM", [128, 128], BF16)
        ONESB = sb("ONESB", [128, 128], BF16)
        DMK = sb("DMK", [128, 512], F32)
        QDEC = sb("QDEC", [128, 512], F32)
        KD = sb("KD", [128, 512], F32)
        DEC = sb("DEC", [128, 512], F32)
        GG = sb("GG", [128, DM], F32)
        SMALL = sb("SMALL", [128, NSMALL], F32)
        MISC = sb("MISC", [128, 128], F32)
        IO = SA[:, 0:384].rearrange("p (a b) -> p a b", a=3)
        TMPD = SA[:, 384:512]
        CBC = MT[:, 1, :].bitcast(BF16).rearrange("p (a b) -> p a b", a=8)
        JUNK = MT[:, 0, :].bitcast(BF16)
        CBF = sb("CBF", [128, 8], BF16)
        STAT = sb("STAT", [128, 2, 4, 8], F32)

        M_ACOL, M_SHCOL, M_LG, M_DC, M_WD, M_E, M_KD8, M_KDV, M_IOP, M_IOPR, M_GK8 = 0, 8, 16, 24, 32, 40, 48, 56, 64, 65, 66
        M_SS, M_RSTD, M_CACT, M_TMP8 = 70, 74, 80, 88
        M_GNR, M_GNN = 96, 104
        S_C, S_GPRE, S_BSC, S_BSH, S_QG, S_KG, S_GN = 0, 8, 16, 24, 32, 33, 34

        def fsz(ap):
            n = 1
            for d in list(ap.shape)[1:]:
                n *= int(d)
            return n

        def mm(o, l, r, start, stop, rd, wr):
            n = fsz(o)
            k = int(l.shape[0])
            d = 0.06 + n * 0.00041
            if k <= 64 and n >= 256:
                d = 0.17
            S.add("pe", lambda e: e.matmul(o, l, r, start=start, stop=stop), reads=rd, writes=wr, dur=d)

        def trp(o, i, rd, wr):
            S.add("pe", lambda e: e.transpose(o, i, IDENT[:]), reads=rd + [tt("ident")], writes=wr, dur=0.115)

        _ACLS = {AF.Exp: "exp", AF.Ln: "exp", AF.Silu: "sig", AF.Sigmoid: "sig"}

        def act(o, i, f, rd, wr, **kw):
            S.add("act", lambda e: e.activation(out=o, in_=i, func=f, **kw), reads=rd, writes=wr,
                  dur=0.22 + fsz(i) * 0.00083 + (0.1 if "accum_out" in kw else 0.0), cls=_ACLS.get(f))

        def _vd(eng, n):
            return (0.1 + n * 0.00104) if eng == "dve" else (0.15 + n * 0.0021)

        def tten(eng, o, a, b, op, rd, wr):
            S.add(eng, lambda e: e.tensor_tensor(out=o, in0=a, in1=b, op=op), reads=rd, writes=wr, dur=_vd(eng, fsz(o)))

        def tsc(eng, o, a, s1, s2, op0, op1, rd, wr):
            if op1 is None:
                S.add(eng, lambda e: e.tensor_scalar(out=o, in0=a, scalar1=s1, scalar2=None, op0=op0), reads=rd, writes=wr, dur=_vd(eng, fsz(o)))
            else:
                S.add(eng, lambda e: e.tensor_scalar(out=o, in0=a, scalar1=s1, scalar2=s2, op0=op0, op1=op1), reads=rd, writes=wr, dur=_vd(eng, fsz(o)))

        def stt(eng, o, a, s, b, op0, op1, rd, wr):
            S.add(eng, lambda e: e.scalar_tensor_tensor(out=o, in0=a, scalar=s, in1=b, op0=op0, op1=op1), reads=rd, writes=wr, dur=_vd(eng, fsz(o)))

        def cpy(eng, o, i, rd, wr):
            S.add(eng, lambda e: e.tensor_copy(out=o, in_=i), reads=rd, writes=wr, dur=_vd(eng, fsz(o)))

        def dma(eng, o, i, rd, wr):
            nbytes = fsz(o) * int(o.shape[0]) * (2 if o.dtype == BF16 else 4)
            S.add(eng, lambda e: e.dma_start(out=o, in_=i), reads=rd, writes=wr, dma=True,
                  dur=(0.08 if eng == "sp" else 1.0), lat=2.0 + nbytes / 150000.0)

        MUL, ADD, MAX = ALU.mult, ALU.add, ALU.max

        def ckpt(name):
            if stop == name:
                raise _Stop()

        def bnstats(o, i, rd, wr):
            S.add("dve", lambda e: e.bn_stats(out=o, in_=i), reads=rd, writes=wr, dur=0.25)

        def bnaggr(o, i, rd, wr):
            S.add("dve", lambda e: e.bn_aggr(out=o, in_=i), reads=rd, writes=wr, dur=0.15)

        def recip(o, i, rd, wr):
            S.add("dve", lambda e: e.reciprocal(out=o, in_=i), reads=rd, writes=wr, dur=0.1 + fsz(o) * 0.00625)

        order = [BLK_ADA + j for j in range(6)]
        for tg in range(NG):
            order += [BLK_P2KV, BLK_RK, BLK_RV]
        for tg in range(NG):
            order += [BLK_Q, BLK_ZA, BLK_RQ, BLK_RK, BLK_RV, BLK_ZR] + [BLK_M + n for n in range(8)] + [BLK_WO, BLK_WO + 1]
        conv_done = set()
        ws_state = {"issued": 0, "cur": 0}

        def conv(blk):
            if blk in conv_done:
                return
            conv_done.add(blk)
            dma("pool", wbf_d[blk], wsrc[blk], [], [tt("wbf%d" % blk)])

        def ws_issue_upto(n):
            while ws_state["issued"] < min(n, len(order)):
                i = ws_state["issued"]
                blk = order[i]
                conv(blk)
                dma("sp", WB[:, i % NWB, :], wbf_d[blk], [tt("wbf%d" % blk)], [tt("wb%d" % (i % NWB))])
                ws_state["issued"] += 1

        def ws_get(expect, la=None):
            i = ws_state["cur"]
            assert order[i] == expect, (i, order[i], expect)
            ws_issue_upto(i + (NWB if la is None else la))
            ws_state["cur"] += 1
            return WB[:, i % NWB, :], tt("wb%d" % (i % NWB))

        def run_all():
            for blk in [BLK_ADA + j for j in range(6)] + [BLK_P2KV, BLK_RK, BLK_RV]:
                conv(blk)
            dma("sp", SMALL[:], small_d, [], [tt("small")])
            dma("sp", GG[:], rows_d[0:1, :].partition_broadcast(128), [], [tt("gg")])
            dma("sp", YT[:], rows_d[1:2, :].partition_broadcast(128), [], [tt("yt")])
            dma("sp", MISC[:, M_WD:M_WD + 8], wdec_d.partition_broadcast(128), [], [tt("wd")])
            dma("pool", IDENT[:], cst_d[0], [], [tt("ident")])
            dma("pool", PERM[:], cst_d[1], [], [tt("perm")])
            dma("pool", ONESB[:], cst_d[2], [], [tt("onesb")])
            ws_issue_upto(NWB)

            act(MISC[:, M_CACT:M_CACT + 8], SMALL[:, S_C:S_C + 8], AF.Silu, [tt("small")], [tt("cact")])
            cpy("dve", CBF[:], MISC[:, M_CACT:M_CACT + 8], [tt("cact")], [tt("cbf")])
            cpy("dve", CBC, MISC[:, M_CACT:M_CACT + 8].unsqueeze(2).broadcast_to([128, 8, 128]), [tt("cact")], [tt("cbc")])
            act(MISC[:, M_E:M_E + 8], MISC[:, M_WD:M_WD + 8], AF.Exp, [tt("wd")], [tt("e")], scale=-1.0)
            act(MISC[:, M_E:M_E + 8], MISC[:, M_E:M_E + 8], AF.Ln, [tt("e")], [tt("e")], bias=1.0)
            tsc("dve", MISC[:, M_LG:M_LG + 8], MISC[:, M_E:M_E + 8], -1.0, None, MUL, None, [tt("e")], [tt("lg")])
            LGF = lambda h, lo=0, hi=128: MISC[lo:hi, M_LG + h:M_LG + h + 1]
            LGB = lambda h, lo=0, hi=128: MISC[lo:hi, M_LG + 4 + h:M_LG + 5 + h]
            S.add("pool", lambda e: e.iota(IO[:, 0, :], pattern=[[1, 128]], base=0, channel_multiplier=-1,
                                           allow_small_or_imprecise_dtypes=True), writes=[tt("io0")])
            S.add("pool", lambda e: e.iota(IO[:, 1, :], pattern=[[1, 128]], base=1, channel_multiplier=0,
                                           allow_small_or_imprecise_dtypes=True), writes=[tt("io1")])
            S.add("pool", lambda e: e.iota(IO[:, 2, :], pattern=[[-1, 128]], base=128, channel_multiplier=0,
                                           allow_small_or_imprecise_dtypes=True), writes=[tt("io2")])
            S.add("pool", lambda e: e.iota(MISC[:, M_IOP:M_IOP + 1], pattern=[[1, 1]], base=0, channel_multiplier=1,
                                           allow_small_or_imprecise_dtypes=True), writes=[tt("iop")])
            S.add("pool", lambda e: e.iota(MISC[:, M_IOPR:M_IOPR + 1], pattern=[[1, 1]], base=127, channel_multiplier=-1,
                                           allow_small_or_imprecise_dtypes=True), writes=[tt("iopr")])
            NEG = RRS[:, 0, 0:128]
            POS = RRS[:, 0, 128:256]
            tsc("dve", POS, IO[:, 0, :], 0.0, None, MAX, None, [tt("io0")], [tt("pos")])
            tten("dve", NEG, POS, IO[:, 0, :], ALU.subtract, [tt("pos"), tt("io0")], [tt("neg")])
            for h in range(4):
                tsc("dve", TMPD, POS, LGF(h), None, MUL, None, [tt("pos"), tt("lg")], [tt("tmpd")])
                stt("dve", TMPD, NEG, LGB(h), TMPD, MUL, ADD, [tt("neg"), tt("lg"), tt("tmpd")], [tt("tmpd")])
                act(DMK[:, h * 128:(h + 1) * 128], TMPD, AF.Exp, [tt("tmpd")], [tt("dmk")])
                tsc("dve", TMPD[0:64, :], IO[0:64, 1, :], LGF(h, 0, 64), None, MUL, None, [tt("io1"), tt("lg"), tt("dmk")], [tt("tmpd")])
                tsc("dve", TMPD[64:128, :], IO[64:128, 2, :], LGB(h, 64, 128), None, MUL, None, [tt("io2"), tt("lg"), tt("tmpd")], [tt("tmpd")])
                act(QDEC[:, h * 128:(h + 1) * 128], TMPD, AF.Exp, [tt("tmpd")], [tt("qdec")])
            KDV = MISC[:, M_KDV:M_KDV + 8].rearrange("p (h d) -> p h d", d=2)
            tsc("dve", MISC[:, M_KD8:M_KD8 + 4], MISC[:, M_LG:M_LG + 4], MISC[:, M_IOPR:M_IOPR + 1], None, MUL, None,
                [tt("lg"), tt("iopr")], [tt("kd8")])
            tsc("dve", MISC[:, M_KD8 + 4:M_KD8 + 8], MISC[:, M_LG + 4:M_LG + 8], MISC[:, M_IOP:M_IOP + 1], None, MUL, None,
                [tt("lg"), tt("iop"), tt("kd8")], [tt("kd8")])
            act(KDV[:, :, 0], MISC[:, M_KD8:M_KD8 + 4], AF.Exp, [tt("kd8")], [tt("kdv")])
            act(KDV[:, :, 1], MISC[:, M_KD8 + 4:M_KD8 + 8], AF.Exp, [tt("kd8"), tt("kdv")], [tt("kdv")])
            cpy("dve", KD[:].rearrange("p (a c) -> p a c", c=64),
                MISC[:, M_KDV:M_KDV + 8].unsqueeze(2).broadcast_to([128, 8, 64]), [tt("kdv")], [tt("kd")])
            act(MISC[:, M_DC:M_DC + 8], MISC[:, M_LG:M_LG + 8], AF.Exp, [tt("lg")], [tt("dc")], scale=128.0)
            cpy("dve", DEC[0:64, :].rearrange("p (h c) -> p h c", c=128),
                MISC[0:64, M_DC:M_DC + 4].unsqueeze(2).broadcast_to([64, 4, 128]), [tt("dc")], [tt("dec")])
            cpy("dve", DEC[64:128, :].rearrange("p (h c) -> p h c", c=128),
                MISC[64:128, M_DC + 4:M_DC + 8].unsqueeze(2).broadcast_to([64, 4, 128]), [tt("dc"), tt("dec")], [tt("dec")])
            tsc("dve", MISC[:, M_GK8:M_GK8 + 1], SMALL[:, S_KG:S_KG + 1], 8.0, None, MUL, None, [tt("small")], [tt("gk8")])
            S.add("pool", lambda e: e.memset(SST[:], 0.0), writes=[tt("sst")])
            S.add("pool", lambda e: e.memset(VX[:], 1.0), writes=[tt("vx%d" % g) for g in range(NG)])

            ckpt('setup0')
            pmod = bank(6)[:, 0:16]
            for blk in range(4):
                wb, twb = ws_get(BLK_ADA + blk)
                wbv = wb.rearrange("p (k c) -> p k c", k=8)
                for c in range(4):
                    for kc in range(8):
                        mm(pmod[:, blk * 4 + c:blk * 4 + c + 1], wbv[:, kc, c * 128:(c + 1) * 128], CBF[:, kc:kc + 1],
                           kc == 0, kc == 7, [twb, tt("cbf")], [TB[6]])
            tten("dve", MISC[:, M_TMP8:M_TMP8 + 8], pmod[:, 8:16], SMALL[:, S_BSC:S_BSC + 8], ADD, [TB[6], tt("small")], [tt("tmp8")])
            stt("dve", MISC[:, M_ACOL:M_ACOL + 8], MISC[:, M_TMP8:M_TMP8 + 8], 1.0, SMALL[:, S_GPRE:S_GPRE + 8], ADD, MUL,
                [tt("tmp8"), tt("small")], [tt("acol")])
            tten("dve", MISC[:, M_SHCOL:M_SHCOL + 8], pmod[:, 0:8], SMALL[:, S_BSH:S_BSH + 8], ADD, [TB[6], tt("small")], [tt("shcol")])
            for hf in range(2):
                wb, twb = ws_get(BLK_ADA + 4 + hf)
                wbv = wb.rearrange("p (k c) -> p k c", k=8)
                for kc in range(8):
                    mm(bank(4 + hf), CBC[:, kc, :], wbv[:, kc, :], kc == 0, kc == 7, [twb, tt("cbc")], [TB[4 + hf]])
                tten("dve", GG[:, hf * 512:(hf + 1) * 512], bank(4 + hf), GG[:, hf * 512:(hf + 1) * 512], ADD,
                     [TB[4 + hf], tt("gg")], [tt("gg")])
                tten("dve", GG[:, hf * 512:(hf + 1) * 512], GG[:, hf * 512:(hf + 1) * 512], YT[:, hf * 512:(hf + 1) * 512], MUL,
                     [tt("gg"), tt("yt")], [tt("gg")])

            ckpt('ada')
            xt_rr = {"i": 0}

            def load_x(tile):
                i = xt_rr["i"] % 2
                xt_rr["i"] += 1
                dma("sp", XT[:, i, :], x_d[tile * 128:(tile + 1) * 128, :], [], [tt("xt%d" % i)])
                return i

            cur = {"ht": HT, "htT": tt("ht")}

            def hT_tile(tg, j):
                tile = tg * 4 + j
                xi = load_x(tile)
                xb = j % 2
                act(JUNK, XT[:, xi, :], AF.Square, [tt("xt%d" % xi)], [tt("junk"), tt("ss")], accum_out=MISC[:, M_SS:M_SS + 1])
                act(MISC[:, M_RSTD:M_RSTD + 1], MISC[:, M_SS:M_SS + 1], AF.Ln, [tt("ss")], [tt("rstd")], scale=1.0 / DM, bias=EPS)
                act(MISC[:, M_RSTD:M_RSTD + 1], MISC[:, M_RSTD:M_RSTD + 1], AF.Exp, [tt("rstd")], [tt("rstd")], scale=-0.5)
                tsc("dve", XN[:, xb, :], XT[:, xi, :], MISC[:, M_RSTD:M_RSTD + 1], None, MUL, None,
                    [tt("xt%d" % xi), tt("rstd")], [tt("xn%d" % xb)])
                for kc in range(8):
                    pb = bank(kc // 2).bitcast(BF16)[:, (kc % 2) * 512 + j * 128:(kc % 2) * 512 + (j + 1) * 128]
                    trp(pb, XN[:, xb, kc * 128:(kc + 1) * 128], [tt("xn%d" % xb)], [TB[kc // 2]])

            def hT_evac(dst, dstT):
                for kc in range(8):
                    pb = bank(kc // 2).bitcast(BF16)[:, (kc % 2) * 512:(kc % 2 + 1) * 512]
                    if kc < 4:
                        tsc("dve", dst[:, kc, :], pb, MISC[:, M_ACOL + kc:M_ACOL + kc + 1], MISC[:, M_SHCOL + kc:M_SHCOL + kc + 1],
                            MUL, ADD, [TB[kc // 2], tt("acol"), tt("shcol")], [dstT])
                    else:
                        act(dst[:, kc, :], pb, AF.Identity, [TB[kc // 2], tt("acol"), tt("shcol")], [dstT],
                            scale=MISC[:, M_ACOL + kc:M_ACOL + kc + 1], bias=MISC[:, M_SHCOL + kc:M_SHCOL + kc + 1])

            def make_hT(tg):
                cur["ht"], cur["htT"] = HT, tt("ht")
                for j in range(4):
                    hT_tile(tg, j)
                hT_evac(HT, tt("ht"))

            def load_cs(tg):
                dma("sp", CS[:, 0, :], rope_d[0][:, tg * 512:(tg + 1) * 512], [], [tt("cs")])
                dma("sp", CS[:, 1, :], rope_d[1][:, tg * 512:(tg + 1) * 512], [], [tt("cs")])

            def proj_fm(wbv, twb, c0, bk):
                for kc in range(8):
                    mm(bank(bk), wbv[:, kc, c0:c0 + 128], cur["ht"][:, kc, :], kc == 0, kc == 7, [twb, cur["htT"]], [TB[bk]])

            def proj_tm(wbv, twb, j, c0, ncol, o, tbk):
                for kc in range(8):
                    mm(o, cur["ht"][:, kc, j * 128:(j + 1) * 128], wbv[:, kc, c0:c0 + ncol], kc == 0, kc == 7, [twb, cur["htT"]], [tbk])

            rope_ctr = {"i": 0, "alt": [(4, 5)]}

            def rope(bkA, gcol, gT, rms, out_ap, out_T):
                i = rope_ctr["i"] % 2
                rope_ctr["i"] += 1
                bS, bC = rope_ctr["alt"][i % len(rope_ctr["alt"])]
                A = bank(bkA)
                rdg = [gT] if gT is not None else []
                gsc = gcol if gT is not None else float(gcol)
                tsc("dve", QBF[:, i, :], A, gsc, None, MUL, None, [TB[bkA]] + rdg, [tt("qbf%d" % i)])
                if rms:
                    act(SQ[:, i, :], A, AF.Square, [TB[bkA]], [tt("sq%d" % i)])
                mm(bank(bS), PERM[:], QBF[:, i, :], True, True, [tt("perm"), tt("qbf%d" % i)], [TB[bS]])
                if rms:
                    mm(bank(bC), ONESB[:], SQ[:, i, :], True, True, [tt("onesb"), tt("sq%d" % i)], [TB[bC]])
                stt("dve", T1[:, i, :], A, gsc, CS[:, 0, :], MUL, MUL, [TB[bkA], tt("cs")] + rdg, [tt("t1_%d" % i)])
                tten("dve", T2[:, i, :], bank(bS), CS[:, 1, :], MUL, [TB[bS], tt("cs")], [tt("t2_%d" % i)])
                if rms:
                    act(RRS[:, i, :], bank(bC), AF.Ln, [TB[bC]], [tt("rrs%d" % i)], bias=64.0 * EPS)
                    act(RRS[:, i, :], RRS[:, i, :], AF.Exp, [tt("rrs%d" % i)], [tt("rrs%d" % i)], scale=-0.5)
                    tten("pool", T1[:, i, :], T1[:, i, :], T2[:, i, :], ADD, [tt("t1_%d" % i), tt("t2_%d" % i)], [tt("t1_%d" % i)])
                    tten("pool", out_ap, T1[:, i, :], RRS[:, i, :], MUL, [tt("t1_%d" % i), tt("rrs%d" % i)], [out_T])
                else:
                    tten("pool", out_ap, T1[:, i, :], T2[:, i, :], ADD, [tt("t1_%d" % i), tt("t2_%d" % i)], [out_T])

            def ret_rk(tg):
                wb, twb = ws_get(BLK_RK)
                wbv = wb.rearrange("p (k c) -> p k c", k=8)
                for pr in range(2):
                    proj_fm(wbv, twb, pr * 128, 6 + pr)
                    rope(6 + pr, 0.125, None, False, RKT[:, pr, :], tt("rkt"))

            def ret_rv(tg):
                wb, twb = ws_get(BLK_RV)
                wbv = wb.rearrange("p (k c) -> p k c", k=8)
                for j in range(4):
                    bk = 6 + (j % 2)
                    proj_tm(wbv, twb, j, 0, 512, bank(bk), TB[bk])
                    act(RVG[:, j, :], bank(bk), AF.Copy, [TB[bk]], [tt("rvg")])

            def ret_kv_inputs(tg):
                ret_rk(tg)
                ret_rv(tg)

            def kv_matmul(j):
                ptk = bank(6).bitcast(BF16)[:, 0:256]
                for pr in range(2):
                    trp(ptk[:, pr * 128:(pr + 1) * 128], RKT[:, pr, j * 128:(j + 1) * 128], [tt("rkt")], [TB[6]])
                kb = j % 2
                tten("dve", KK[:, kb, :].rearrange("p (h d c) -> p h d c", h=4, d=2),
                     ptk.rearrange("p (h c) -> p h c", h=4).unsqueeze(2).broadcast_to([128, 4, 2, 64]),
                     KD[:].rearrange("p (h d c) -> p h d c", h=4, d=2), MUL, [TB[6], tt("kd")], [tt("kk%d" % kb)])
                for h in range(4):
                    mm(bank(7)[:, h * 128:(h + 1) * 128], KK[:, kb, h * 128:(h + 1) * 128], RVG[:, j, h * 128:(h + 1) * 128],
                       True, True, [tt("kk%d" % kb), tt("rvg")], [TB[7]])

            bufs = [(HT, tt("ht")), (MG, tt("mg"))]
            for j in range(4):
                hT_tile(0, j)
            hT_evac(*bufs[0])
            for tg in range(NG):
                cur["ht"], cur["htT"] = bufs[tg % 2]
                nxt = bufs[(tg + 1) % 2] if tg + 1 < NG else None
                ckpt('ht%d' % tg)
                load_cs(tg)
                wb, twb = ws_get(BLK_P2KV)
                wbv = wb.rearrange("p (k c) -> p k c", k=8)
                for g in range(2):
                    proj_fm(wbv, twb, g * 128, 6 + g)
                    rope(6 + g, MISC[:, M_GK8:M_GK8 + 1], tt("gk8"), True, K2[:, g, tg * 512:(tg + 1) * 512], tt("k2_%d_%d" % (g, tg)))
                ckpt('k%d' % tg)
                if nxt:
                    hT_tile(tg + 1, 0)
                for j in range(4):
                    proj_tm(wbv, twb, j, 256, 128, bank(6)[:, j * 128:(j + 1) * 128], TB[6])
                cpy("dve", VX[:, tg * 4:(tg + 1) * 4, 64:320].rearrange("p j (g c) -> p j g c", g=2)[:, :, :, 0:64],
                    bank(6).rearrange("p (j g c) -> p j g c", j=4, g=2), [TB[6]], [tt("vx%d" % tg)])
                ckpt('v%d' % tg)
                if nxt:
                    hT_tile(tg + 1, 1)
                ret_rk(tg)
                if nxt:
                    hT_tile(tg + 1, 2)
                ret_rv(tg)
                ckpt('rkv%d' % tg)
                if nxt:
                    hT_tile(tg + 1, 3)
                    hT_evac(*nxt)
                for j in range(4):
                    ci = tg * 4 + j
                    kv_matmul(j)
                    act(RF[(ci % 2) * 64:(ci % 2) * 64 + 64, ci // 2, :], SST[0:64, :], AF.Copy, [tt("sst")], [tt("rf")])
                    tten("dve", SST[0:64, :], SST[0:64, :], DEC[0:64, :], MUL, [tt("sst"), tt("dec")], [tt("sst")])
                    tten("dve", SST[0:64, :], SST[0:64, :], bank(7)[0:64, :], ADD, [tt("sst"), TB[7]], [tt("sst")])

            rope_ctr["alt"] = [(4, 5), (2, 3)]
            ckpt('p2')
            for tg in range(NG - 1, -1, -1):
                make_hT(tg)
                load_cs(tg)
                wq, twq = ws_get(BLK_Q)
                wqv = wq.rearrange("p (k c) -> p k c", k=8)
                for p in range(4):
                    proj_fm(wqv, twq, p * 128, 6 + (p % 2))
                    rope(6 + (p % 2), SMALL[:, S_QG:S_QG + 1], tt("small"), True, QT[:, p, :], tt("qt%d" % p))
                wz, twz = ws_get(BLK_ZA)
                wzv = wz.rearrange("p (k c) -> p k c", k=8)
                for p in range(4):
                    bk = 6 + (p % 2)
                    proj_fm(wzv, twz, p * 128, bk)
                    act(ZA[:, p, :], bank(bk), AF.Silu, [TB[bk]], [tt("za%d" % p)])
                ckpt('q%d' % tg)
                for p in range(4):
                    g = p // 2

                    def qk(kc):
                        si = kc % 2
                        mm(bank(2 * si), K2[0:64, g, kc * 128:(kc + 1) * 128], QT[0:64, p, :], True, True,
                           [tt("k2_%d_%d" % (g, kc // 4)), tt("qt%d" % p)], [TB[2 * si]])
                        mm(bank(2 * si + 1), K2[64:128, g, kc * 128:(kc + 1) * 128], QT[64:128, p, :], True, True,
                           [tt("k2_%d_%d" % (g, kc // 4)), tt("qt%d" % p)], [TB[2 * si + 1]])

                    def pv(kc):
                        pi = kc % NPT
                        mm(bank(4), VX[:, kc, 64 + g * 128:192 + g * 128], PT[:, pi, 0:512], kc == 0, kc == 31,
                           [tt("vx%d" % (kc // 4)), tt("pt%d" % pi)], [TB[4]])
                        mm(bank(5), VX[:, kc, g * 128:g * 128 + 128], PT[:, pi, 512:1024], kc == 0, kc == 31,
                           [tt("vx%d" % (kc // 4)), tt("pt%d" % pi)], [TB[5]])

                    qk(0)
                    qk(1)
                    for kc in range(32):
                        si = kc % 2
                        act(PT[:, kc % NPT, :], bank(2 * si, 2), AF.Exp, [TB[2 * si], TB[2 * si + 1]], [tt("pt%d" % (kc % NPT))])
                        if kc + 2 < 32:
                            qk(kc + 2)
                        pv(kc)
                    cpy("dve", SA[:], bank(4), [TB[4]], [tt("sa")])
                    cpy("dve", SB_[:], bank(5), [TB[5]], [tt("sb")])
                    recip(DEN[0:64, :], SA[64:128, :], [tt("sa")], [tt("den")])
                    recip(DEN[64:128, :], SB_[0:64, :], [tt("sb")], [tt("den")])
                    tten("pool", TMPN[0:64, :], SA[0:64, :], DEN[0:64, :], MUL, [tt("sa"), tt("den")], [tt("tmpn")])
                    tten("pool", TMPN[64:128, :], SB_[64:128, :], DEN[64:128, :], MUL, [tt("sb"), tt("den")], [tt("tmpn")])
                    tten("pool", YA[:, p, :], TMPN[:], ZA[:, p, :], MUL, [tt("tmpn"), tt("za%d" % p)], [tt("ya")])
                ckpt('att%d' % tg)
                wr, twr = ws_get(BLK_RQ)
                wrv = wr.rearrange("p (k c) -> p k c", k=8)
                for h in range(4):
                    proj_fm(wrv, twr, h * 128, 6 + (h % 2))
                    rope(6 + (h % 2), 1.0, None, False, RQ[:, h, :], tt("rq"))
                    tten("pool", QD[:, h, :].rearrange("p (j c) -> p j c", j=4), RQ[:, h, :].rearrange("p (j c) -> p j c", j=4),
                         QDEC[:, h * 128:(h + 1) * 128].unsqueeze(1).broadcast_to([128, 4, 128]), MUL, [tt("rq"), tt("qdec")], [tt("qd")])
                ckpt('rqd%d' % tg)
                ret_kv_inputs(tg)
                wz, twz = ws_get(BLK_ZR)
                wzv = wz.rearrange("p (k c) -> p k c", k=8)
                for h in range(4):
                    bk = 6 + (h % 2)
                    proj_fm(wzv, twz, h * 128, bk)
                    act(ZR[:, h, :], bank(bk), AF.Silu, [TB[bk]], [tt("zr")])
                ckpt('rin%d' % tg)
                for j in range(3, -1, -1):
                    ci = tg * 4 + j
                    sb_i = j % 2
                    for h in range(4):
                        r0 = (h % 2) * 64
                        mm(bank(h % 2)[:, (h // 2) * 128:(h // 2 + 1) * 128], RKT[r0:r0 + 64, h // 2, j * 128:(j + 1) * 128],
                           RQ[r0:r0 + 64, h, j * 128:(j + 1) * 128], True, True, [tt("rkt"), tt("rq")], [TB[h % 2]])
                    for hb in range(2):
                        tten("dve", SM[:, sb_i, :].rearrange("p (a b c) -> p a b c", a=2, b=2)[:, :, hb, :],
                             bank(hb)[:, 0:256].rearrange("p (a c) -> p a c", a=2),
                             DMK[:].rearrange("p (a b c) -> p a b c", a=2, b=2)[:, :, hb, :], MUL, [TB[hb], tt("dmk")], [tt("sm%d" % sb_i)])
                    ckpt('rs%d' % ci)
                    act(RRO[0:64, sb_i, :], RF[(ci % 2) * 64:(ci % 2) * 64 + 64, ci // 2, :], AF.Copy, [tt("rf")], [tt("rro%d" % sb_i)])
                    cpy("pool", RRO[64:128, sb_i, :], SST[64:128, :], [tt("sst")], [tt("rro%d" % sb_i)])
                    for h in range(4):
                        mm(bank(2)[:, h * 128:(h + 1) * 128], SM[:, sb_i, h * 128:(h + 1) * 128], RVG[:, j, h * 128:(h + 1) * 128],
                           True, False, [tt("sm%d" % sb_i), tt("rvg")], [TB[2]])
                        mm(bank(2)[:, h * 128:(h + 1) * 128], QD[:, h, j * 128:(j + 1) * 128], RRO[:, sb_i, h * 128:(h + 1) * 128],
                           False, True, [tt("qd"), tt("rro%d" % sb_i)], [TB[2]])
                    ckpt('ro%d' % ci)
                    for h in range(4):
                        bnstats(STAT[:, sb_i, h, 0:6], bank(2)[:, h * 128:(h + 1) * 128], [TB[2]], [tt("stat%d" % sb_i)])
                        bnaggr(STAT[:, sb_i, h, 6:8], STAT[:, sb_i, h, 0:6], [tt("stat%d" % sb_i)], [tt("stat%d" % sb_i)])
                    gr = MISC[:, M_GNR + sb_i * 4:M_GNR + sb_i * 4 + 4]
                    gn = MISC[:, M_GNN + sb_i * 4:M_GNN + sb_i * 4 + 4]
                    act(gr, STAT[:, sb_i, :, 7], AF.Ln, [tt("stat%d" % sb_i)], [tt("gr%d" % sb_i)], bias=EPS)
                    act(gr, gr, AF.Exp, [tt("gr%d" % sb_i)], [tt("gr%d" % sb_i)], scale=-0.5)
                    stt("dve", gn, STAT[:, sb_i, :, 6], -1.0, gr, MUL, MUL, [tt("stat%d" % sb_i), tt("gr%d" % sb_i)], [tt("gn%d" % sb_i)])
                    for h in range(4):
                        tsc("dve", ON[:, sb_i, h * 128:(h + 1) * 128], bank(2)[:, h * 128:(h + 1) * 128],
                            MISC[:, M_GNR + sb_i * 4 + h:M_GNR + sb_i * 4 + h + 1], MISC[:, M_GNN + sb_i * 4 + h:M_GNN + sb_i * 4 + h + 1],
                            MUL, ADD, [TB[2], tt("gr%d" % sb_i), tt("gn%d" % sb_i)], [tt("on%d" % sb_i)])
                    ckpt('rgn%d' % ci)
                    ptn = bank(3).bitcast(BF16)[:, 0:512]
                    for h in range(4):
                        trp(ptn[:, h * 128:(h + 1) * 128], ON[:, sb_i, h * 128:(h + 1) * 128], [tt("on%d" % sb_i)], [TB[3]])
                    for h in range(4):
                        stt("dve", YR[:, h, j * 128:(j + 1) * 128], ptn[:, h * 128:(h + 1) * 128], SMALL[:, S_GN + h:S_GN + h + 1],
                            ZR[:, h, j * 128:(j + 1) * 128], MUL, MUL, [TB[3], tt("small"), tt("zr")], [tt("yr")])
                    ckpt('ryr%d' % ci)
                    kv_matmul(j)
                    tten("dve", SST[64:128, :], SST[64:128, :], DEC[64:128, :], MUL, [tt("sst"), tt("dec")], [tt("sst")])
                    tten("dve", SST[64:128, :], SST[64:128, :], bank(7)[64:128, :], ADD, [tt("sst"), TB[7]], [tt("sst")])
                ckpt('ret%d' % tg)
                for n in range(8):
                    wm, twm = ws_get(BLK_M + n)
                    glv = wm[:, 0:2048].rearrange("p (k c) -> p k c", k=8)
                    pav = wm[:, 2048:2560].rearrange("p (k c) -> p k c", k=4)
                    prv = wm[:, 2560:3072].rearrange("p (k c) -> p k c", k=4)
                    gi = n % 2
                    b0 = 4 * (n % 2)
                    for br in range(2):
                        bk = b0 + 2 + br
                        for kc in range(8):
                            mm(bank(bk), glv[:, kc, br * 128:(br + 1) * 128], HT[:, kc, :], kc == 0, kc == 7, [twm, tt("ht")], [TB[bk]])
                        act(GATE[:, gi, br, :], bank(bk), AF.Sigmoid, [TB[bk]], [tt("gate%d_%d" % (gi, br))])
                    for cc in range(4):
                        mm(bank(b0), pav[:, cc, :], YA[:, cc, :], cc == 0, cc == 3, [twm, tt("ya")], [TB[b0]])
                    for cc in range(4):
                        mm(bank(b0 + 1), prv[:, cc, :], YR[:, cc, :], cc == 0, cc == 3, [twm, tt("yr")], [TB[b0 + 1]])
                    tten("dve", MT[:, 0, :], bank(b0), GATE[:, gi, 0, :], MUL, [TB[b0], tt("gate%d_0" % gi)], [tt("mt0")])
                    tten("dve", MT[:, 1, :], bank(b0 + 1), GATE[:, gi, 1, :], MUL, [TB[b0 + 1], tt("gate%d_1" % gi)], [tt("mt1")])
                    tten("pool", MG[:, n, :], MT[:, 0, :], MT[:, 1, :], ADD, [tt("mt0"), tt("mt1")], [tt("mg")])
                ckpt('mrg%d' % tg)
                wo0, two0 = ws_get(BLK_WO)
                wo1, two1 = ws_get(BLK_WO + 1, la=NWB - 1)
                wov = [wo0.rearrange("p (k c) -> p k c", k=8), wo1.rearrange("p (k c) -> p k c", k=8)]
                twos = [two0, two1]
                for j in range(4):
                    tile = tg * 4 + j
                    pb0 = 2 * j
                    for hf in range(2):
                        for kc in range(8):
                            mm(bank(pb0 + hf), MG[:, kc, j * 128:(j + 1) * 128], wov[hf][:, kc, :], kc == 0, kc == 7,
                               [tt("mg"), twos[hf]], [TB[pb0 + hf]])
                    xi = load_x(tile)
                    act(JUNK, bank(pb0, 2), AF.Square, [TB[pb0], TB[pb0 + 1]], [tt("junk"), tt("ss2")], accum_out=MISC[:, M_SS + 1:M_SS + 2])
                    act(MISC[:, M_RSTD + 1:M_RSTD + 2], MISC[:, M_SS + 1:M_SS + 2], AF.Ln, [tt("ss2")], [tt("rstd2")], scale=1.0 / DM, bias=EPS)
                    act(MISC[:, M_RSTD + 1:M_RSTD + 2], MISC[:, M_RSTD + 1:M_RSTD + 2], AF.Exp, [tt("rstd2")], [tt("rstd2")], scale=-0.5)
                    stt("dve", YT[:], bank(pb0, 2), MISC[:, M_RSTD + 1:M_RSTD + 2], GG[:], MUL, MUL,
                        [TB[pb0], TB[pb0 + 1], tt("rstd2"), tt("gg")], [tt("yt")])
                    tten("pool", XT[:, xi, :], XT[:, xi, :], YT[:], ADD, [tt("xt%d" % xi), tt("yt")], [tt("xt%d" % xi)])
                    dma("pool", out_d[tile * 128:(tile + 1) * 128, :], XT[:, xi, :], [tt("xt%d" % xi)], [])
        try:
            run_all()
        except _Stop:
            pass
        dump_aps = {"HT": HT, "K2": K2, "VX": VX, "RF": RF, "MISC": MISC, "GG": GG, "DMK": DMK, "QDEC": QDEC, "KD": KD,
                    "DEC": DEC, "YA": YA, "YR": YR, "MG": MG, "QT": QT, "ZA": ZA, "RKT": RKT, "RVG": RVG, "SST": SST,
                    "RQ": RQ, "QD": QD, "ZR": ZR, "XN": XN, "CS": CS}
        for nm in dumps:
            src = dump_aps[nm]
            shp = list(src.shape)
            flat = int(np.prod(shp[1:]))
            dd = nc.dram_tensor("dbg_" + nm, [128, flat], src.dtype, kind="ExternalOutput").ap()
            allT = list(tdict.values()) + TB
            sv = src[:] if len(shp) == 2 else src[:].rearrange("p a b -> p (a b)") if len(shp) == 3 else src[:].rearrange("p a b c -> p (a b c)")
            dma("sp", dd, sv, allT, [])
        import os
        if os.environ.get('MK_RESCHED', '1') == '1':
            S.reschedule()
        S.emit(nc)
    return nc, S


_CACHE = {}


def kernel(x, c, w_ada, b_ada, g_pre, w_in, qn_g, kn_g, w_dec_f, w_dec_b, gn_g, w_pa, w_pr, w_out, g_post):
    x = np.asarray(x, np.float32)
    c = np.asarray(c, np.float32)
    f = lambda a: np.asarray(a, np.float32)[0]
    w_ada, b_ada, g_pre, w_in, qn_g, kn_g = f(w_ada), f(b_ada), f(g_pre), f(w_in), f(qn_g), f(kn_g)
    w_dec_f, w_dec_b, gn_g, w_pa, w_pr, w_out, g_post = f(w_dec_f), f(w_dec_b), f(gn_g), f(w_pa), f(w_pr), f(w_out), f(g_post)
    if "nc" not in _CACHE:
        _CACHE["nc"] = build_program()[0]
        _CACHE["consts"] = _host_consts()
    nc = _CACHE["nc"]
    rope, cst = _CACHE["consts"]
    blocks = _host_blocks(w_ada, w_in, w_pa, w_pr, w_out)
    col8 = lambda v: np.ascontiguousarray(v.reshape(8, 128).T)
    rows = np.ascontiguousarray(np.stack([b_ada[2048:3072], g_post], 0))
    wdec = np.concatenate([w_dec_f, w_dec_b])[None, :].astype(np.float32)
    in_maps = []
    for b in range(8):
        small = np.zeros((128, NSMALL), np.float32)
        small[:, 0:8] = col8(c[b])
        small[:, 8:16] = col8(g_pre)
        small[:, 16:24] = col8(b_ada[1024:2048])
        small[:, 24:32] = col8(b_ada[0:1024])
        small[:, 32] = np.concatenate([qn_g, qn_g])
        small[:, 33] = np.concatenate([kn_g, kn_g])
        small[:, 34:38] = gn_g.reshape(4, 128).T
        in_maps.append({"x": np.ascontiguousarray(x[b]), "wsrc": blocks, "rope": rope, "cst": cst,
                        "small": small, "rows": rows, "wdec": wdec})
    res = run_bass_kernel_spmd(nc, in_maps, core_ids=list(range(8)))
    return np.stack([np.asarray(r["out"], np.float32) for r in res.results], 0)
```

```python
import numpy as np
import concourse.bass as bass
import concourse.mybir as mybir
from concourse.bass_utils import run_bass_kernel_spmd

F32 = mybir.dt.float32
BF16 = mybir.dt.bfloat16
AF = mybir.ActivationFunctionType
ALU = mybir.AluOpType
AX = mybir.AxisListType


class T:
    __slots__ = ("name", "last_w", "readers", "excl")

    def __init__(self, name="", excl=False):
        self.name = name
        self.last_w = None
        self.readers = []
        self.excl = excl


class Op:
    __slots__ = ("eng", "fn", "deps", "signal", "token", "is_dma", "idx", "gidx", "preds", "succs", "dur", "lat", "cls", "prio", "t_start", "t_done", "npred", "tag")

    def __init__(self, eng, fn, is_dma):
        self.eng = eng
        self.fn = fn
        self.deps = []
        self.signal = False
        self.token = None
        self.is_dma = is_dma
        self.preds = []
        self.succs = []
        self.dur = 0.2
        self.lat = 0.0
        self.cls = None


class Sched:
    ENGS = ("pe", "act", "dve", "pool", "sp")

    def __init__(self, n_dma_sems=8):
        self.ops = {e: [] for e in self.ENGS}
        self.n_dma_sems = n_dma_sems
        self.dma_rr = {e: 0 for e in self.ENGS}
        self.dma_last = {}
        self.gcount = 0

    def add(self, eng, fn, reads=(), writes=(), dma=False, dur=0.2, lat=0.0, cls=None):
        op = Op(eng, fn, dma)
        op.dur, op.lat, op.cls = dur, lat, cls
        import sys as _sys
        op.tag = (_sys._getframe(2).f_lineno, _sys._getframe(3).f_lineno)
        op.gidx = self.gcount
        self.gcount += 1
        writes = list(writes) + [t for t in reads if t.excl]
        reads = [t for t in reads if not t.excl]
        raw = []
        war = []
        for t in reads:
            if t.last_w is not None:
                raw.append(t.last_w)
        for t in writes:
            if t.last_w is not None:
                raw.append(t.last_w)
            war.extend(t.readers)
        for t in reads:
            t.readers.append(op)
        for t in writes:
            t.last_w = op
            t.readers = []
        if dma:
            k = self.dma_rr[eng]
            self.dma_rr[eng] = (k + 1) % self.n_dma_sems
            prev = self.dma_last.get((eng, k))
            if prev is not None:
                raw.append(prev)
            self.dma_last[(eng, k)] = op
            op.token = (eng, k)
        seen = set()
        pseen = set()
        for lst, is_war in ((raw, False), (war, True)):
            for d in lst:
                if d is op or id(d) in seen:
                    continue
                if id(d) not in pseen:
                    pseen.add(id(d))
                    op.preds.append(d)
                if (not d.is_dma) and (not dma) and d.eng == eng:
                    if eng == "pe" or is_war:
                        continue
                seen.add(id(d))
                op.deps.append(d)
        op.idx = len(self.ops[eng])
        self.ops[eng].append(op)
        return op

    def reschedule(self, hop=0.15, act_switch=1.4):
        import heapq
        allops = []
        for e in self.ENGS:
            allops.extend(self.ops[e])
        allops.sort(key=lambda o: o.gidx)
        for o in allops:
            o.succs = []
        for o in allops:
            o.npred = len(o.preds)
            for p in o.preds:
                p.succs.append(o)
        for o in reversed(allops):
            m = 0.0
            for s_ in o.succs:
                if s_.prio > m:
                    m = s_.prio
            o.prio = m + o.dur + o.lat
        ready = {e: [] for e in self.ENGS}
        for o in allops:
            o.t_done = None
            if o.npred == 0:
                ready[o.eng].append(o)
        free_at = {e: 0.0 for e in self.ENGS}
        cur_cls = {e: None for e in self.ENGS}
        new_order = {e: [] for e in self.ENGS}
        nleft = len(allops)

        def rtime(o):
            t = 0.0
            for p in o.preds:
                tp = p.t_done + (hop if (p.eng != o.eng or p.is_dma) else 0.0)
                if tp > t:
                    t = tp
            return t

        rt_cache = {}
        while nleft:
            best = None
            for e in self.ENGS:
                lst = ready[e]
                if not lst:
                    continue
                fa = free_at[e]
                for o in lst:
                    r = rt_cache.get(id(o))
                    if r is None:
                        r = rtime(o)
                        rt_cache[id(o)] = r
                    st = r if r > fa else fa
                    if e == "act" and o.cls is not None and cur_cls[e] is not None and o.cls != cur_cls[e]:
                        st += act_switch
                    key = (st, -o.prio)
                    if best is None or key < best[0]:
                        best = (key, o)
            (st, _), o = best
            e = o.eng
            ready[e].remove(o)
            o.t_start = st
            free_at[e] = st + o.dur
            o.t_done = st + o.dur + o.lat
            if e == "act" and o.cls is not None:
                cur_cls[e] = o.cls
            new_order[e].append(o)
            nleft -= 1
            for s_ in o.succs:
                s_.npred -= 1
                if s_.npred == 0:
                    ready[s_.eng].append(s_)
        for e in self.ENGS:
            self.ops[e] = new_order[e]
        self.makespan = max(free_at.values())

    def emit(self, nc, block_ctx_extra=None):
        for e in self.ENGS:
            for op in self.ops[e]:
                for d in op.deps:
                    d.signal = True
        import contextlib
        with contextlib.ExitStack() as st:
            sems = {e: st.enter_context(nc.semaphore("s_" + e)) for e in ("pe", "act", "dve", "pool")}
            dsems = {}
            for e in self.ENGS:
                if any(o.is_dma for o in self.ops[e]):
                    for k in range(self.n_dma_sems):
                        dsems[(e, k)] = st.enter_context(nc.semaphore("d_%s_%d" % (e, k)))
            for e in self.ENGS:
                cnt = 0
                dcnt = {}
                for op in self.ops[e]:
                    if op.is_dma:
                        key = op.token
                        dcnt[key] = dcnt.get(key, 0) + 16
                        op.token = (dsems[key], dcnt[key], key)
                    elif op.signal:
                        cnt += 1
                        op.token = (sems[e], cnt, e)
            block = st.enter_context(nc.Block())
            handles = {"pe": block.tensor, "act": block.scalar, "dve": block.vector, "pool": block.gpsimd,
                       "sp": block.sync}
            nwaits = {e: 0 for e in self.ENGS}

            def make(e):
                ops = self.ops[e]

                def body(eng):
                    known = {}
                    for op in ops:
                        need = {}
                        for d in op.deps:
                            sem, val, key = d.token
                            if known.get(key, 0) >= val:
                                continue
                            if key not in need or need[key][1] < val:
                                need[key] = (sem, val)
                        for key, (sem, val) in need.items():
                            eng.wait_ge(sem, val)
                            known[key] = val
                            nwaits[e] += 1
                        inst = op.fn(eng)
                        if op.is_dma:
                            inst.then_inc(op.token[0], 16)
                        elif op.signal:
                            inst.then_inc(op.token[0], 1)
                    fin = {}
                    for op in ops:
                        if op.is_dma:
                            fin[op.token[2]] = (op.token[0], op.token[1])
                    for key, (sem, val) in fin.items():
                        if known.get(key, 0) < val:
                            eng.wait_ge(sem, val)
                return body

            for e in self.ENGS:
                if self.ops[e]:
                    handles[e](make(e))
            self.nwaits = nwaits

import contextlib

SEQ = 4096
DM = 1024
NG = 8
EPS = 1e-6

BLK_ADA = 0
BLK_P2KV = 6
BLK_RK = 7
BLK_RV = 8
BLK_Q = 9
BLK_ZA = 10
BLK_RQ = 11
BLK_ZR = 12
BLK_M = 13
BLK_WO = 21
NBLK = 23
NSMALL = 40


def _host_blocks(w_ada, w_in, w_pa, w_pr, w_out):
    blocks = np.zeros((NBLK, 128, 4096), np.float32)

    def kmaj(w):
        nc_ = w.shape[1]
        t = np.zeros((128, 8, 512), np.float32)
        t[:, :, :nc_] = w.reshape(8, 128, nc_).transpose(1, 0, 2)
        return t.reshape(128, 4096)

    wa = w_ada
    for j in range(6):
        blocks[BLK_ADA + j] = kmaj(wa[:, j * 512:(j + 1) * 512])
    o_q, o_k, o_v, o_za, o_rq, o_rk, o_rv, o_zr, o_gl = 0, 512, 640, 768, 1280, 1536, 1792, 2304, 2816
    k0 = w_in[:, o_k:o_k + 64]
    k1 = w_in[:, o_k + 64:o_k + 128]
    blocks[BLK_P2KV] = kmaj(np.concatenate([k0, k0, k1, k1, w_in[:, o_v:o_v + 128]], axis=1))
    blocks[BLK_RK] = kmaj(w_in[:, o_rk:o_rk + 256])
    blocks[BLK_RV] = kmaj(w_in[:, o_rv:o_rv + 512])
    blocks[BLK_Q] = kmaj(w_in[:, o_q:o_q + 512])
    blocks[BLK_ZA] = kmaj(w_in[:, o_za:o_za + 512])
    rq = [w_in[:, o_rq + 64 * h:o_rq + 64 * (h + 1)] for h in range(4)]
    blocks[BLK_RQ] = kmaj(np.concatenate([rq[0], rq[0], rq[1], rq[1], rq[2], rq[2], rq[3], rq[3]], axis=1))
    blocks[BLK_ZR] = kmaj(w_in[:, o_zr:o_zr + 512])
    for n in range(8):
        t = np.zeros((128, 4096), np.float32)
        gl = np.concatenate([w_in[:, o_gl + n * 128:o_gl + (n + 1) * 128],
                             w_in[:, o_gl + 1024 + n * 128:o_gl + 1024 + (n + 1) * 128]], axis=1)
        t[:, 0:2048] = gl.reshape(8, 128, 256).transpose(1, 0, 2).reshape(128, 2048)
        t[:, 2048:2560] = w_pa[:, n * 128:(n + 1) * 128].reshape(4, 128, 128).transpose(1, 0, 2).reshape(128, 512)
        t[:, 2560:3072] = w_pr[:, n * 128:(n + 1) * 128].reshape(4, 128, 128).transpose(1, 0, 2).reshape(128, 512)
        blocks[BLK_M + n] = t
    for hf in range(2):
        blocks[BLK_WO + hf] = kmaj(w_out[:, hf * 512:(hf + 1) * 512])
    return blocks


def _host_consts():
    t = np.arange(SEQ)
    row = (t // 64).astype(np.float32)
    col = (t % 64).astype(np.float32)
    inv = (10000.0 ** (-np.arange(0, 32, 2, dtype=np.float32) / 32.0)).astype(np.float32)
    ang_r = row[None, :] * inv[:, None]
    ang_c = col[None, :] * inv[:, None]
    cos64 = np.concatenate([np.cos(ang_r), np.cos(ang_r), np.cos(ang_c), np.cos(ang_c)], 0)
    sin64 = np.concatenate([np.sin(ang_r), np.sin(ang_r), np.sin(ang_c), np.sin(ang_c)], 0)
    rope = np.stack([np.concatenate([cos64, cos64], 0), np.concatenate([sin64, sin64], 0)], 0).astype(np.float32)
    ident = np.eye(128, dtype=np.float32)
    perm = np.zeros((128, 128), np.float32)
    for hb in range(2):
        for d in range(64):
            blk = d // 16
            if blk % 2 == 0:
                p, s = d + 16, -1.0
            else:
                p, s = d - 16, 1.0
            perm[hb * 64 + p, hb * 64 + d] = s
    ones = np.zeros((128, 128), np.float32)
    ones[:64, :64] = 1.0
    ones[64:, 64:] = 1.0
    return rope, np.stack([ident, perm, ones], 0)


class _Stop(Exception):
    pass


def build_program(stop=None, dumps=()):
    nc = bass.Bass("TRN2", target_bir_lowering=False)
    x_d = nc.dram_tensor("x", [SEQ, DM], F32, kind="ExternalInput").ap()
    wsrc = nc.dram_tensor("wsrc", [NBLK, 128, 4096], F32, kind="ExternalInput").ap()
    rope_d = nc.dram_tensor("rope", [2, 128, SEQ], F32, kind="ExternalInput").ap()
    cst_d = nc.dram_tensor("cst", [3, 128, 128], F32, kind="ExternalInput").ap()
    small_d = nc.dram_tensor("small", [128, NSMALL], F32, kind="ExternalInput").ap()
    rows_d = nc.dram_tensor("rows", [2, DM], F32, kind="ExternalInput").ap()
    wdec_d = nc.dram_tensor("wdec", [1, 8], F32, kind="ExternalInput").ap()
    out_d = nc.dram_tensor("out", [SEQ, DM], F32, kind="ExternalOutput").ap()
    wbf_d = nc.dram_tensor("wbf", [NBLK, 128, 4096], BF16, kind="Internal").ap()

    S = Sched(n_dma_sems=12)
    tdict = {}

    import os
    _alias = {"io0": "sa", "io1": "sa", "io2": "sa", "tmpd": "sa", "cbc": "mt1", "junk": "mt0"}
    if os.environ.get("MK_JUNK") == "1":
        del _alias["junk"]

    def tt(name):
        name = _alias.get(name, name)
        if name not in tdict:
            tdict[name] = T(name)
        return tdict[name]

    with contextlib.ExitStack() as st:
        def sb(name, shape, dt):
            return st.enter_context(nc.sbuf_tensor(name, shape, dt))

        PS = st.enter_context(nc.psum_tensor("ps", [128, 4096], F32))
        TB = [T("bank%d" % i, excl=True) for i in range(8)]

        def bank(i, n=1):
            return PS[:, i * 512:(i + n) * 512]

        import os
        _EXP = os.environ.get("MK_EXPLORE") == "1"
        _PDT = mybir.dt.int8 if _EXP else BF16
        K2 = sb("K2", [128, 2, SEQ], _PDT)
        VX = sb("VX", [128, 32, 320], _PDT)
        RF = sb("RF", [128, 16, 512], _PDT)
        NWB = int(os.environ.get('MK_NWB', '3'))
        WB = sb("WB", [128, NWB, 4096], BF16)
        NXT = int(os.environ.get('MK_NXT', '3'))
        XT = sb("XT", [128, NXT, DM], F32)
        XN = sb("XN", [128, 2, DM], BF16)
        HT = sb("HT", [128, 8, 512], BF16)
        HT2 = sb("HT2", [128, 8, 512], BF16)
        CS = sb("CS", [128, 2, 512], F32)
        QBF = sb("QBF", [128, 2, 512], BF16)
        SQ = sb("SQ", [128, 2, 512], BF16)
        T1 = sb("T1", [128, 2, 512], F32)
        T2 = sb("T2", [128, 2, 512], F32)
        RRS = sb("RRS", [128, 2, 512], F32)
        QT = sb("QT", [128, 4, 512], BF16)
        ZA = sb("ZA", [128, 4, 512], BF16)
        NPT = int(os.environ.get('MK_NPT', '3'))
        PT = sb("PT", [128, NPT, 1024], BF16)
        SA = sb("SA", [128, 512], F32)
        SB_ = sb("SB_", [128, 512], F32)
        DEN = sb("DEN", [128, 512], F32)
        YA = sb("YA", [128, 4, 512], BF16)
        YR = sb("YR", [128, 4, 512], BF16)
        RQ = sb("RQ", [128, 4, 512], BF16)
        QD = sb("QD", [128, 4, 512], BF16)
        ZR = sb("ZR", [128, 4, 512], BF16)
        RKT = sb("RKT", [128, 2, 512], BF16)
        RVG = sb("RVG", [128, 4, 512], BF16)
        KK = sb("KK", [128, 2, 512], BF16)
        SM = sb("SM", [128, 2, 512], BF16)
        RRO = sb("RRO", [128, 2, 512], BF16)
        ON = sb("ON", [128, 2, 512], BF16)
        SST = sb("SST", [128, 512], F32)
        NGT = int(os.environ.get("MK_NGT", "1"))
        GATE = sb("GATE", [128, NGT, 2, 512], BF16)
        MT = sb("MT", [128, 2, 512], F32)
        MG = sb("MG", [128, 8, 512], BF16)
        IDENT = sb("IDENT", [128, 128], BF16)
        PERM = sb("PERM", [128, 128], BF16)
        ONESB = sb("ONESB", [128, 128], BF16)
        DMK = sb("DMK", [128, 512], F32)
        QDEC = sb("QDEC", [128, 512], F32)
        GG = sb("GG", [128, DM], F32)
        SMALL = sb("SMALL", [128, NSMALL], F32)
        MISC = sb("MISC", [128, 128], F32)
        IO = SA[:, 0:384].rearrange("p (a b) -> p a b", a=3)
        TMPD = SA[:, 384:512]
        CBC = MT[:, 1, :].bitcast(BF16).rearrange("p (a b) -> p a b", a=8)
        JUNK = sb("JUNKB", [128, DM], BF16)[:] if os.environ.get("MK_JUNK") == "1" else MT[:, 0, :].bitcast(BF16)
        CBF = sb("CBF", [128, 8], BF16)
        STAT = sb("STAT", [128, 2, 4, 8], F32)

        _CACHE['sbuf_left'] = nc.sbuf_bytes_remaining
        M_ACOL, M_SHCOL, M_LG, M_DC, M_WD, M_E, M_KD8, M_KDV, M_IOP, M_IOPR, M_GK8 = 0, 8, 16, 24, 32, 40, 48, 56, 64, 65, 66
        M_SS, M_RSTD, M_CACT, M_TMP8 = 70, 74, 80, 88
        M_GNR, M_GNN = 96, 104
        S_C, S_GPRE, S_BSC, S_BSH, S_QG, S_KG, S_GN = 0, 8, 16, 24, 32, 33, 34

        def fsz(ap):
            n = 1
            for d in list(ap.shape)[1:]:
                n *= int(d)
            return n

        def mm(o, l, r, start, stop, rd, wr):
            n = fsz(o)
            k = int(l.shape[0])
            d = 0.06 + n * 0.00041
            if k <= 64 and n >= 256:
                d = 0.17
            S.add("pe", lambda e: e.matmul(o, l, r, start=start, stop=stop), reads=rd, writes=wr, dur=d)

        def trp(o, i, rd, wr):
            S.add("pe", lambda e: e.transpose(o, i, IDENT[:]), reads=rd + [tt("ident")], writes=wr, dur=0.115)

        _ACLS = {AF.Exp: "exp", AF.Ln: "exp", AF.Silu: "sig", AF.Sigmoid: "sig"}

        def act(o, i, f, rd, wr, **kw):
            S.add("act", lambda e: e.activation(out=o, in_=i, func=f, **kw), reads=rd, writes=wr,
                  dur=0.22 + fsz(i) * 0.00083 + (0.1 if "accum_out" in kw else 0.0), cls=_ACLS.get(f))

        def _vd(eng, n):
            return (0.1 + n * 0.00104) if eng == "dve" else (0.15 + n * 0.0021)

        def tten(eng, o, a, b, op, rd, wr):
            S.add(eng, lambda e: e.tensor_tensor(out=o, in0=a, in1=b, op=op), reads=rd, writes=wr, dur=_vd(eng, fsz(o)))

        def tsc(eng, o, a, s1, s2, op0, op1, rd, wr):
            if op1 is None:
                S.add(eng, lambda e: e.tensor_scalar(out=o, in0=a, scalar1=s1, scalar2=None, op0=op0), reads=rd, writes=wr, dur=_vd(eng, fsz(o)))
            else:
                S.add(eng, lambda e: e.tensor_scalar(out=o, in0=a, scalar1=s1, scalar2=s2, op0=op0, op1=op1), reads=rd, writes=wr, dur=_vd(eng, fsz(o)))

        def stt(eng, o, a, s, b, op0, op1, rd, wr):
            S.add(eng, lambda e: e.scalar_tensor_tensor(out=o, in0=a, scalar=s, in1=b, op0=op0, op1=op1), reads=rd, writes=wr, dur=_vd(eng, fsz(o)))

        def cpy(eng, o, i, rd, wr):
            S.add(eng, lambda e: e.tensor_copy(out=o, in_=i), reads=rd, writes=wr, dur=_vd(eng, fsz(o)))

        def dma(eng, o, i, rd, wr):
            nbytes = fsz(o) * int(o.shape[0]) * (2 if o.dtype == BF16 else 4)
            S.add(eng, lambda e: e.dma_start(out=o, in_=i), reads=rd, writes=wr, dma=True,
                  dur=(0.08 if eng == "sp" else 1.0), lat=2.0 + nbytes / 150000.0)

        MUL, ADD, MAX = ALU.mult, ALU.add, ALU.max

        def ckpt(name):
            if stop == name:
                raise _Stop()

        def bnstats(o, i, rd, wr):
            S.add("dve", lambda e: e.bn_stats(out=o, in_=i), reads=rd, writes=wr, dur=0.25)

        def bnaggr(o, i, rd, wr):
            S.add("dve", lambda e: e.bn_aggr(out=o, in_=i), reads=rd, writes=wr, dur=0.15)

        def recip(o, i, rd, wr):
            S.add("dve", lambda e: e.reciprocal(out=o, in_=i), reads=rd, writes=wr, dur=0.1 + fsz(o) * 0.00625)

        order = [BLK_ADA + j for j in range(6)]
        for tg in range(NG):
            order += [BLK_P2KV, BLK_RK, BLK_RV]
        for tg in range(NG):
            order += [BLK_Q, BLK_ZA, BLK_RQ, BLK_RK, BLK_RV, BLK_ZR] + [BLK_M + n for n in range(8)] + [BLK_WO, BLK_WO + 1]
        conv_done = set()
        ws_state = {"issued": 0, "cur": 0}

        def conv(blk):
            if blk in conv_done:
                return
            conv_done.add(blk)
            dma("pool", wbf_d[blk], wsrc[blk], [], [tt("wbf%d" % blk)])

        def ws_issue_upto(n):
            while ws_state["issued"] < min(n, len(order)):
                i = ws_state["issued"]
                blk = order[i]
                conv(blk)
                dma("sp", WB[:, i % NWB, :], wbf_d[blk], [tt("wbf%d" % blk)], [tt("wb%d" % (i % NWB))])
                ws_state["issued"] += 1

        def ws_get(expect, la=None):
            i = ws_state["cur"]
            assert order[i] == expect, (i, order[i], expect)
            ws_issue_upto(i + (NWB if la is None else la))
            ws_state["cur"] += 1
            return WB[:, i % NWB, :], tt("wb%d" % (i % NWB))

        def run_all():
            for blk in [BLK_ADA + j for j in range(6)] + [BLK_P2KV, BLK_RK, BLK_RV]:
                conv(blk)
            dma("sp", SMALL[:], small_d, [], [tt("small")])
            dma("sp", GG[:], rows_d[0:1, :].partition_broadcast(128), [], [tt("gg")])
            dma("sp", XT[:, 0, :], rows_d[1:2, :].partition_broadcast(128), [], [tt("xt0")])
            dma("sp", MISC[:, M_WD:M_WD + 8], wdec_d.partition_broadcast(128), [], [tt("wd")])
            dma("pool", IDENT[:], cst_d[0], [], [tt("ident")])
            dma("pool", PERM[:], cst_d[1], [], [tt("perm")])
            dma("pool", ONESB[:], cst_d[2], [], [tt("onesb")])
            ws_issue_upto(NWB)

            act(MISC[:, M_CACT:M_CACT + 8], SMALL[:, S_C:S_C + 8], AF.Silu, [tt("small")], [tt("cact")])
            cpy("dve", CBF[:], MISC[:, M_CACT:M_CACT + 8], [tt("cact")], [tt("cbf")])
            cpy("dve", CBC, MISC[:, M_CACT:M_CACT + 8].unsqueeze(2).broadcast_to([128, 8, 128]), [tt("cact")], [tt("cbc")])
            act(MISC[:, M_E:M_E + 8], MISC[:, M_WD:M_WD + 8], AF.Exp, [tt("wd")], [tt("e")], scale=-1.0)
            act(MISC[:, M_E:M_E + 8], MISC[:, M_E:M_E + 8], AF.Ln, [tt("e")], [tt("e")], bias=1.0)
            tsc("dve", MISC[:, M_LG:M_LG + 8], MISC[:, M_E:M_E + 8], -1.0, None, MUL, None, [tt("e")], [tt("lg")])
            LGF = lambda h, lo=0, hi=128: MISC[lo:hi, M_LG + h:M_LG + h + 1]
            LGB = lambda h, lo=0, hi=128: MISC[lo:hi, M_LG + 4 + h:M_LG + 5 + h]
            S.add("pool", lambda e: e.iota(IO[:, 0, :], pattern=[[1, 128]], base=0, channel_multiplier=-1,
                                           allow_small_or_imprecise_dtypes=True), writes=[tt("io0")])
            S.add("pool", lambda e: e.iota(IO[:, 1, :], pattern=[[1, 128]], base=1, channel_multiplier=0,
                                           allow_small_or_imprecise_dtypes=True), writes=[tt("io1")])
            S.add("pool", lambda e: e.iota(IO[:, 2, :], pattern=[[-1, 128]], base=128, channel_multiplier=0,
                                           allow_small_or_imprecise_dtypes=True), writes=[tt("io2")])
            S.add("pool", lambda e: e.iota(MISC[:, M_IOP:M_IOP + 1], pattern=[[1, 1]], base=0, channel_multiplier=1,
                                           allow_small_or_imprecise_dtypes=True), writes=[tt("iop")])
            S.add("pool", lambda e: e.iota(MISC[:, M_IOPR:M_IOPR + 1], pattern=[[1, 1]], base=127, channel_multiplier=-1,
                                           allow_small_or_imprecise_dtypes=True), writes=[tt("iopr")])
            NEG = RRS[:, 0, 0:128]
            POS = RRS[:, 0, 128:256]
            tsc("dve", POS, IO[:, 0, :], 0.0, None, MAX, None, [tt("io0")], [tt("pos")])
            tten("dve", NEG, POS, IO[:, 0, :], ALU.subtract, [tt("pos"), tt("io0")], [tt("neg")])
            for h in range(4):
                tsc("dve", TMPD, POS, LGF(h), None, MUL, None, [tt("pos"), tt("lg")], [tt("tmpd")])
                stt("dve", TMPD, NEG, LGB(h), TMPD, MUL, ADD, [tt("neg"), tt("lg"), tt("tmpd")], [tt("tmpd")])
                act(DMK[:, h * 128:(h + 1) * 128], TMPD, AF.Exp, [tt("tmpd")], [tt("dmk")])
                tsc("dve", TMPD[0:64, :], IO[0:64, 1, :], LGF(h, 0, 64), None, MUL, None, [tt("io1"), tt("lg"), tt("dmk")], [tt("tmpd")])
                tsc("dve", TMPD[64:128, :], IO[64:128, 2, :], LGB(h, 64, 128), None, MUL, None, [tt("io2"), tt("lg"), tt("tmpd")], [tt("tmpd")])
                act(QDEC[:, h * 128:(h + 1) * 128], TMPD, AF.Exp, [tt("tmpd")], [tt("qdec")])
            KDV = MISC[:, M_KDV:M_KDV + 8].rearrange("p (h d) -> p h d", d=2)
            tsc("dve", MISC[:, M_KD8:M_KD8 + 4], MISC[:, M_LG:M_LG + 4], MISC[:, M_IOPR:M_IOPR + 1], None, MUL, None,
                [tt("lg"), tt("iopr")], [tt("kd8")])
            tsc("dve", MISC[:, M_KD8 + 4:M_KD8 + 8], MISC[:, M_LG + 4:M_LG + 8], MISC[:, M_IOP:M_IOP + 1], None, MUL, None,
                [tt("lg"), tt("iop"), tt("kd8")], [tt("kd8")])
            act(KDV[:, :, 0], MISC[:, M_KD8:M_KD8 + 4], AF.Exp, [tt("kd8")], [tt("kdv")])
            act(KDV[:, :, 1], MISC[:, M_KD8 + 4:M_KD8 + 8], AF.Exp, [tt("kd8"), tt("kdv")], [tt("kdv")])
            act(MISC[:, M_DC:M_DC + 8], MISC[:, M_LG:M_LG + 8], AF.Exp, [tt("lg")], [tt("dc")], scale=128.0)
            tsc("dve", MISC[:, M_GK8:M_GK8 + 1], SMALL[:, S_KG:S_KG + 1], 8.0, None, MUL, None, [tt("small")], [tt("gk8")])
            S.add("pool", lambda e: e.memset(SST[:], 0.0), writes=[tt("sst")])
            S.add("pool", lambda e: e.memset(VX[:], 1.0), writes=[tt("vx%d" % g) for g in range(NG)])

            ckpt('setup0')
            pmod = bank(6)[:, 0:16]
            for blk in range(4):
                wb, twb = ws_get(BLK_ADA + blk)
                wbv = wb.rearrange("p (k c) -> p k c", k=8)
                for c in range(4):
                    for kc in range(8):
                        mm(pmod[:, blk * 4 + c:blk * 4 + c + 1], wbv[:, kc, c * 128:(c + 1) * 128], CBF[:, kc:kc + 1],
                           kc == 0, kc == 7, [twb, tt("cbf")], [TB[6]])
            tten("dve", MISC[:, M_TMP8:M_TMP8 + 8], pmod[:, 8:16], SMALL[:, S_BSC:S_BSC + 8], ADD, [TB[6], tt("small")], [tt("tmp8")])
            stt("dve", MISC[:, M_ACOL:M_ACOL + 8], MISC[:, M_TMP8:M_TMP8 + 8], 1.0, SMALL[:, S_GPRE:S_GPRE + 8], ADD, MUL,
                [tt("tmp8"), tt("small")], [tt("acol")])
            tten("dve", MISC[:, M_SHCOL:M_SHCOL + 8], pmod[:, 0:8], SMALL[:, S_BSH:S_BSH + 8], ADD, [TB[6], tt("small")], [tt("shcol")])
            for hf in range(2):
                wb, twb = ws_get(BLK_ADA + 4 + hf)
                wbv = wb.rearrange("p (k c) -> p k c", k=8)
                for kc in range(8):
                    mm(bank(4 + hf), CBC[:, kc, :], wbv[:, kc, :], kc == 0, kc == 7, [twb, tt("cbc")], [TB[4 + hf]])
                tten("dve", GG[:, hf * 512:(hf + 1) * 512], bank(4 + hf), GG[:, hf * 512:(hf + 1) * 512], ADD,
                     [TB[4 + hf], tt("gg")], [tt("gg")])
                tten("dve", GG[:, hf * 512:(hf + 1) * 512], GG[:, hf * 512:(hf + 1) * 512], XT[:, 0, hf * 512:(hf + 1) * 512], MUL,
                     [tt("gg"), tt("xt0")], [tt("gg")])

            ckpt('ada')
            xt_rr = {"i": 0}

            def load_x(tile):
                i = xt_rr["i"] % NXT
                xt_rr["i"] += 1
                dma("sp", XT[:, i, :], x_d[tile * 128:(tile + 1) * 128, :], [], [tt("xt%d" % i)])
                return i

            cur = {"ht": HT, "htT": tt("ht")}

            hcfg = {"banks": [0, 1, 2, 3]}

            def _hreg(kc, jj, ntp):
                hb = hcfg["banks"]
                per = 8 // len(hb)
                bk = hb[kc // per]
                off = (kc % per) * (ntp * 128) + jj * 128
                return bk, off

            def hT_tile(tg, j):
                hb = hcfg["banks"]
                ntp = len(hb)
                tile = tg * 4 + j
                xi = load_x(tile)
                xb = j % 2
                act(JUNK, XT[:, xi, :], AF.Square, [tt("xt%d" % xi)], [tt("junk"), tt("ss")], accum_out=MISC[:, M_SS:M_SS + 1])
                act(MISC[:, M_RSTD:M_RSTD + 1], MISC[:, M_SS:M_SS + 1], AF.Ln, [tt("ss")], [tt("rstd")], scale=1.0 / DM, bias=EPS)
                act(MISC[:, M_RSTD:M_RSTD + 1], MISC[:, M_RSTD:M_RSTD + 1], AF.Exp, [tt("rstd")], [tt("rstd")], scale=-0.5)
                tsc("dve", XN[:, xb, :], XT[:, xi, :], MISC[:, M_RSTD:M_RSTD + 1], None, MUL, None,
                    [tt("xt%d" % xi), tt("rstd")], [tt("xn%d" % xb)])
                for kc in range(8):
                    bk, off = _hreg(kc, j % ntp, ntp)
                    trp(bank(bk).bitcast(BF16)[:, off:off + 128], XN[:, xb, kc * 128:(kc + 1) * 128], [tt("xn%d" % xb)], [TB[bk]])

            def hT_evac(dst, dstT, ps=0):
                hb = hcfg["banks"]
                ntp = len(hb)
                for kc in range(8):
                    bk, off = _hreg(kc, 0, ntp)
                    pb = bank(bk).bitcast(BF16)[:, off:off + ntp * 128]
                    o = dst[:, kc, ps * ntp * 128:(ps + 1) * ntp * 128]
                    if kc < 4:
                        tsc("dve", o, pb, MISC[:, M_ACOL + kc:M_ACOL + kc + 1], MISC[:, M_SHCOL + kc:M_SHCOL + kc + 1],
                            MUL, ADD, [TB[bk], tt("acol"), tt("shcol")], [dstT])
                    else:
                        act(o, pb, AF.Identity, [TB[bk], tt("acol"), tt("shcol")], [dstT],
                            scale=MISC[:, M_ACOL + kc:M_ACOL + kc + 1], bias=MISC[:, M_SHCOL + kc:M_SHCOL + kc + 1])

            def make_hT(tg):
                cur["ht"], cur["htT"] = bufs[tg % 2]
                if os.environ.get("MK_HT67") == "1":
                    hcfg["banks"] = [6, 7]
                ntp = len(hcfg["banks"])
                for ps in range(4 // ntp):
                    for jj in range(ntp):
                        hT_tile(tg, ps * ntp + jj)
                    hT_evac(bufs[tg % 2][0], bufs[tg % 2][1], ps)

            def load_cs(tg):
                dma("sp", CS[:, 0, :], rope_d[0][:, tg * 512:(tg + 1) * 512], [], [tt("cs")])
                dma("sp", CS[:, 1, :], rope_d[1][:, tg * 512:(tg + 1) * 512], [], [tt("cs")])

            def proj_fm(wbv, twb, c0, bk):
                for kc in range(8):
                    mm(bank(bk), wbv[:, kc, c0:c0 + 128], cur["ht"][:, kc, :], kc == 0, kc == 7, [twb, cur["htT"]], [TB[bk]])

            def proj_tm(wbv, twb, j, c0, ncol, o, tbk):
                for kc in range(8):
                    mm(o, cur["ht"][:, kc, j * 128:(j + 1) * 128], wbv[:, kc, c0:c0 + ncol], kc == 0, kc == 7, [twb, cur["htT"]], [tbk])

            rope_ctr = {"i": 0, "alt": [(4, 5)]}

            def rope(bkA, gcol, gT, rms, out_ap, out_T):
                i = rope_ctr["i"] % 2
                rope_ctr["i"] += 1
                bS, bC = rope_ctr["alt"][i % len(rope_ctr["alt"])]
                if os.environ.get("MK_ROPE67", "1") == "1" and len(rope_ctr["alt"]) == 2:
                    bS = bC = 13 - bkA
                A = bank(bkA)
                rdg = [gT] if gT is not None else []
                gsc = gcol if gT is not None else float(gcol)
                tsc("dve", QBF[:, i, :], A, gsc, None, MUL, None, [TB[bkA]] + rdg, [tt("qbf%d" % i)])
                if rms:
                    act(SQ[:, i, :], A, AF.Square, [TB[bkA]], [tt("sq%d" % i)])
                mm(bank(bS), PERM[:], QBF[:, i, :], True, True, [tt("perm"), tt("qbf%d" % i)], [TB[bS]])
                stt("dve", T1[:, i, :], A, gsc, CS[:, 0, :], MUL, MUL, [TB[bkA], tt("cs")] + rdg, [tt("t1_%d" % i)])
                tten("dve", T2[:, i, :], bank(bS), CS[:, 1, :], MUL, [TB[bS], tt("cs")], [tt("t2_%d" % i)])
                if rms:
                    mm(bank(bC), ONESB[:], SQ[:, i, :], True, True, [tt("onesb"), tt("sq%d" % i)], [TB[bC]])
                    act(RRS[:, i, :], bank(bC), AF.Ln, [TB[bC]], [tt("rrs%d" % i)], bias=64.0 * EPS)
                    act(RRS[:, i, :], RRS[:, i, :], AF.Exp, [tt("rrs%d" % i)], [tt("rrs%d" % i)], scale=-0.5)
                    tten("pool", T1[:, i, :], T1[:, i, :], T2[:, i, :], ADD, [tt("t1_%d" % i), tt("t2_%d" % i)], [tt("t1_%d" % i)])
                    tten("pool", out_ap, T1[:, i, :], RRS[:, i, :], MUL, [tt("t1_%d" % i), tt("rrs%d" % i)], [out_T])
                else:
                    tten("pool", out_ap, T1[:, i, :], T2[:, i, :], ADD, [tt("t1_%d" % i), tt("t2_%d" % i)], [out_T])

            def ret_rk(tg):
                wb, twb = ws_get(BLK_RK)
                wbv = wb.rearrange("p (k c) -> p k c", k=8)
                for pr in range(2):
                    proj_fm(wbv, twb, pr * 128, 6 + pr)
                    rope(6 + pr, 0.125, None, False, RKT[:, pr, :], tt("rkt"))

            def ret_rv(tg):
                wb, twb = ws_get(BLK_RV)
                wbv = wb.rearrange("p (k c) -> p k c", k=8)
                for j in range(4):
                    bk = 6 + (j % 2)
                    proj_tm(wbv, twb, j, 0, 512, bank(bk), TB[bk])
                    act(RVG[:, j, :], bank(bk), AF.Copy, [TB[bk]], [tt("rvg")])

            def ret_kv_inputs(tg):
                ret_rk(tg)
                ret_rv(tg)

            def kv_matmul(j):
                ptk = bank(6).bitcast(BF16)[:, 0:256]
                for pr in range(2):
                    trp(ptk[:, pr * 128:(pr + 1) * 128], RKT[:, pr, j * 128:(j + 1) * 128], [tt("rkt")], [TB[6]])
                kb = j % 2
                tten("dve", KK[:, kb, :].rearrange("p (h d c) -> p h d c", h=4, d=2),
                     ptk.rearrange("p (h c) -> p h c", h=4).unsqueeze(2).broadcast_to([128, 4, 2, 64]),
                     MISC[:, M_KDV:M_KDV + 8].rearrange("p (h d) -> p h d", d=2).unsqueeze(3).broadcast_to([128, 4, 2, 64]), MUL,
                     [TB[6], tt("kdv")], [tt("kk%d" % kb)])
                for h in range(4):
                    mm(bank(7)[:, h * 128:(h + 1) * 128], KK[:, kb, h * 128:(h + 1) * 128], RVG[:, j, h * 128:(h + 1) * 128],
                       True, True, [tt("kk%d" % kb), tt("rvg")], [TB[7]])

            bufs = [(HT, tt("ht")), (HT2, tt("ht2"))]
            for j in range(4):
                hT_tile(0, j)
            hT_evac(*bufs[0])
            for tg in range(NG):
                cur["ht"], cur["htT"] = bufs[tg % 2]
                nxt = bufs[(tg + 1) % 2] if tg + 1 < NG else None
                ckpt('ht%d' % tg)
                load_cs(tg)
                wb, twb = ws_get(BLK_P2KV)
                wbv = wb.rearrange("p (k c) -> p k c", k=8)
                for g in range(2):
                    proj_fm(wbv, twb, g * 128, 6 + g)
                    rope(6 + g, MISC[:, M_GK8:M_GK8 + 1], tt("gk8"), True, K2[:, g, tg * 512:(tg + 1) * 512], tt("k2_%d_%d" % (g, tg)))
                ckpt('k%d' % tg)
                if nxt:
                    hT_tile(tg + 1, 0)
                for j in range(4):
                    proj_tm(wbv, twb, j, 256, 128, bank(6)[:, j * 128:(j + 1) * 128], TB[6])
                cpy("dve", VX[:, tg * 4:(tg + 1) * 4, 64:320].rearrange("p j (g c) -> p j g c", g=2)[:, :, :, 0:64],
                    bank(6).rearrange("p (j g c) -> p j g c", j=4, g=2), [TB[6]], [tt("vx%d" % tg)])
                ckpt('v%d' % tg)
                if nxt:
                    hT_tile(tg + 1, 1)
                ret_rk(tg)
                if nxt:
                    hT_tile(tg + 1, 2)
                ret_rv(tg)
                ckpt('rkv%d' % tg)
                if nxt:
                    hT_tile(tg + 1, 3)
                    hT_evac(*nxt)
                for j in range(4):
                    ci = tg * 4 + j
                    kv_matmul(j)
                    act(RF[(ci % 2) * 64:(ci % 2) * 64 + 64, ci // 2, :], SST[0:64, :], AF.Copy, [tt("sst")], [tt("rf")])
                    tten("dve", SST[0:64, :].rearrange("p (h c) -> p h c", h=4), SST[0:64, :].rearrange("p (h c) -> p h c", h=4),
                         MISC[0:64, M_DC:M_DC + 4].unsqueeze(2).broadcast_to([64, 4, 128]), MUL, [tt("sst"), tt("dc")], [tt("sst")])
                    tten("dve", SST[0:64, :], SST[0:64, :], bank(7)[0:64, :], ADD, [tt("sst"), TB[7]], [tt("sst")])

            rope_ctr["alt"] = [(4, 5), (2, 3)]
            ckpt('p2')
            for tg in range(NG - 1, -1, -1):
                make_hT(tg)
                load_cs(tg)
                wq, twq = ws_get(BLK_Q)
                wqv = wq.rearrange("p (k c) -> p k c", k=8)
                for p in range(4):
                    proj_fm(wqv, twq, p * 128, 6 + (p % 2))
                    rope(6 + (p % 2), SMALL[:, S_QG:S_QG + 1], tt("small"), True, QT[:, p, :], tt("qt%d" % p))
                wz, twz = ws_get(BLK_ZA)
                wzv = wz.rearrange("p (k c) -> p k c", k=8)
                for p in range(4):
                    bk = 6 + (p % 2)
                    proj_fm(wzv, twz, p * 128, bk)
                    act(ZA[:, p, :], bank(bk), AF.Silu, [TB[bk]], [tt("za%d" % p)])
                ckpt('q%d' % tg)
                for p in range(4):
                    g = p // 2

                    def qk(kc):
                        si = kc % 2
                        mm(bank(2 * si), K2[0:64, g, kc * 128:(kc + 1) * 128], QT[0:64, p, :], True, True,
                           [tt("k2_%d_%d" % (g, kc // 4)), tt("qt%d" % p)], [TB[2 * si]])
                        mm(bank(2 * si + 1), K2[64:128, g, kc * 128:(kc + 1) * 128], QT[64:128, p, :], True, True,
                           [tt("k2_%d_%d" % (g, kc // 4)), tt("qt%d" % p)], [TB[2 * si + 1]])

                    def pv(kc):
                        pi = kc % NPT
                        mm(bank(4), VX[:, kc, 64 + g * 128:192 + g * 128], PT[:, pi, 0:512], kc == 0, kc == 31,
                           [tt("vx%d" % (kc // 4)), tt("pt%d" % pi)], [TB[4]])
                        mm(bank(5), VX[:, kc, g * 128:g * 128 + 128], PT[:, pi, 512:1024], kc == 0, kc == 31,
                           [tt("vx%d" % (kc // 4)), tt("pt%d" % pi)], [TB[5]])

                    qk(0)
                    qk(1)
                    for kc in range(32):
                        si = kc % 2
                        act(PT[:, kc % NPT, :], bank(2 * si, 2), AF.Exp, [TB[2 * si], TB[2 * si + 1]], [tt("pt%d" % (kc % NPT))])
                        if kc + 2 < 32:
                            qk(kc + 2)
                        pv(kc)
                    cpy("dve", SA[:], bank(4), [TB[4]], [tt("sa")])
                    cpy("dve", SB_[:], bank(5), [TB[5]], [tt("sb")])
                    recip(DEN[0:64, :], SA[64:128, :], [tt("sa")], [tt("den")])
                    recip(DEN[64:128, :], SB_[0:64, :], [tt("sb")], [tt("den")])
                    tten("pool", DEN[0:64, :], SA[0:64, :], DEN[0:64, :], MUL, [tt("sa"), tt("den")], [tt("den")])
                    tten("pool", DEN[64:128, :], SB_[64:128, :], DEN[64:128, :], MUL, [tt("sb"), tt("den")], [tt("den")])
                    tten("pool", YA[:, p, :], DEN[:], ZA[:, p, :], MUL, [tt("den"), tt("za%d" % p)], [tt("ya")])
                ckpt('att%d' % tg)
                wr, twr = ws_get(BLK_RQ)
                wrv = wr.rearrange("p (k c) -> p k c", k=8)
                for h in range(4):
                    proj_fm(wrv, twr, h * 128, 6 + (h % 2))
                    rope(6 + (h % 2), 1.0, None, False, RQ[:, h, :], tt("rq"))
                    tten("pool", QD[:, h, :].rearrange("p (j c) -> p j c", j=4), RQ[:, h, :].rearrange("p (j c) -> p j c", j=4),
                         QDEC[:, h * 128:(h + 1) * 128].unsqueeze(1).broadcast_to([128, 4, 128]), MUL, [tt("rq"), tt("qdec")], [tt("qd")])
                ckpt('rqd%d' % tg)
                ret_kv_inputs(tg)
                wz, twz = ws_get(BLK_ZR)
                wzv = wz.rearrange("p (k c) -> p k c", k=8)
                for h in range(4):
                    bk = 6 + (h % 2)
                    proj_fm(wzv, twz, h * 128, bk)
                    act(ZR[:, h, :], bank(bk), AF.Silu, [TB[bk]], [tt("zr")])
                ckpt('rin%d' % tg)
                for j in range(3, -1, -1):
                    ci = tg * 4 + j
                    sb_i = j % 2
                    for h in range(4):
                        r0 = (h % 2) * 64
                        mm(bank(h % 2)[:, (h // 2) * 128:(h // 2 + 1) * 128], RKT[r0:r0 + 64, h // 2, j * 128:(j + 1) * 128],
                           RQ[r0:r0 + 64, h, j * 128:(j + 1) * 128], True, True, [tt("rkt"), tt("rq")], [TB[h % 2]])
                    for hb in range(2):
                        tten("dve", SM[:, sb_i, :].rearrange("p (a b c) -> p a b c", a=2, b=2)[:, :, hb, :],
                             bank(hb)[:, 0:256].rearrange("p (a c) -> p a c", a=2),
                             DMK[:].rearrange("p (a b c) -> p a b c", a=2, b=2)[:, :, hb, :], MUL, [TB[hb], tt("dmk")], [tt("sm%d" % sb_i)])
                    ckpt('rs%d' % ci)
                    act(RRO[0:64, sb_i, :], RF[(ci % 2) * 64:(ci % 2) * 64 + 64, ci // 2, :], AF.Copy, [tt("rf")], [tt("rro%d" % sb_i)])
                    cpy("pool", RRO[64:128, sb_i, :], SST[64:128, :], [tt("sst")], [tt("rro%d" % sb_i)])
                    for h in range(4):
                        mm(bank(2)[:, h * 128:(h + 1) * 128], SM[:, sb_i, h * 128:(h + 1) * 128], RVG[:, j, h * 128:(h + 1) * 128],
                           True, False, [tt("sm%d" % sb_i), tt("rvg")], [TB[2]])
                        mm(bank(2)[:, h * 128:(h + 1) * 128], QD[:, h, j * 128:(j + 1) * 128], RRO[:, sb_i, h * 128:(h + 1) * 128],
                           False, True, [tt("qd"), tt("rro%d" % sb_i)], [TB[2]])
                    ckpt('ro%d' % ci)
                    for h in range(4):
                        bnstats(STAT[:, sb_i, h, 0:6], bank(2)[:, h * 128:(h + 1) * 128], [TB[2]], [tt("stat%d" % sb_i)])
                        bnaggr(STAT[:, sb_i, h, 6:8], STAT[:, sb_i, h, 0:6], [tt("stat%d" % sb_i)], [tt("stat%d" % sb_i)])
                    gr = MISC[:, M_GNR + sb_i * 4:M_GNR + sb_i * 4 + 4]
                    gn = MISC[:, M_GNN + sb_i * 4:M_GNN + sb_i * 4 + 4]
                    act(gr, STAT[:, sb_i, :, 7], AF.Ln, [tt("stat%d" % sb_i)], [tt("gr%d" % sb_i)], bias=EPS)
                    act(gr, gr, AF.Exp, [tt("gr%d" % sb_i)], [tt("gr%d" % sb_i)], scale=-0.5)
                    stt("dve", gn, STAT[:, sb_i, :, 6], -1.0, gr, MUL, MUL, [tt("stat%d" % sb_i), tt("gr%d" % sb_i)], [tt("gn%d" % sb_i)])
                    for h in range(4):
                        tsc("dve", ON[:, sb_i, h * 128:(h + 1) * 128], bank(2)[:, h * 128:(h + 1) * 128],
                            MISC[:, M_GNR + sb_i * 4 + h:M_GNR + sb_i * 4 + h + 1], MISC[:, M_GNN + sb_i * 4 + h:M_GNN + sb_i * 4 + h + 1],
                            MUL, ADD, [TB[2], tt("gr%d" % sb_i), tt("gn%d" % sb_i)], [tt("on%d" % sb_i)])
                    ckpt('rgn%d' % ci)
                    ptn = bank(3).bitcast(BF16)[:, 0:512]
                    for h in range(4):
                        trp(ptn[:, h * 128:(h + 1) * 128], ON[:, sb_i, h * 128:(h + 1) * 128], [tt("on%d" % sb_i)], [TB[3]])
                    for h in range(4):
                        stt("dve", YR[:, h, j * 128:(j + 1) * 128], ptn[:, h * 128:(h + 1) * 128], SMALL[:, S_GN + h:S_GN + h + 1],
                            ZR[:, h, j * 128:(j + 1) * 128], MUL, MUL, [TB[3], tt("small"), tt("zr")], [tt("yr")])
                    ckpt('ryr%d' % ci)
                    kv_matmul(j)
                    tten("dve", SST[64:128, :].rearrange("p (h c) -> p h c", h=4), SST[64:128, :].rearrange("p (h c) -> p h c", h=4),
                         MISC[64:128, M_DC + 4:M_DC + 8].unsqueeze(2).broadcast_to([64, 4, 128]), MUL, [tt("sst"), tt("dc")], [tt("sst")])
                    tten("dve", SST[64:128, :], SST[64:128, :], bank(7)[64:128, :], ADD, [tt("sst"), TB[7]], [tt("sst")])
                ckpt('ret%d' % tg)
                for n in range(8):
                    wm, twm = ws_get(BLK_M + n)
                    glv = wm[:, 0:2048].rearrange("p (k c) -> p k c", k=8)
                    pav = wm[:, 2048:2560].rearrange("p (k c) -> p k c", k=4)
                    prv = wm[:, 2560:3072].rearrange("p (k c) -> p k c", k=4)
                    gi = n % NGT
                    b0 = 4 * (n % 2)
                    for br in range(2):
                        bk = b0 + 2 + br
                        for kc in range(8):
                            mm(bank(bk), glv[:, kc, br * 128:(br + 1) * 128], cur["ht"][:, kc, :], kc == 0, kc == 7, [twm, cur["htT"]], [TB[bk]])
                        act(GATE[:, gi, br, :], bank(bk), AF.Sigmoid, [TB[bk]], [tt("gate%d_%d" % (gi, br))])
                    for cc in range(4):
                        mm(bank(b0), pav[:, cc, :], YA[:, cc, :], cc == 0, cc == 3, [twm, tt("ya")], [TB[b0]])
                    for cc in range(4):
                        mm(bank(b0 + 1), prv[:, cc, :], YR[:, cc, :], cc == 0, cc == 3, [twm, tt("yr")], [TB[b0 + 1]])
                    tten("dve", MT[:, 0, :], bank(b0), GATE[:, gi, 0, :], MUL, [TB[b0], tt("gate%d_0" % gi)], [tt("mt0")])
                    tten("dve", MT[:, 1, :], bank(b0 + 1), GATE[:, gi, 1, :], MUL, [TB[b0 + 1], tt("gate%d_1" % gi)], [tt("mt1")])
                    tten("pool", MG[:, n, :], MT[:, 0, :], MT[:, 1, :], ADD, [tt("mt0"), tt("mt1")], [tt("mg")])
                ckpt('mrg%d' % tg)
                wo0, two0 = ws_get(BLK_WO)
                wo1, two1 = ws_get(BLK_WO + 1, la=NWB - 1)
                wov = [wo0.rearrange("p (k c) -> p k c", k=8), wo1.rearrange("p (k c) -> p k c", k=8)]
                twos = [two0, two1]
                for j in range(4):
                    tile = tg * 4 + j
                    pb0 = 2 * j
                    for hf in range(2):
                        for kc in range(8):
                            mm(bank(pb0 + hf), MG[:, kc, j * 128:(j + 1) * 128], wov[hf][:, kc, :], kc == 0, kc == 7,
                               [tt("mg"), twos[hf]], [TB[pb0 + hf]])
                    xi = load_x(tile)
                    act(JUNK, bank(pb0, 2), AF.Square, [TB[pb0], TB[pb0 + 1]], [tt("junk"), tt("ss2")], accum_out=MISC[:, M_SS + 1:M_SS + 2])
                    act(MISC[:, M_RSTD + 1:M_RSTD + 2], MISC[:, M_SS + 1:M_SS + 2], AF.Ln, [tt("ss2")], [tt("rstd2")], scale=1.0 / DM, bias=EPS)
                    act(MISC[:, M_RSTD + 1:M_RSTD + 2], MISC[:, M_RSTD + 1:M_RSTD + 2], AF.Exp, [tt("rstd2")], [tt("rstd2")], scale=-0.5)
                    stt("dve", bank(pb0, 2), bank(pb0, 2), MISC[:, M_RSTD + 1:M_RSTD + 2], GG[:], MUL, MUL,
                        [TB[pb0], TB[pb0 + 1], tt("rstd2"), tt("gg")], [TB[pb0], TB[pb0 + 1]])
                    tten("dve", XT[:, xi, :], bank(pb0, 2), XT[:, xi, :], ADD, [TB[pb0], TB[pb0 + 1], tt("xt%d" % xi)], [tt("xt%d" % xi)])
                    dma(os.environ.get("MK_STQ", "sp"), out_d[tile * 128:(tile + 1) * 128, :], XT[:, xi, :], [tt("xt%d" % xi)], [])
        try:
            run_all()
        except _Stop:
            pass
        dump_aps = {"HT": HT, "K2": K2, "VX": VX, "RF": RF, "MISC": MISC, "GG": GG, "DMK": DMK, "QDEC": QDEC,
                    "YA": YA, "YR": YR, "MG": MG, "QT": QT, "ZA": ZA, "RKT": RKT, "RVG": RVG, "SST": SST,
                    "RQ": RQ, "QD": QD, "ZR": ZR, "XN": XN, "CS": CS}
        for nm in dumps:
            src = dump_aps[nm]
            shp = list(src.shape)
            flat = int(np.prod(shp[1:]))
            dd = nc.dram_tensor("dbg_" + nm, [128, flat], src.dtype, kind="ExternalOutput").ap()
            allT = list(tdict.values()) + TB
            sv = src[:] if len(shp) == 2 else src[:].rearrange("p a b -> p (a b)") if len(shp) == 3 else src[:].rearrange("p a b c -> p (a b c)")
            dma("sp", dd, sv, allT, [])
        import os
        if os.environ.get('MK_RESCHED', '1') == '1':
            S.reschedule()
        if not _EXP:
            S.emit(nc)
    return nc, S


_CACHE = {}


def kernel(x, c, w_ada, b_ada, g_pre, w_in, qn_g, kn_g, w_dec_f, w_dec_b, gn_g, w_pa, w_pr, w_out, g_post):
    x = np.asarray(x, np.float32)
    c = np.asarray(c, np.float32)
    f = lambda a: np.asarray(a, np.float32)[0]
    w_ada, b_ada, g_pre, w_in, qn_g, kn_g = f(w_ada), f(b_ada), f(g_pre), f(w_in), f(qn_g), f(kn_g)
    w_dec_f, w_dec_b, gn_g, w_pa, w_pr, w_out, g_post = f(w_dec_f), f(w_dec_b), f(gn_g), f(w_pa), f(w_pr), f(w_out), f(g_post)
    if "nc" not in _CACHE:
        _CACHE["nc"] = build_program()[0]
        _CACHE["consts"] = _host_consts()
    nc = _CACHE["nc"]
    rope, cst = _CACHE["consts"]
    blocks = _host_blocks(w_ada, w_in, w_pa, w_pr, w_out)
    col8 = lambda v: np.ascontiguousarray(v.reshape(8, 128).T)
    rows = np.ascontiguousarray(np.stack([b_ada[2048:3072], g_post], 0))
    wdec = np.concatenate([w_dec_f, w_dec_b])[None, :].astype(np.float32)
    in_maps = []
    for b in range(8):
        small = np.zeros((128, NSMALL), np.float32)
        small[:, 0:8] = col8(c[b])
        small[:, 8:16] = col8(g_pre)
        small[:, 16:24] = col8(b_ada[1024:2048])
        small[:, 24:32] = col8(b_ada[0:1024])
        small[:, 32] = np.concatenate([qn_g, qn_g])
        small[:, 33] = np.concatenate([kn_g, kn_g])
        small[:, 34:38] = gn_g.reshape(4, 128).T
        in_maps.append({"x": np.ascontiguousarray(x[b]), "wsrc": blocks, "rope": rope, "cst": cst,
                        "small": small, "rows": rows, "wdec": wdec})
    res = run_bass_kernel_spmd(nc, in_maps, core_ids=list(range(8)))
    return np.stack([np.asarray(r["out"], np.float32) for r in res.results], 0)
```

```python
import numpy as np
import concourse.bass as bass
import concourse.mybir as mybir
from concourse.bass_utils import run_bass_kernel_spmd

F32 = mybir.dt.float32
BF16 = mybir.dt.bfloat16
AF = mybir.ActivationFunctionType
ALU = mybir.AluOpType
AX = mybir.AxisListType


class T:
    __slots__ = ("name", "last_w", "readers", "excl")

    def __init__(self, name="", excl=False):
        self.name = name
        self.last_w = None
        self.readers = []
        self.excl = excl


class Op:
    __slots__ = ("eng", "fn", "deps", "signal", "token", "is_dma", "idx", "gidx", "preds", "succs", "dur", "lat", "cls", "prio", "t_start", "t_done", "npred", "tag")

    def __init__(self, eng, fn, is_dma):
        self.eng = eng
        self.fn = fn
        self.deps = []
        self.signal = False
        self.token = None
        self.is_dma = is_dma
        self.preds = []
        self.succs = []
        self.dur = 0.2
        self.lat = 0.0
        self.cls = None


class Sched:
    ENGS = ("pe", "act", "dve", "pool", "sp")

    def __init__(self, n_dma_sems=8):
        self.ops = {e: [] for e in self.ENGS}
        self.n_dma_sems = n_dma_sems
        self.dma_rr = {e: 0 for e in self.ENGS}
        self.dma_last = {}
        self.gcount = 0

    def add(self, eng, fn, reads=(), writes=(), dma=False, dur=0.2, lat=0.0, cls=None):
        op = Op(eng, fn, dma)
        op.dur, op.lat, op.cls = dur, lat, cls
        import sys as _sys
        op.tag = (_sys._getframe(2).f_lineno, _sys._getframe(3).f_lineno)
        op.gidx = self.gcount
        self.gcount += 1
        writes = list(writes) + [t for t in reads if t.excl]
        reads = [t for t in reads if not t.excl]
        raw = []
        war = []
        for t in reads:
            if t.last_w is not None:
                raw.append(t.last_w)
        for t in writes:
            if t.last_w is not None:
                raw.append(t.last_w)
            war.extend(t.readers)
        for t in reads:
            t.readers.append(op)
        for t in writes:
            t.last_w = op
            t.readers = []
        if dma:
            k = self.dma_rr[eng]
            self.dma_rr[eng] = (k + 1) % self.n_dma_sems
            prev = self.dma_last.get((eng, k))
            if prev is not None:
                raw.append(prev)
            self.dma_last[(eng, k)] = op
            op.token = (eng, k)
        seen = set()
        pseen = set()
        for lst, is_war in ((raw, False), (war, True)):
            for d in lst:
                if d is op or id(d) in seen:
                    continue
                if id(d) not in pseen:
                    pseen.add(id(d))
                    op.preds.append(d)
                if (not d.is_dma) and (not dma) and d.eng == eng:
                    if eng == "pe" or is_war:
                        continue
                seen.add(id(d))
                op.deps.append(d)
        op.idx = len(self.ops[eng])
        self.ops[eng].append(op)
        return op

    def reschedule(self, hop=0.15, act_switch=1.4):
        import heapq
        allops = []
        for e in self.ENGS:
            allops.extend(self.ops[e])
        allops.sort(key=lambda o: o.gidx)
        for o in allops:
            o.succs = []
        for o in allops:
            o.npred = len(o.preds)
            for p in o.preds:
                p.succs.append(o)
        for o in reversed(allops):
            m = 0.0
            for s_ in o.succs:
                if s_.prio > m:
                    m = s_.prio
            o.prio = m + o.dur + o.lat
        ready = {e: [] for e in self.ENGS}
        for o in allops:
            o.t_done = None
            if o.npred == 0:
                ready[o.eng].append(o)
        free_at = {e: 0.0 for e in self.ENGS}
        cur_cls = {e: None for e in self.ENGS}
        new_order = {e: [] for e in self.ENGS}
        nleft = len(allops)

        def rtime(o):
            t = 0.0
            for p in o.preds:
                tp = p.t_done + (hop if (p.eng != o.eng or p.is_dma) else 0.0)
                if tp > t:
                    t = tp
            return t

        rt_cache = {}
        while nleft:
            best = None
            for e in self.ENGS:
                lst = ready[e]
                if not lst:
                    continue
                fa = free_at[e]
                for o in lst:
                    r = rt_cache.get(id(o))
                    if r is None:
                        r = rtime(o)
                        rt_cache[id(o)] = r
                    st = r if r > fa else fa
                    if e == "act" and o.cls is not None and cur_cls[e] is not None and o.cls != cur_cls[e]:
                        st += act_switch
                    key = (st, -o.prio)
                    if best is None or key < best[0]:
                        best = (key, o)
            (st, _), o = best
            e = o.eng
            ready[e].remove(o)
            o.t_start = st
            free_at[e] = st + o.dur
            o.t_done = st + o.dur + o.lat
            if e == "act" and o.cls is not None:
                cur_cls[e] = o.cls
            new_order[e].append(o)
            nleft -= 1
            for s_ in o.succs:
                s_.npred -= 1
                if s_.npred == 0:
                    ready[s_.eng].append(s_)
        for e in self.ENGS:
            self.ops[e] = new_order[e]
        self.makespan = max(free_at.values())

    def emit(self, nc, block_ctx_extra=None):
        for e in self.ENGS:
            for op in self.ops[e]:
                for d in op.deps:
                    d.signal = True
        import contextlib
        with contextlib.ExitStack() as st:
            sems = {e: st.enter_context(nc.semaphore("s_" + e)) for e in ("pe", "act", "dve", "pool")}
            dsems = {}
            for e in self.ENGS:
                if any(o.is_dma for o in self.ops[e]):
                    for k in range(self.n_dma_sems):
                        dsems[(e, k)] = st.enter_context(nc.semaphore("d_%s_%d" % (e, k)))
            for e in self.ENGS:
                cnt = 0
                dcnt = {}
                for op in self.ops[e]:
                    if op.is_dma:
                        key = op.token
                        dcnt[key] = dcnt.get(key, 0) + 16
                        op.token = (dsems[key], dcnt[key], key)
                    elif op.signal:
                        cnt += 1
                        op.token = (sems[e], cnt, e)
            block = st.enter_context(nc.Block())
            handles = {"pe": block.tensor, "act": block.scalar, "dve": block.vector, "pool": block.gpsimd,
                       "sp": block.sync}
            nwaits = {e: 0 for e in self.ENGS}

            def make(e):
                ops = self.ops[e]

                def body(eng):
                    known = {}
                    for op in ops:
                        need = {}
                        for d in op.deps:
                            sem, val, key = d.token
                            if known.get(key, 0) >= val:
                                continue
                            if key not in need or need[key][1] < val:
                                need[key] = (sem, val)
                        for key, (sem, val) in need.items():
                            eng.wait_ge(sem, val)
                            known[key] = val
                            nwaits[e] += 1
                        inst = op.fn(eng)
                        if op.is_dma:
                            inst.then_inc(op.token[0], 16)
                        elif op.signal:
                            inst.then_inc(op.token[0], 1)
                    fin = {}
                    for op in ops:
                        if op.is_dma:
                            fin[op.token[2]] = (op.token[0], op.token[1])
                    for key, (sem, val) in fin.items():
                        if known.get(key, 0) < val:
                            eng.wait_ge(sem, val)
                return body

            for e in self.ENGS:
                if self.ops[e]:
                    handles[e](make(e))
            self.nwaits = nwaits

import contextlib

SEQ = 4096
DM = 1024
NG = 8
EPS = 1e-6

BLK_ADA = 0
BLK_P2KV = 6
BLK_RK = 7
BLK_RV = 8
BLK_Q = 9
BLK_ZA = 10
BLK_RQ = 11
BLK_ZR = 12
BLK_M = 13
BLK_WO = 21
NBLK = 23
NSMALL = 40


def _host_blocks(w_ada, w_in, w_pa, w_pr, w_out):
    blocks = np.zeros((NBLK, 128, 4096), np.float32)

    def kmaj(w):
        nc_ = w.shape[1]
        t = np.zeros((128, 8, 512), np.float32)
        t[:, :, :nc_] = w.reshape(8, 128, nc_).transpose(1, 0, 2)
        return t.reshape(128, 4096)

    wa = w_ada
    for j in range(6):
        blocks[BLK_ADA + j] = kmaj(wa[:, j * 512:(j + 1) * 512])
    o_q, o_k, o_v, o_za, o_rq, o_rk, o_rv, o_zr, o_gl = 0, 512, 640, 768, 1280, 1536, 1792, 2304, 2816
    k0 = w_in[:, o_k:o_k + 64]
    k1 = w_in[:, o_k + 64:o_k + 128]
    blocks[BLK_P2KV] = kmaj(np.concatenate([k0, k0, k1, k1, w_in[:, o_v:o_v + 128]], axis=1))
    blocks[BLK_RK] = kmaj(w_in[:, o_rk:o_rk + 256])
    blocks[BLK_RV] = kmaj(w_in[:, o_rv:o_rv + 512])
    blocks[BLK_Q] = kmaj(w_in[:, o_q:o_q + 512])
    blocks[BLK_ZA] = kmaj(w_in[:, o_za:o_za + 512])
    rq = [w_in[:, o_rq + 64 * h:o_rq + 64 * (h + 1)] for h in range(4)]
    blocks[BLK_RQ] = kmaj(np.concatenate([rq[0], rq[0], rq[1], rq[1], rq[2], rq[2], rq[3], rq[3]], axis=1))
    blocks[BLK_ZR] = kmaj(w_in[:, o_zr:o_zr + 512])
    for n in range(8):
        t = np.zeros((128, 4096), np.float32)
        gl = np.concatenate([w_in[:, o_gl + n * 128:o_gl + (n + 1) * 128],
                             w_in[:, o_gl + 1024 + n * 128:o_gl + 1024 + (n + 1) * 128]], axis=1)
        t[:, 0:2048] = gl.reshape(8, 128, 256).transpose(1, 0, 2).reshape(128, 2048)
        t[:, 2048:2560] = w_pa[:, n * 128:(n + 1) * 128].reshape(4, 128, 128).transpose(1, 0, 2).reshape(128, 512)
        t[:, 2560:3072] = w_pr[:, n * 128:(n + 1) * 128].reshape(4, 128, 128).transpose(1, 0, 2).reshape(128, 512)
        blocks[BLK_M + n] = t
    for hf in range(2):
        blocks[BLK_WO + hf] = kmaj(w_out[:, hf * 512:(hf + 1) * 512])
    return blocks


def _host_consts():
    t = np.arange(SEQ)
    row = (t // 64).astype(np.float32)
    col = (t % 64).astype(np.float32)
    inv = (10000.0 ** (-np.arange(0, 32, 2, dtype=np.float32) / 32.0)).astype(np.float32)
    ang_r = row[None, :] * inv[:, None]
    ang_c = col[None, :] * inv[:, None]
    cos64 = np.concatenate([np.cos(ang_r), np.cos(ang_r), np.cos(ang_c), np.cos(ang_c)], 0)
    sin64 = np.concatenate([np.sin(ang_r), np.sin(ang_r), np.sin(ang_c), np.sin(ang_c)], 0)
    rope = np.stack([np.concatenate([cos64, cos64], 0), np.concatenate([sin64, sin64], 0)], 0).astype(np.float32)
    ident = np.eye(128, dtype=np.float32)
    perm = np.zeros((128, 128), np.float32)
    for hb in range(2):
        for d in range(64):
            blk = d // 16
            if blk % 2 == 0:
                p, s = d + 16, -1.0
            else:
                p, s = d - 16, 1.0
            perm[hb * 64 + p, hb * 64 + d] = s
    ones = np.zeros((128, 128), np.float32)
    ones[:64, :64] = 1.0
    ones[64:, 64:] = 1.0
    return rope, np.stack([ident, perm, ones], 0)


class _Stop(Exception):
    pass


def build_program(stop=None, dumps=()):
    nc = bass.Bass("TRN2", target_bir_lowering=False)
    x_d = nc.dram_tensor("x", [SEQ, DM], F32, kind="ExternalInput").ap()
    wsrc = nc.dram_tensor("wsrc", [NBLK, 128, 4096], F32, kind="ExternalInput").ap()
    rope_d = nc.dram_tensor("rope", [2, 128, SEQ], F32, kind="ExternalInput").ap()
    cst_d = nc.dram_tensor("cst", [3, 128, 128], F32, kind="ExternalInput").ap()
    small_d = nc.dram_tensor("small", [128, NSMALL], F32, kind="ExternalInput").ap()
    rows_d = nc.dram_tensor("rows", [2, DM], F32, kind="ExternalInput").ap()
    wdec_d = nc.dram_tensor("wdec", [1, 8], F32, kind="ExternalInput").ap()
    out_d = nc.dram_tensor("out", [SEQ, DM], F32, kind="ExternalOutput").ap()
    wbf_d = nc.dram_tensor("wbf", [NBLK, 128, 4096], BF16, kind="Internal").ap()

    S = Sched(n_dma_sems=12)
    tdict = {}

    import os
    _alias = {"io0": "sa", "io1": "sa", "io2": "sa", "tmpd": "sa", "cbc": "mt1", "junk": "mt0"}
    if os.environ.get("MK_JUNK") == "1":
        del _alias["junk"]

    def tt(name):
        name = _alias.get(name, name)
        if name not in tdict:
            tdict[name] = T(name)
        return tdict[name]

    with contextlib.ExitStack() as st:
        def sb(name, shape, dt):
            return st.enter_context(nc.sbuf_tensor(name, shape, dt))

        PS = st.enter_context(nc.psum_tensor("ps", [128, 4096], F32))
        TB = [T("bank%d" % i, excl=True) for i in range(8)]

        def bank(i, n=1):
            return PS[:, i * 512:(i + n) * 512]

        import os
        _EXP = os.environ.get("MK_EXPLORE") == "1"
        _PDT = mybir.dt.int8 if _EXP else BF16
        K2 = sb("K2", [128, 2, SEQ], _PDT)
        VX = sb("VX", [128, 32, 320], _PDT)
        RF = sb("RF", [128, 16, 512], _PDT)
        NWB = int(os.environ.get('MK_NWB', '3'))
        WB = sb("WB", [128, NWB, 4096], BF16)
        NXT = int(os.environ.get('MK_NXT', '3'))
        XT = sb("XT", [128, NXT, DM], F32)
        XN = sb("XN", [128, 2, DM], BF16)
        HT = sb("HT", [128, 8, 512], BF16)
        HT2 = sb("HT2", [128, 8, 512], BF16)
        CS = sb("CS", [128, 2, 512], F32)
        QBF = sb("QBF", [128, 2, 512], BF16)
        SQ = sb("SQ", [128, 2, 512], BF16)
        T1 = sb("T1", [128, 2, 512], F32)
        T2 = sb("T2", [128, 2, 512], F32)
        RRS = sb("RRS", [128, 2, 512], F32)
        QT = sb("QT", [128, 4, 512], BF16)
        ZA = sb("ZA", [128, 4, 512], BF16)
        NPT = int(os.environ.get('MK_NPT', '3'))
        PT = sb("PT", [128, NPT, 1024], BF16)
        SA = sb("SA", [128, 512], F32)
        SB_ = sb("SB_", [128, 512], F32)
        DEN = sb("DEN", [128, 512], F32)
        YA = sb("YA", [128, 4, 512], BF16)
        YR = sb("YR", [128, 4, 512], BF16)
        RQ = sb("RQ", [128, 4, 512], BF16)
        QD = sb("QD", [128, 4, 512], BF16)
        ZR = sb("ZR", [128, 4, 512], BF16)
        RKT = sb("RKT", [128, 2, 512], BF16)
        RVG = sb("RVG", [128, 4, 512], BF16)
        KK = sb("KK", [128, 2, 512], BF16)
        SM = sb("SM", [128, 2, 512], BF16)
        RRO = sb("RRO", [128, 2, 512], BF16)
        ON = sb("ON", [128, 2, 512], BF16)
        SST = sb("SST", [128, 512], F32)
        NGT = int(os.environ.get("MK_NGT", "1"))
        GATE = sb("GATE", [128, NGT, 2, 512], BF16)
        MT = sb("MT", [128, 2, 512], F32)
        MG = sb("MG", [128, 8, 512], BF16)
        IDENT = sb("IDENT", [128, 128], BF16)
        PERM = sb("PERM", [128, 128], BF16)
        ONESB = sb("ONESB", [128, 128], BF16)
        DMK = sb("DMK", [128, 512], F32)
        QDEC = sb("QDEC", [128, 512], F32)
        GG = sb("GG", [128, DM], F32)
        SMALL = sb("SMALL", [128, NSMALL], F32)
        MISC = sb("MISC", [128, 128], F32)
        IO = SA[:, 0:384].rearrange("p (a b) -> p a b", a=3)
        TMPD = SA[:, 384:512]
        CBC = MT[:, 1, :].bitcast(BF16).rearrange("p (a b) -> p a b", a=8)
        JUNK = sb("JUNKB", [128, DM], BF16)[:] if os.environ.get("MK_JUNK") == "1" else MT[:, 0, :].bitcast(BF16)
        CBF = sb("CBF", [128, 8], BF16)
        STAT = sb("STAT", [128, 2, 4, 8], F32)

        _CACHE['sbuf_left'] = nc.sbuf_bytes_remaining
        M_ACOL, M_SHCOL, M_LG, M_DC, M_WD, M_E, M_KD8, M_KDV, M_IOP, M_IOPR, M_GK8 = 0, 8, 16, 24, 32, 40, 48, 56, 64, 65, 66
        M_SS, M_RSTD, M_CACT, M_TMP8 = 70, 74, 80, 88
        M_GNR, M_GNN = 96, 104
        S_C, S_GPRE, S_BSC, S_BSH, S_QG, S_KG, S_GN = 0, 8, 16, 24, 32, 33, 34

        def fsz(ap):
            n = 1
            for d in list(ap.shape)[1:]:
                n *= int(d)
            return n

        def mm(o, l, r, start, stop, rd, wr):
            n = fsz(o)
            k = int(l.shape[0])
            d = 0.06 + n * 0.00041
            if k <= 64 and n >= 256:
                d = 0.17
            S.add("pe", lambda e: e.matmul(o, l, r, start=start, stop=stop), reads=rd, writes=wr, dur=d)

        def trp(o, i, rd, wr):
            S.add("pe", lambda e: e.transpose(o, i, IDENT[:]), reads=rd + [tt("ident")], writes=wr, dur=0.115)

        _ACLS = {AF.Exp: "exp", AF.Ln: "exp", AF.Silu: "sig", AF.Sigmoid: "sig"}

        def act(o, i, f, rd, wr, **kw):
            S.add("act", lambda e: e.activation(out=o, in_=i, func=f, **kw), reads=rd, writes=wr,
                  dur=0.22 + fsz(i) * 0.00083 + (0.1 if "accum_out" in kw else 0.0), cls=_ACLS.get(f))

        def _vd(eng, n):
            return (0.1 + n * 0.00104) if eng == "dve" else (0.15 + n * 0.0021)

        def tten(eng, o, a, b, op, rd, wr):
            S.add(eng, lambda e: e.tensor_tensor(out=o, in0=a, in1=b, op=op), reads=rd, writes=wr, dur=_vd(eng, fsz(o)))

        def tsc(eng, o, a, s1, s2, op0, op1, rd, wr):
            if op1 is None:
                S.add(eng, lambda e: e.tensor_scalar(out=o, in0=a, scalar1=s1, scalar2=None, op0=op0), reads=rd, writes=wr, dur=_vd(eng, fsz(o)))
            else:
                S.add(eng, lambda e: e.tensor_scalar(out=o, in0=a, scalar1=s1, scalar2=s2, op0=op0, op1=op1), reads=rd, writes=wr, dur=_vd(eng, fsz(o)))

        def stt(eng, o, a, s, b, op0, op1, rd, wr):
            S.add(eng, lambda e: e.scalar_tensor_tensor(out=o, in0=a, scalar=s, in1=b, op0=op0, op1=op1), reads=rd, writes=wr, dur=_vd(eng, fsz(o)))

        def cpy(eng, o, i, rd, wr):
            S.add(eng, lambda e: e.tensor_copy(out=o, in_=i), reads=rd, writes=wr, dur=_vd(eng, fsz(o)))

        def dma(eng, o, i, rd, wr):
            nbytes = fsz(o) * int(o.shape[0]) * (2 if o.dtype == BF16 else 4)
            S.add(eng, lambda e: e.dma_start(out=o, in_=i), reads=rd, writes=wr, dma=True,
                  dur=(0.08 if eng == "sp" else 1.0), lat=2.0 + nbytes / 150000.0)

        MUL, ADD, MAX = ALU.mult, ALU.add, ALU.max

        def ckpt(name):
            if stop == name:
                raise _Stop()

        def bnstats(o, i, rd, wr):
            S.add("dve", lambda e: e.bn_stats(out=o, in_=i), reads=rd, writes=wr, dur=0.25)

        def bnaggr(o, i, rd, wr):
            S.add("dve", lambda e: e.bn_aggr(out=o, in_=i), reads=rd, writes=wr, dur=0.15)

        def recip(o, i, rd, wr):
            S.add("dve", lambda e: e.reciprocal(out=o, in_=i), reads=rd, writes=wr, dur=0.1 + fsz(o) * 0.00625)

        order = [BLK_ADA + j for j in range(6)]
        for tg in range(NG):
            order += [BLK_P2KV, BLK_RK, BLK_RV]
        for tg in range(NG):
            order += [BLK_Q, BLK_ZA, BLK_RQ] + ([BLK_RK, BLK_RV] if tg > 0 else []) + [BLK_ZR] \
                + [BLK_M + n for n in range(8)] + [BLK_WO, BLK_WO + 1]
        conv_done = set()
        ws_state = {"issued": 0, "cur": 0}

        def conv(blk):
            if blk in conv_done:
                return
            conv_done.add(blk)
            dma("pool", wbf_d[blk], wsrc[blk], [], [tt("wbf%d" % blk)])

        def ws_issue_upto(n):
            while ws_state["issued"] < min(n, len(order)):
                i = ws_state["issued"]
                blk = order[i]
                conv(blk)
                dma("sp", WB[:, i % NWB, :], wbf_d[blk], [tt("wbf%d" % blk)], [tt("wb%d" % (i % NWB))])
                ws_state["issued"] += 1

        def ws_get(expect, la=None):
            i = ws_state["cur"]
            assert order[i] == expect, (i, order[i], expect)
            ws_issue_upto(i + (NWB if la is None else la))
            ws_state["cur"] += 1
            return WB[:, i % NWB, :], tt("wb%d" % (i % NWB))

        def run_all():
            for blk in [BLK_ADA + j for j in range(6)] + [BLK_P2KV, BLK_RK, BLK_RV]:
                conv(blk)
            dma("sp", SMALL[:], small_d, [], [tt("small")])
            dma("sp", GG[:], rows_d[0:1, :].partition_broadcast(128), [], [tt("gg")])
            dma("sp", XT[:, 0, :], rows_d[1:2, :].partition_broadcast(128), [], [tt("xt0")])
            dma("sp", MISC[:, M_WD:M_WD + 8], wdec_d.partition_broadcast(128), [], [tt("wd")])
            dma("pool", IDENT[:], cst_d[0], [], [tt("ident")])
            dma("pool", PERM[:], cst_d[1], [], [tt("perm")])
            dma("pool", ONESB[:], cst_d[2], [], [tt("onesb")])
            ws_issue_upto(NWB)

            act(MISC[:, M_CACT:M_CACT + 8], SMALL[:, S_C:S_C + 8], AF.Silu, [tt("small")], [tt("cact")])
            cpy("dve", CBF[:], MISC[:, M_CACT:M_CACT + 8], [tt("cact")], [tt("cbf")])
            cpy("dve", CBC, MISC[:, M_CACT:M_CACT + 8].unsqueeze(2).broadcast_to([128, 8, 128]), [tt("cact")], [tt("cbc")])
            act(MISC[:, M_E:M_E + 8], MISC[:, M_WD:M_WD + 8], AF.Exp, [tt("wd")], [tt("e")], scale=-1.0)
            act(MISC[:, M_E:M_E + 8], MISC[:, M_E:M_E + 8], AF.Ln, [tt("e")], [tt("e")], bias=1.0)
            tsc("dve", MISC[:, M_LG:M_LG + 8], MISC[:, M_E:M_E + 8], -1.0, None, MUL, None, [tt("e")], [tt("lg")])
            LGF = lambda h, lo=0, hi=128: MISC[lo:hi, M_LG + h:M_LG + h + 1]
            LGB = lambda h, lo=0, hi=128: MISC[lo:hi, M_LG + 4 + h:M_LG + 5 + h]
            S.add("pool", lambda e: e.iota(IO[:, 0, :], pattern=[[1, 128]], base=0, channel_multiplier=-1,
                                           allow_small_or_imprecise_dtypes=True), writes=[tt("io0")])
            S.add("pool", lambda e: e.iota(IO[:, 1, :], pattern=[[1, 128]], base=1, channel_multiplier=0,
                                           allow_small_or_imprecise_dtypes=True), writes=[tt("io1")])
            S.add("pool", lambda e: e.iota(IO[:, 2, :], pattern=[[-1, 128]], base=128, channel_multiplier=0,
                                           allow_small_or_imprecise_dtypes=True), writes=[tt("io2")])
            S.add("pool", lambda e: e.iota(MISC[:, M_IOP:M_IOP + 1], pattern=[[1, 1]], base=0, channel_multiplier=1,
                                           allow_small_or_imprecise_dtypes=True), writes=[tt("iop")])
            S.add("pool", lambda e: e.iota(MISC[:, M_IOPR:M_IOPR + 1], pattern=[[1, 1]], base=127, channel_multiplier=-1,
                                           allow_small_or_imprecise_dtypes=True), writes=[tt("iopr")])
            NEG = RRS[:, 0, 0:128]
            POS = RRS[:, 0, 128:256]
            tsc("dve", POS, IO[:, 0, :], 0.0, None, MAX, None, [tt("io0")], [tt("pos")])
            tten("dve", NEG, POS, IO[:, 0, :], ALU.subtract, [tt("pos"), tt("io0")], [tt("neg")])
            for h in range(4):
                tsc("dve", TMPD, POS, LGF(h), None, MUL, None, [tt("pos"), tt("lg")], [tt("tmpd")])
                stt("dve", TMPD, NEG, LGB(h), TMPD, MUL, ADD, [tt("neg"), tt("lg"), tt("tmpd")], [tt("tmpd")])
                act(DMK[:, h * 128:(h + 1) * 128], TMPD, AF.Exp, [tt("tmpd")], [tt("dmk")])
                tsc("dve", TMPD[0:64, :], IO[0:64, 1, :], LGF(h, 0, 64), None, MUL, None, [tt("io1"), tt("lg"), tt("dmk")], [tt("tmpd")])
                tsc("dve", TMPD[64:128, :], IO[64:128, 2, :], LGB(h, 64, 128), None, MUL, None, [tt("io2"), tt("lg"), tt("tmpd")], [tt("tmpd")])
                act(QDEC[:, h * 128:(h + 1) * 128], TMPD, AF.Exp, [tt("tmpd")], [tt("qdec")])
            KDV = MISC[:, M_KDV:M_KDV + 8].rearrange("p (h d) -> p h d", d=2)
            tsc("dve", MISC[:, M_KD8:M_KD8 + 4], MISC[:, M_LG:M_LG + 4], MISC[:, M_IOPR:M_IOPR + 1], None, MUL, None,
                [tt("lg"), tt("iopr")], [tt("kd8")])
            tsc("dve", MISC[:, M_KD8 + 4:M_KD8 + 8], MISC[:, M_LG + 4:M_LG + 8], MISC[:, M_IOP:M_IOP + 1], None, MUL, None,
                [tt("lg"), tt("iop"), tt("kd8")], [tt("kd8")])
            act(KDV[:, :, 0], MISC[:, M_KD8:M_KD8 + 4], AF.Exp, [tt("kd8")], [tt("kdv")])
            act(KDV[:, :, 1], MISC[:, M_KD8 + 4:M_KD8 + 8], AF.Exp, [tt("kd8"), tt("kdv")], [tt("kdv")])
            act(MISC[:, M_DC:M_DC + 8], MISC[:, M_LG:M_LG + 8], AF.Exp, [tt("lg")], [tt("dc")], scale=128.0)
            tsc("dve", MISC[:, M_GK8:M_GK8 + 1], SMALL[:, S_KG:S_KG + 1], 8.0, None, MUL, None, [tt("small")], [tt("gk8")])
            S.add("pool", lambda e: e.memset(SST[:], 0.0), writes=[tt("sst")])
            S.add("pool", lambda e: e.memset(VX[:], 1.0), writes=[tt("vx%d" % g) for g in range(NG)])

            ckpt('setup0')
            pmod = bank(6)[:, 0:16]
            for blk in range(4):
                wb, twb = ws_get(BLK_ADA + blk)
                wbv = wb.rearrange("p (k c) -> p k c", k=8)
                for c in range(4):
                    for kc in range(8):
                        mm(pmod[:, blk * 4 + c:blk * 4 + c + 1], wbv[:, kc, c * 128:(c + 1) * 128], CBF[:, kc:kc + 1],
                           kc == 0, kc == 7, [twb, tt("cbf")], [TB[6]])
            tten("dve", MISC[:, M_TMP8:M_TMP8 + 8], pmod[:, 8:16], SMALL[:, S_BSC:S_BSC + 8], ADD, [TB[6], tt("small")], [tt("tmp8")])
            stt("dve", MISC[:, M_ACOL:M_ACOL + 8], MISC[:, M_TMP8:M_TMP8 + 8], 1.0, SMALL[:, S_GPRE:S_GPRE + 8], ADD, MUL,
                [tt("tmp8"), tt("small")], [tt("acol")])
            tten("dve", MISC[:, M_SHCOL:M_SHCOL + 8], pmod[:, 0:8], SMALL[:, S_BSH:S_BSH + 8], ADD, [TB[6], tt("small")], [tt("shcol")])
            for hf in range(2):
                wb, twb = ws_get(BLK_ADA + 4 + hf)
                wbv = wb.rearrange("p (k c) -> p k c", k=8)
                for kc in range(8):
                    mm(bank(4 + hf), CBC[:, kc, :], wbv[:, kc, :], kc == 0, kc == 7, [twb, tt("cbc")], [TB[4 + hf]])
                tten("dve", GG[:, hf * 512:(hf + 1) * 512], bank(4 + hf), GG[:, hf * 512:(hf + 1) * 512], ADD,
                     [TB[4 + hf], tt("gg")], [tt("gg")])
                tten("dve", GG[:, hf * 512:(hf + 1) * 512], GG[:, hf * 512:(hf + 1) * 512], XT[:, 0, hf * 512:(hf + 1) * 512], MUL,
                     [tt("gg"), tt("xt0")], [tt("gg")])

            ckpt('ada')
            xt_rr = {"i": 0}

            def load_x(tile):
                i = xt_rr["i"] % NXT
                xt_rr["i"] += 1
                dma("sp", XT[:, i, :], x_d[tile * 128:(tile + 1) * 128, :], [], [tt("xt%d" % i)])
                return i

            cur = {"ht": HT, "htT": tt("ht")}

            hcfg = {"banks": [0, 1, 2, 3]}

            def _hreg(kc, jj, ntp):
                hb = hcfg["banks"]
                per = 8 // len(hb)
                bk = hb[kc // per]
                off = (kc % per) * (ntp * 128) + jj * 128
                return bk, off

            def hT_tile(tg, j):
                hb = hcfg["banks"]
                ntp = len(hb)
                tile = tg * 4 + j
                xi = load_x(tile)
                xb = j % 2
                act(JUNK, XT[:, xi, :], AF.Square, [tt("xt%d" % xi)], [tt("junk"), tt("ss")], accum_out=MISC[:, M_SS:M_SS + 1])
                act(MISC[:, M_RSTD:M_RSTD + 1], MISC[:, M_SS:M_SS + 1], AF.Ln, [tt("ss")], [tt("rstd")], scale=1.0 / DM, bias=EPS)
                act(MISC[:, M_RSTD:M_RSTD + 1], MISC[:, M_RSTD:M_RSTD + 1], AF.Exp, [tt("rstd")], [tt("rstd")], scale=-0.5)
                tsc("dve", XN[:, xb, :], XT[:, xi, :], MISC[:, M_RSTD:M_RSTD + 1], None, MUL, None,
                    [tt("xt%d" % xi), tt("rstd")], [tt("xn%d" % xb)])
                for kc in range(8):
                    bk, off = _hreg(kc, j % ntp, ntp)
                    trp(bank(bk).bitcast(BF16)[:, off:off + 128], XN[:, xb, kc * 128:(kc + 1) * 128], [tt("xn%d" % xb)], [TB[bk]])

            def hT_evac(dst, dstT, ps=0):
                hb = hcfg["banks"]
                ntp = len(hb)
                for kc in range(8):
                    bk, off = _hreg(kc, 0, ntp)
                    pb = bank(bk).bitcast(BF16)[:, off:off + ntp * 128]
                    o = dst[:, kc, ps * ntp * 128:(ps + 1) * ntp * 128]
                    if kc < 4:
                        tsc("dve", o, pb, MISC[:, M_ACOL + kc:M_ACOL + kc + 1], MISC[:, M_SHCOL + kc:M_SHCOL + kc + 1],
                            MUL, ADD, [TB[bk], tt("acol"), tt("shcol")], [dstT])
                    else:
                        act(o, pb, AF.Identity, [TB[bk], tt("acol"), tt("shcol")], [dstT],
                            scale=MISC[:, M_ACOL + kc:M_ACOL + kc + 1], bias=MISC[:, M_SHCOL + kc:M_SHCOL + kc + 1])

            def make_hT(tg):
                cur["ht"], cur["htT"] = bufs[tg % 2]
                if os.environ.get("MK_HT67") == "1":
                    hcfg["banks"] = [6, 7]
                ntp = len(hcfg["banks"])
                for ps in range(4 // ntp):
                    for jj in range(ntp):
                        hT_tile(tg, ps * ntp + jj)
                    hT_evac(bufs[tg % 2][0], bufs[tg % 2][1], ps)

            def load_cs(tg):
                dma("sp", CS[:, 0, :], rope_d[0][:, tg * 512:(tg + 1) * 512], [], [tt("cs")])
                dma("sp", CS[:, 1, :], rope_d[1][:, tg * 512:(tg + 1) * 512], [], [tt("cs")])

            def proj_fm(wbv, twb, c0, bk):
                for kc in range(8):
                    mm(bank(bk), wbv[:, kc, c0:c0 + 128], cur["ht"][:, kc, :], kc == 0, kc == 7, [twb, cur["htT"]], [TB[bk]])

            def proj_tm(wbv, twb, j, c0, ncol, o, tbk):
                for kc in range(8):
                    mm(o, cur["ht"][:, kc, j * 128:(j + 1) * 128], wbv[:, kc, c0:c0 + ncol], kc == 0, kc == 7, [twb, cur["htT"]], [tbk])

            rope_ctr = {"i": 0, "alt": [(4, 5)]}

            def rope(bkA, gcol, gT, rms, out_ap, out_T):
                i = rope_ctr["i"] % 2
                rope_ctr["i"] += 1
                bS, bC = rope_ctr["alt"][i % len(rope_ctr["alt"])]
                if os.environ.get("MK_ROPE67", "1") == "1" and len(rope_ctr["alt"]) == 2:
                    bS = bC = 13 - bkA
                A = bank(bkA)
                rdg = [gT] if gT is not None else []
                gsc = gcol if gT is not None else float(gcol)
                tsc("dve", QBF[:, i, :], A, gsc, None, MUL, None, [TB[bkA]] + rdg, [tt("qbf%d" % i)])
                if rms:
                    act(SQ[:, i, :], A, AF.Square, [TB[bkA]], [tt("sq%d" % i)])
                mm(bank(bS), PERM[:], QBF[:, i, :], True, True, [tt("perm"), tt("qbf%d" % i)], [TB[bS]])
                stt("dve", T1[:, i, :], A, gsc, CS[:, 0, :], MUL, MUL, [TB[bkA], tt("cs")] + rdg, [tt("t1_%d" % i)])
                tten("dve", T2[:, i, :], bank(bS), CS[:, 1, :], MUL, [TB[bS], tt("cs")], [tt("t2_%d" % i)])
                if rms:
                    mm(bank(bC), ONESB[:], SQ[:, i, :], True, True, [tt("onesb"), tt("sq%d" % i)], [TB[bC]])
                    act(RRS[:, i, :], bank(bC), AF.Ln, [TB[bC]], [tt("rrs%d" % i)], bias=64.0 * EPS)
                    act(RRS[:, i, :], RRS[:, i, :], AF.Exp, [tt("rrs%d" % i)], [tt("rrs%d" % i)], scale=-0.5)
                    tten(os.environ.get("MK_RA", "pool"), T1[:, i, :], T1[:, i, :], T2[:, i, :], ADD, [tt("t1_%d" % i), tt("t2_%d" % i)], [tt("t1_%d" % i)])
                    tten(os.environ.get("MK_RM", "pool"), out_ap, T1[:, i, :], RRS[:, i, :], MUL, [tt("t1_%d" % i), tt("rrs%d" % i)], [out_T])
                else:
                    tten(os.environ.get("MK_RP", "pool"), out_ap, T1[:, i, :], T2[:, i, :], ADD, [tt("t1_%d" % i), tt("t2_%d" % i)], [out_T])

            def ret_rk(tg):
                wb, twb = ws_get(BLK_RK)
                wbv = wb.rearrange("p (k c) -> p k c", k=8)
                for pr in range(2):
                    proj_fm(wbv, twb, pr * 128, 6 + pr)
                    rope(6 + pr, 0.125, None, False, RKT[:, pr, :], tt("rkt"))

            def ret_rv(tg):
                wb, twb = ws_get(BLK_RV)
                wbv = wb.rearrange("p (k c) -> p k c", k=8)
                for j in range(4):
                    bk = 6 + (j % 2)
                    proj_tm(wbv, twb, j, 0, 512, bank(bk), TB[bk])
                    act(RVG[:, j, :], bank(bk), AF.Copy, [TB[bk]], [tt("rvg")])

            def ret_kv_inputs(tg):
                ret_rk(tg)
                ret_rv(tg)

            def kv_matmul(j):
                ptk = bank(6).bitcast(BF16)[:, 0:256]
                for pr in range(2):
                    trp(ptk[:, pr * 128:(pr + 1) * 128], RKT[:, pr, j * 128:(j + 1) * 128], [tt("rkt")], [TB[6]])
                kb = j % 2
                tten("dve", KK[:, kb, :].rearrange("p (h d c) -> p h d c", h=4, d=2),
                     ptk.rearrange("p (h c) -> p h c", h=4).unsqueeze(2).broadcast_to([128, 4, 2, 64]),
                     MISC[:, M_KDV:M_KDV + 8].rearrange("p (h d) -> p h d", d=2).unsqueeze(3).broadcast_to([128, 4, 2, 64]), MUL,
                     [TB[6], tt("kdv")], [tt("kk%d" % kb)])
                for h in range(4):
                    mm(bank(7)[:, h * 128:(h + 1) * 128], KK[:, kb, h * 128:(h + 1) * 128], RVG[:, j, h * 128:(h + 1) * 128],
                       True, True, [tt("kk%d" % kb), tt("rvg")], [TB[7]])

            bufs = [(HT, tt("ht")), (HT2, tt("ht2"))]
            for j in range(4):
                hT_tile(0, j)
            hT_evac(*bufs[0])
            for tg in range(NG):
                cur["ht"], cur["htT"] = bufs[tg % 2]
                nxt = bufs[(tg + 1) % 2] if tg + 1 < NG else None
                ckpt('ht%d' % tg)
                load_cs(tg)
                wb, twb = ws_get(BLK_P2KV)
                wbv = wb.rearrange("p (k c) -> p k c", k=8)
                for g in range(2):
                    proj_fm(wbv, twb, g * 128, 6 + g)
                    rope(6 + g, MISC[:, M_GK8:M_GK8 + 1], tt("gk8"), True, K2[:, g, tg * 512:(tg + 1) * 512], tt("k2_%d_%d" % (g, tg)))
                ckpt('k%d' % tg)
                if nxt:
                    hT_tile(tg + 1, 0)
                for j in range(4):
                    proj_tm(wbv, twb, j, 256, 128, bank(6)[:, j * 128:(j + 1) * 128], TB[6])
                cpy("dve", VX[:, tg * 4:(tg + 1) * 4, 64:320].rearrange("p j (g c) -> p j g c", g=2)[:, :, :, 0:64],
                    bank(6).rearrange("p (j g c) -> p j g c", j=4, g=2), [TB[6]], [tt("vx%d" % tg)])
                ckpt('v%d' % tg)
                if nxt:
                    hT_tile(tg + 1, 1)
                ret_rk(tg)
                if nxt:
                    hT_tile(tg + 1, 2)
                ret_rv(tg)
                ckpt('rkv%d' % tg)
                if nxt:
                    hT_tile(tg + 1, 3)
                    hT_evac(*nxt)
                for j in range(4):
                    ci = tg * 4 + j
                    kv_matmul(j)
                    act(RF[(ci % 2) * 64:(ci % 2) * 64 + 64, ci // 2, :], SST[0:64, :], AF.Copy, [tt("sst")], [tt("rf")])
                    tten("dve", SST[0:64, :].rearrange("p (h c) -> p h c", h=4), SST[0:64, :].rearrange("p (h c) -> p h c", h=4),
                         MISC[0:64, M_DC:M_DC + 4].unsqueeze(2).broadcast_to([64, 4, 128]), MUL, [tt("sst"), tt("dc")], [tt("sst")])
                    tten("dve", SST[0:64, :], SST[0:64, :], bank(7)[0:64, :], ADD, [tt("sst"), TB[7]], [tt("sst")])

            rope_ctr["alt"] = [(4, 5), (2, 3)]
            ckpt('p2')
            for tg in range(NG - 1, -1, -1):
                if tg == NG - 1:
                    cur["ht"], cur["htT"] = bufs[tg % 2]
                else:
                    make_hT(tg)
                load_cs(tg)
                wq, twq = ws_get(BLK_Q)
                wqv = wq.rearrange("p (k c) -> p k c", k=8)
                for p in range(4):
                    proj_fm(wqv, twq, p * 128, 6 + (p % 2))
                    rope(6 + (p % 2), SMALL[:, S_QG:S_QG + 1], tt("small"), True, QT[:, p, :], tt("qt%d" % p))
                wz, twz = ws_get(BLK_ZA)
                wzv = wz.rearrange("p (k c) -> p k c", k=8)
                for p in range(4):
                    bk = 6 + (p % 2)
                    proj_fm(wzv, twz, p * 128, bk)
                    act(ZA[:, p, :], bank(bk), AF.Silu, [TB[bk]], [tt("za%d" % p)])
                ckpt('q%d' % tg)
                for p in range(4):
                    g = p // 2

                    def qk(kc):
                        si = kc % 2
                        mm(bank(2 * si), K2[0:64, g, kc * 128:(kc + 1) * 128], QT[0:64, p, :], True, True,
                           [tt("k2_%d_%d" % (g, kc // 4)), tt("qt%d" % p)], [TB[2 * si]])
                        mm(bank(2 * si + 1), K2[64:128, g, kc * 128:(kc + 1) * 128], QT[64:128, p, :], True, True,
                           [tt("k2_%d_%d" % (g, kc // 4)), tt("qt%d" % p)], [TB[2 * si + 1]])

                    def pv(kc):
                        pi = kc % NPT
                        mm(bank(4), VX[:, kc, 64 + g * 128:192 + g * 128], PT[:, pi, 0:512], kc == 0, kc == 31,
                           [tt("vx%d" % (kc // 4)), tt("pt%d" % pi)], [TB[4]])
                        mm(bank(5), VX[:, kc, g * 128:g * 128 + 128], PT[:, pi, 512:1024], kc == 0, kc == 31,
                           [tt("vx%d" % (kc // 4)), tt("pt%d" % pi)], [TB[5]])

                    qk(0)
                    qk(1)
                    for kc in range(32):
                        si = kc % 2
                        act(PT[:, kc % NPT, :], bank(2 * si, 2), AF.Exp, [TB[2 * si], TB[2 * si + 1]], [tt("pt%d" % (kc % NPT))])
                        if kc + 2 < 32:
                            qk(kc + 2)
                        pv(kc)
                    cpy("dve", SA[:], bank(4), [TB[4]], [tt("sa")])
                    cpy("dve", SB_[:], bank(5), [TB[5]], [tt("sb")])
                    recip(DEN[0:64, :], SA[64:128, :], [tt("sa")], [tt("den")])
                    recip(DEN[64:128, :], SB_[0:64, :], [tt("sb")], [tt("den")])
                    tten("pool", DEN[0:64, :], SA[0:64, :], DEN[0:64, :], MUL, [tt("sa"), tt("den")], [tt("den")])
                    tten("pool", DEN[64:128, :], SB_[64:128, :], DEN[64:128, :], MUL, [tt("sb"), tt("den")], [tt("den")])
                    tten("pool", YA[:, p, :], DEN[:], ZA[:, p, :], MUL, [tt("den"), tt("za%d" % p)], [tt("ya")])
                ckpt('att%d' % tg)
                wr, twr = ws_get(BLK_RQ)
                wrv = wr.rearrange("p (k c) -> p k c", k=8)
                for h in range(4):
                    proj_fm(wrv, twr, h * 128, 6 + (h % 2))
                    rope(6 + (h % 2), 1.0, None, False, RQ[:, h, :], tt("rq"))
                    tten("pool", QD[:, h, :].rearrange("p (j c) -> p j c", j=4), RQ[:, h, :].rearrange("p (j c) -> p j c", j=4),
                         QDEC[:, h * 128:(h + 1) * 128].unsqueeze(1).broadcast_to([128, 4, 128]), MUL, [tt("rq"), tt("qdec")], [tt("qd")])
                ckpt('rqd%d' % tg)
                if tg != NG - 1:
                    ret_kv_inputs(tg)
                wz, twz = ws_get(BLK_ZR)
                wzv = wz.rearrange("p (k c) -> p k c", k=8)
                for h in range(4):
                    bk = 6 + (h % 2)
                    proj_fm(wzv, twz, h * 128, bk)
                    act(ZR[:, h, :], bank(bk), AF.Silu, [TB[bk]], [tt("zr")])
                ckpt('rin%d' % tg)
                for j in range(3, -1, -1):
                    ci = tg * 4 + j
                    sb_i = j % 2
                    for h in range(4):
                        r0 = (h % 2) * 64
                        mm(bank(h % 2)[:, (h // 2) * 128:(h // 2 + 1) * 128], RKT[r0:r0 + 64, h // 2, j * 128:(j + 1) * 128],
                           RQ[r0:r0 + 64, h, j * 128:(j + 1) * 128], True, True, [tt("rkt"), tt("rq")], [TB[h % 2]])
                    for hb in range(2):
                        tten("dve", SM[:, sb_i, :].rearrange("p (a b c) -> p a b c", a=2, b=2)[:, :, hb, :],
                             bank(hb)[:, 0:256].rearrange("p (a c) -> p a c", a=2),
                             DMK[:].rearrange("p (a b c) -> p a b c", a=2, b=2)[:, :, hb, :], MUL, [TB[hb], tt("dmk")], [tt("sm%d" % sb_i)])
                    ckpt('rs%d' % ci)
                    act(RRO[0:64, sb_i, :], RF[(ci % 2) * 64:(ci % 2) * 64 + 64, ci // 2, :], AF.Copy, [tt("rf")], [tt("rro%d" % sb_i)])
                    cpy("pool", RRO[64:128, sb_i, :], SST[64:128, :], [tt("sst")], [tt("rro%d" % sb_i)])
                    for h in range(4):
                        mm(bank(2)[:, h * 128:(h + 1) * 128], SM[:, sb_i, h * 128:(h + 1) * 128], RVG[:, j, h * 128:(h + 1) * 128],
                           True, False, [tt("sm%d" % sb_i), tt("rvg")], [TB[2]])
                        mm(bank(2)[:, h * 128:(h + 1) * 128], QD[:, h, j * 128:(j + 1) * 128], RRO[:, sb_i, h * 128:(h + 1) * 128],
                           False, True, [tt("qd"), tt("rro%d" % sb_i)], [TB[2]])
                    ckpt('ro%d' % ci)
                    for h in range(4):
                        bnstats(STAT[:, sb_i, h, 0:6], bank(2)[:, h * 128:(h + 1) * 128], [TB[2]], [tt("stat%d" % sb_i)])
                        bnaggr(STAT[:, sb_i, h, 6:8], STAT[:, sb_i, h, 0:6], [tt("stat%d" % sb_i)], [tt("stat%d" % sb_i)])
                    gr = MISC[:, M_GNR + sb_i * 4:M_GNR + sb_i * 4 + 4]
                    gn = MISC[:, M_GNN + sb_i * 4:M_GNN + sb_i * 4 + 4]
                    act(gr, STAT[:, sb_i, :, 7], AF.Ln, [tt("stat%d" % sb_i)], [tt("gr%d" % sb_i)], bias=EPS)
                    act(gr, gr, AF.Exp, [tt("gr%d" % sb_i)], [tt("gr%d" % sb_i)], scale=-0.5)
                    stt("dve", gn, STAT[:, sb_i, :, 6], -1.0, gr, MUL, MUL, [tt("stat%d" % sb_i), tt("gr%d" % sb_i)], [tt("gn%d" % sb_i)])
                    for h in range(4):
                        tsc("dve", ON[:, sb_i, h * 128:(h + 1) * 128], bank(2)[:, h * 128:(h + 1) * 128],
                            MISC[:, M_GNR + sb_i * 4 + h:M_GNR + sb_i * 4 + h + 1], MISC[:, M_GNN + sb_i * 4 + h:M_GNN + sb_i * 4 + h + 1],
                            MUL, ADD, [TB[2], tt("gr%d" % sb_i), tt("gn%d" % sb_i)], [tt("on%d" % sb_i)])
                    ckpt('rgn%d' % ci)
                    ptn = bank(3).bitcast(BF16)[:, 0:512]
                    for h in range(4):
                        trp(ptn[:, h * 128:(h + 1) * 128], ON[:, sb_i, h * 128:(h + 1) * 128], [tt("on%d" % sb_i)], [TB[3]])
                    for h in range(4):
                        stt("dve", YR[:, h, j * 128:(j + 1) * 128], ptn[:, h * 128:(h + 1) * 128], SMALL[:, S_GN + h:S_GN + h + 1],
                            ZR[:, h, j * 128:(j + 1) * 128], MUL, MUL, [TB[3], tt("small"), tt("zr")], [tt("yr")])
                    ckpt('ryr%d' % ci)
                    kv_matmul(j)
                    tten("dve", SST[64:128, :].rearrange("p (h c) -> p h c", h=4), SST[64:128, :].rearrange("p (h c) -> p h c", h=4),
                         MISC[64:128, M_DC + 4:M_DC + 8].unsqueeze(2).broadcast_to([64, 4, 128]), MUL, [tt("sst"), tt("dc")], [tt("sst")])
                    tten("dve", SST[64:128, :], SST[64:128, :], bank(7)[64:128, :], ADD, [tt("sst"), TB[7]], [tt("sst")])
                ckpt('ret%d' % tg)
                for n in range(8):
                    wm, twm = ws_get(BLK_M + n)
                    glv = wm[:, 0:2048].rearrange("p (k c) -> p k c", k=8)
                    pav = wm[:, 2048:2560].rearrange("p (k c) -> p k c", k=4)
                    prv = wm[:, 2560:3072].rearrange("p (k c) -> p k c", k=4)
                    gi = n % NGT
                    b0 = 4 * (n % 2)
                    for br in range(2):
                        bk = b0 + 2 + br
                        for kc in range(8):
                            mm(bank(bk), glv[:, kc, br * 128:(br + 1) * 128], cur["ht"][:, kc, :], kc == 0, kc == 7, [twm, cur["htT"]], [TB[bk]])
                        act(GATE[:, gi, br, :], bank(bk), AF.Sigmoid, [TB[bk]], [tt("gate%d_%d" % (gi, br))])
                    for cc in range(4):
                        mm(bank(b0), pav[:, cc, :], YA[:, cc, :], cc == 0, cc == 3, [twm, tt("ya")], [TB[b0]])
                    for cc in range(4):
                        mm(bank(b0 + 1), prv[:, cc, :], YR[:, cc, :], cc == 0, cc == 3, [twm, tt("yr")], [TB[b0 + 1]])
                    tten("dve", MT[:, 0, :], bank(b0), GATE[:, gi, 0, :], MUL, [TB[b0], tt("gate%d_0" % gi)], [tt("mt0")])
                    tten("dve", MT[:, 1, :], bank(b0 + 1), GATE[:, gi, 1, :], MUL, [TB[b0 + 1], tt("gate%d_1" % gi)], [tt("mt1")])
                    tten("pool", MG[:, n, :], MT[:, 0, :], MT[:, 1, :], ADD, [tt("mt0"), tt("mt1")], [tt("mg")])
                ckpt('mrg%d' % tg)
                wo0, two0 = ws_get(BLK_WO)
                wo1, two1 = ws_get(BLK_WO + 1, la=NWB - 1)
                wov = [wo0.rearrange("p (k c) -> p k c", k=8), wo1.rearrange("p (k c) -> p k c", k=8)]
                twos = [two0, two1]
                for j in range(4):
                    tile = tg * 4 + j
                    pb0 = 2 * j
                    for hf in range(2):
                        for kc in range(8):
                            mm(bank(pb0 + hf), MG[:, kc, j * 128:(j + 1) * 128], wov[hf][:, kc, :], kc == 0, kc == 7,
                               [tt("mg"), twos[hf]], [TB[pb0 + hf]])
                    xi = load_x(tile)
                    act(JUNK, bank(pb0, 2), AF.Square, [TB[pb0], TB[pb0 + 1]], [tt("junk"), tt("ss2")], accum_out=MISC[:, M_SS + 1:M_SS + 2])
                    act(MISC[:, M_RSTD + 1:M_RSTD + 2], MISC[:, M_SS + 1:M_SS + 2], AF.Ln, [tt("ss2")], [tt("rstd2")], scale=1.0 / DM, bias=EPS)
                    act(MISC[:, M_RSTD + 1:M_RSTD + 2], MISC[:, M_RSTD + 1:M_RSTD + 2], AF.Exp, [tt("rstd2")], [tt("rstd2")], scale=-0.5)
                    stt("dve", bank(pb0, 2), bank(pb0, 2), MISC[:, M_RSTD + 1:M_RSTD + 2], GG[:], MUL, MUL,
                        [TB[pb0], TB[pb0 + 1], tt("rstd2"), tt("gg")], [TB[pb0], TB[pb0 + 1]])
                    tten("dve", XT[:, xi, :], bank(pb0, 2), XT[:, xi, :], ADD, [TB[pb0], TB[pb0 + 1], tt("xt%d" % xi)], [tt("xt%d" % xi)])
                    dma(os.environ.get("MK_STQ", "sp"), out_d[tile * 128:(tile + 1) * 128, :], XT[:, xi, :], [tt("xt%d" % xi)], [])
        try:
            run_all()
        except _Stop:
            pass
        dump_aps = {"HT": HT, "K2": K2, "VX": VX, "RF": RF, "MISC": MISC, "GG": GG, "DMK": DMK, "QDEC": QDEC,
                    "YA": YA, "YR": YR, "MG": MG, "QT": QT, "ZA": ZA, "RKT": RKT, "RVG": RVG, "SST": SST,
                    "RQ": RQ, "QD": QD, "ZR": ZR, "XN": XN, "CS": CS}
        for nm in dumps:
            src = dump_aps[nm]
            shp = list(src.shape)
            flat = int(np.prod(shp[1:]))
            dd = nc.dram_tensor("dbg_" + nm, [128, flat], src.dtype, kind="ExternalOutput").ap()
            allT = list(tdict.values()) + TB
            sv = src[:] if len(shp) == 2 else src[:].rearrange("p a b -> p (a b)") if len(shp) == 3 else src[:].rearrange("p a b c -> p (a b c)")
            dma("sp", dd, sv, allT, [])
        import os
        if os.environ.get('MK_RESCHED', '1') == '1':
            S.reschedule()
        if not _EXP:
            S.emit(nc)
    return nc, S


_CACHE = {}


def kernel(x, c, w_ada, b_ada, g_pre, w_in, qn_g, kn_g, w_dec_f, w_dec_b, gn_g, w_pa, w_pr, w_out, g_post):
    x = np.asarray(x, np.float32)
    c = np.asarray(c, np.float32)
    f = lambda a: np.asarray(a, np.float32)[0]
    w_ada, b_ada, g_pre, w_in, qn_g, kn_g = f(w_ada), f(b_ada), f(g_pre), f(w_in), f(qn_g), f(kn_g)
    w_dec_f, w_dec_b, gn_g, w_pa, w_pr, w_out, g_post = f(w_dec_f), f(w_dec_b), f(gn_g), f(w_pa), f(w_pr), f(w_out), f(g_post)
    if "nc" not in _CACHE:
        _CACHE["nc"] = build_program()[0]
        _CACHE["consts"] = _host_consts()
    nc = _CACHE["nc"]
    rope, cst = _CACHE["consts"]
    blocks = _host_blocks(w_ada, w_in, w_pa, w_pr, w_out)
    col8 = lambda v: np.ascontiguousarray(v.reshape(8, 128).T)
    rows = np.ascontiguousarray(np.stack([b_ada[2048:3072], g_post], 0))
    wdec = np.concatenate([w_dec_f, w_dec_b])[None, :].astype(np.float32)
    in_maps = []
    for b in range(8):
        small = np.zeros((128, NSMALL), np.float32)
        small[:, 0:8] = col8(c[b])
        small[:, 8:16] = col8(g_pre)
        small[:, 16:24] = col8(b_ada[1024:2048])
        small[:, 24:32] = col8(b_ada[0:1024])
        small[:, 32] = np.concatenate([qn_g, qn_g])
        small[:, 33] = np.concatenate([kn_g, kn_g])
        small[:, 34:38] = gn_g.reshape(4, 128).T
        in_maps.append({"x": np.ascontiguousarray(x[b]), "wsrc": blocks, "rope": rope, "cst": cst,
                        "small": small, "rows": rows, "wdec": wdec})
    res = run_bass_kernel_spmd(nc, in_maps, core_ids=list(range(8)))
    return np.stack([np.asarray(r["out"], np.float32) for r in res.results], 0)
```

```python
import numpy as np
import concourse.bass as bass
import concourse.mybir as mybir
from concourse.bass_utils import run_bass_kernel_spmd

F32 = mybir.dt.float32
BF16 = mybir.dt.bfloat16
AF = mybir.ActivationFunctionType
ALU = mybir.AluOpType
AX = mybir.AxisListType


class T:
    __slots__ = ("name", "last_w", "readers", "excl")

    def __init__(self, name="", excl=False):
        self.name = name
        self.last_w = None
        self.readers = []
        self.excl = excl


class Op:
    __slots__ = ("eng", "fn", "deps", "signal", "token", "is_dma", "idx", "gidx", "preds", "succs", "dur", "lat", "cls", "prio", "t_start", "t_done", "npred", "tag")

    def __init__(self, eng, fn, is_dma):
        self.eng = eng
        self.fn = fn
        self.deps = []
        self.signal = False
        self.token = None
        self.is_dma = is_dma
        self.preds = []
        self.succs = []
        self.dur = 0.2
        self.lat = 0.0
        self.cls = None


class Sched:
    ENGS = ("pe", "act", "dve", "pool", "sp")

    def __init__(self, n_dma_sems=8):
        self.ops = {e: [] for e in self.ENGS}
        self.n_dma_sems = n_dma_sems
        self.dma_rr = {e: 0 for e in self.ENGS}
        self.dma_last = {}
        self.gcount = 0

    def add(self, eng, fn, reads=(), writes=(), dma=False, dur=0.2, lat=0.0, cls=None):
        op = Op(eng, fn, dma)
        op.dur, op.lat, op.cls = dur, lat, cls
        import sys as _sys
        op.tag = (_sys._getframe(2).f_lineno, _sys._getframe(3).f_lineno)
        op.gidx = self.gcount
        self.gcount += 1
        writes = list(writes) + [t for t in reads if t.excl]
        reads = [t for t in reads if not t.excl]
        raw = []
        war = []
        for t in reads:
            if t.last_w is not None:
                raw.append(t.last_w)
        for t in writes:
            if t.last_w is not None:
                raw.append(t.last_w)
            war.extend(t.readers)
        for t in reads:
            t.readers.append(op)
        for t in writes:
            t.last_w = op
            t.readers = []
        if dma:
            k = self.dma_rr[eng]
            self.dma_rr[eng] = (k + 1) % self.n_dma_sems
            prev = self.dma_last.get((eng, k))
            if prev is not None:
                raw.append(prev)
            self.dma_last[(eng, k)] = op
            op.token = (eng, k)
        seen = set()
        pseen = set()
        for lst, is_war in ((raw, False), (war, True)):
            for d in lst:
                if d is op or id(d) in seen:
                    continue
                if id(d) not in pseen:
                    pseen.add(id(d))
                    op.preds.append(d)
                if (not d.is_dma) and (not dma) and d.eng == eng:
                    if eng == "pe" or is_war:
                        continue
                seen.add(id(d))
                op.deps.append(d)
        op.idx = len(self.ops[eng])
        self.ops[eng].append(op)
        return op

    def reschedule(self, hop=0.15, act_switch=1.4):
        import heapq
        allops = []
        for e in self.ENGS:
            allops.extend(self.ops[e])
        allops.sort(key=lambda o: o.gidx)
        for o in allops:
            o.succs = []
        for o in allops:
            o.npred = len(o.preds)
            for p in o.preds:
                p.succs.append(o)
        for o in reversed(allops):
            m = 0.0
            for s_ in o.succs:
                if s_.prio > m:
                    m = s_.prio
            o.prio = m + o.dur + o.lat
        ready = {e: [] for e in self.ENGS}
        for o in allops:
            o.t_done = None
            if o.npred == 0:
                ready[o.eng].append(o)
        free_at = {e: 0.0 for e in self.ENGS}
        cur_cls = {e: None for e in self.ENGS}
        new_order = {e: [] for e in self.ENGS}
        nleft = len(allops)

        def rtime(o):
            t = 0.0
            for p in o.preds:
                tp = p.t_done + (hop if (p.eng != o.eng or p.is_dma) else 0.0)
                if tp > t:
                    t = tp
            return t

        rt_cache = {}
        while nleft:
            best = None
            for e in self.ENGS:
                lst = ready[e]
                if not lst:
                    continue
                fa = free_at[e]
                for o in lst:
                    r = rt_cache.get(id(o))
                    if r is None:
                        r = rtime(o)
                        rt_cache[id(o)] = r
                    st = r if r > fa else fa
                    if e == "act" and o.cls is not None and cur_cls[e] is not None and o.cls != cur_cls[e]:
                        st += act_switch
                    key = (st, -o.prio)
                    if best is None or key < best[0]:
                        best = (key, o)
            (st, _), o = best
            e = o.eng
            ready[e].remove(o)
            o.t_start = st
            free_at[e] = st + o.dur
            o.t_done = st + o.dur + o.lat
            if e == "act" and o.cls is not None:
                cur_cls[e] = o.cls
            new_order[e].append(o)
            nleft -= 1
            for s_ in o.succs:
                s_.npred -= 1
                if s_.npred == 0:
                    ready[s_.eng].append(s_)
        for e in self.ENGS:
            self.ops[e] = new_order[e]
        self.makespan = max(free_at.values())

    def emit(self, nc, block_ctx_extra=None):
        for e in self.ENGS:
            for op in self.ops[e]:
                for d in op.deps:
                    d.signal = True
        import contextlib
        with contextlib.ExitStack() as st:
            sems = {e: st.enter_context(nc.semaphore("s_" + e)) for e in ("pe", "act", "dve", "pool")}
            dsems = {}
            for e in self.ENGS:
                if any(o.is_dma for o in self.ops[e]):
                    for k in range(self.n_dma_sems):
                        dsems[(e, k)] = st.enter_context(nc.semaphore("d_%s_%d" % (e, k)))
            for e in self.ENGS:
                cnt = 0
                dcnt = {}
                for op in self.ops[e]:
                    if op.is_dma:
                        key = op.token
                        dcnt[key] = dcnt.get(key, 0) + 16
                        op.token = (dsems[key], dcnt[key], key)
                    elif op.signal:
                        cnt += 1
                        op.token = (sems[e], cnt, e)
            block = st.enter_context(nc.Block())
            handles = {"pe": block.tensor, "act": block.scalar, "dve": block.vector, "pool": block.gpsimd,
                       "sp": block.sync}
            nwaits = {e: 0 for e in self.ENGS}

            def make(e):
                ops = self.ops[e]

                def body(eng):
                    known = {}
                    for op in ops:
                        need = {}
                        for d in op.deps:
                            sem, val, key = d.token
                            if known.get(key, 0) >= val:
                                continue
                            if key not in need or need[key][1] < val:
                                need[key] = (sem, val)
                        for key, (sem, val) in need.items():
                            eng.wait_ge(sem, val)
                            known[key] = val
                            nwaits[e] += 1
                        inst = op.fn(eng)
                        if op.is_dma:
                            inst.then_inc(op.token[0], 16)
                        elif op.signal:
                            inst.then_inc(op.token[0], 1)
                    fin = {}
                    for op in ops:
                        if op.is_dma:
                            fin[op.token[2]] = (op.token[0], op.token[1])
                    for key, (sem, val) in fin.items():
                        if known.get(key, 0) < val:
                            eng.wait_ge(sem, val)
                return body

            for e in self.ENGS:
                if self.ops[e]:
                    handles[e](make(e))
            self.nwaits = nwaits

import contextlib

SEQ = 4096
DM = 1024
NG = 8
EPS = 1e-6

BLK_ADA = 0
BLK_P2KV = 6
BLK_RK = 7
BLK_RV = 8
BLK_Q = 9
BLK_ZA = 10
BLK_RQ = 11
BLK_ZR = 12
BLK_M = 13
BLK_WO = 21
NBLK = 23
NSMALL = 40


def _host_blocks(w_ada, w_in, w_pa, w_pr, w_out):
    blocks = np.zeros((NBLK, 128, 4096), np.float32)

    def kmaj(w):
        nc_ = w.shape[1]
        t = np.zeros((128, 8, 512), np.float32)
        t[:, :, :nc_] = w.reshape(8, 128, nc_).transpose(1, 0, 2)
        return t.reshape(128, 4096)

    wa = w_ada
    for j in range(6):
        blocks[BLK_ADA + j] = kmaj(wa[:, j * 512:(j + 1) * 512])
    o_q, o_k, o_v, o_za, o_rq, o_rk, o_rv, o_zr, o_gl = 0, 512, 640, 768, 1280, 1536, 1792, 2304, 2816
    k0 = w_in[:, o_k:o_k + 64]
    k1 = w_in[:, o_k + 64:o_k + 128]
    blocks[BLK_P2KV] = kmaj(np.concatenate([k0, k0, k1, k1, w_in[:, o_v:o_v + 128]], axis=1))
    blocks[BLK_RK] = kmaj(w_in[:, o_rk:o_rk + 256])
    blocks[BLK_RV] = kmaj(w_in[:, o_rv:o_rv + 512])
    blocks[BLK_Q] = kmaj(w_in[:, o_q:o_q + 512])
    blocks[BLK_ZA] = kmaj(w_in[:, o_za:o_za + 512])
    rq = [w_in[:, o_rq + 64 * h:o_rq + 64 * (h + 1)] for h in range(4)]
    blocks[BLK_RQ] = kmaj(np.concatenate([rq[0], rq[0], rq[1], rq[1], rq[2], rq[2], rq[3], rq[3]], axis=1))
    blocks[BLK_ZR] = kmaj(w_in[:, o_zr:o_zr + 512])
    for n in range(8):
        t = np.zeros((128, 4096), np.float32)
        gl = np.concatenate([w_in[:, o_gl + n * 128:o_gl + (n + 1) * 128],
                             w_in[:, o_gl + 1024 + n * 128:o_gl + 1024 + (n + 1) * 128]], axis=1)
        t[:, 0:2048] = gl.reshape(8, 128, 256).transpose(1, 0, 2).reshape(128, 2048)
        t[:, 2048:2560] = w_pa[:, n * 128:(n + 1) * 128].reshape(4, 128, 128).transpose(1, 0, 2).reshape(128, 512)
        t[:, 2560:3072] = w_pr[:, n * 128:(n + 1) * 128].reshape(4, 128, 128).transpose(1, 0, 2).reshape(128, 512)
        blocks[BLK_M + n] = t
    for hf in range(2):
        blocks[BLK_WO + hf] = kmaj(w_out[:, hf * 512:(hf + 1) * 512])
    return blocks


def _host_consts():
    t = np.arange(SEQ)
    row = (t // 64).astype(np.float32)
    col = (t % 64).astype(np.float32)
    inv = (10000.0 ** (-np.arange(0, 32, 2, dtype=np.float32) / 32.0)).astype(np.float32)
    ang_r = row[None, :] * inv[:, None]
    ang_c = col[None, :] * inv[:, None]
    cos64 = np.concatenate([np.cos(ang_r), np.cos(ang_r), np.cos(ang_c), np.cos(ang_c)], 0)
    sin64 = np.concatenate([np.sin(ang_r), np.sin(ang_r), np.sin(ang_c), np.sin(ang_c)], 0)
    rope = np.stack([np.concatenate([cos64, cos64], 0), np.concatenate([sin64, sin64], 0)], 0).astype(np.float32)
    ident = np.eye(128, dtype=np.float32)
    perm = np.zeros((128, 128), np.float32)
    for hb in range(2):
        for d in range(64):
            blk = d // 16
            if blk % 2 == 0:
                p, s = d + 16, -1.0
            else:
                p, s = d - 16, 1.0
            perm[hb * 64 + p, hb * 64 + d] = s
    ones = np.zeros((128, 128), np.float32)
    ones[:64, :64] = 1.0
    ones[64:, 64:] = 1.0
    return rope, np.stack([ident, perm, ones], 0)


class _Stop(Exception):
    pass


def build_program(stop=None, dumps=()):
    nc = bass.Bass("TRN2", target_bir_lowering=False)
    x_d = nc.dram_tensor("x", [SEQ, DM], F32, kind="ExternalInput").ap()
    wsrc = nc.dram_tensor("wsrc", [NBLK, 128, 4096], F32, kind="ExternalInput").ap()
    rope_d = nc.dram_tensor("rope", [2, 128, SEQ], F32, kind="ExternalInput").ap()
    cst_d = nc.dram_tensor("cst", [3, 128, 128], F32, kind="ExternalInput").ap()
    small_d = nc.dram_tensor("small", [128, NSMALL], F32, kind="ExternalInput").ap()
    rows_d = nc.dram_tensor("rows", [2, DM], F32, kind="ExternalInput").ap()
    wdec_d = nc.dram_tensor("wdec", [1, 8], F32, kind="ExternalInput").ap()
    out_d = nc.dram_tensor("out", [SEQ, DM], F32, kind="ExternalOutput").ap()
    wbf_d = nc.dram_tensor("wbf", [NBLK, 128, 4096], BF16, kind="Internal").ap()

    S = Sched(n_dma_sems=12)
    tdict = {}

    import os
    _alias = {"io0": "sa", "io1": "sa", "io2": "sa", "tmpd": "sa", "cbc": "mt1", "junk": "mt0"}
    if os.environ.get("MK_JUNK") == "1":
        del _alias["junk"]

    def tt(name):
        name = _alias.get(name, name)
        if name not in tdict:
            tdict[name] = T(name)
        return tdict[name]

    with contextlib.ExitStack() as st:
        def sb(name, shape, dt):
            return st.enter_context(nc.sbuf_tensor(name, shape, dt))

        PS = st.enter_context(nc.psum_tensor("ps", [128, 4096], F32))
        TB = [T("bank%d" % i, excl=True) for i in range(8)]

        def bank(i, n=1):
            return PS[:, i * 512:(i + n) * 512]

        import os
        _EXP = os.environ.get("MK_EXPLORE") == "1"
        _PDT = mybir.dt.int8 if _EXP else BF16
        K2 = sb("K2", [128, 2, SEQ], _PDT)
        VX = sb("VX", [128, 32, 320], _PDT)
        RF = sb("RF", [128, 16, 512], _PDT)
        NWB = int(os.environ.get('MK_NWB', '3'))
        WB = sb("WB", [128, NWB, 4096], BF16)
        NXT = int(os.environ.get('MK_NXT', '3'))
        XT = sb("XT", [128, NXT, DM], F32)
        XN = sb("XN", [128, 2, DM], BF16)
        HT = sb("HT", [128, 8, 512], BF16)
        HT2 = sb("HT2", [128, 8, 512], BF16)
        CS = sb("CS", [128, 2, 512], F32)
        QBF = sb("QBF", [128, 2, 512], BF16)
        SQ = sb("SQ", [128, 2, 512], BF16)
        T1 = sb("T1", [128, 2, 512], F32)
        T2 = sb("T2", [128, 2, 512], F32)
        RRS = sb("RRS", [128, 2, 512], F32)
        QT = sb("QT", [128, 4, 512], BF16)
        ZA = sb("ZA", [128, 4, 512], BF16)
        NPT = int(os.environ.get('MK_NPT', '3'))
        PT = sb("PT", [128, NPT, 1024], BF16)
        SA = sb("SA", [128, 512], F32)
        SB_ = sb("SB_", [128, 512], F32)
        DEN = sb("DEN", [128, 512], F32)
        YA = sb("YA", [128, 4, 512], BF16)
        YR = sb("YR", [128, 4, 512], BF16)
        RQ = sb("RQ", [128, 4, 512], BF16)
        QD = sb("QD", [128, 4, 512], BF16)
        ZR = sb("ZR", [128, 4, 512], BF16)
        RKT = sb("RKT", [128, 2, 512], BF16)
        RVG = sb("RVG", [128, 4, 512], BF16)
        KK = sb("KK", [128, 2, 512], BF16)
        SM = sb("SM", [128, 2, 512], BF16)
        RRO = sb("RRO", [128, 2, 512], BF16)
        ON = sb("ON", [128, 2, 512], BF16)
        SST = sb("SST", [128, 512], F32)
        NGT = int(os.environ.get("MK_NGT", "1"))
        GATE = sb("GATE", [128, NGT, 2, 512], BF16)
        MT = sb("MT", [128, 2, 512], F32)
        MG = sb("MG", [128, 8, 512], BF16)
        IDENT = sb("IDENT", [128, 128], BF16)
        PERM = sb("PERM", [128, 128], BF16)
        ONESB = sb("ONESB", [128, 128], BF16)
        DMK = sb("DMK", [128, 512], F32)
        QDEC = sb("QDEC", [128, 512], F32)
        GG = sb("GG", [128, DM], F32)
        SMALL = sb("SMALL", [128, NSMALL], F32)
        MISC = sb("MISC", [128, 128], F32)
        IO = SA[:, 0:384].rearrange("p (a b) -> p a b", a=3)
        TMPD = SA[:, 384:512]
        CBC = MT[:, 1, :].bitcast(BF16).rearrange("p (a b) -> p a b", a=8)
        JUNK = sb("JUNKB", [128, DM], BF16)[:] if os.environ.get("MK_JUNK") == "1" else MT[:, 0, :].bitcast(BF16)
        CBF = sb("CBF", [128, 8], BF16)
        STAT = sb("STAT", [128, 2, 4, 8], F32)

        _CACHE['sbuf_left'] = nc.sbuf_bytes_remaining
        M_ACOL, M_SHCOL, M_LG, M_DC, M_WD, M_E, M_KD8, M_KDV, M_IOP, M_IOPR, M_GK8 = 0, 8, 16, 24, 32, 40, 48, 56, 64, 65, 66
        M_SS, M_RSTD, M_CACT, M_TMP8 = 70, 74, 80, 88
        M_GNR, M_GNN = 96, 104
        S_C, S_GPRE, S_BSC, S_BSH, S_QG, S_KG, S_GN = 0, 8, 16, 24, 32, 33, 34

        def fsz(ap):
            n = 1
            for d in list(ap.shape)[1:]:
                n *= int(d)
            return n

        def mm(o, l, r, start, stop, rd, wr):
            n = fsz(o)
            k = int(l.shape[0])
            d = 0.06 + n * 0.00041
            if k <= 64 and n >= 256:
                d = 0.17
            S.add("pe", lambda e: e.matmul(o, l, r, start=start, stop=stop), reads=rd, writes=wr, dur=d)

        def trp(o, i, rd, wr):
            S.add("pe", lambda e: e.transpose(o, i, IDENT[:]), reads=rd + [tt("ident")], writes=wr, dur=0.115)

        _ACLS = {AF.Exp: "exp", AF.Ln: "exp", AF.Silu: "sig", AF.Sigmoid: "sig"}

        def act(o, i, f, rd, wr, **kw):
            S.add("act", lambda e: e.activation(out=o, in_=i, func=f, **kw), reads=rd, writes=wr,
                  dur=0.22 + fsz(i) * 0.00083 + (0.1 if "accum_out" in kw else 0.0), cls=_ACLS.get(f))

        def _vd(eng, n):
            return (0.1 + n * 0.00104) if eng == "dve" else (0.15 + n * 0.0021)

        def tten(eng, o, a, b, op, rd, wr):
            S.add(eng, lambda e: e.tensor_tensor(out=o, in0=a, in1=b, op=op), reads=rd, writes=wr, dur=_vd(eng, fsz(o)))

        def tsc(eng, o, a, s1, s2, op0, op1, rd, wr):
            if op1 is None:
                S.add(eng, lambda e: e.tensor_scalar(out=o, in0=a, scalar1=s1, scalar2=None, op0=op0), reads=rd, writes=wr, dur=_vd(eng, fsz(o)))
            else:
                S.add(eng, lambda e: e.tensor_scalar(out=o, in0=a, scalar1=s1, scalar2=s2, op0=op0, op1=op1), reads=rd, writes=wr, dur=_vd(eng, fsz(o)))

        def stt(eng, o, a, s, b, op0, op1, rd, wr):
            S.add(eng, lambda e: e.scalar_tensor_tensor(out=o, in0=a, scalar=s, in1=b, op0=op0, op1=op1), reads=rd, writes=wr, dur=_vd(eng, fsz(o)))

        def cpy(eng, o, i, rd, wr):
            S.add(eng, lambda e: e.tensor_copy(out=o, in_=i), reads=rd, writes=wr, dur=_vd(eng, fsz(o)))

        def dma(eng, o, i, rd, wr):
            nbytes = fsz(o) * int(o.shape[0]) * (2 if o.dtype == BF16 else 4)
            S.add(eng, lambda e: e.dma_start(out=o, in_=i), reads=rd, writes=wr, dma=True,
                  dur=(0.08 if eng == "sp" else 1.0), lat=2.0 + nbytes / 150000.0)

        MUL, ADD, MAX = ALU.mult, ALU.add, ALU.max

        def ckpt(name):
            if stop == name:
                raise _Stop()

        def bnstats(o, i, rd, wr):
            S.add("dve", lambda e: e.bn_stats(out=o, in_=i), reads=rd, writes=wr, dur=0.25)

        def bnaggr(o, i, rd, wr):
            S.add("dve", lambda e: e.bn_aggr(out=o, in_=i), reads=rd, writes=wr, dur=0.15)

        def recip(o, i, rd, wr):
            S.add("dve", lambda e: e.reciprocal(out=o, in_=i), reads=rd, writes=wr, dur=0.1 + fsz(o) * 0.00625)

        order = [BLK_ADA + j for j in range(6)]
        for tg in range(NG):
            order += [BLK_P2KV, BLK_RK, BLK_RV]
        for tg in range(NG):
            order += [BLK_Q, BLK_ZA, BLK_RQ] + ([BLK_RK, BLK_RV] if tg > 0 else []) + [BLK_ZR] \
                + [BLK_M + n for n in range(8)] + [BLK_WO, BLK_WO + 1]
        conv_done = set()
        ws_state = {"issued": 0, "cur": 0}

        def conv(blk):
            if blk in conv_done:
                return
            conv_done.add(blk)
            dma("pool", wbf_d[blk], wsrc[blk], [], [tt("wbf%d" % blk)])

        def ws_issue_upto(n):
            while ws_state["issued"] < min(n, len(order)):
                i = ws_state["issued"]
                blk = order[i]
                conv(blk)
                dma("sp", WB[:, i % NWB, :], wbf_d[blk], [tt("wbf%d" % blk)], [tt("wb%d" % (i % NWB))])
                ws_state["issued"] += 1

        def ws_get(expect, la=None):
            i = ws_state["cur"]
            assert order[i] == expect, (i, order[i], expect)
            ws_issue_upto(i + (NWB if la is None else la))
            ws_state["cur"] += 1
            return WB[:, i % NWB, :], tt("wb%d" % (i % NWB))

        def run_all():
            for blk in [BLK_ADA + j for j in range(6)] + [BLK_P2KV, BLK_RK, BLK_RV]:
                conv(blk)
            dma("sp", SMALL[:], small_d, [], [tt("small")])
            dma("sp", GG[:], rows_d[0:1, :].partition_broadcast(128), [], [tt("gg")])
            dma("sp", XT[:, 0, :], rows_d[1:2, :].partition_broadcast(128), [], [tt("xt0")])
            dma("sp", MISC[:, M_WD:M_WD + 8], wdec_d.partition_broadcast(128), [], [tt("wd")])
            dma("pool", IDENT[:], cst_d[0], [], [tt("ident")])
            dma("pool", PERM[:], cst_d[1], [], [tt("perm")])
            dma("pool", ONESB[:], cst_d[2], [], [tt("onesb")])
            ws_issue_upto(NWB)

            act(MISC[:, M_CACT:M_CACT + 8], SMALL[:, S_C:S_C + 8], AF.Silu, [tt("small")], [tt("cact")])
            cpy("dve", CBF[:], MISC[:, M_CACT:M_CACT + 8], [tt("cact")], [tt("cbf")])
            cpy("dve", CBC, MISC[:, M_CACT:M_CACT + 8].unsqueeze(2).broadcast_to([128, 8, 128]), [tt("cact")], [tt("cbc")])
            act(MISC[:, M_E:M_E + 8], MISC[:, M_WD:M_WD + 8], AF.Exp, [tt("wd")], [tt("e")], scale=-1.0)
            act(MISC[:, M_E:M_E + 8], MISC[:, M_E:M_E + 8], AF.Ln, [tt("e")], [tt("e")], bias=1.0)
            tsc("dve", MISC[:, M_LG:M_LG + 8], MISC[:, M_E:M_E + 8], -1.0, None, MUL, None, [tt("e")], [tt("lg")])
            LGF = lambda h, lo=0, hi=128: MISC[lo:hi, M_LG + h:M_LG + h + 1]
            LGB = lambda h, lo=0, hi=128: MISC[lo:hi, M_LG + 4 + h:M_LG + 5 + h]
            S.add("pool", lambda e: e.iota(IO[:, 0, :], pattern=[[1, 128]], base=0, channel_multiplier=-1,
                                           allow_small_or_imprecise_dtypes=True), writes=[tt("io0")])
            S.add("pool", lambda e: e.iota(IO[:, 1, :], pattern=[[1, 128]], base=1, channel_multiplier=0,
                                           allow_small_or_imprecise_dtypes=True), writes=[tt("io1")])
            S.add("pool", lambda e: e.iota(IO[:, 2, :], pattern=[[-1, 128]], base=128, channel_multiplier=0,
                                           allow_small_or_imprecise_dtypes=True), writes=[tt("io2")])
            S.add("pool", lambda e: e.iota(MISC[:, M_IOP:M_IOP + 1], pattern=[[1, 1]], base=0, channel_multiplier=1,
                                           allow_small_or_imprecise_dtypes=True), writes=[tt("iop")])
            S.add("pool", lambda e: e.iota(MISC[:, M_IOPR:M_IOPR + 1], pattern=[[1, 1]], base=127, channel_multiplier=-1,
                                           allow_small_or_imprecise_dtypes=True), writes=[tt("iopr")])
            NEG = RRS[:, 0, 0:128]
            POS = RRS[:, 0, 128:256]
            tsc("dve", POS, IO[:, 0, :], 0.0, None, MAX, None, [tt("io0")], [tt("pos")])
            tten("dve", NEG, POS, IO[:, 0, :], ALU.subtract, [tt("pos"), tt("io0")], [tt("neg")])
            for h in range(4):
                tsc("dve", TMPD, POS, LGF(h), None, MUL, None, [tt("pos"), tt("lg")], [tt("tmpd")])
                stt("dve", TMPD, NEG, LGB(h), TMPD, MUL, ADD, [tt("neg"), tt("lg"), tt("tmpd")], [tt("tmpd")])
                act(DMK[:, h * 128:(h + 1) * 128], TMPD, AF.Exp, [tt("tmpd")], [tt("dmk")])
                tsc("dve", TMPD[0:64, :], IO[0:64, 1, :], LGF(h, 0, 64), None, MUL, None, [tt("io1"), tt("lg"), tt("dmk")], [tt("tmpd")])
                tsc("dve", TMPD[64:128, :], IO[64:128, 2, :], LGB(h, 64, 128), None, MUL, None, [tt("io2"), tt("lg"), tt("tmpd")], [tt("tmpd")])
                act(QDEC[:, h * 128:(h + 1) * 128], TMPD, AF.Exp, [tt("tmpd")], [tt("qdec")])
            KDV = MISC[:, M_KDV:M_KDV + 8].rearrange("p (h d) -> p h d", d=2)
            tsc("dve", MISC[:, M_KD8:M_KD8 + 4], MISC[:, M_LG:M_LG + 4], MISC[:, M_IOPR:M_IOPR + 1], None, MUL, None,
                [tt("lg"), tt("iopr")], [tt("kd8")])
            tsc("dve", MISC[:, M_KD8 + 4:M_KD8 + 8], MISC[:, M_LG + 4:M_LG + 8], MISC[:, M_IOP:M_IOP + 1], None, MUL, None,
                [tt("lg"), tt("iop"), tt("kd8")], [tt("kd8")])
            act(KDV[:, :, 0], MISC[:, M_KD8:M_KD8 + 4], AF.Exp, [tt("kd8")], [tt("kdv")])
            act(KDV[:, :, 1], MISC[:, M_KD8 + 4:M_KD8 + 8], AF.Exp, [tt("kd8"), tt("kdv")], [tt("kdv")])
            act(MISC[:, M_DC:M_DC + 8], MISC[:, M_LG:M_LG + 8], AF.Exp, [tt("lg")], [tt("dc")], scale=128.0)
            tsc("dve", MISC[:, M_GK8:M_GK8 + 1], SMALL[:, S_KG:S_KG + 1], 8.0, None, MUL, None, [tt("small")], [tt("gk8")])
            S.add("pool", lambda e: e.memset(SST[:], 0.0), writes=[tt("sst")])
            S.add("pool", lambda e: e.memset(VX[:], 1.0), writes=[tt("vx%d" % g) for g in range(NG)])

            ckpt('setup0')
            pmod = bank(6)[:, 0:16]
            for blk in range(4):
                wb, twb = ws_get(BLK_ADA + blk)
                wbv = wb.rearrange("p (k c) -> p k c", k=8)
                for c in range(4):
                    for kc in range(8):
                        mm(pmod[:, blk * 4 + c:blk * 4 + c + 1], wbv[:, kc, c * 128:(c + 1) * 128], CBF[:, kc:kc + 1],
                           kc == 0, kc == 7, [twb, tt("cbf")], [TB[6]])
            tten("dve", MISC[:, M_TMP8:M_TMP8 + 8], pmod[:, 8:16], SMALL[:, S_BSC:S_BSC + 8], ADD, [TB[6], tt("small")], [tt("tmp8")])
            stt("dve", MISC[:, M_ACOL:M_ACOL + 8], MISC[:, M_TMP8:M_TMP8 + 8], 1.0, SMALL[:, S_GPRE:S_GPRE + 8], ADD, MUL,
                [tt("tmp8"), tt("small")], [tt("acol")])
            tten("dve", MISC[:, M_SHCOL:M_SHCOL + 8], pmod[:, 0:8], SMALL[:, S_BSH:S_BSH + 8], ADD, [TB[6], tt("small")], [tt("shcol")])
            for hf in range(2):
                wb, twb = ws_get(BLK_ADA + 4 + hf)
                wbv = wb.rearrange("p (k c) -> p k c", k=8)
                for kc in range(8):
                    mm(bank(4 + hf), CBC[:, kc, :], wbv[:, kc, :], kc == 0, kc == 7, [twb, tt("cbc")], [TB[4 + hf]])
                tten("dve", GG[:, hf * 512:(hf + 1) * 512], bank(4 + hf), GG[:, hf * 512:(hf + 1) * 512], ADD,
                     [TB[4 + hf], tt("gg")], [tt("gg")])
                tten("dve", GG[:, hf * 512:(hf + 1) * 512], GG[:, hf * 512:(hf + 1) * 512], XT[:, 0, hf * 512:(hf + 1) * 512], MUL,
                     [tt("gg"), tt("xt0")], [tt("gg")])

            ckpt('ada')
            xt_rr = {"i": 0}

            def load_x(tile):
                i = xt_rr["i"] % NXT
                xt_rr["i"] += 1
                dma("sp", XT[:, i, :], x_d[tile * 128:(tile + 1) * 128, :], [], [tt("xt%d" % i)])
                return i

            cur = {"ht": HT, "htT": tt("ht")}

            hcfg = {"banks": [0, 1, 2, 3]}

            def _hreg(kc, jj, ntp):
                hb = hcfg["banks"]
                per = 8 // len(hb)
                bk = hb[kc // per]
                off = (kc % per) * (ntp * 128) + jj * 128
                return bk, off

            def hT_tile(tg, j):
                hb = hcfg["banks"]
                ntp = len(hb)
                tile = tg * 4 + j
                xi = load_x(tile)
                xb = j % 2
                act(JUNK, XT[:, xi, :], AF.Square, [tt("xt%d" % xi)], [tt("junk"), tt("ss")], accum_out=MISC[:, M_SS:M_SS + 1])
                act(MISC[:, M_RSTD:M_RSTD + 1], MISC[:, M_SS:M_SS + 1], AF.Ln, [tt("ss")], [tt("rstd")], scale=1.0 / DM, bias=EPS)
                act(MISC[:, M_RSTD:M_RSTD + 1], MISC[:, M_RSTD:M_RSTD + 1], AF.Exp, [tt("rstd")], [tt("rstd")], scale=-0.5)
                tsc("dve", XN[:, xb, :], XT[:, xi, :], MISC[:, M_RSTD:M_RSTD + 1], None, MUL, None,
                    [tt("xt%d" % xi), tt("rstd")], [tt("xn%d" % xb)])
                for kc in range(8):
                    bk, off = _hreg(kc, j % ntp, ntp)
                    trp(bank(bk).bitcast(BF16)[:, off:off + 128], XN[:, xb, kc * 128:(kc + 1) * 128], [tt("xn%d" % xb)], [TB[bk]])

            def hT_evac(dst, dstT, ps=0):
                hb = hcfg["banks"]
                ntp = len(hb)
                for kc in range(8):
                    bk, off = _hreg(kc, 0, ntp)
                    pb = bank(bk).bitcast(BF16)[:, off:off + ntp * 128]
                    o = dst[:, kc, ps * ntp * 128:(ps + 1) * ntp * 128]
                    if kc < 4:
                        tsc("dve", o, pb, MISC[:, M_ACOL + kc:M_ACOL + kc + 1], MISC[:, M_SHCOL + kc:M_SHCOL + kc + 1],
                            MUL, ADD, [TB[bk], tt("acol"), tt("shcol")], [dstT])
                    else:
                        act(o, pb, AF.Identity, [TB[bk], tt("acol"), tt("shcol")], [dstT],
                            scale=MISC[:, M_ACOL + kc:M_ACOL + kc + 1], bias=MISC[:, M_SHCOL + kc:M_SHCOL + kc + 1])

            def make_hT(tg, set_cur=True):
                if set_cur:
                    cur["ht"], cur["htT"] = bufs[tg % 2]
                if os.environ.get("MK_HT67", "1") == "1":
                    hcfg["banks"] = [6, 7]
                ntp = len(hcfg["banks"])
                for ps in range(4 // ntp):
                    for jj in range(ntp):
                        hT_tile(tg, ps * ntp + jj)
                    hT_evac(bufs[tg % 2][0], bufs[tg % 2][1], ps)

            def load_cs(tg):
                dma("sp", CS[:, 0, :], rope_d[0][:, tg * 512:(tg + 1) * 512], [], [tt("cs")])
                dma("sp", CS[:, 1, :], rope_d[1][:, tg * 512:(tg + 1) * 512], [], [tt("cs")])

            def proj_fm(wbv, twb, c0, bk):
                for kc in range(8):
                    mm(bank(bk), wbv[:, kc, c0:c0 + 128], cur["ht"][:, kc, :], kc == 0, kc == 7, [twb, cur["htT"]], [TB[bk]])

            def proj_tm(wbv, twb, j, c0, ncol, o, tbk):
                for kc in range(8):
                    mm(o, cur["ht"][:, kc, j * 128:(j + 1) * 128], wbv[:, kc, c0:c0 + ncol], kc == 0, kc == 7, [twb, cur["htT"]], [tbk])

            rope_ctr = {"i": 0, "alt": [(4, 5)]}

            def rope(bkA, gcol, gT, rms, out_ap, out_T):
                i = rope_ctr["i"] % 2
                rope_ctr["i"] += 1
                bS, bC = rope_ctr["alt"][i % len(rope_ctr["alt"])]
                if os.environ.get("MK_ROPE67", "1") == "1" and len(rope_ctr["alt"]) == 2:
                    bS = bC = 13 - bkA
                A = bank(bkA)
                rdg = [gT] if gT is not None else []
                gsc = gcol if gT is not None else float(gcol)
                tsc("dve", QBF[:, i, :], A, gsc, None, MUL, None, [TB[bkA]] + rdg, [tt("qbf%d" % i)])
                if rms:
                    act(SQ[:, i, :], A, AF.Square, [TB[bkA]], [tt("sq%d" % i)])
                mm(bank(bS), PERM[:], QBF[:, i, :], True, True, [tt("perm"), tt("qbf%d" % i)], [TB[bS]])
                stt("dve", T1[:, i, :], A, gsc, CS[:, 0, :], MUL, MUL, [TB[bkA], tt("cs")] + rdg, [tt("t1_%d" % i)])
                tten("dve", T2[:, i, :], bank(bS), CS[:, 1, :], MUL, [TB[bS], tt("cs")], [tt("t2_%d" % i)])
                if rms:
                    mm(bank(bC), ONESB[:], SQ[:, i, :], True, True, [tt("onesb"), tt("sq%d" % i)], [TB[bC]])
                    act(RRS[:, i, :], bank(bC), AF.Ln, [TB[bC]], [tt("rrs%d" % i)], bias=64.0 * EPS)
                    act(RRS[:, i, :], RRS[:, i, :], AF.Exp, [tt("rrs%d" % i)], [tt("rrs%d" % i)], scale=-0.5)
                    tten(os.environ.get("MK_RA", "pool"), T1[:, i, :], T1[:, i, :], T2[:, i, :], ADD, [tt("t1_%d" % i), tt("t2_%d" % i)], [tt("t1_%d" % i)])
                    tten(os.environ.get("MK_RM", "pool"), out_ap, T1[:, i, :], RRS[:, i, :], MUL, [tt("t1_%d" % i), tt("rrs%d" % i)], [out_T])
                else:
                    tten(os.environ.get("MK_RP", "pool"), out_ap, T1[:, i, :], T2[:, i, :], ADD, [tt("t1_%d" % i), tt("t2_%d" % i)], [out_T])

            def ret_rk(tg):
                wb, twb = ws_get(BLK_RK)
                wbv = wb.rearrange("p (k c) -> p k c", k=8)
                for pr in range(2):
                    proj_fm(wbv, twb, pr * 128, 6 + pr)
                    rope(6 + pr, 0.125, None, False, RKT[:, pr, :], tt("rkt"))

            def ret_rv(tg):
                wb, twb = ws_get(BLK_RV)
                wbv = wb.rearrange("p (k c) -> p k c", k=8)
                for j in range(4):
                    bk = 6 + (j % 2)
                    proj_tm(wbv, twb, j, 0, 512, bank(bk), TB[bk])
                    act(RVG[:, j, :], bank(bk), AF.Copy, [TB[bk]], [tt("rvg")])

            def ret_kv_inputs(tg):
                ret_rk(tg)
                ret_rv(tg)

            def kv_matmul(j):
                ptk = bank(6).bitcast(BF16)[:, 0:256]
                for pr in range(2):
                    trp(ptk[:, pr * 128:(pr + 1) * 128], RKT[:, pr, j * 128:(j + 1) * 128], [tt("rkt")], [TB[6]])
                kb = j % 2
                tten("dve", KK[:, kb, :].rearrange("p (h d c) -> p h d c", h=4, d=2),
                     ptk.rearrange("p (h c) -> p h c", h=4).unsqueeze(2).broadcast_to([128, 4, 2, 64]),
                     MISC[:, M_KDV:M_KDV + 8].rearrange("p (h d) -> p h d", d=2).unsqueeze(3).broadcast_to([128, 4, 2, 64]), MUL,
                     [TB[6], tt("kdv")], [tt("kk%d" % kb)])
                for h in range(4):
                    mm(bank(7)[:, h * 128:(h + 1) * 128], KK[:, kb, h * 128:(h + 1) * 128], RVG[:, j, h * 128:(h + 1) * 128],
                       True, True, [tt("kk%d" % kb), tt("rvg")], [TB[7]])

            bufs = [(HT, tt("ht")), (HT2, tt("ht2"))]
            for j in range(4):
                hT_tile(0, j)
            hT_evac(*bufs[0])
            for tg in range(NG):
                cur["ht"], cur["htT"] = bufs[tg % 2]
                nxt = bufs[(tg + 1) % 2] if tg + 1 < NG else None
                ckpt('ht%d' % tg)
                load_cs(tg)
                wb, twb = ws_get(BLK_P2KV)
                wbv = wb.rearrange("p (k c) -> p k c", k=8)
                for g in range(2):
                    proj_fm(wbv, twb, g * 128, 6 + g)
                    rope(6 + g, MISC[:, M_GK8:M_GK8 + 1], tt("gk8"), True, K2[:, g, tg * 512:(tg + 1) * 512], tt("k2_%d_%d" % (g, tg)))
                ckpt('k%d' % tg)
                if nxt:
                    hT_tile(tg + 1, 0)
                for j in range(4):
                    proj_tm(wbv, twb, j, 256, 128, bank(6)[:, j * 128:(j + 1) * 128], TB[6])
                cpy("dve", VX[:, tg * 4:(tg + 1) * 4, 64:320].rearrange("p j (g c) -> p j g c", g=2)[:, :, :, 0:64],
                    bank(6).rearrange("p (j g c) -> p j g c", j=4, g=2), [TB[6]], [tt("vx%d" % tg)])
                ckpt('v%d' % tg)
                if nxt:
                    hT_tile(tg + 1, 1)
                ret_rk(tg)
                if nxt:
                    hT_tile(tg + 1, 2)
                ret_rv(tg)
                ckpt('rkv%d' % tg)
                if nxt:
                    hT_tile(tg + 1, 3)
                    hT_evac(*nxt)
                for j in range(4):
                    ci = tg * 4 + j
                    kv_matmul(j)
                    act(RF[(ci % 2) * 64:(ci % 2) * 64 + 64, ci // 2, :], SST[0:64, :], AF.Copy, [tt("sst")], [tt("rf")])
                    tten("dve", SST[0:64, :].rearrange("p (h c) -> p h c", h=4), SST[0:64, :].rearrange("p (h c) -> p h c", h=4),
                         MISC[0:64, M_DC:M_DC + 4].unsqueeze(2).broadcast_to([64, 4, 128]), MUL, [tt("sst"), tt("dc")], [tt("sst")])
                    tten("dve", SST[0:64, :], SST[0:64, :], bank(7)[0:64, :], ADD, [tt("sst"), TB[7]], [tt("sst")])

            rope_ctr["alt"] = [(4, 5), (2, 3)]
            ckpt('p2')
            for tg in range(NG - 1, -1, -1):
                EARLY = os.environ.get("MK_EARLY", "3")
                if tg == NG - 1 or EARLY != "0":
                    cur["ht"], cur["htT"] = bufs[tg % 2]
                else:
                    make_hT(tg)
                load_cs(tg)
                wq, twq = ws_get(BLK_Q)
                wqv = wq.rearrange("p (k c) -> p k c", k=8)
                for p in range(4):
                    proj_fm(wqv, twq, p * 128, 6 + (p % 2))
                    rope(6 + (p % 2), SMALL[:, S_QG:S_QG + 1], tt("small"), True, QT[:, p, :], tt("qt%d" % p))
                wz, twz = ws_get(BLK_ZA)
                wzv = wz.rearrange("p (k c) -> p k c", k=8)
                for p in range(4):
                    bk = 6 + (p % 2)
                    proj_fm(wzv, twz, p * 128, bk)
                    act(ZA[:, p, :], bank(bk), AF.Silu, [TB[bk]], [tt("za%d" % p)])
                ckpt('q%d' % tg)
                for p in range(4):
                    g = p // 2

                    def qk(kc):
                        si = kc % 2
                        mm(bank(2 * si), K2[0:64, g, kc * 128:(kc + 1) * 128], QT[0:64, p, :], True, True,
                           [tt("k2_%d_%d" % (g, kc // 4)), tt("qt%d" % p)], [TB[2 * si]])
                        mm(bank(2 * si + 1), K2[64:128, g, kc * 128:(kc + 1) * 128], QT[64:128, p, :], True, True,
                           [tt("k2_%d_%d" % (g, kc // 4)), tt("qt%d" % p)], [TB[2 * si + 1]])

                    def pv(kc):
                        pi = kc % NPT
                        mm(bank(4), VX[:, kc, 64 + g * 128:192 + g * 128], PT[:, pi, 0:512], kc == 0, kc == 31,
                           [tt("vx%d" % (kc // 4)), tt("pt%d" % pi)], [TB[4]])
                        mm(bank(5), VX[:, kc, g * 128:g * 128 + 128], PT[:, pi, 512:1024], kc == 0, kc == 31,
                           [tt("vx%d" % (kc // 4)), tt("pt%d" % pi)], [TB[5]])

                    qk(0)
                    qk(1)
                    for kc in range(32):
                        si = kc % 2
                        act(PT[:, kc % NPT, :], bank(2 * si, 2), AF.Exp, [TB[2 * si], TB[2 * si + 1]], [tt("pt%d" % (kc % NPT))])
                        if kc + 2 < 32:
                            qk(kc + 2)
                        pv(kc)
                    cpy("dve", SA[:], bank(4), [TB[4]], [tt("sa")])
                    cpy("dve", SB_[:], bank(5), [TB[5]], [tt("sb")])
                    recip(DEN[0:64, :], SA[64:128, :], [tt("sa")], [tt("den")])
                    recip(DEN[64:128, :], SB_[0:64, :], [tt("sb")], [tt("den")])
                    tten("pool", DEN[0:64, :], SA[0:64, :], DEN[0:64, :], MUL, [tt("sa"), tt("den")], [tt("den")])
                    tten("pool", DEN[64:128, :], SB_[64:128, :], DEN[64:128, :], MUL, [tt("sb"), tt("den")], [tt("den")])
                    tten("pool", YA[:, p, :], DEN[:], ZA[:, p, :], MUL, [tt("den"), tt("za%d" % p)], [tt("ya")])
                ckpt('att%d' % tg)
                if EARLY == '1' and tg > 0:
                    make_hT(tg - 1, set_cur=False)
                wr, twr = ws_get(BLK_RQ)
                wrv = wr.rearrange("p (k c) -> p k c", k=8)
                for h in range(4):
                    proj_fm(wrv, twr, h * 128, 6 + (h % 2))
                    rope(6 + (h % 2), 1.0, None, False, RQ[:, h, :], tt("rq"))
                    tten("pool", QD[:, h, :].rearrange("p (j c) -> p j c", j=4), RQ[:, h, :].rearrange("p (j c) -> p j c", j=4),
                         QDEC[:, h * 128:(h + 1) * 128].unsqueeze(1).broadcast_to([128, 4, 128]), MUL, [tt("rq"), tt("qdec")], [tt("qd")])
                ckpt('rqd%d' % tg)
                if tg != NG - 1:
                    ret_kv_inputs(tg)
                wz, twz = ws_get(BLK_ZR)
                wzv = wz.rearrange("p (k c) -> p k c", k=8)
                for h in range(4):
                    bk = 6 + (h % 2)
                    proj_fm(wzv, twz, h * 128, bk)
                    act(ZR[:, h, :], bank(bk), AF.Silu, [TB[bk]], [tt("zr")])
                ckpt('rin%d' % tg)
                for j in range(3, -1, -1):
                    ci = tg * 4 + j
                    sb_i = j % 2
                    for h in range(4):
                        r0 = (h % 2) * 64
                        mm(bank(h % 2)[:, (h // 2) * 128:(h // 2 + 1) * 128], RKT[r0:r0 + 64, h // 2, j * 128:(j + 1) * 128],
                           RQ[r0:r0 + 64, h, j * 128:(j + 1) * 128], True, True, [tt("rkt"), tt("rq")], [TB[h % 2]])
                    for hb in range(2):
                        tten("dve", SM[:, sb_i, :].rearrange("p (a b c) -> p a b c", a=2, b=2)[:, :, hb, :],
                             bank(hb)[:, 0:256].rearrange("p (a c) -> p a c", a=2),
                             DMK[:].rearrange("p (a b c) -> p a b c", a=2, b=2)[:, :, hb, :], MUL, [TB[hb], tt("dmk")], [tt("sm%d" % sb_i)])
                    ckpt('rs%d' % ci)
                    act(RRO[0:64, sb_i, :], RF[(ci % 2) * 64:(ci % 2) * 64 + 64, ci // 2, :], AF.Copy, [tt("rf")], [tt("rro%d" % sb_i)])
                    cpy("pool", RRO[64:128, sb_i, :], SST[64:128, :], [tt("sst")], [tt("rro%d" % sb_i)])
                    for h in range(4):
                        mm(bank(2)[:, h * 128:(h + 1) * 128], SM[:, sb_i, h * 128:(h + 1) * 128], RVG[:, j, h * 128:(h + 1) * 128],
                           True, False, [tt("sm%d" % sb_i), tt("rvg")], [TB[2]])
                        mm(bank(2)[:, h * 128:(h + 1) * 128], QD[:, h, j * 128:(j + 1) * 128], RRO[:, sb_i, h * 128:(h + 1) * 128],
                           False, True, [tt("qd"), tt("rro%d" % sb_i)], [TB[2]])
                    ckpt('ro%d' % ci)
                    for h in range(4):
                        bnstats(STAT[:, sb_i, h, 0:6], bank(2)[:, h * 128:(h + 1) * 128], [TB[2]], [tt("stat%d" % sb_i)])
                        bnaggr(STAT[:, sb_i, h, 6:8], STAT[:, sb_i, h, 0:6], [tt("stat%d" % sb_i)], [tt("stat%d" % sb_i)])
                    gr = MISC[:, M_GNR + sb_i * 4:M_GNR + sb_i * 4 + 4]
                    gn = MISC[:, M_GNN + sb_i * 4:M_GNN + sb_i * 4 + 4]
                    act(gr, STAT[:, sb_i, :, 7], AF.Ln, [tt("stat%d" % sb_i)], [tt("gr%d" % sb_i)], bias=EPS)
                    act(gr, gr, AF.Exp, [tt("gr%d" % sb_i)], [tt("gr%d" % sb_i)], scale=-0.5)
                    stt("dve", gn, STAT[:, sb_i, :, 6], -1.0, gr, MUL, MUL, [tt("stat%d" % sb_i), tt("gr%d" % sb_i)], [tt("gn%d" % sb_i)])
                    for h in range(4):
                        tsc("dve", ON[:, sb_i, h * 128:(h + 1) * 128], bank(2)[:, h * 128:(h + 1) * 128],
                            MISC[:, M_GNR + sb_i * 4 + h:M_GNR + sb_i * 4 + h + 1], MISC[:, M_GNN + sb_i * 4 + h:M_GNN + sb_i * 4 + h + 1],
                            MUL, ADD, [TB[2], tt("gr%d" % sb_i), tt("gn%d" % sb_i)], [tt("on%d" % sb_i)])
                    ckpt('rgn%d' % ci)
                    ptn = bank(3).bitcast(BF16)[:, 0:512]
                    for h in range(4):
                        trp(ptn[:, h * 128:(h + 1) * 128], ON[:, sb_i, h * 128:(h + 1) * 128], [tt("on%d" % sb_i)], [TB[3]])
                    for h in range(4):
                        stt("dve", YR[:, h, j * 128:(j + 1) * 128], ptn[:, h * 128:(h + 1) * 128], SMALL[:, S_GN + h:S_GN + h + 1],
                            ZR[:, h, j * 128:(j + 1) * 128], MUL, MUL, [TB[3], tt("small"), tt("zr")], [tt("yr")])
                    ckpt('ryr%d' % ci)
                    kv_matmul(j)
                    tten("dve", SST[64:128, :].rearrange("p (h c) -> p h c", h=4), SST[64:128, :].rearrange("p (h c) -> p h c", h=4),
                         MISC[64:128, M_DC + 4:M_DC + 8].unsqueeze(2).broadcast_to([64, 4, 128]), MUL, [tt("sst"), tt("dc")], [tt("sst")])
                    tten("dve", SST[64:128, :], SST[64:128, :], bank(7)[64:128, :], ADD, [tt("sst"), TB[7]], [tt("sst")])
                ckpt('ret%d' % tg)
                if EARLY == '2' and tg > 0:
                    make_hT(tg - 1, set_cur=False)
                for n in range(8):
                    wm, twm = ws_get(BLK_M + n)
                    glv = wm[:, 0:2048].rearrange("p (k c) -> p k c", k=8)
                    pav = wm[:, 2048:2560].rearrange("p (k c) -> p k c", k=4)
                    prv = wm[:, 2560:3072].rearrange("p (k c) -> p k c", k=4)
                    gi = n % NGT
                    b0 = 4 * (n % 2)
                    for br in range(2):
                        bk = b0 + 2 + br
                        for kc in range(8):
                            mm(bank(bk), glv[:, kc, br * 128:(br + 1) * 128], cur["ht"][:, kc, :], kc == 0, kc == 7, [twm, cur["htT"]], [TB[bk]])
                        act(GATE[:, gi, br, :], bank(bk), AF.Sigmoid, [TB[bk]], [tt("gate%d_%d" % (gi, br))])
                    for cc in range(4):
                        mm(bank(b0), pav[:, cc, :], YA[:, cc, :], cc == 0, cc == 3, [twm, tt("ya")], [TB[b0]])
                    for cc in range(4):
                        mm(bank(b0 + 1), prv[:, cc, :], YR[:, cc, :], cc == 0, cc == 3, [twm, tt("yr")], [TB[b0 + 1]])
                    tten("dve", MT[:, 0, :], bank(b0), GATE[:, gi, 0, :], MUL, [TB[b0], tt("gate%d_0" % gi)], [tt("mt0")])
                    tten("dve", MT[:, 1, :], bank(b0 + 1), GATE[:, gi, 1, :], MUL, [TB[b0 + 1], tt("gate%d_1" % gi)], [tt("mt1")])
                    tten("pool", MG[:, n, :], MT[:, 0, :], MT[:, 1, :], ADD, [tt("mt0"), tt("mt1")], [tt("mg")])
                ckpt('mrg%d' % tg)
                if EARLY == '3' and tg > 0:
                    make_hT(tg - 1, set_cur=False)
                wo0, two0 = ws_get(BLK_WO)
                wo1, two1 = ws_get(BLK_WO + 1, la=NWB - 1)
                wov = [wo0.rearrange("p (k c) -> p k c", k=8), wo1.rearrange("p (k c) -> p k c", k=8)]
                twos = [two0, two1]
                for j in range(4):
                    tile = tg * 4 + j
                    pb0 = 2 * j
                    for hf in range(2):
                        for kc in range(8):
                            mm(bank(pb0 + hf), MG[:, kc, j * 128:(j + 1) * 128], wov[hf][:, kc, :], kc == 0, kc == 7,
                               [tt("mg"), twos[hf]], [TB[pb0 + hf]])
                    xi = load_x(tile)
                    act(JUNK, bank(pb0, 2), AF.Square, [TB[pb0], TB[pb0 + 1]], [tt("junk"), tt("ss2")], accum_out=MISC[:, M_SS + 1:M_SS + 2])
                    act(MISC[:, M_RSTD + 1:M_RSTD + 2], MISC[:, M_SS + 1:M_SS + 2], AF.Ln, [tt("ss2")], [tt("rstd2")], scale=1.0 / DM, bias=EPS)
                    act(MISC[:, M_RSTD + 1:M_RSTD + 2], MISC[:, M_RSTD + 1:M_RSTD + 2], AF.Exp, [tt("rstd2")], [tt("rstd2")], scale=-0.5)
                    stt("dve", bank(pb0, 2), bank(pb0, 2), MISC[:, M_RSTD + 1:M_RSTD + 2], GG[:], MUL, MUL,
                        [TB[pb0], TB[pb0 + 1], tt("rstd2"), tt("gg")], [TB[pb0], TB[pb0 + 1]])
                    tten("dve", XT[:, xi, :], bank(pb0, 2), XT[:, xi, :], ADD, [TB[pb0], TB[pb0 + 1], tt("xt%d" % xi)], [tt("xt%d" % xi)])
                    dma(os.environ.get("MK_STQ", "sp"), out_d[tile * 128:(tile + 1) * 128, :], XT[:, xi, :], [tt("xt%d" % xi)], [])
        try:
            run_all()
        except _Stop:
            pass
        dump_aps = {"HT": HT, "K2": K2, "VX": VX, "RF": RF, "MISC": MISC, "GG": GG, "DMK": DMK, "QDEC": QDEC,
                    "YA": YA, "YR": YR, "MG": MG, "QT": QT, "ZA": ZA, "RKT": RKT, "RVG": RVG, "SST": SST,
                    "RQ": RQ, "QD": QD, "ZR": ZR, "XN": XN, "CS": CS}
        for nm in dumps:
            src = dump_aps[nm]
            shp = list(src.shape)
            flat = int(np.prod(shp[1:]))
            dd = nc.dram_tensor("dbg_" + nm, [128, flat], src.dtype, kind="ExternalOutput").ap()
            allT = list(tdict.values()) + TB
            sv = src[:] if len(shp) == 2 else src[:].rearrange("p a b -> p (a b)") if len(shp) == 3 else src[:].rearrange("p a b c -> p (a b c)")
            dma("sp", dd, sv, allT, [])
        import os
        if os.environ.get('MK_RESCHED', '1') == '1':
            S.reschedule()
        if not _EXP:
            S.emit(nc)
    return nc, S


_CACHE = {}


def kernel(x, c, w_ada, b_ada, g_pre, w_in, qn_g, kn_g, w_dec_f, w_dec_b, gn_g, w_pa, w_pr, w_out, g_post):
    x = np.asarray(x, np.float32)
    c = np.asarray(c, np.float32)
    f = lambda a: np.asarray(a, np.float32)[0]
    w_ada, b_ada, g_pre, w_in, qn_g, kn_g = f(w_ada), f(b_ada), f(g_pre), f(w_in), f(qn_g), f(kn_g)
    w_dec_f, w_dec_b, gn_g, w_pa, w_pr, w_out, g_post = f(w_dec_f), f(w_dec_b), f(gn_g), f(w_pa), f(w_pr), f(w_out), f(g_post)
    if "nc" not in _CACHE:
        _CACHE["nc"] = build_program()[0]
        _CACHE["consts"] = _host_consts()
    nc = _CACHE["nc"]
    rope, cst = _CACHE["consts"]
    blocks = _host_blocks(w_ada, w_in, w_pa, w_pr, w_out)
    col8 = lambda v: np.ascontiguousarray(v.reshape(8, 128).T)
    rows = np.ascontiguousarray(np.stack([b_ada[2048:3072], g_post], 0))
    wdec = np.concatenate([w_dec_f, w_dec_b])[None, :].astype(np.float32)
    in_maps = []
    for b in range(8):
        small = np.zeros((128, NSMALL), np.float32)
        small[:, 0:8] = col8(c[b])
        small[:, 8:16] = col8(g_pre)
        small[:, 16:24] = col8(b_ada[1024:2048])
        small[:, 24:32] = col8(b_ada[0:1024])
        small[:, 32] = np.concatenate([qn_g, qn_g])
        small[:, 33] = np.concatenate([kn_g, kn_g])
        small[:, 34:38] = gn_g.reshape(4, 128).T
        in_maps.append({"x": np.ascontiguousarray(x[b]), "wsrc": blocks, "rope": rope, "cst": cst,
                        "small": small, "rows": rows, "wdec": wdec})
    res = run_bass_kernel_spmd(nc, in_maps, core_ids=list(range(8)))
    return np.stack([np.asarray(r["out"], np.float32) for r in res.results], 0)
```

```python
import numpy as np
import concourse.bass as bass
import concourse.mybir as mybir
from concourse.bass_utils import run_bass_kernel_spmd

F32 = mybir.dt.float32
BF16 = mybir.dt.bfloat16
AF = mybir.ActivationFunctionType
ALU = mybir.AluOpType
AX = mybir.AxisListType


class T:
    __slots__ = ("name", "last_w", "readers", "excl")

    def __init__(self, name="", excl=False):
        self.name = name
        self.last_w = None
        self.readers = []
        self.excl = excl


class Op:
    __slots__ = ("eng", "fn", "deps", "signal", "token", "is_dma", "idx", "gidx", "preds", "succs", "dur", "lat", "cls", "prio", "t_start", "t_done", "npred", "tag")

    def __init__(self, eng, fn, is_dma):
        self.eng = eng
        self.fn = fn
        self.deps = []
        self.signal = False
        self.token = None
        self.is_dma = is_dma
        self.preds = []
        self.succs = []
        self.dur = 0.2
        self.lat = 0.0
        self.cls = None


class Sched:
    ENGS = ("pe", "act", "dve", "pool", "sp")

    def __init__(self, n_dma_sems=8):
        self.ops = {e: [] for e in self.ENGS}
        self.n_dma_sems = n_dma_sems
        self.dma_rr = {e: 0 for e in self.ENGS}
        self.dma_last = {}
        self.gcount = 0

    def add(self, eng, fn, reads=(), writes=(), dma=False, dur=0.2, lat=0.0, cls=None):
        op = Op(eng, fn, dma)
        op.dur, op.lat, op.cls = dur, lat, cls
        import sys as _sys
        op.tag = (_sys._getframe(2).f_lineno, _sys._getframe(3).f_lineno)
        op.gidx = self.gcount
        self.gcount += 1
        writes = list(writes) + [t for t in reads if t.excl]
        reads = [t for t in reads if not t.excl]
        raw = []
        war = []
        for t in reads:
            if t.last_w is not None:
                raw.append(t.last_w)
        for t in writes:
            if t.last_w is not None:
                raw.append(t.last_w)
            war.extend(t.readers)
        for t in reads:
            t.readers.append(op)
        for t in writes:
            t.last_w = op
            t.readers = []
        if dma:
            k = self.dma_rr[eng]
            self.dma_rr[eng] = (k + 1) % self.n_dma_sems
            prev = self.dma_last.get((eng, k))
            if prev is not None:
                raw.append(prev)
            self.dma_last[(eng, k)] = op
            op.token = (eng, k)
        seen = set()
        pseen = set()
        for lst, is_war in ((raw, False), (war, True)):
            for d in lst:
                if d is op or id(d) in seen:
                    continue
                if id(d) not in pseen:
                    pseen.add(id(d))
                    op.preds.append(d)
                if (not d.is_dma) and (not dma) and d.eng == eng:
                    if eng == "pe" or is_war:
                        continue
                seen.add(id(d))
                op.deps.append(d)
        op.idx = len(self.ops[eng])
        self.ops[eng].append(op)
        return op

    def reschedule(self, hop=0.15, act_switch=1.4):
        import heapq
        allops = []
        for e in self.ENGS:
            allops.extend(self.ops[e])
        allops.sort(key=lambda o: o.gidx)
        for o in allops:
            o.succs = []
        for o in allops:
            o.npred = len(o.preds)
            for p in o.preds:
                p.succs.append(o)
        for o in reversed(allops):
            m = 0.0
            for s_ in o.succs:
                if s_.prio > m:
                    m = s_.prio
            o.prio = m + o.dur + o.lat
        ready = {e: [] for e in self.ENGS}
        for o in allops:
            o.t_done = None
            if o.npred == 0:
                ready[o.eng].append(o)
        free_at = {e: 0.0 for e in self.ENGS}
        cur_cls = {e: None for e in self.ENGS}
        new_order = {e: [] for e in self.ENGS}
        nleft = len(allops)

        def rtime(o):
            t = 0.0
            for p in o.preds:
                tp = p.t_done + (hop if (p.eng != o.eng or p.is_dma) else 0.0)
                if tp > t:
                    t = tp
            return t

        rt_cache = {}
        while nleft:
            best = None
            for e in self.ENGS:
                lst = ready[e]
                if not lst:
                    continue
                fa = free_at[e]
                for o in lst:
                    r = rt_cache.get(id(o))
                    if r is None:
                        r = rtime(o)
                        rt_cache[id(o)] = r
                    st = r if r > fa else fa
                    if e == "act" and o.cls is not None and cur_cls[e] is not None and o.cls != cur_cls[e]:
                        st += act_switch
                    key = (st, -o.prio)
                    if best is None or key < best[0]:
                        best = (key, o)
            (st, _), o = best
            e = o.eng
            ready[e].remove(o)
            o.t_start = st
            free_at[e] = st + o.dur
            o.t_done = st + o.dur + o.lat
            if e == "act" and o.cls is not None:
                cur_cls[e] = o.cls
            new_order[e].append(o)
            nleft -= 1
            for s_ in o.succs:
                s_.npred -= 1
                if s_.npred == 0:
                    ready[s_.eng].append(s_)
        for e in self.ENGS:
            self.ops[e] = new_order[e]
        self.makespan = max(free_at.values())

    def emit(self, nc, block_ctx_extra=None):
        for e in self.ENGS:
            for op in self.ops[e]:
                for d in op.deps:
                    d.signal = True
        import contextlib
        with contextlib.ExitStack() as st:
            sems = {e: st.enter_context(nc.semaphore("s_" + e)) for e in ("pe", "act", "dve", "pool")}
            dsems = {}
            for e in self.ENGS:
                if any(o.is_dma for o in self.ops[e]):
                    for k in range(self.n_dma_sems):
                        dsems[(e, k)] = st.enter_context(nc.semaphore("d_%s_%d" % (e, k)))
            for e in self.ENGS:
                cnt = 0
                dcnt = {}
                for op in self.ops[e]:
                    if op.is_dma:
                        key = op.token
                        dcnt[key] = dcnt.get(key, 0) + 16
                        op.token = (dsems[key], dcnt[key], key)
                    elif op.signal:
                        cnt += 1
                        op.token = (sems[e], cnt, e)
            block = st.enter_context(nc.Block())
            handles = {"pe": block.tensor, "act": block.scalar, "dve": block.vector, "pool": block.gpsimd,
                       "sp": block.sync}
            nwaits = {e: 0 for e in self.ENGS}

            def make(e):
                ops = self.ops[e]

                def body(eng):
                    known = {}
                    for op in ops:
                        need = {}
                        for d in op.deps:
                            sem, val, key = d.token
                            if known.get(key, 0) >= val:
                                continue
                            if key not in need or need[key][1] < val:
                                need[key] = (sem, val)
                        for key, (sem, val) in need.items():
                            eng.wait_ge(sem, val)
                            known[key] = val
                            nwaits[e] += 1
                        inst = op.fn(eng)
                        if op.is_dma:
                            inst.then_inc(op.token[0], 16)
                        elif op.signal:
                            inst.then_inc(op.token[0], 1)
                    fin = {}
                    for op in ops:
                        if op.is_dma:
                            fin[op.token[2]] = (op.token[0], op.token[1])
                    for key, (sem, val) in fin.items():
                        if known.get(key, 0) < val:
                            eng.wait_ge(sem, val)
                return body

            for e in self.ENGS:
                if self.ops[e]:
                    handles[e](make(e))
            self.nwaits = nwaits

import contextlib

SEQ = 4096
DM = 1024
NG = 8
EPS = 1e-6

BLK_ADA = 0
BLK_P2KV = 6
BLK_RK = 7
BLK_RV = 8
BLK_Q = 9
BLK_ZA = 10
BLK_RQ = 11
BLK_ZR = 12
BLK_M = 13
BLK_WO = 21
NBLK = 23
NSMALL = 40


def _host_blocks(w_ada, w_in, w_pa, w_pr, w_out):
    blocks = np.zeros((NBLK, 128, 4096), np.float32)

    def kmaj(w):
        nc_ = w.shape[1]
        t = np.zeros((128, 8, 512), np.float32)
        t[:, :, :nc_] = w.reshape(8, 128, nc_).transpose(1, 0, 2)
        return t.reshape(128, 4096)

    wa = w_ada
    for j in range(6):
        blocks[BLK_ADA + j] = kmaj(wa[:, j * 512:(j + 1) * 512])
    o_q, o_k, o_v, o_za, o_rq, o_rk, o_rv, o_zr, o_gl = 0, 512, 640, 768, 1280, 1536, 1792, 2304, 2816
    k0 = w_in[:, o_k:o_k + 64]
    k1 = w_in[:, o_k + 64:o_k + 128]
    blocks[BLK_P2KV] = kmaj(np.concatenate([k0, k0, k1, k1, w_in[:, o_v:o_v + 128]], axis=1))
    blocks[BLK_RK] = kmaj(w_in[:, o_rk:o_rk + 256])
    blocks[BLK_RV] = kmaj(w_in[:, o_rv:o_rv + 512])
    blocks[BLK_Q] = kmaj(w_in[:, o_q:o_q + 512])
    blocks[BLK_ZA] = kmaj(w_in[:, o_za:o_za + 512])
    rq = [w_in[:, o_rq + 64 * h:o_rq + 64 * (h + 1)] for h in range(4)]
    blocks[BLK_RQ] = kmaj(np.concatenate([rq[0], rq[0], rq[1], rq[1], rq[2], rq[2], rq[3], rq[3]], axis=1))
    blocks[BLK_ZR] = kmaj(w_in[:, o_zr:o_zr + 512])
    for n in range(8):
        t = np.zeros((128, 4096), np.float32)
        gl = np.concatenate([w_in[:, o_gl + n * 128:o_gl + (n + 1) * 128],
                             w_in[:, o_gl + 1024 + n * 128:o_gl + 1024 + (n + 1) * 128]], axis=1)
        t[:, 0:2048] = gl.reshape(8, 128, 256).transpose(1, 0, 2).reshape(128, 2048)
        t[:, 2048:2560] = w_pa[:, n * 128:(n + 1) * 128].reshape(4, 128, 128).transpose(1, 0, 2).reshape(128, 512)
        t[:, 2560:3072] = w_pr[:, n * 128:(n + 1) * 128].reshape(4, 128, 128).transpose(1, 0, 2).reshape(128, 512)
        blocks[BLK_M + n] = t
    for hf in range(2):
        blocks[BLK_WO + hf] = kmaj(w_out[:, hf * 512:(hf + 1) * 512])
    return blocks


def _host_consts():
    t = np.arange(SEQ)
    row = (t // 64).astype(np.float32)
    col = (t % 64).astype(np.float32)
    inv = (10000.0 ** (-np.arange(0, 32, 2, dtype=np.float32) / 32.0)).astype(np.float32)
    ang_r = row[None, :] * inv[:, None]
    ang_c = col[None, :] * inv[:, None]
    cos64 = np.concatenate([np.cos(ang_r), np.cos(ang_r), np.cos(ang_c), np.cos(ang_c)], 0)
    sin64 = np.concatenate([np.sin(ang_r), np.sin(ang_r), np.sin(ang_c), np.sin(ang_c)], 0)
    rope = np.stack([np.concatenate([cos64, cos64], 0), np.concatenate([sin64, sin64], 0)], 0).astype(np.float32)
    ident = np.eye(128, dtype=np.float32)
    perm = np.zeros((128, 128), np.float32)
    for hb in range(2):
        for d in range(64):
            blk = d // 16
            if blk % 2 == 0:
                p, s = d + 16, -1.0
            else:
                p, s = d - 16, 1.0
            perm[hb * 64 + p, hb * 64 + d] = s
    ones = np.zeros((128, 128), np.float32)
    ones[:64, :64] = 1.0
    ones[64:, 64:] = 1.0
    return rope, np.stack([ident, perm, ones], 0)


class _Stop(Exception):
    pass


def build_program(stop=None, dumps=()):
    nc = bass.Bass("TRN2", target_bir_lowering=False)
    x_d = nc.dram_tensor("x", [SEQ, DM], F32, kind="ExternalInput").ap()
    wsrc = nc.dram_tensor("wsrc", [NBLK, 128, 4096], F32, kind="ExternalInput").ap()
    rope_d = nc.dram_tensor("rope", [2, 128, SEQ], F32, kind="ExternalInput").ap()
    cst_d = nc.dram_tensor("cst", [3, 128, 128], F32, kind="ExternalInput").ap()
    small_d = nc.dram_tensor("small", [128, NSMALL], F32, kind="ExternalInput").ap()
    rows_d = nc.dram_tensor("rows", [2, DM], F32, kind="ExternalInput").ap()
    wdec_d = nc.dram_tensor("wdec", [1, 8], F32, kind="ExternalInput").ap()
    out_d = nc.dram_tensor("out", [SEQ, DM], F32, kind="ExternalOutput").ap()
    wbf_d = nc.dram_tensor("wbf", [NBLK, 128, 4096], BF16, kind="Internal").ap()

    S = Sched(n_dma_sems=12)
    tdict = {}

    import os
    _alias = {"io0": "sa", "io1": "sa", "io2": "sa", "tmpd": "sa", "cbc": "mt1", "junk": "mt0"}
    if os.environ.get("MK_JUNK") == "1":
        del _alias["junk"]

    def tt(name):
        name = _alias.get(name, name)
        if name not in tdict:
            tdict[name] = T(name)
        return tdict[name]

    with contextlib.ExitStack() as st:
        def sb(name, shape, dt):
            return st.enter_context(nc.sbuf_tensor(name, shape, dt))

        PS = st.enter_context(nc.psum_tensor("ps", [128, 4096], F32))
        TB = [T("bank%d" % i, excl=True) for i in range(8)]

        def bank(i, n=1):
            return PS[:, i * 512:(i + n) * 512]

        import os
        _EXP = os.environ.get("MK_EXPLORE") == "1"
        _PDT = mybir.dt.int8 if _EXP else BF16
        K2 = sb("K2", [128, 2, SEQ], _PDT)
        VX = sb("VX", [128, 32, 320], _PDT)
        RF = sb("RF", [128, 16, 512], _PDT)
        NWB = int(os.environ.get('MK_NWB', '3'))
        WB = sb("WB", [128, NWB, 4096], BF16)
        NXT = int(os.environ.get('MK_NXT', '3'))
        XT = sb("XT", [128, NXT, DM], F32)
        XN = sb("XN", [128, 2, DM], BF16)
        HT = sb("HT", [128, 8, 512], BF16)
        HT2 = sb("HT2", [128, 8, 512], BF16)
        CS = sb("CS", [128, 2, 512], F32)
        QBF = sb("QBF", [128, 2, 512], BF16)
        SQ = sb("SQ", [128, 2, 512], BF16)
        T1 = sb("T1", [128, 2, 512], F32)
        T2 = sb("T2", [128, 2, 512], F32)
        RRS = sb("RRS", [128, 2, 512], F32)
        QT = sb("QT", [128, 4, 512], BF16)
        ZA = sb("ZA", [128, 4, 512], BF16)
        NPT = int(os.environ.get('MK_NPT', '3'))
        PT = sb("PT", [128, NPT, 1024], BF16)
        SA = sb("SA", [128, 512], F32)
        SB_ = sb("SB_", [128, 512], F32)
        DEN = sb("DEN", [128, 512], F32)
        YA = sb("YA", [128, 4, 512], BF16)
        YR = sb("YR", [128, 4, 512], BF16)
        RQ = sb("RQ", [128, 4, 512], BF16)
        QD = sb("QD", [128, 4, 512], BF16)
        ZR = sb("ZR", [128, 4, 512], BF16)
        RKT = sb("RKT", [128, 2, 512], BF16)
        RVG = sb("RVG", [128, 4, 512], BF16)
        KK = sb("KK", [128, 2, 512], BF16)
        SM = sb("SM", [128, 2, 512], BF16)
        RRO = sb("RRO", [128, 2, 512], BF16)
        ON = sb("ON", [128, 2, 512], BF16)
        SST = sb("SST", [128, 512], F32)
        NGT = int(os.environ.get("MK_NGT", "1"))
        GATE = sb("GATE", [128, NGT, 2, 512], BF16)
        MT = sb("MT", [128, 2, 512], F32)
        MG = sb("MG", [128, 8, 512], BF16)
        IDENT = sb("IDENT", [128, 128], BF16)
        PERM = sb("PERM", [128, 128], BF16)
        ONESB = sb("ONESB", [128, 128], BF16)
        DMK = sb("DMK", [128, 512], F32)
        QDEC = sb("QDEC", [128, 512], F32)
        GG = sb("GG", [128, DM], F32)
        SMALL = sb("SMALL", [128, NSMALL], F32)
        MISC = sb("MISC", [128, 128], F32)
        IO = SA[:, 0:384].rearrange("p (a b) -> p a b", a=3)
        TMPD = SA[:, 384:512]
        CBC = MT[:, 1, :].bitcast(BF16).rearrange("p (a b) -> p a b", a=8)
        JUNK = sb("JUNKB", [128, DM], BF16)[:] if os.environ.get("MK_JUNK") == "1" else MT[:, 0, :].bitcast(BF16)
        CBF = sb("CBF", [128, 8], BF16)
        STAT = sb("STAT", [128, 2, 4, 8], F32)

        _CACHE['sbuf_left'] = nc.sbuf_bytes_remaining
        M_ACOL, M_SHCOL, M_LG, M_DC, M_WD, M_E, M_KD8, M_KDV, M_IOP, M_IOPR, M_GK8 = 0, 8, 16, 24, 32, 40, 48, 56, 64, 65, 66
        M_SS, M_RSTD, M_CACT, M_TMP8 = 70, 74, 80, 88
        M_GNR, M_GNN = 96, 104
        S_C, S_GPRE, S_BSC, S_BSH, S_QG, S_KG, S_GN = 0, 8, 16, 24, 32, 33, 34

        def fsz(ap):
            n = 1
            for d in list(ap.shape)[1:]:
                n *= int(d)
            return n

        def mm(o, l, r, start, stop, rd, wr):
            n = fsz(o)
            k = int(l.shape[0])
            d = 0.06 + n * 0.00041
            if k <= 64 and n >= 256:
                d = 0.17
            S.add("pe", lambda e: e.matmul(o, l, r, start=start, stop=stop), reads=rd, writes=wr, dur=d)

        def trp(o, i, rd, wr):
            S.add("pe", lambda e: e.transpose(o, i, IDENT[:]), reads=rd + [tt("ident")], writes=wr, dur=0.115)

        _ACLS = {AF.Exp: "exp", AF.Ln: "exp", AF.Silu: "sig", AF.Sigmoid: "sig"}

        def act(o, i, f, rd, wr, **kw):
            S.add("act", lambda e: e.activation(out=o, in_=i, func=f, **kw), reads=rd, writes=wr,
                  dur=0.22 + fsz(i) * 0.00083 + (0.1 if "accum_out" in kw else 0.0), cls=_ACLS.get(f))

        def _vd(eng, n):
            return (0.1 + n * 0.00104) if eng == "dve" else (0.15 + n * 0.0021)

        def tten(eng, o, a, b, op, rd, wr):
            S.add(eng, lambda e: e.tensor_tensor(out=o, in0=a, in1=b, op=op), reads=rd, writes=wr, dur=_vd(eng, fsz(o)))

        def tsc(eng, o, a, s1, s2, op0, op1, rd, wr):
            if op1 is None:
                S.add(eng, lambda e: e.tensor_scalar(out=o, in0=a, scalar1=s1, scalar2=None, op0=op0), reads=rd, writes=wr, dur=_vd(eng, fsz(o)))
            else:
                S.add(eng, lambda e: e.tensor_scalar(out=o, in0=a, scalar1=s1, scalar2=s2, op0=op0, op1=op1), reads=rd, writes=wr, dur=_vd(eng, fsz(o)))

        def stt(eng, o, a, s, b, op0, op1, rd, wr):
            S.add(eng, lambda e: e.scalar_tensor_tensor(out=o, in0=a, scalar=s, in1=b, op0=op0, op1=op1), reads=rd, writes=wr, dur=_vd(eng, fsz(o)))

        def cpy(eng, o, i, rd, wr):
            S.add(eng, lambda e: e.tensor_copy(out=o, in_=i), reads=rd, writes=wr, dur=_vd(eng, fsz(o)))

        def dma(eng, o, i, rd, wr):
            nbytes = fsz(o) * int(o.shape[0]) * (2 if o.dtype == BF16 else 4)
            S.add(eng, lambda e: e.dma_start(out=o, in_=i), reads=rd, writes=wr, dma=True,
                  dur=(0.08 if eng == "sp" else 1.0), lat=2.0 + nbytes / 150000.0)

        MUL, ADD, MAX = ALU.mult, ALU.add, ALU.max

        def ckpt(name):
            if stop == name:
                raise _Stop()

        def bnstats(o, i, rd, wr):
            S.add("dve", lambda e: e.bn_stats(out=o, in_=i), reads=rd, writes=wr, dur=0.25)

        def bnaggr(o, i, rd, wr):
            S.add("dve", lambda e: e.bn_aggr(out=o, in_=i), reads=rd, writes=wr, dur=0.15)

        def recip(o, i, rd, wr):
            S.add("dve", lambda e: e.reciprocal(out=o, in_=i), reads=rd, writes=wr, dur=0.1 + fsz(o) * 0.00625)

        order = [BLK_ADA + j for j in range(4)]
        for tg in range(NG):
            order += [BLK_P2KV, BLK_RK, BLK_RV]
        order += [BLK_ADA + 4, BLK_ADA + 5]
        for tg in range(NG):
            order += [BLK_Q, BLK_ZA, BLK_RQ] + ([BLK_RK, BLK_RV] if tg > 0 else []) + [BLK_ZR] \
                + [BLK_M + n for n in range(8)] + [BLK_WO, BLK_WO + 1]
        conv_done = set()
        ws_state = {"issued": 0, "cur": 0}

        def conv(blk):
            if blk in conv_done:
                return
            conv_done.add(blk)
            dma("pool", wbf_d[blk], wsrc[blk], [], [tt("wbf%d" % blk)])

        def ws_issue_upto(n):
            while ws_state["issued"] < min(n, len(order)):
                i = ws_state["issued"]
                blk = order[i]
                conv(blk)
                dma("sp", WB[:, i % NWB, :], wbf_d[blk], [tt("wbf%d" % blk)], [tt("wb%d" % (i % NWB))])
                ws_state["issued"] += 1

        def ws_get(expect, la=None):
            i = ws_state["cur"]
            assert order[i] == expect, (i, order[i], expect)
            ws_issue_upto(i + (NWB if la is None else la))
            ws_state["cur"] += 1
            return WB[:, i % NWB, :], tt("wb%d" % (i % NWB))

        def run_all():
            for blk in [BLK_ADA + j for j in range(4)] + [BLK_P2KV, BLK_RK, BLK_RV]:
                conv(blk)
            dma("sp", SMALL[:], small_d, [], [tt("small")])
            dma("sp", GG[:], rows_d[0:1, :].partition_broadcast(128), [], [tt("gg")])
            dma("sp", MISC[:, M_WD:M_WD + 8], wdec_d.partition_broadcast(128), [], [tt("wd")])
            dma("pool", IDENT[:], cst_d[0], [], [tt("ident")])
            dma("pool", PERM[:], cst_d[1], [], [tt("perm")])
            dma("pool", ONESB[:], cst_d[2], [], [tt("onesb")])
            ws_issue_upto(NWB)

            act(MISC[:, M_CACT:M_CACT + 8], SMALL[:, S_C:S_C + 8], AF.Silu, [tt("small")], [tt("cact")])
            cpy("dve", CBF[:], MISC[:, M_CACT:M_CACT + 8], [tt("cact")], [tt("cbf")])
            cpy("dve", CBC, MISC[:, M_CACT:M_CACT + 8].unsqueeze(2).broadcast_to([128, 8, 128]), [tt("cact")], [tt("cbc")])
            act(MISC[:, M_E:M_E + 8], MISC[:, M_WD:M_WD + 8], AF.Exp, [tt("wd")], [tt("e")], scale=-1.0)
            act(MISC[:, M_E:M_E + 8], MISC[:, M_E:M_E + 8], AF.Ln, [tt("e")], [tt("e")], bias=1.0)
            tsc("dve", MISC[:, M_LG:M_LG + 8], MISC[:, M_E:M_E + 8], -1.0, None, MUL, None, [tt("e")], [tt("lg")])
            LGF = lambda h, lo=0, hi=128: MISC[lo:hi, M_LG + h:M_LG + h + 1]
            LGB = lambda h, lo=0, hi=128: MISC[lo:hi, M_LG + 4 + h:M_LG + 5 + h]
            S.add("pool", lambda e: e.iota(IO[:, 0, :], pattern=[[1, 128]], base=0, channel_multiplier=-1,
                                           allow_small_or_imprecise_dtypes=True), writes=[tt("io0")])
            S.add("pool", lambda e: e.iota(IO[:, 1, :], pattern=[[1, 128]], base=1, channel_multiplier=0,
                                           allow_small_or_imprecise_dtypes=True), writes=[tt("io1")])
            S.add("pool", lambda e: e.iota(IO[:, 2, :], pattern=[[-1, 128]], base=128, channel_multiplier=0,
                                           allow_small_or_imprecise_dtypes=True), writes=[tt("io2")])
            S.add("pool", lambda e: e.iota(MISC[:, M_IOP:M_IOP + 1], pattern=[[1, 1]], base=0, channel_multiplier=1,
                                           allow_small_or_imprecise_dtypes=True), writes=[tt("iop")])
            S.add("pool", lambda e: e.iota(MISC[:, M_IOPR:M_IOPR + 1], pattern=[[1, 1]], base=127, channel_multiplier=-1,
                                           allow_small_or_imprecise_dtypes=True), writes=[tt("iopr")])
            NEG = RRS[:, 0, 0:128]
            POS = RRS[:, 0, 128:256]
            tsc("dve", POS, IO[:, 0, :], 0.0, None, MAX, None, [tt("io0")], [tt("pos")])
            tten("dve", NEG, POS, IO[:, 0, :], ALU.subtract, [tt("pos"), tt("io0")], [tt("neg")])
            for h in range(4):
                tsc("dve", TMPD, POS, LGF(h), None, MUL, None, [tt("pos"), tt("lg")], [tt("tmpd")])
                stt("dve", TMPD, NEG, LGB(h), TMPD, MUL, ADD, [tt("neg"), tt("lg"), tt("tmpd")], [tt("tmpd")])
                act(DMK[:, h * 128:(h + 1) * 128], TMPD, AF.Exp, [tt("tmpd")], [tt("dmk")])
                tsc("dve", TMPD[0:64, :], IO[0:64, 1, :], LGF(h, 0, 64), None, MUL, None, [tt("io1"), tt("lg"), tt("dmk")], [tt("tmpd")])
                tsc("dve", TMPD[64:128, :], IO[64:128, 2, :], LGB(h, 64, 128), None, MUL, None, [tt("io2"), tt("lg"), tt("tmpd")], [tt("tmpd")])
                act(QDEC[:, h * 128:(h + 1) * 128], TMPD, AF.Exp, [tt("tmpd")], [tt("qdec")])
            KDV = MISC[:, M_KDV:M_KDV + 8].rearrange("p (h d) -> p h d", d=2)
            tsc("dve", MISC[:, M_KD8:M_KD8 + 4], MISC[:, M_LG:M_LG + 4], MISC[:, M_IOPR:M_IOPR + 1], None, MUL, None,
                [tt("lg"), tt("iopr")], [tt("kd8")])
            tsc("dve", MISC[:, M_KD8 + 4:M_KD8 + 8], MISC[:, M_LG + 4:M_LG + 8], MISC[:, M_IOP:M_IOP + 1], None, MUL, None,
                [tt("lg"), tt("iop"), tt("kd8")], [tt("kd8")])
            act(KDV[:, :, 0], MISC[:, M_KD8:M_KD8 + 4], AF.Exp, [tt("kd8")], [tt("kdv")])
            act(KDV[:, :, 1], MISC[:, M_KD8 + 4:M_KD8 + 8], AF.Exp, [tt("kd8"), tt("kdv")], [tt("kdv")])
            act(MISC[:, M_DC:M_DC + 8], MISC[:, M_LG:M_LG + 8], AF.Exp, [tt("lg")], [tt("dc")], scale=128.0)
            tsc("dve", MISC[:, M_GK8:M_GK8 + 1], SMALL[:, S_KG:S_KG + 1], 8.0, None, MUL, None, [tt("small")], [tt("gk8")])
            S.add("pool", lambda e: e.memset(SST[:], 0.0), writes=[tt("sst")])
            S.add("pool", lambda e: e.memset(VX[:], 1.0), writes=[tt("vx%d" % g) for g in range(NG)])

            ckpt('setup0')
            pmod = bank(6)[:, 0:16]
            for blk in range(4):
                wb, twb = ws_get(BLK_ADA + blk)
                wbv = wb.rearrange("p (k c) -> p k c", k=8)
                for c in range(4):
                    for kc in range(8):
                        mm(pmod[:, blk * 4 + c:blk * 4 + c + 1], wbv[:, kc, c * 128:(c + 1) * 128], CBF[:, kc:kc + 1],
                           kc == 0, kc == 7, [twb, tt("cbf")], [TB[6]])
            tten("dve", MISC[:, M_TMP8:M_TMP8 + 8], pmod[:, 8:16], SMALL[:, S_BSC:S_BSC + 8], ADD, [TB[6], tt("small")], [tt("tmp8")])
            stt("dve", MISC[:, M_ACOL:M_ACOL + 8], MISC[:, M_TMP8:M_TMP8 + 8], 1.0, SMALL[:, S_GPRE:S_GPRE + 8], ADD, MUL,
                [tt("tmp8"), tt("small")], [tt("acol")])
            tten("dve", MISC[:, M_SHCOL:M_SHCOL + 8], pmod[:, 0:8], SMALL[:, S_BSH:S_BSH + 8], ADD, [TB[6], tt("small")], [tt("shcol")])
            def gate_setup():
                dma("sp", XT[:, 0, :], rows_d[1:2, :].partition_broadcast(128), [], [tt("xt0")])
                for hf in range(2):
                    wb, twb = ws_get(BLK_ADA + 4 + hf)
                    wbv = wb.rearrange("p (k c) -> p k c", k=8)
                    for kc in range(8):
                        mm(bank(4 + hf), CBC[:, kc, :], wbv[:, kc, :], kc == 0, kc == 7, [twb, tt("cbc")], [TB[4 + hf]])
                    tten("dve", GG[:, hf * 512:(hf + 1) * 512], bank(4 + hf), GG[:, hf * 512:(hf + 1) * 512], ADD,
                         [TB[4 + hf], tt("gg")], [tt("gg")])
                    tten("dve", GG[:, hf * 512:(hf + 1) * 512], GG[:, hf * 512:(hf + 1) * 512], XT[:, 0, hf * 512:(hf + 1) * 512], MUL,
                         [tt("gg"), tt("xt0")], [tt("gg")])

            ckpt('ada')
            xt_rr = {"i": 0}

            def load_x(tile):
                i = xt_rr["i"] % NXT
                xt_rr["i"] += 1
                dma("sp", XT[:, i, :], x_d[tile * 128:(tile + 1) * 128, :], [], [tt("xt%d" % i)])
                return i

            cur = {"ht": HT, "htT": tt("ht")}

            hcfg = {"banks": [0, 1, 2, 3]}

            def _hreg(kc, jj, ntp):
                hb = hcfg["banks"]
                per = 8 // len(hb)
                bk = hb[kc // per]
                off = (kc % per) * (ntp * 128) + jj * 128
                return bk, off

            def hT_tile(tg, j):
                hb = hcfg["banks"]
                ntp = len(hb)
                tile = tg * 4 + j
                xi = load_x(tile)
                xb = j % 2
                act(JUNK, XT[:, xi, :], AF.Square, [tt("xt%d" % xi)], [tt("junk"), tt("ss")], accum_out=MISC[:, M_SS:M_SS + 1])
                act(MISC[:, M_RSTD:M_RSTD + 1], MISC[:, M_SS:M_SS + 1], AF.Ln, [tt("ss")], [tt("rstd")], scale=1.0 / DM, bias=EPS)
                act(MISC[:, M_RSTD:M_RSTD + 1], MISC[:, M_RSTD:M_RSTD + 1], AF.Exp, [tt("rstd")], [tt("rstd")], scale=-0.5)
                tsc("dve", XN[:, xb, :], XT[:, xi, :], MISC[:, M_RSTD:M_RSTD + 1], None, MUL, None,
                    [tt("xt%d" % xi), tt("rstd")], [tt("xn%d" % xb)])
                for kc in range(8):
                    bk, off = _hreg(kc, j % ntp, ntp)
                    trp(bank(bk).bitcast(BF16)[:, off:off + 128], XN[:, xb, kc * 128:(kc + 1) * 128], [tt("xn%d" % xb)], [TB[bk]])

            def hT_evac(dst, dstT, ps=0):
                hb = hcfg["banks"]
                ntp = len(hb)
                for kc in range(8):
                    bk, off = _hreg(kc, 0, ntp)
                    pb = bank(bk).bitcast(BF16)[:, off:off + ntp * 128]
                    o = dst[:, kc, ps * ntp * 128:(ps + 1) * ntp * 128]
                    if kc < 4:
                        tsc("dve", o, pb, MISC[:, M_ACOL + kc:M_ACOL + kc + 1], MISC[:, M_SHCOL + kc:M_SHCOL + kc + 1],
                            MUL, ADD, [TB[bk], tt("acol"), tt("shcol")], [dstT])
                    else:
                        act(o, pb, AF.Identity, [TB[bk], tt("acol"), tt("shcol")], [dstT],
                            scale=MISC[:, M_ACOL + kc:M_ACOL + kc + 1], bias=MISC[:, M_SHCOL + kc:M_SHCOL + kc + 1])

            def make_hT(tg, set_cur=True):
                if set_cur:
                    cur["ht"], cur["htT"] = bufs[tg % 2]
                if os.environ.get("MK_HT67", "1") == "1":
                    hcfg["banks"] = [6, 7]
                ntp = len(hcfg["banks"])
                for ps in range(4 // ntp):
                    for jj in range(ntp):
                        hT_tile(tg, ps * ntp + jj)
                    hT_evac(bufs[tg % 2][0], bufs[tg % 2][1], ps)

            def load_cs(tg):
                dma("sp", CS[:, 0, :], rope_d[0][:, tg * 512:(tg + 1) * 512], [], [tt("cs")])
                dma("sp", CS[:, 1, :], rope_d[1][:, tg * 512:(tg + 1) * 512], [], [tt("cs")])

            def proj_fm(wbv, twb, c0, bk):
                for kc in range(8):
                    mm(bank(bk), wbv[:, kc, c0:c0 + 128], cur["ht"][:, kc, :], kc == 0, kc == 7, [twb, cur["htT"]], [TB[bk]])

            def proj_tm(wbv, twb, j, c0, ncol, o, tbk):
                for kc in range(8):
                    mm(o, cur["ht"][:, kc, j * 128:(j + 1) * 128], wbv[:, kc, c0:c0 + ncol], kc == 0, kc == 7, [twb, cur["htT"]], [tbk])

            rope_ctr = {"i": 0, "alt": [(4, 5)]}

            def rope(bkA, gcol, gT, rms, out_ap, out_T):
                i = rope_ctr["i"] % 2
                rope_ctr["i"] += 1
                bS, bC = rope_ctr["alt"][i % len(rope_ctr["alt"])]
                if os.environ.get("MK_ROPE67", "1") == "1" and len(rope_ctr["alt"]) == 2:
                    bS = bC = 13 - bkA
                A = bank(bkA)
                rdg = [gT] if gT is not None else []
                gsc = gcol if gT is not None else float(gcol)
                tsc("dve", QBF[:, i, :], A, gsc, None, MUL, None, [TB[bkA]] + rdg, [tt("qbf%d" % i)])
                if rms:
                    act(SQ[:, i, :], A, AF.Square, [TB[bkA]], [tt("sq%d" % i)])
                mm(bank(bS), PERM[:], QBF[:, i, :], True, True, [tt("perm"), tt("qbf%d" % i)], [TB[bS]])
                stt("dve", T1[:, i, :], A, gsc, CS[:, 0, :], MUL, MUL, [TB[bkA], tt("cs")] + rdg, [tt("t1_%d" % i)])
                tten("dve", T2[:, i, :], bank(bS), CS[:, 1, :], MUL, [TB[bS], tt("cs")], [tt("t2_%d" % i)])
                if rms:
                    mm(bank(bC), ONESB[:], SQ[:, i, :], True, True, [tt("onesb"), tt("sq%d" % i)], [TB[bC]])
                    act(RRS[:, i, :], bank(bC), AF.Ln, [TB[bC]], [tt("rrs%d" % i)], bias=64.0 * EPS)
                    act(RRS[:, i, :], RRS[:, i, :], AF.Exp, [tt("rrs%d" % i)], [tt("rrs%d" % i)], scale=-0.5)
                    tten(os.environ.get("MK_RA", "pool"), T1[:, i, :], T1[:, i, :], T2[:, i, :], ADD, [tt("t1_%d" % i), tt("t2_%d" % i)], [tt("t1_%d" % i)])
                    tten(os.environ.get("MK_RM", "pool"), out_ap, T1[:, i, :], RRS[:, i, :], MUL, [tt("t1_%d" % i), tt("rrs%d" % i)], [out_T])
                else:
                    tten(os.environ.get("MK_RP", "pool"), out_ap, T1[:, i, :], T2[:, i, :], ADD, [tt("t1_%d" % i), tt("t2_%d" % i)], [out_T])

            def ret_rk(tg):
                wb, twb = ws_get(BLK_RK)
                wbv = wb.rearrange("p (k c) -> p k c", k=8)
                for pr in range(2):
                    proj_fm(wbv, twb, pr * 128, 6 + pr)
                    rope(6 + pr, 0.125, None, False, RKT[:, pr, :], tt("rkt"))

            def ret_rv(tg):
                wb, twb = ws_get(BLK_RV)
                wbv = wb.rearrange("p (k c) -> p k c", k=8)
                for j in range(4):
                    bk = 6 + (j % 2)
                    proj_tm(wbv, twb, j, 0, 512, bank(bk), TB[bk])
                    act(RVG[:, j, :], bank(bk), AF.Copy, [TB[bk]], [tt("rvg")])

            def ret_kv_inputs(tg):
                ret_rk(tg)
                ret_rv(tg)

            def kv_matmul(j):
                ptk = bank(6).bitcast(BF16)[:, 0:256]
                for pr in range(2):
                    trp(ptk[:, pr * 128:(pr + 1) * 128], RKT[:, pr, j * 128:(j + 1) * 128], [tt("rkt")], [TB[6]])
                kb = j % 2
                tten("dve", KK[:, kb, :].rearrange("p (h d c) -> p h d c", h=4, d=2),
                     ptk.rearrange("p (h c) -> p h c", h=4).unsqueeze(2).broadcast_to([128, 4, 2, 64]),
                     MISC[:, M_KDV:M_KDV + 8].rearrange("p (h d) -> p h d", d=2).unsqueeze(3).broadcast_to([128, 4, 2, 64]), MUL,
                     [TB[6], tt("kdv")], [tt("kk%d" % kb)])
                for h in range(4):
                    mm(bank(7)[:, h * 128:(h + 1) * 128], KK[:, kb, h * 128:(h + 1) * 128], RVG[:, j, h * 128:(h + 1) * 128],
                       True, True, [tt("kk%d" % kb), tt("rvg")], [TB[7]])

            bufs = [(HT, tt("ht")), (HT2, tt("ht2"))]
            for j in range(4):
                hT_tile(0, j)
            hT_evac(*bufs[0])
            for tg in range(NG):
                cur["ht"], cur["htT"] = bufs[tg % 2]
                nxt = bufs[(tg + 1) % 2] if tg + 1 < NG else None
                ckpt('ht%d' % tg)
                load_cs(tg)
                wb, twb = ws_get(BLK_P2KV)
                wbv = wb.rearrange("p (k c) -> p k c", k=8)
                for g in range(2):
                    proj_fm(wbv, twb, g * 128, 6 + g)
                    rope(6 + g, MISC[:, M_GK8:M_GK8 + 1], tt("gk8"), True, K2[:, g, tg * 512:(tg + 1) * 512], tt("k2_%d_%d" % (g, tg)))
                ckpt('k%d' % tg)
                if nxt:
                    hT_tile(tg + 1, 0)
                for j in range(4):
                    proj_tm(wbv, twb, j, 256, 128, bank(6)[:, j * 128:(j + 1) * 128], TB[6])
                cpy("dve", VX[:, tg * 4:(tg + 1) * 4, 64:320].rearrange("p j (g c) -> p j g c", g=2)[:, :, :, 0:64],
                    bank(6).rearrange("p (j g c) -> p j g c", j=4, g=2), [TB[6]], [tt("vx%d" % tg)])
                ckpt('v%d' % tg)
                if nxt:
                    hT_tile(tg + 1, 1)
                ret_rk(tg)
                if nxt:
                    hT_tile(tg + 1, 2)
                ret_rv(tg)
                ckpt('rkv%d' % tg)
                if nxt:
                    hT_tile(tg + 1, 3)
                    hT_evac(*nxt)
                for j in range(4):
                    ci = tg * 4 + j
                    kv_matmul(j)
                    act(RF[(ci % 2) * 64:(ci % 2) * 64 + 64, ci // 2, :], SST[0:64, :], AF.Copy, [tt("sst")], [tt("rf")])
                    tten("dve", SST[0:64, :].rearrange("p (h c) -> p h c", h=4), SST[0:64, :].rearrange("p (h c) -> p h c", h=4),
                         MISC[0:64, M_DC:M_DC + 4].unsqueeze(2).broadcast_to([64, 4, 128]), MUL, [tt("sst"), tt("dc")], [tt("sst")])
                    tten("dve", SST[0:64, :], SST[0:64, :], bank(7)[0:64, :], ADD, [tt("sst"), TB[7]], [tt("sst")])

            rope_ctr["alt"] = [(4, 5), (2, 3)]
            gate_setup()
            ckpt('p2')
            for tg in range(NG - 1, -1, -1):
                EARLY = os.environ.get("MK_EARLY", "3")
                if tg == NG - 1 or EARLY != "0":
                    cur["ht"], cur["htT"] = bufs[tg % 2]
                else:
                    make_hT(tg)
                load_cs(tg)
                wq, twq = ws_get(BLK_Q)
                wqv = wq.rearrange("p (k c) -> p k c", k=8)
                for p in range(4):
                    proj_fm(wqv, twq, p * 128, 6 + (p % 2))
                    rope(6 + (p % 2), SMALL[:, S_QG:S_QG + 1], tt("small"), True, QT[:, p, :], tt("qt%d" % p))
                wz, twz = ws_get(BLK_ZA)
                wzv = wz.rearrange("p (k c) -> p k c", k=8)
                for p in range(4):
                    bk = 6 + (p % 2)
                    proj_fm(wzv, twz, p * 128, bk)
                    act(ZA[:, p, :], bank(bk), AF.Silu, [TB[bk]], [tt("za%d" % p)])
                ckpt('q%d' % tg)
                for p in range(4):
                    g = p // 2

                    def qk(kc):
                        si = kc % 2
                        mm(bank(2 * si), K2[0:64, g, kc * 128:(kc + 1) * 128], QT[0:64, p, :], True, True,
                           [tt("k2_%d_%d" % (g, kc // 4)), tt("qt%d" % p)], [TB[2 * si]])
                        mm(bank(2 * si + 1), K2[64:128, g, kc * 128:(kc + 1) * 128], QT[64:128, p, :], True, True,
                           [tt("k2_%d_%d" % (g, kc // 4)), tt("qt%d" % p)], [TB[2 * si + 1]])

                    def pv(kc):
                        pi = kc % NPT
                        mm(bank(4), VX[:, kc, 64 + g * 128:192 + g * 128], PT[:, pi, 0:512], kc == 0, kc == 31,
                           [tt("vx%d" % (kc // 4)), tt("pt%d" % pi)], [TB[4]])
                        mm(bank(5), VX[:, kc, g * 128:g * 128 + 128], PT[:, pi, 512:1024], kc == 0, kc == 31,
                           [tt("vx%d" % (kc // 4)), tt("pt%d" % pi)], [TB[5]])

                    qk(0)
                    qk(1)
                    for kc in range(32):
                        si = kc % 2
                        act(PT[:, kc % NPT, :], bank(2 * si, 2), AF.Exp, [TB[2 * si], TB[2 * si + 1]], [tt("pt%d" % (kc % NPT))])
                        if kc + 2 < 32:
                            qk(kc + 2)
                        pv(kc)
                    cpy("dve", SA[:], bank(4), [TB[4]], [tt("sa")])
                    cpy("dve", SB_[:], bank(5), [TB[5]], [tt("sb")])
                    recip(DEN[0:64, :], SA[64:128, :], [tt("sa")], [tt("den")])
                    recip(DEN[64:128, :], SB_[0:64, :], [tt("sb")], [tt("den")])
                    tten("pool", DEN[0:64, :], SA[0:64, :], DEN[0:64, :], MUL, [tt("sa"), tt("den")], [tt("den")])
                    tten("pool", DEN[64:128, :], SB_[64:128, :], DEN[64:128, :], MUL, [tt("sb"), tt("den")], [tt("den")])
                    tten("pool", YA[:, p, :], DEN[:], ZA[:, p, :], MUL, [tt("den"), tt("za%d" % p)], [tt("ya")])
                ckpt('att%d' % tg)
                if EARLY == '1' and tg > 0:
                    make_hT(tg - 1, set_cur=False)
                wr, twr = ws_get(BLK_RQ)
                wrv = wr.rearrange("p (k c) -> p k c", k=8)
                for h in range(4):
                    proj_fm(wrv, twr, h * 128, 6 + (h % 2))
                    rope(6 + (h % 2), 1.0, None, False, RQ[:, h, :], tt("rq"))
                    tten("pool", QD[:, h, :].rearrange("p (j c) -> p j c", j=4), RQ[:, h, :].rearrange("p (j c) -> p j c", j=4),
                         QDEC[:, h * 128:(h + 1) * 128].unsqueeze(1).broadcast_to([128, 4, 128]), MUL, [tt("rq"), tt("qdec")], [tt("qd")])
                ckpt('rqd%d' % tg)
                if tg != NG - 1:
                    ret_kv_inputs(tg)
                wz, twz = ws_get(BLK_ZR)
                wzv = wz.rearrange("p (k c) -> p k c", k=8)
                for h in range(4):
                    bk = 6 + (h % 2)
                    proj_fm(wzv, twz, h * 128, bk)
                    act(ZR[:, h, :], bank(bk), AF.Silu, [TB[bk]], [tt("zr")])
                ckpt('rin%d' % tg)
                for j in range(3, -1, -1):
                    ci = tg * 4 + j
                    sb_i = j % 2
                    for h in range(4):
                        r0 = (h % 2) * 64
                        mm(bank(h % 2)[:, (h // 2) * 128:(h // 2 + 1) * 128], RKT[r0:r0 + 64, h // 2, j * 128:(j + 1) * 128],
                           RQ[r0:r0 + 64, h, j * 128:(j + 1) * 128], True, True, [tt("rkt"), tt("rq")], [TB[h % 2]])
                    for hb in range(2):
                        tten("dve", SM[:, sb_i, :].rearrange("p (a b c) -> p a b c", a=2, b=2)[:, :, hb, :],
                             bank(hb)[:, 0:256].rearrange("p (a c) -> p a c", a=2),
                             DMK[:].rearrange("p (a b c) -> p a b c", a=2, b=2)[:, :, hb, :], MUL, [TB[hb], tt("dmk")], [tt("sm%d" % sb_i)])
                    ckpt('rs%d' % ci)
                    act(RRO[0:64, sb_i, :], RF[(ci % 2) * 64:(ci % 2) * 64 + 64, ci // 2, :], AF.Copy, [tt("rf")], [tt("rro%d" % sb_i)])
                    cpy("pool", RRO[64:128, sb_i, :], SST[64:128, :], [tt("sst")], [tt("rro%d" % sb_i)])
                    for h in range(4):
                        mm(bank(2)[:, h * 128:(h + 1) * 128], SM[:, sb_i, h * 128:(h + 1) * 128], RVG[:, j, h * 128:(h + 1) * 128],
                           True, False, [tt("sm%d" % sb_i), tt("rvg")], [TB[2]])
                        mm(bank(2)[:, h * 128:(h + 1) * 128], QD[:, h, j * 128:(j + 1) * 128], RRO[:, sb_i, h * 128:(h + 1) * 128],
                           False, True, [tt("qd"), tt("rro%d" % sb_i)], [TB[2]])
                    ckpt('ro%d' % ci)
                    for h in range(4):
                        bnstats(STAT[:, sb_i, h, 0:6], bank(2)[:, h * 128:(h + 1) * 128], [TB[2]], [tt("stat%d" % sb_i)])
                        bnaggr(STAT[:, sb_i, h, 6:8], STAT[:, sb_i, h, 0:6], [tt("stat%d" % sb_i)], [tt("stat%d" % sb_i)])
                    gr = MISC[:, M_GNR + sb_i * 4:M_GNR + sb_i * 4 + 4]
                    gn = MISC[:, M_GNN + sb_i * 4:M_GNN + sb_i * 4 + 4]
                    act(gr, STAT[:, sb_i, :, 7], AF.Ln, [tt("stat%d" % sb_i)], [tt("gr%d" % sb_i)], bias=EPS)
                    act(gr, gr, AF.Exp, [tt("gr%d" % sb_i)], [tt("gr%d" % sb_i)], scale=-0.5)
                    stt("dve", gn, STAT[:, sb_i, :, 6], -1.0, gr, MUL, MUL, [tt("stat%d" % sb_i), tt("gr%d" % sb_i)], [tt("gn%d" % sb_i)])
                    for h in range(4):
                        tsc("dve", ON[:, sb_i, h * 128:(h + 1) * 128], bank(2)[:, h * 128:(h + 1) * 128],
                            MISC[:, M_GNR + sb_i * 4 + h:M_GNR + sb_i * 4 + h + 1], MISC[:, M_GNN + sb_i * 4 + h:M_GNN + sb_i * 4 + h + 1],
                            MUL, ADD, [TB[2], tt("gr%d" % sb_i), tt("gn%d" % sb_i)], [tt("on%d" % sb_i)])
                    ckpt('rgn%d' % ci)
                    ptn = bank(3).bitcast(BF16)[:, 0:512]
                    for h in range(4):
                        trp(ptn[:, h * 128:(h + 1) * 128], ON[:, sb_i, h * 128:(h + 1) * 128], [tt("on%d" % sb_i)], [TB[3]])
                    for h in range(4):
                        stt("dve", YR[:, h, j * 128:(j + 1) * 128], ptn[:, h * 128:(h + 1) * 128], SMALL[:, S_GN + h:S_GN + h + 1],
                            ZR[:, h, j * 128:(j + 1) * 128], MUL, MUL, [TB[3], tt("small"), tt("zr")], [tt("yr")])
                    ckpt('ryr%d' % ci)
                    kv_matmul(j)
                    tten("dve", SST[64:128, :].rearrange("p (h c) -> p h c", h=4), SST[64:128, :].rearrange("p (h c) -> p h c", h=4),
                         MISC[64:128, M_DC + 4:M_DC + 8].unsqueeze(2).broadcast_to([64, 4, 128]), MUL, [tt("sst"), tt("dc")], [tt("sst")])
                    tten("dve", SST[64:128, :], SST[64:128, :], bank(7)[64:128, :], ADD, [tt("sst"), TB[7]], [tt("sst")])
                ckpt('ret%d' % tg)
                if EARLY == '2' and tg > 0:
                    make_hT(tg - 1, set_cur=False)
                for n in range(8):
                    wm, twm = ws_get(BLK_M + n)
                    glv = wm[:, 0:2048].rearrange("p (k c) -> p k c", k=8)
                    pav = wm[:, 2048:2560].rearrange("p (k c) -> p k c", k=4)
                    prv = wm[:, 2560:3072].rearrange("p (k c) -> p k c", k=4)
                    gi = n % NGT
                    b0 = 4 * (n % 2)
                    for br in range(2):
                        bk = b0 + 2 + br
                        for kc in range(8):
                            mm(bank(bk), glv[:, kc, br * 128:(br + 1) * 128], cur["ht"][:, kc, :], kc == 0, kc == 7, [twm, cur["htT"]], [TB[bk]])
                        act(GATE[:, gi, br, :], bank(bk), AF.Sigmoid, [TB[bk]], [tt("gate%d_%d" % (gi, br))])
                    for cc in range(4):
                        mm(bank(b0), pav[:, cc, :], YA[:, cc, :], cc == 0, cc == 3, [twm, tt("ya")], [TB[b0]])
                    for cc in range(4):
                        mm(bank(b0 + 1), prv[:, cc, :], YR[:, cc, :], cc == 0, cc == 3, [twm, tt("yr")], [TB[b0 + 1]])
                    tten("dve", MT[:, 0, :], bank(b0), GATE[:, gi, 0, :], MUL, [TB[b0], tt("gate%d_0" % gi)], [tt("mt0")])
                    tten("dve", MT[:, 1, :], bank(b0 + 1), GATE[:, gi, 1, :], MUL, [TB[b0 + 1], tt("gate%d_1" % gi)], [tt("mt1")])
                    tten("pool", MG[:, n, :], MT[:, 0, :], MT[:, 1, :], ADD, [tt("mt0"), tt("mt1")], [tt("mg")])
                ckpt('mrg%d' % tg)
                if EARLY == '3' and tg > 0:
                    make_hT(tg - 1, set_cur=False)
                wo0, two0 = ws_get(BLK_WO)
                wo1, two1 = ws_get(BLK_WO + 1, la=NWB - 1)
                wov = [wo0.rearrange("p (k c) -> p k c", k=8), wo1.rearrange("p (k c) -> p k c", k=8)]
                twos = [two0, two1]
                for j in range(4):
                    tile = tg * 4 + j
                    pb0 = 2 * j
                    for hf in range(2):
                        for kc in range(8):
                            mm(bank(pb0 + hf), MG[:, kc, j * 128:(j + 1) * 128], wov[hf][:, kc, :], kc == 0, kc == 7,
                               [tt("mg"), twos[hf]], [TB[pb0 + hf]])
                    xi = load_x(tile)
                    act(JUNK, bank(pb0, 2), AF.Square, [TB[pb0], TB[pb0 + 1]], [tt("junk"), tt("ss2")], accum_out=MISC[:, M_SS + 1:M_SS + 2])
                    act(MISC[:, M_RSTD + 1:M_RSTD + 2], MISC[:, M_SS + 1:M_SS + 2], AF.Ln, [tt("ss2")], [tt("rstd2")], scale=1.0 / DM, bias=EPS)
                    act(MISC[:, M_RSTD + 1:M_RSTD + 2], MISC[:, M_RSTD + 1:M_RSTD + 2], AF.Exp, [tt("rstd2")], [tt("rstd2")], scale=-0.5)
                    stt("dve", bank(pb0, 2), bank(pb0, 2), MISC[:, M_RSTD + 1:M_RSTD + 2], GG[:], MUL, MUL,
                        [TB[pb0], TB[pb0 + 1], tt("rstd2"), tt("gg")], [TB[pb0], TB[pb0 + 1]])
                    tten("dve", XT[:, xi, :], bank(pb0, 2), XT[:, xi, :], ADD, [TB[pb0], TB[pb0 + 1], tt("xt%d" % xi)], [tt("xt%d" % xi)])
                    dma(os.environ.get("MK_STQ", "sp"), out_d[tile * 128:(tile + 1) * 128, :], XT[:, xi, :], [tt("xt%d" % xi)], [])
        try:
            run_all()
        except _Stop:
            pass
        dump_aps = {"HT": HT, "K2": K2, "VX": VX, "RF": RF, "MISC": MISC, "GG": GG, "DMK": DMK, "QDEC": QDEC,
                    "YA": YA, "YR": YR, "MG": MG, "QT": QT, "ZA": ZA, "RKT": RKT, "RVG": RVG, "SST": SST,
                    "RQ": RQ, "QD": QD, "ZR": ZR, "XN": XN, "CS": CS}
        for nm in dumps:
            src = dump_aps[nm]
            shp = list(src.shape)
            flat = int(np.prod(shp[1:]))
            dd = nc.dram_tensor("dbg_" + nm, [128, flat], src.dtype, kind="ExternalOutput").ap()
            allT = list(tdict.values()) + TB
            sv = src[:] if len(shp) == 2 else src[:].rearrange("p a b -> p (a b)") if len(shp) == 3 else src[:].rearrange("p a b c -> p (a b c)")
            dma("sp", dd, sv, allT, [])
        import os
        if os.environ.get('MK_RESCHED', '1') == '1':
            S.reschedule()
        if not _EXP:
            S.emit(nc)
    return nc, S


_CACHE = {}


def kernel(x, c, w_ada, b_ada, g_pre, w_in, qn_g, kn_g, w_dec_f, w_dec_b, gn_g, w_pa, w_pr, w_out, g_post):
    x = np.asarray(x, np.float32)
    c = np.asarray(c, np.float32)
    f = lambda a: np.asarray(a, np.float32)[0]
    w_ada, b_ada, g_pre, w_in, qn_g, kn_g = f(w_ada), f(b_ada), f(g_pre), f(w_in), f(qn_g), f(kn_g)
    w_dec_f, w_dec_b, gn_g, w_pa, w_pr, w_out, g_post = f(w_dec_f), f(w_dec_b), f(gn_g), f(w_pa), f(w_pr), f(w_out), f(g_post)
    if "nc" not in _CACHE:
        _CACHE["nc"] = build_program()[0]
        _CACHE["consts"] = _host_consts()
    nc = _CACHE["nc"]
    rope, cst = _CACHE["consts"]
    blocks = _host_blocks(w_ada, w_in, w_pa, w_pr, w_out)
    col8 = lambda v: np.ascontiguousarray(v.reshape(8, 128).T)
    rows = np.ascontiguousarray(np.stack([b_ada[2048:3072], g_post], 0))
    wdec = np.concatenate([w_dec_f, w_dec_b])[None, :].astype(np.float32)
    in_maps = []
    for b in range(8):
        small = np.zeros((128, NSMALL), np.float32)
        small[:, 0:8] = col8(c[b])
        small[:, 8:16] = col8(g_pre)
        small[:, 16:24] = col8(b_ada[1024:2048])
        small[:, 24:32] = col8(b_ada[0:1024])
        small[:, 32] = np.concatenate([qn_g, qn_g])
        small[:, 33] = np.concatenate([kn_g, kn_g])
        small[:, 34:38] = gn_g.reshape(4, 128).T
        in_maps.append({"x": np.ascontiguousarray(x[b]), "wsrc": blocks, "rope": rope, "cst": cst,
                        "small": small, "rows": rows, "wdec": wdec})
    res = run_bass_kernel_spmd(nc, in_maps, core_ids=list(range(8)))
    return np.stack([np.asarray(r["out"], np.float32) for r in res.results], 0)
```

```python
import numpy as np
import concourse.bass as bass
import concourse.mybir as mybir
from concourse.bass_utils import run_bass_kernel_spmd

F32 = mybir.dt.float32
BF16 = mybir.dt.bfloat16
AF = mybir.ActivationFunctionType
ALU = mybir.AluOpType
AX = mybir.AxisListType


class T:
    __slots__ = ("name", "last_w", "readers", "excl")

    def __init__(self, name="", excl=False):
        self.name = name
        self.last_w = None
        self.readers = []
        self.excl = excl


class Op:
    __slots__ = ("eng", "fn", "deps", "signal", "token", "is_dma", "idx", "gidx", "preds", "succs", "dur", "lat", "cls", "prio", "t_start", "t_done", "npred", "tag")

    def __init__(self, eng, fn, is_dma):
        self.eng = eng
        self.fn = fn
        self.deps = []
        self.signal = False
        self.token = None
        self.is_dma = is_dma
        self.preds = []
        self.succs = []
        self.dur = 0.2
        self.lat = 0.0
        self.cls = None


class Sched:
    ENGS = ("pe", "act", "dve", "pool", "sp")

    def __init__(self, n_dma_sems=8):
        self.ops = {e: [] for e in self.ENGS}
        self.n_dma_sems = n_dma_sems
        self.dma_rr = {e: 0 for e in self.ENGS}
        self.dma_last = {}
        self.gcount = 0

    def add(self, eng, fn, reads=(), writes=(), dma=False, dur=0.2, lat=0.0, cls=None):
        op = Op(eng, fn, dma)
        op.dur, op.lat, op.cls = dur, lat, cls
        import sys as _sys
        op.tag = (_sys._getframe(2).f_lineno, _sys._getframe(3).f_lineno)
        op.gidx = self.gcount
        self.gcount += 1
        writes = list(writes) + [t for t in reads if t.excl]
        reads = [t for t in reads if not t.excl]
        raw = []
        war = []
        for t in reads:
            if t.last_w is not None:
                raw.append(t.last_w)
        for t in writes:
            if t.last_w is not None:
                raw.append(t.last_w)
            war.extend(t.readers)
        for t in reads:
            t.readers.append(op)
        for t in writes:
            t.last_w = op
            t.readers = []
        if dma:
            k = self.dma_rr[eng]
            self.dma_rr[eng] = (k + 1) % self.n_dma_sems
            prev = self.dma_last.get((eng, k))
            if prev is not None:
                raw.append(prev)
            self.dma_last[(eng, k)] = op
            op.token = (eng, k)
        seen = set()
        pseen = set()
        for lst, is_war in ((raw, False), (war, True)):
            for d in lst:
                if d is op or id(d) in seen:
                    continue
                if id(d) not in pseen:
                    pseen.add(id(d))
                    op.preds.append(d)
                if (not d.is_dma) and (not dma) and d.eng == eng:
                    if eng == "pe" or is_war:
                        continue
                seen.add(id(d))
                op.deps.append(d)
        op.idx = len(self.ops[eng])
        self.ops[eng].append(op)
        return op

    def reschedule(self, hop=0.15, act_switch=1.4):
        import heapq
        allops = []
        for e in self.ENGS:
            allops.extend(self.ops[e])
        allops.sort(key=lambda o: o.gidx)
        for o in allops:
            o.succs = []
        for o in allops:
            o.npred = len(o.preds)
            for p in o.preds:
                p.succs.append(o)
        for o in reversed(allops):
            m = 0.0
            for s_ in o.succs:
                if s_.prio > m:
                    m = s_.prio
            o.prio = m + o.dur + o.lat
        ready = {e: [] for e in self.ENGS}
        for o in allops:
            o.t_done = None
            if o.npred == 0:
                ready[o.eng].append(o)
        free_at = {e: 0.0 for e in self.ENGS}
        cur_cls = {e: None for e in self.ENGS}
        new_order = {e: [] for e in self.ENGS}
        nleft = len(allops)

        def rtime(o):
            t = 0.0
            for p in o.preds:
                tp = p.t_done + (hop if (p.eng != o.eng or p.is_dma) else 0.0)
                if tp > t:
                    t = tp
            return t

        rt_cache = {}
        while nleft:
            best = None
            for e in self.ENGS:
                lst = ready[e]
                if not lst:
                    continue
                fa = free_at[e]
                for o in lst:
                    r = rt_cache.get(id(o))
                    if r is None:
                        r = rtime(o)
                        rt_cache[id(o)] = r
                    st = r if r > fa else fa
                    if e == "act" and o.cls is not None and cur_cls[e] is not None and o.cls != cur_cls[e]:
                        st += act_switch
                    key = (st, -o.prio)
                    if best is None or key < best[0]:
                        best = (key, o)
            (st, _), o = best
            e = o.eng
            ready[e].remove(o)
            o.t_start = st
            free_at[e] = st + o.dur
            o.t_done = st + o.dur + o.lat
            if e == "act" and o.cls is not None:
                cur_cls[e] = o.cls
            new_order[e].append(o)
            nleft -= 1
            for s_ in o.succs:
                s_.npred -= 1
                if s_.npred == 0:
                    ready[s_.eng].append(s_)
        for e in self.ENGS:
            self.ops[e] = new_order[e]
        self.makespan = max(free_at.values())

    def emit(self, nc, block_ctx_extra=None):
        for e in self.ENGS:
            for op in self.ops[e]:
                for d in op.deps:
                    d.signal = True
        import contextlib
        with contextlib.ExitStack() as st:
            sems = {e: st.enter_context(nc.semaphore("s_" + e)) for e in ("pe", "act", "dve", "pool")}
            dsems = {}
            for e in self.ENGS:
                if any(o.is_dma for o in self.ops[e]):
                    for k in range(self.n_dma_sems):
                        dsems[(e, k)] = st.enter_context(nc.semaphore("d_%s_%d" % (e, k)))
            for e in self.ENGS:
                cnt = 0
                dcnt = {}
                for op in self.ops[e]:
                    if op.is_dma:
                        key = op.token
                        dcnt[key] = dcnt.get(key, 0) + 16
                        op.token = (dsems[key], dcnt[key], key)
                    elif op.signal:
                        cnt += 1
                        op.token = (sems[e], cnt, e)
            block = st.enter_context(nc.Block())
            handles = {"pe": block.tensor, "act": block.scalar, "dve": block.vector, "pool": block.gpsimd,
                       "sp": block.sync}
            nwaits = {e: 0 for e in self.ENGS}

            def make(e):
                ops = self.ops[e]

                def body(eng):
                    known = {}
                    for op in ops:
                        need = {}
                        for d in op.deps:
                            sem, val, key = d.token
                            if known.get(key, 0) >= val:
                                continue
                            if key not in need or need[key][1] < val:
                                need[key] = (sem, val)
                        for key, (sem, val) in need.items():
                            eng.wait_ge(sem, val)
                            known[key] = val
                            nwaits[e] += 1
                        inst = op.fn(eng)
                        if op.is_dma:
                            inst.then_inc(op.token[0], 16)
                        elif op.signal:
                            inst.then_inc(op.token[0], 1)
                    fin = {}
                    for op in ops:
                        if op.is_dma:
                            fin[op.token[2]] = (op.token[0], op.token[1])
                    for key, (sem, val) in fin.items():
                        if known.get(key, 0) < val:
                            eng.wait_ge(sem, val)
                return body

            for e in self.ENGS:
                if self.ops[e]:
                    handles[e](make(e))
            self.nwaits = nwaits

import contextlib

SEQ = 4096
DM = 1024
NG = 8
EPS = 1e-6

BLK_ADA = 0
BLK_P2KV = 6
BLK_RK = 7
BLK_RV = 8
BLK_Q = 9
BLK_ZA = 10
BLK_RQ = 11
BLK_ZR = 12
BLK_M = 13
BLK_WO = 21
NBLK = 23
NSMALL = 40


def _host_blocks(w_ada, w_in, w_pa, w_pr, w_out):
    blocks = np.zeros((NBLK, 128, 4096), np.float32)

    def kmaj(w):
        nc_ = w.shape[1]
        t = np.zeros((128, 8, 512), np.float32)
        t[:, :, :nc_] = w.reshape(8, 128, nc_).transpose(1, 0, 2)
        return t.reshape(128, 4096)

    wa = w_ada
    for j in range(6):
        blocks[BLK_ADA + j] = kmaj(wa[:, j * 512:(j + 1) * 512])
    o_q, o_k, o_v, o_za, o_rq, o_rk, o_rv, o_zr, o_gl = 0, 512, 640, 768, 1280, 1536, 1792, 2304, 2816
    k0 = w_in[:, o_k:o_k + 64]
    k1 = w_in[:, o_k + 64:o_k + 128]
    blocks[BLK_P2KV] = kmaj(np.concatenate([k0, k0, k1, k1, w_in[:, o_v:o_v + 128]], axis=1))
    blocks[BLK_RK] = kmaj(w_in[:, o_rk:o_rk + 256])
    blocks[BLK_RV] = kmaj(w_in[:, o_rv:o_rv + 512])
    blocks[BLK_Q] = kmaj(w_in[:, o_q:o_q + 512])
    blocks[BLK_ZA] = kmaj(w_in[:, o_za:o_za + 512])
    rq = [w_in[:, o_rq + 64 * h:o_rq + 64 * (h + 1)] for h in range(4)]
    blocks[BLK_RQ] = kmaj(np.concatenate([rq[0], rq[0], rq[1], rq[1], rq[2], rq[2], rq[3], rq[3]], axis=1))
    blocks[BLK_ZR] = kmaj(w_in[:, o_zr:o_zr + 512])
    for n in range(8):
        t = np.zeros((128, 4096), np.float32)
        gl = np.concatenate([w_in[:, o_gl + n * 128:o_gl + (n + 1) * 128],
                             w_in[:, o_gl + 1024 + n * 128:o_gl + 1024 + (n + 1) * 128]], axis=1)
        t[:, 0:2048] = gl.reshape(8, 128, 256).transpose(1, 0, 2).reshape(128, 2048)
        t[:, 2048:2560] = w_pa[:, n * 128:(n + 1) * 128].reshape(4, 128, 128).transpose(1, 0, 2).reshape(128, 512)
        t[:, 2560:3072] = w_pr[:, n * 128:(n + 1) * 128].reshape(4, 128, 128).transpose(1, 0, 2).reshape(128, 512)
        blocks[BLK_M + n] = t
    for hf in range(2):
        blocks[BLK_WO + hf] = kmaj(w_out[:, hf * 512:(hf + 1) * 512])
    return blocks


def _host_consts():
    t = np.arange(SEQ)
    row = (t // 64).astype(np.float32)
    col = (t % 64).astype(np.float32)
    inv = (10000.0 ** (-np.arange(0, 32, 2, dtype=np.float32) / 32.0)).astype(np.float32)
    ang_r = row[None, :] * inv[:, None]
    ang_c = col[None, :] * inv[:, None]
    cos64 = np.concatenate([np.cos(ang_r), np.cos(ang_r), np.cos(ang_c), np.cos(ang_c)], 0)
    sin64 = np.concatenate([np.sin(ang_r), np.sin(ang_r), np.sin(ang_c), np.sin(ang_c)], 0)
    rope = np.stack([np.concatenate([cos64, cos64], 0), np.concatenate([sin64, sin64], 0)], 0).astype(np.float32)
    ident = np.eye(128, dtype=np.float32)
    perm = np.zeros((128, 128), np.float32)
    for hb in range(2):
        for d in range(64):
            blk = d // 16
            if blk % 2 == 0:
                p, s = d + 16, -1.0
            else:
                p, s = d - 16, 1.0
            perm[hb * 64 + p, hb * 64 + d] = s
    ones = np.zeros((128, 128), np.float32)
    ones[:64, :64] = 1.0
    ones[64:, 64:] = 1.0
    return rope, np.stack([ident, perm, ones], 0)


class _Stop(Exception):
    pass


def build_program(stop=None, dumps=()):
    nc = bass.Bass("TRN2", target_bir_lowering=False)
    x_d = nc.dram_tensor("x", [SEQ, DM], F32, kind="ExternalInput").ap()
    wsrc = nc.dram_tensor("wsrc", [NBLK, 128, 4096], F32, kind="ExternalInput").ap()
    rope_d = nc.dram_tensor("rope", [2, 128, SEQ], F32, kind="ExternalInput").ap()
    cst_d = nc.dram_tensor("cst", [3, 128, 128], F32, kind="ExternalInput").ap()
    small_d = nc.dram_tensor("small", [128, NSMALL], F32, kind="ExternalInput").ap()
    rows_d = nc.dram_tensor("rows", [2, DM], F32, kind="ExternalInput").ap()
    wdec_d = nc.dram_tensor("wdec", [1, 8], F32, kind="ExternalInput").ap()
    out_d = nc.dram_tensor("out", [SEQ, DM], F32, kind="ExternalOutput").ap()
    wbf_d = nc.dram_tensor("wbf", [NBLK, 128, 4096], BF16, kind="Internal").ap()

    S = Sched(n_dma_sems=12)
    tdict = {}

    import os
    _alias = {"io0": "sa", "io1": "sa", "io2": "sa", "tmpd": "sa", "cbc": "mt1", "junk": "mt0"}
    if os.environ.get("MK_JUNK") == "1":
        del _alias["junk"]

    def tt(name):
        name = _alias.get(name, name)
        if name not in tdict:
            tdict[name] = T(name)
        return tdict[name]

    with contextlib.ExitStack() as st:
        def sb(name, shape, dt):
            return st.enter_context(nc.sbuf_tensor(name, shape, dt))

        PS = st.enter_context(nc.psum_tensor("ps", [128, 4096], F32))
        TB = [T("bank%d" % i, excl=True) for i in range(8)]

        def bank(i, n=1):
            return PS[:, i * 512:(i + n) * 512]

        import os
        _EXP = os.environ.get("MK_EXPLORE") == "1"
        _PDT = mybir.dt.int8 if _EXP else BF16
        K2 = sb("K2", [128, 2, SEQ], _PDT)
        VX = sb("VX", [128, 32, 320], _PDT)
        RF = sb("RF", [128, 16, 512], _PDT)
        NWB = int(os.environ.get('MK_NWB', '3'))
        WB = sb("WB", [128, NWB, 4096], BF16)
        NXT = int(os.environ.get('MK_NXT', '3'))
        XT = sb("XT", [128, NXT, DM], F32)
        XN = sb("XN", [128, 2, DM], BF16)
        HT = sb("HT", [128, 8, 512], BF16)
        HT2 = sb("HT2", [128, 8, 512], BF16)
        CS = sb("CS", [128, 2, 512], F32)
        QBF = sb("QBF", [128, 2, 512], BF16)
        SQ = sb("SQ", [128, 2, 512], BF16)
        T1 = sb("T1", [128, 2, 512], F32)
        T2 = sb("T2", [128, 2, 512], F32)
        RRS = sb("RRS", [128, 2, 512], F32)
        QT = sb("QT", [128, 4, 512], BF16)
        ZA = sb("ZA", [128, 4, 512], BF16)
        NPT = int(os.environ.get('MK_NPT', '3'))
        PT = sb("PT", [128, NPT, 1024], BF16)
        SA = sb("SA", [128, 512], F32)
        SB_ = sb("SB_", [128, 512], F32)
        DEN = sb("DEN", [128, 512], F32)
        YA = sb("YA", [128, 4, 512], BF16)
        YR = sb("YR", [128, 4, 512], BF16)
        RQ = sb("RQ", [128, 4, 512], BF16)
        QD = sb("QD", [128, 4, 512], BF16)
        ZR = sb("ZR", [128, 4, 512], BF16)
        RKT = sb("RKT", [128, 2, 512], BF16)
        RVG = sb("RVG", [128, 4, 512], BF16)
        KK = sb("KK", [128, 2, 512], BF16)
        SM = sb("SM", [128, 2, 512], BF16)
        RRO = sb("RRO", [128, 2, 512], BF16)
        ON = sb("ON", [128, 2, 512], BF16)
        SST = sb("SST", [128, 512], F32)
        NGT = int(os.environ.get("MK_NGT", "1"))
        GATE = sb("GATE", [128, NGT, 2, 512], BF16)
        MT = sb("MT", [128, 2, 512], F32)
        MG = sb("MG", [128, 8, 512], BF16)
        IDENT = sb("IDENT", [128, 128], BF16)
        PERM = sb("PERM", [128, 128], BF16)
        ONESB = sb("ONESB", [128, 128], BF16)
        DMK = sb("DMK", [128, 512], F32)
        QDEC = sb("QDEC", [128, 512], F32)
        GG = sb("GG", [128, DM], F32)
        SMALL = sb("SMALL", [128, NSMALL], F32)
        MISC = sb("MISC", [128, 128], F32)
        IO = SA[:, 0:384].rearrange("p (a b) -> p a b", a=3)
        TMPD = SA[:, 384:512]
        CBC = MT[:, 1, :].bitcast(BF16).rearrange("p (a b) -> p a b", a=8)
        JUNK = sb("JUNKB", [128, DM], BF16)[:] if os.environ.get("MK_JUNK") == "1" else MT[:, 0, :].bitcast(BF16)
        CBF = sb("CBF", [128, 8], BF16)
        STAT = sb("STAT", [128, 2, 4, 8], F32)

        _CACHE['sbuf_left'] = nc.sbuf_bytes_remaining
        M_ACOL, M_SHCOL, M_LG, M_DC, M_WD, M_E, M_KD8, M_KDV, M_IOP, M_IOPR, M_GK8 = 0, 8, 16, 24, 32, 40, 48, 56, 64, 65, 66
        M_SS, M_RSTD, M_CACT, M_TMP8 = 70, 74, 80, 88
        M_GNR, M_GNN = 96, 104
        S_C, S_GPRE, S_BSC, S_BSH, S_QG, S_KG, S_GN = 0, 8, 16, 24, 32, 33, 34

        def fsz(ap):
            n = 1
            for d in list(ap.shape)[1:]:
                n *= int(d)
            return n

        def mm(o, l, r, start, stop, rd, wr):
            n = fsz(o)
            k = int(l.shape[0])
            d = 0.06 + n * 0.00041
            if k <= 64 and n >= 256:
                d = 0.17
            S.add("pe", lambda e: e.matmul(o, l, r, start=start, stop=stop), reads=rd, writes=wr, dur=d)

        def trp(o, i, rd, wr):
            S.add("pe", lambda e: e.transpose(o, i, IDENT[:]), reads=rd + [tt("ident")], writes=wr, dur=0.115)

        _ACLS = {AF.Exp: "exp", AF.Ln: "exp", AF.Silu: "sig", AF.Sigmoid: "sig"}

        def act(o, i, f, rd, wr, **kw):
            S.add("act", lambda e: e.activation(out=o, in_=i, func=f, **kw), reads=rd, writes=wr,
                  dur=0.22 + fsz(i) * 0.00083 + (0.1 if "accum_out" in kw else 0.0), cls=_ACLS.get(f))

        def _vd(eng, n):
            return (0.1 + n * 0.00104) if eng == "dve" else (0.15 + n * 0.0021)

        def tten(eng, o, a, b, op, rd, wr):
            S.add(eng, lambda e: e.tensor_tensor(out=o, in0=a, in1=b, op=op), reads=rd, writes=wr, dur=_vd(eng, fsz(o)))

        def tsc(eng, o, a, s1, s2, op0, op1, rd, wr):
            if op1 is None:
                S.add(eng, lambda e: e.tensor_scalar(out=o, in0=a, scalar1=s1, scalar2=None, op0=op0), reads=rd, writes=wr, dur=_vd(eng, fsz(o)))
            else:
                S.add(eng, lambda e: e.tensor_scalar(out=o, in0=a, scalar1=s1, scalar2=s2, op0=op0, op1=op1), reads=rd, writes=wr, dur=_vd(eng, fsz(o)))

        def stt(eng, o, a, s, b, op0, op1, rd, wr):
            S.add(eng, lambda e: e.scalar_tensor_tensor(out=o, in0=a, scalar=s, in1=b, op0=op0, op1=op1), reads=rd, writes=wr, dur=_vd(eng, fsz(o)))

        def cpy(eng, o, i, rd, wr):
            S.add(eng, lambda e: e.tensor_copy(out=o, in_=i), reads=rd, writes=wr, dur=_vd(eng, fsz(o)))

        def dma(eng, o, i, rd, wr):
            nbytes = fsz(o) * int(o.shape[0]) * (2 if o.dtype == BF16 else 4)
            S.add(eng, lambda e: e.dma_start(out=o, in_=i), reads=rd, writes=wr, dma=True,
                  dur=(0.08 if eng == "sp" else 1.0), lat=2.0 + nbytes / 150000.0)

        MUL, ADD, MAX = ALU.mult, ALU.add, ALU.max

        def ckpt(name):
            if stop == name:
                raise _Stop()

        def bnstats(o, i, rd, wr):
            S.add("dve", lambda e: e.bn_stats(out=o, in_=i), reads=rd, writes=wr, dur=0.25)

        def bnaggr(o, i, rd, wr):
            S.add("dve", lambda e: e.bn_aggr(out=o, in_=i), reads=rd, writes=wr, dur=0.15)

        def recip(o, i, rd, wr):
            S.add("dve", lambda e: e.reciprocal(out=o, in_=i), reads=rd, writes=wr, dur=0.1 + fsz(o) * 0.00625)

        order = [BLK_ADA + j for j in range(4)]
        for tg in range(NG):
            order += [BLK_P2KV, BLK_RK, BLK_RV]
        order += [BLK_ADA + 4, BLK_ADA + 5]
        for tg in range(NG):
            order += [BLK_Q, BLK_ZA, BLK_RQ] + ([BLK_RK, BLK_RV] if tg > 0 else []) + [BLK_ZR] \
                + [BLK_M + n for n in range(8)] + [BLK_WO, BLK_WO + 1]
        conv_done = set()
        ws_state = {"issued": 0, "cur": 0}

        def conv(blk):
            if blk in conv_done:
                return
            conv_done.add(blk)
            dma("pool", wbf_d[blk], wsrc[blk], [], [tt("wbf%d" % blk)])

        def ws_issue_upto(n):
            while ws_state["issued"] < min(n, len(order)):
                i = ws_state["issued"]
                blk = order[i]
                if BLK_ADA <= blk < BLK_ADA + 6:
                    dma("pool", WB[:, i % NWB, :], wsrc[blk], [], [tt("wb%d" % (i % NWB))])
                else:
                    conv(blk)
                    dma("sp", WB[:, i % NWB, :], wbf_d[blk], [tt("wbf%d" % blk)], [tt("wb%d" % (i % NWB))])
                ws_state["issued"] += 1

        def ws_get(expect, la=None):
            i = ws_state["cur"]
            assert order[i] == expect, (i, order[i], expect)
            ws_issue_upto(i + (NWB if la is None else la))
            ws_state["cur"] += 1
            return WB[:, i % NWB, :], tt("wb%d" % (i % NWB))

        def run_all():
            for blk in [BLK_P2KV, BLK_RK, BLK_RV]:
                conv(blk)
            dma("sp", SMALL[:], small_d, [], [tt("small")])
            dma("sp", GG[:], rows_d[0:1, :].partition_broadcast(128), [], [tt("gg")])
            dma("sp", MISC[:, M_WD:M_WD + 8], wdec_d.partition_broadcast(128), [], [tt("wd")])
            dma("pool", IDENT[:], cst_d[0], [], [tt("ident")])
            dma("pool", PERM[:], cst_d[1], [], [tt("perm")])
            dma("pool", ONESB[:], cst_d[2], [], [tt("onesb")])
            ws_issue_upto(NWB)

            act(MISC[:, M_CACT:M_CACT + 8], SMALL[:, S_C:S_C + 8], AF.Silu, [tt("small")], [tt("cact")])
            cpy("dve", CBF[:], MISC[:, M_CACT:M_CACT + 8], [tt("cact")], [tt("cbf")])
            cpy("dve", CBC, MISC[:, M_CACT:M_CACT + 8].unsqueeze(2).broadcast_to([128, 8, 128]), [tt("cact")], [tt("cbc")])
            act(MISC[:, M_E:M_E + 8], MISC[:, M_WD:M_WD + 8], AF.Exp, [tt("wd")], [tt("e")], scale=-1.0)
            act(MISC[:, M_E:M_E + 8], MISC[:, M_E:M_E + 8], AF.Ln, [tt("e")], [tt("e")], bias=1.0)
            tsc("dve", MISC[:, M_LG:M_LG + 8], MISC[:, M_E:M_E + 8], -1.0, None, MUL, None, [tt("e")], [tt("lg")])
            LGF = lambda h, lo=0, hi=128: MISC[lo:hi, M_LG + h:M_LG + h + 1]
            LGB = lambda h, lo=0, hi=128: MISC[lo:hi, M_LG + 4 + h:M_LG + 5 + h]
            S.add("pool", lambda e: e.iota(IO[:, 0, :], pattern=[[1, 128]], base=0, channel_multiplier=-1,
                                           allow_small_or_imprecise_dtypes=True), writes=[tt("io0")])
            S.add("pool", lambda e: e.iota(IO[:, 1, :], pattern=[[1, 128]], base=1, channel_multiplier=0,
                                           allow_small_or_imprecise_dtypes=True), writes=[tt("io1")])
            S.add("pool", lambda e: e.iota(IO[:, 2, :], pattern=[[-1, 128]], base=128, channel_multiplier=0,
                                           allow_small_or_imprecise_dtypes=True), writes=[tt("io2")])
            S.add("pool", lambda e: e.iota(MISC[:, M_IOP:M_IOP + 1], pattern=[[1, 1]], base=0, channel_multiplier=1,
                                           allow_small_or_imprecise_dtypes=True), writes=[tt("iop")])
            S.add("pool", lambda e: e.iota(MISC[:, M_IOPR:M_IOPR + 1], pattern=[[1, 1]], base=127, channel_multiplier=-1,
                                           allow_small_or_imprecise_dtypes=True), writes=[tt("iopr")])
            NEG = RRS[:, 0, 0:128]
            POS = RRS[:, 0, 128:256]
            tsc("dve", POS, IO[:, 0, :], 0.0, None, MAX, None, [tt("io0")], [tt("pos")])
            tten("dve", NEG, POS, IO[:, 0, :], ALU.subtract, [tt("pos"), tt("io0")], [tt("neg")])
            for h in range(4):
                tsc("dve", TMPD, POS, LGF(h), None, MUL, None, [tt("pos"), tt("lg")], [tt("tmpd")])
                stt("dve", TMPD, NEG, LGB(h), TMPD, MUL, ADD, [tt("neg"), tt("lg"), tt("tmpd")], [tt("tmpd")])
                act(DMK[:, h * 128:(h + 1) * 128], TMPD, AF.Exp, [tt("tmpd")], [tt("dmk")])
                tsc("dve", TMPD[0:64, :], IO[0:64, 1, :], LGF(h, 0, 64), None, MUL, None, [tt("io1"), tt("lg"), tt("dmk")], [tt("tmpd")])
                tsc("dve", TMPD[64:128, :], IO[64:128, 2, :], LGB(h, 64, 128), None, MUL, None, [tt("io2"), tt("lg"), tt("tmpd")], [tt("tmpd")])
                act(QDEC[:, h * 128:(h + 1) * 128], TMPD, AF.Exp, [tt("tmpd")], [tt("qdec")])
            KDV = MISC[:, M_KDV:M_KDV + 8].rearrange("p (h d) -> p h d", d=2)
            tsc("dve", MISC[:, M_KD8:M_KD8 + 4], MISC[:, M_LG:M_LG + 4], MISC[:, M_IOPR:M_IOPR + 1], None, MUL, None,
                [tt("lg"), tt("iopr")], [tt("kd8")])
            tsc("dve", MISC[:, M_KD8 + 4:M_KD8 + 8], MISC[:, M_LG + 4:M_LG + 8], MISC[:, M_IOP:M_IOP + 1], None, MUL, None,
                [tt("lg"), tt("iop"), tt("kd8")], [tt("kd8")])
            act(KDV[:, :, 0], MISC[:, M_KD8:M_KD8 + 4], AF.Exp, [tt("kd8")], [tt("kdv")])
            act(KDV[:, :, 1], MISC[:, M_KD8 + 4:M_KD8 + 8], AF.Exp, [tt("kd8"), tt("kdv")], [tt("kdv")])
            act(MISC[:, M_DC:M_DC + 8], MISC[:, M_LG:M_LG + 8], AF.Exp, [tt("lg")], [tt("dc")], scale=128.0)
            tsc("dve", MISC[:, M_GK8:M_GK8 + 1], SMALL[:, S_KG:S_KG + 1], 8.0, None, MUL, None, [tt("small")], [tt("gk8")])
            S.add("pool", lambda e: e.memset(SST[:], 0.0), writes=[tt("sst")])
            S.add("pool", lambda e: e.memset(VX[:], 1.0), writes=[tt("vx%d" % g) for g in range(NG)])

            ckpt('setup0')
            pmod = bank(6)[:, 0:16]
            for blk in range(4):
                wb, twb = ws_get(BLK_ADA + blk)
                wbv = wb.rearrange("p (k c) -> p k c", k=8)
                for c in range(4):
                    for kc in range(8):
                        mm(pmod[:, blk * 4 + c:blk * 4 + c + 1], wbv[:, kc, c * 128:(c + 1) * 128], CBF[:, kc:kc + 1],
                           kc == 0, kc == 7, [twb, tt("cbf")], [TB[6]])
            tten("dve", MISC[:, M_TMP8:M_TMP8 + 8], pmod[:, 8:16], SMALL[:, S_BSC:S_BSC + 8], ADD, [TB[6], tt("small")], [tt("tmp8")])
            stt("dve", MISC[:, M_ACOL:M_ACOL + 8], MISC[:, M_TMP8:M_TMP8 + 8], 1.0, SMALL[:, S_GPRE:S_GPRE + 8], ADD, MUL,
                [tt("tmp8"), tt("small")], [tt("acol")])
            tten("dve", MISC[:, M_SHCOL:M_SHCOL + 8], pmod[:, 0:8], SMALL[:, S_BSH:S_BSH + 8], ADD, [TB[6], tt("small")], [tt("shcol")])
            def gate_setup():
                dma("sp", XT[:, 0, :], rows_d[1:2, :].partition_broadcast(128), [], [tt("xt0")])
                for hf in range(2):
                    wb, twb = ws_get(BLK_ADA + 4 + hf)
                    wbv = wb.rearrange("p (k c) -> p k c", k=8)
                    for kc in range(8):
                        mm(bank(4 + hf), CBC[:, kc, :], wbv[:, kc, :], kc == 0, kc == 7, [twb, tt("cbc")], [TB[4 + hf]])
                    tten("dve", GG[:, hf * 512:(hf + 1) * 512], bank(4 + hf), GG[:, hf * 512:(hf + 1) * 512], ADD,
                         [TB[4 + hf], tt("gg")], [tt("gg")])
                    tten("dve", GG[:, hf * 512:(hf + 1) * 512], GG[:, hf * 512:(hf + 1) * 512], XT[:, 0, hf * 512:(hf + 1) * 512], MUL,
                         [tt("gg"), tt("xt0")], [tt("gg")])

            ckpt('ada')
            xt_rr = {"i": 0}

            def load_x(tile):
                i = xt_rr["i"] % NXT
                xt_rr["i"] += 1
                dma("sp", XT[:, i, :], x_d[tile * 128:(tile + 1) * 128, :], [], [tt("xt%d" % i)])
                return i

            cur = {"ht": HT, "htT": tt("ht")}

            hcfg = {"banks": [0, 1, 2, 3]}

            def _hreg(kc, jj, ntp):
                hb = hcfg["banks"]
                per = 8 // len(hb)
                bk = hb[kc // per]
                off = (kc % per) * (ntp * 128) + jj * 128
                return bk, off

            def hT_tile(tg, j):
                hb = hcfg["banks"]
                ntp = len(hb)
                tile = tg * 4 + j
                xi = load_x(tile)
                xb = j % 2
                act(JUNK, XT[:, xi, :], AF.Square, [tt("xt%d" % xi)], [tt("junk"), tt("ss")], accum_out=MISC[:, M_SS:M_SS + 1])
                act(MISC[:, M_RSTD:M_RSTD + 1], MISC[:, M_SS:M_SS + 1], AF.Ln, [tt("ss")], [tt("rstd")], scale=1.0 / DM, bias=EPS)
                act(MISC[:, M_RSTD:M_RSTD + 1], MISC[:, M_RSTD:M_RSTD + 1], AF.Exp, [tt("rstd")], [tt("rstd")], scale=-0.5)
                tsc("dve", XN[:, xb, :], XT[:, xi, :], MISC[:, M_RSTD:M_RSTD + 1], None, MUL, None,
                    [tt("xt%d" % xi), tt("rstd")], [tt("xn%d" % xb)])
                for kc in range(8):
                    bk, off = _hreg(kc, j % ntp, ntp)
                    trp(bank(bk).bitcast(BF16)[:, off:off + 128], XN[:, xb, kc * 128:(kc + 1) * 128], [tt("xn%d" % xb)], [TB[bk]])

            def hT_evac(dst, dstT, ps=0):
                hb = hcfg["banks"]
                ntp = len(hb)
                for kc in range(8):
                    bk, off = _hreg(kc, 0, ntp)
                    pb = bank(bk).bitcast(BF16)[:, off:off + ntp * 128]
                    o = dst[:, kc, ps * ntp * 128:(ps + 1) * ntp * 128]
                    if kc < 4:
                        tsc("dve", o, pb, MISC[:, M_ACOL + kc:M_ACOL + kc + 1], MISC[:, M_SHCOL + kc:M_SHCOL + kc + 1],
                            MUL, ADD, [TB[bk], tt("acol"), tt("shcol")], [dstT])
                    else:
                        act(o, pb, AF.Identity, [TB[bk], tt("acol"), tt("shcol")], [dstT],
                            scale=MISC[:, M_ACOL + kc:M_ACOL + kc + 1], bias=MISC[:, M_SHCOL + kc:M_SHCOL + kc + 1])

            def make_hT(tg, set_cur=True):
                if set_cur:
                    cur["ht"], cur["htT"] = bufs[tg % 2]
                if os.environ.get("MK_HT67", "1") == "1":
                    hcfg["banks"] = [6, 7]
                ntp = len(hcfg["banks"])
                for ps in range(4 // ntp):
                    for jj in range(ntp):
                        hT_tile(tg, ps * ntp + jj)
                    hT_evac(bufs[tg % 2][0], bufs[tg % 2][1], ps)

            def load_cs(tg):
                dma("sp", CS[:, 0, :], rope_d[0][:, tg * 512:(tg + 1) * 512], [], [tt("cs")])
                dma("sp", CS[:, 1, :], rope_d[1][:, tg * 512:(tg + 1) * 512], [], [tt("cs")])

            def proj_fm(wbv, twb, c0, bk):
                for kc in range(8):
                    mm(bank(bk), wbv[:, kc, c0:c0 + 128], cur["ht"][:, kc, :], kc == 0, kc == 7, [twb, cur["htT"]], [TB[bk]])

            def proj_tm(wbv, twb, j, c0, ncol, o, tbk):
                for kc in range(8):
                    mm(o, cur["ht"][:, kc, j * 128:(j + 1) * 128], wbv[:, kc, c0:c0 + ncol], kc == 0, kc == 7, [twb, cur["htT"]], [tbk])

            rope_ctr = {"i": 0, "alt": [(4, 5)]}

            def rope(bkA, gcol, gT, rms, out_ap, out_T):
                i = rope_ctr["i"] % 2
                rope_ctr["i"] += 1
                bS, bC = rope_ctr["alt"][i % len(rope_ctr["alt"])]
                if os.environ.get("MK_ROPE67", "1") == "1" and len(rope_ctr["alt"]) == 2:
                    bS = bC = 13 - bkA
                A = bank(bkA)
                rdg = [gT] if gT is not None else []
                gsc = gcol if gT is not None else float(gcol)
                tsc("dve", QBF[:, i, :], A, gsc, None, MUL, None, [TB[bkA]] + rdg, [tt("qbf%d" % i)])
                if rms:
                    act(SQ[:, i, :], A, AF.Square, [TB[bkA]], [tt("sq%d" % i)])
                mm(bank(bS), PERM[:], QBF[:, i, :], True, True, [tt("perm"), tt("qbf%d" % i)], [TB[bS]])
                stt("dve", T1[:, i, :], A, gsc, CS[:, 0, :], MUL, MUL, [TB[bkA], tt("cs")] + rdg, [tt("t1_%d" % i)])
                tten("dve", T2[:, i, :], bank(bS), CS[:, 1, :], MUL, [TB[bS], tt("cs")], [tt("t2_%d" % i)])
                if rms:
                    mm(bank(bC), ONESB[:], SQ[:, i, :], True, True, [tt("onesb"), tt("sq%d" % i)], [TB[bC]])
                    act(RRS[:, i, :], bank(bC), AF.Ln, [TB[bC]], [tt("rrs%d" % i)], bias=64.0 * EPS)
                    act(RRS[:, i, :], RRS[:, i, :], AF.Exp, [tt("rrs%d" % i)], [tt("rrs%d" % i)], scale=-0.5)
                    tten(os.environ.get("MK_RA", "pool"), T1[:, i, :], T1[:, i, :], T2[:, i, :], ADD, [tt("t1_%d" % i), tt("t2_%d" % i)], [tt("t1_%d" % i)])
                    tten(os.environ.get("MK_RM", "pool"), out_ap, T1[:, i, :], RRS[:, i, :], MUL, [tt("t1_%d" % i), tt("rrs%d" % i)], [out_T])
                else:
                    tten(os.environ.get("MK_RP", "pool"), out_ap, T1[:, i, :], T2[:, i, :], ADD, [tt("t1_%d" % i), tt("t2_%d" % i)], [out_T])

            def ret_rk(tg):
                wb, twb = ws_get(BLK_RK)
                wbv = wb.rearrange("p (k c) -> p k c", k=8)
                for pr in range(2):
                    proj_fm(wbv, twb, pr * 128, 6 + pr)
                    rope(6 + pr, 0.125, None, False, RKT[:, pr, :], tt("rkt"))

            def ret_rv(tg):
                wb, twb = ws_get(BLK_RV)
                wbv = wb.rearrange("p (k c) -> p k c", k=8)
                for j in range(4):
                    bk = 6 + (j % 2)
                    proj_tm(wbv, twb, j, 0, 512, bank(bk), TB[bk])
                    act(RVG[:, j, :], bank(bk), AF.Copy, [TB[bk]], [tt("rvg")])

            def ret_kv_inputs(tg):
                ret_rk(tg)
                ret_rv(tg)

            def kv_matmul(j):
                ptk = bank(6).bitcast(BF16)[:, 0:256]
                for pr in range(2):
                    trp(ptk[:, pr * 128:(pr + 1) * 128], RKT[:, pr, j * 128:(j + 1) * 128], [tt("rkt")], [TB[6]])
                kb = j % 2
                tten("dve", KK[:, kb, :].rearrange("p (h d c) -> p h d c", h=4, d=2),
                     ptk.rearrange("p (h c) -> p h c", h=4).unsqueeze(2).broadcast_to([128, 4, 2, 64]),
                     MISC[:, M_KDV:M_KDV + 8].rearrange("p (h d) -> p h d", d=2).unsqueeze(3).broadcast_to([128, 4, 2, 64]), MUL,
                     [TB[6], tt("kdv")], [tt("kk%d" % kb)])
                for h in range(4):
                    mm(bank(7)[:, h * 128:(h + 1) * 128], KK[:, kb, h * 128:(h + 1) * 128], RVG[:, j, h * 128:(h + 1) * 128],
                       True, True, [tt("kk%d" % kb), tt("rvg")], [TB[7]])

            bufs = [(HT, tt("ht")), (HT2, tt("ht2"))]
            for j in range(4):
                hT_tile(0, j)
            hT_evac(*bufs[0])
            for tg in range(NG):
                cur["ht"], cur["htT"] = bufs[tg % 2]
                nxt = bufs[(tg + 1) % 2] if tg + 1 < NG else None
                ckpt('ht%d' % tg)
                load_cs(tg)
                wb, twb = ws_get(BLK_P2KV)
                wbv = wb.rearrange("p (k c) -> p k c", k=8)
                for g in range(2):
                    proj_fm(wbv, twb, g * 128, 6 + g)
                    rope(6 + g, MISC[:, M_GK8:M_GK8 + 1], tt("gk8"), True, K2[:, g, tg * 512:(tg + 1) * 512], tt("k2_%d_%d" % (g, tg)))
                ckpt('k%d' % tg)
                if nxt:
                    hT_tile(tg + 1, 0)
                for j in range(4):
                    proj_tm(wbv, twb, j, 256, 128, bank(6)[:, j * 128:(j + 1) * 128], TB[6])
                cpy("dve", VX[:, tg * 4:(tg + 1) * 4, 64:320].rearrange("p j (g c) -> p j g c", g=2)[:, :, :, 0:64],
                    bank(6).rearrange("p (j g c) -> p j g c", j=4, g=2), [TB[6]], [tt("vx%d" % tg)])
                ckpt('v%d' % tg)
                if nxt:
                    hT_tile(tg + 1, 1)
                ret_rk(tg)
                if nxt:
                    hT_tile(tg + 1, 2)
                ret_rv(tg)
                ckpt('rkv%d' % tg)
                if nxt:
                    hT_tile(tg + 1, 3)
                    hT_evac(*nxt)
                for j in range(4):
                    ci = tg * 4 + j
                    kv_matmul(j)
                    act(RF[(ci % 2) * 64:(ci % 2) * 64 + 64, ci // 2, :], SST[0:64, :], AF.Copy, [tt("sst")], [tt("rf")])
                    tten("dve", SST[0:64, :].rearrange("p (h c) -> p h c", h=4), SST[0:64, :].rearrange("p (h c) -> p h c", h=4),
                         MISC[0:64, M_DC:M_DC + 4].unsqueeze(2).broadcast_to([64, 4, 128]), MUL, [tt("sst"), tt("dc")], [tt("sst")])
                    tten("dve", SST[0:64, :], SST[0:64, :], bank(7)[0:64, :], ADD, [tt("sst"), TB[7]], [tt("sst")])

            rope_ctr["alt"] = [(4, 5), (2, 3)]
            gate_setup()
            ckpt('p2')
            for tg in range(NG - 1, -1, -1):
                EARLY = os.environ.get("MK_EARLY", "3")
                if tg == NG - 1 or EARLY != "0":
                    cur["ht"], cur["htT"] = bufs[tg % 2]
                else:
                    make_hT(tg)
                load_cs(tg)
                wq, twq = ws_get(BLK_Q)
                wqv = wq.rearrange("p (k c) -> p k c", k=8)
                for p in range(4):
                    proj_fm(wqv, twq, p * 128, 6 + (p % 2))
                    rope(6 + (p % 2), SMALL[:, S_QG:S_QG + 1], tt("small"), True, QT[:, p, :], tt("qt%d" % p))
                wz, twz = ws_get(BLK_ZA)
                wzv = wz.rearrange("p (k c) -> p k c", k=8)
                for p in range(4):
                    bk = 6 + (p % 2)
                    proj_fm(wzv, twz, p * 128, bk)
                    act(ZA[:, p, :], bank(bk), AF.Silu, [TB[bk]], [tt("za%d" % p)])
                ckpt('q%d' % tg)
                for p in range(4):
                    g = p // 2

                    def qk(kc):
                        si = kc % 2
                        mm(bank(2 * si), K2[0:64, g, kc * 128:(kc + 1) * 128], QT[0:64, p, :], True, True,
                           [tt("k2_%d_%d" % (g, kc // 4)), tt("qt%d" % p)], [TB[2 * si]])
                        mm(bank(2 * si + 1), K2[64:128, g, kc * 128:(kc + 1) * 128], QT[64:128, p, :], True, True,
                           [tt("k2_%d_%d" % (g, kc // 4)), tt("qt%d" % p)], [TB[2 * si + 1]])

                    def pv(kc):
                        pi = kc % NPT
                        mm(bank(4), VX[:, kc, 64 + g * 128:192 + g * 128], PT[:, pi, 0:512], kc == 0, kc == 31,
                           [tt("vx%d" % (kc // 4)), tt("pt%d" % pi)], [TB[4]])
                        mm(bank(5), VX[:, kc, g * 128:g * 128 + 128], PT[:, pi, 512:1024], kc == 0, kc == 31,
                           [tt("vx%d" % (kc // 4)), tt("pt%d" % pi)], [TB[5]])

                    qk(0)
                    qk(1)
                    for kc in range(32):
                        si = kc % 2
                        act(PT[:, kc % NPT, :], bank(2 * si, 2), AF.Exp, [TB[2 * si], TB[2 * si + 1]], [tt("pt%d" % (kc % NPT))])
                        if kc + 2 < 32:
                            qk(kc + 2)
                        pv(kc)
                    cpy("dve", SA[:], bank(4), [TB[4]], [tt("sa")])
                    cpy("dve", SB_[:], bank(5), [TB[5]], [tt("sb")])
                    recip(DEN[0:64, :], SA[64:128, :], [tt("sa")], [tt("den")])
                    recip(DEN[64:128, :], SB_[0:64, :], [tt("sb")], [tt("den")])
                    tten("pool", DEN[0:64, :], SA[0:64, :], DEN[0:64, :], MUL, [tt("sa"), tt("den")], [tt("den")])
                    tten("pool", DEN[64:128, :], SB_[64:128, :], DEN[64:128, :], MUL, [tt("sb"), tt("den")], [tt("den")])
                    tten("pool", YA[:, p, :], DEN[:], ZA[:, p, :], MUL, [tt("den"), tt("za%d" % p)], [tt("ya")])
                ckpt('att%d' % tg)
                if EARLY == '1' and tg > 0:
                    make_hT(tg - 1, set_cur=False)
                wr, twr = ws_get(BLK_RQ)
                wrv = wr.rearrange("p (k c) -> p k c", k=8)
                for h in range(4):
                    proj_fm(wrv, twr, h * 128, 6 + (h % 2))
                    rope(6 + (h % 2), 1.0, None, False, RQ[:, h, :], tt("rq"))
                    tten("pool", QD[:, h, :].rearrange("p (j c) -> p j c", j=4), RQ[:, h, :].rearrange("p (j c) -> p j c", j=4),
                         QDEC[:, h * 128:(h + 1) * 128].unsqueeze(1).broadcast_to([128, 4, 128]), MUL, [tt("rq"), tt("qdec")], [tt("qd")])
                ckpt('rqd%d' % tg)
                if tg != NG - 1:
                    ret_kv_inputs(tg)
                wz, twz = ws_get(BLK_ZR)
                wzv = wz.rearrange("p (k c) -> p k c", k=8)
                for h in range(4):
                    bk = 6 + (h % 2)
                    proj_fm(wzv, twz, h * 128, bk)
                    act(ZR[:, h, :], bank(bk), AF.Silu, [TB[bk]], [tt("zr")])
                ckpt('rin%d' % tg)
                for j in range(3, -1, -1):
                    ci = tg * 4 + j
                    sb_i = j % 2
                    for h in range(4):
                        r0 = (h % 2) * 64
                        mm(bank(h % 2)[:, (h // 2) * 128:(h // 2 + 1) * 128], RKT[r0:r0 + 64, h // 2, j * 128:(j + 1) * 128],
                           RQ[r0:r0 + 64, h, j * 128:(j + 1) * 128], True, True, [tt("rkt"), tt("rq")], [TB[h % 2]])
                    for hb in range(2):
                        tten("dve", SM[:, sb_i, :].rearrange("p (a b c) -> p a b c", a=2, b=2)[:, :, hb, :],
                             bank(hb)[:, 0:256].rearrange("p (a c) -> p a c", a=2),
                             DMK[:].rearrange("p (a b c) -> p a b c", a=2, b=2)[:, :, hb, :], MUL, [TB[hb], tt("dmk")], [tt("sm%d" % sb_i)])
                    ckpt('rs%d' % ci)
                    act(RRO[0:64, sb_i, :], RF[(ci % 2) * 64:(ci % 2) * 64 + 64, ci // 2, :], AF.Copy, [tt("rf")], [tt("rro%d" % sb_i)])
                    cpy("pool", RRO[64:128, sb_i, :], SST[64:128, :], [tt("sst")], [tt("rro%d" % sb_i)])
                    for h in range(4):
                        mm(bank(2)[:, h * 128:(h + 1) * 128], SM[:, sb_i, h * 128:(h + 1) * 128], RVG[:, j, h * 128:(h + 1) * 128],
                           True, False, [tt("sm%d" % sb_i), tt("rvg")], [TB[2]])
                        mm(bank(2)[:, h * 128:(h + 1) * 128], QD[:, h, j * 128:(j + 1) * 128], RRO[:, sb_i, h * 128:(h + 1) * 128],
                           False, True, [tt("qd"), tt("rro%d" % sb_i)], [TB[2]])
                    ckpt('ro%d' % ci)
                    for h in range(4):
                        bnstats(STAT[:, sb_i, h, 0:6], bank(2)[:, h * 128:(h + 1) * 128], [TB[2]], [tt("stat%d" % sb_i)])
                        bnaggr(STAT[:, sb_i, h, 6:8], STAT[:, sb_i, h, 0:6], [tt("stat%d" % sb_i)], [tt("stat%d" % sb_i)])
                    gr = MISC[:, M_GNR + sb_i * 4:M_GNR + sb_i * 4 + 4]
                    gn = MISC[:, M_GNN + sb_i * 4:M_GNN + sb_i * 4 + 4]
                    act(gr, STAT[:, sb_i, :, 7], AF.Ln, [tt("stat%d" % sb_i)], [tt("gr%d" % sb_i)], bias=EPS)
                    act(gr, gr, AF.Exp, [tt("gr%d" % sb_i)], [tt("gr%d" % sb_i)], scale=-0.5)
                    stt("dve", gn, STAT[:, sb_i, :, 6], -1.0, gr, MUL, MUL, [tt("stat%d" % sb_i), tt("gr%d" % sb_i)], [tt("gn%d" % sb_i)])
                    for h in range(4):
                        tsc("dve", ON[:, sb_i, h * 128:(h + 1) * 128], bank(2)[:, h * 128:(h + 1) * 128],
                            MISC[:, M_GNR + sb_i * 4 + h:M_GNR + sb_i * 4 + h + 1], MISC[:, M_GNN + sb_i * 4 + h:M_GNN + sb_i * 4 + h + 1],
                            MUL, ADD, [TB[2], tt("gr%d" % sb_i), tt("gn%d" % sb_i)], [tt("on%d" % sb_i)])
                    ckpt('rgn%d' % ci)
                    ptn = bank(3).bitcast(BF16)[:, 0:512]
                    for h in range(4):
                        trp(ptn[:, h * 128:(h + 1) * 128], ON[:, sb_i, h * 128:(h + 1) * 128], [tt("on%d" % sb_i)], [TB[3]])
                    for h in range(4):
                        stt("dve", YR[:, h, j * 128:(j + 1) * 128], ptn[:, h * 128:(h + 1) * 128], SMALL[:, S_GN + h:S_GN + h + 1],
                            ZR[:, h, j * 128:(j + 1) * 128], MUL, MUL, [TB[3], tt("small"), tt("zr")], [tt("yr")])
                    ckpt('ryr%d' % ci)
                    kv_matmul(j)
                    tten("dve", SST[64:128, :].rearrange("p (h c) -> p h c", h=4), SST[64:128, :].rearrange("p (h c) -> p h c", h=4),
                         MISC[64:128, M_DC + 4:M_DC + 8].unsqueeze(2).broadcast_to([64, 4, 128]), MUL, [tt("sst"), tt("dc")], [tt("sst")])
                    tten("dve", SST[64:128, :], SST[64:128, :], bank(7)[64:128, :], ADD, [tt("sst"), TB[7]], [tt("sst")])
                ckpt('ret%d' % tg)
                if EARLY == '2' and tg > 0:
                    make_hT(tg - 1, set_cur=False)
                for n in range(8):
                    wm, twm = ws_get(BLK_M + n)
                    glv = wm[:, 0:2048].rearrange("p (k c) -> p k c", k=8)
                    pav = wm[:, 2048:2560].rearrange("p (k c) -> p k c", k=4)
                    prv = wm[:, 2560:3072].rearrange("p (k c) -> p k c", k=4)
                    gi = n % NGT
                    b0 = 4 * (n % 2)
                    for br in range(2):
                        bk = b0 + 2 + br
                        for kc in range(8):
                            mm(bank(bk), glv[:, kc, br * 128:(br + 1) * 128], cur["ht"][:, kc, :], kc == 0, kc == 7, [twm, cur["htT"]], [TB[bk]])
                        act(GATE[:, gi, br, :], bank(bk), AF.Sigmoid, [TB[bk]], [tt("gate%d_%d" % (gi, br))])
                    for cc in range(4):
                        mm(bank(b0), pav[:, cc, :], YA[:, cc, :], cc == 0, cc == 3, [twm, tt("ya")], [TB[b0]])
                    for cc in range(4):
                        mm(bank(b0 + 1), prv[:, cc, :], YR[:, cc, :], cc == 0, cc == 3, [twm, tt("yr")], [TB[b0 + 1]])
                    tten("dve", MT[:, 0, :], bank(b0), GATE[:, gi, 0, :], MUL, [TB[b0], tt("gate%d_0" % gi)], [tt("mt0")])
                    tten("dve", MT[:, 1, :], bank(b0 + 1), GATE[:, gi, 1, :], MUL, [TB[b0 + 1], tt("gate%d_1" % gi)], [tt("mt1")])
                    tten("pool", MG[:, n, :], MT[:, 0, :], MT[:, 1, :], ADD, [tt("mt0"), tt("mt1")], [tt("mg")])
                ckpt('mrg%d' % tg)
                if EARLY == '3' and tg > 0:
                    make_hT(tg - 1, set_cur=False)
                wo0, two0 = ws_get(BLK_WO)
                wo1, two1 = ws_get(BLK_WO + 1, la=NWB - 1)
                wov = [wo0.rearrange("p (k c) -> p k c", k=8), wo1.rearrange("p (k c) -> p k c", k=8)]
                twos = [two0, two1]
                for j in range(4):
                    tile = tg * 4 + j
                    pb0 = 2 * j
                    for hf in range(2):
                        for kc in range(8):
                            mm(bank(pb0 + hf), MG[:, kc, j * 128:(j + 1) * 128], wov[hf][:, kc, :], kc == 0, kc == 7,
                               [tt("mg"), twos[hf]], [TB[pb0 + hf]])
                    xi = load_x(tile)
                    act(JUNK, bank(pb0, 2), AF.Square, [TB[pb0], TB[pb0 + 1]], [tt("junk"), tt("ss2")], accum_out=MISC[:, M_SS + 1:M_SS + 2])
                    act(MISC[:, M_RSTD + 1:M_RSTD + 2], MISC[:, M_SS + 1:M_SS + 2], AF.Ln, [tt("ss2")], [tt("rstd2")], scale=1.0 / DM, bias=EPS)
                    act(MISC[:, M_RSTD + 1:M_RSTD + 2], MISC[:, M_RSTD + 1:M_RSTD + 2], AF.Exp, [tt("rstd2")], [tt("rstd2")], scale=-0.5)
                    stt("dve", bank(pb0, 2), bank(pb0, 2), MISC[:, M_RSTD + 1:M_RSTD + 2], GG[:], MUL, MUL,
                        [TB[pb0], TB[pb0 + 1], tt("rstd2"), tt("gg")], [TB[pb0], TB[pb0 + 1]])
                    tten("dve", XT[:, xi, :], bank(pb0, 2), XT[:, xi, :], ADD, [TB[pb0], TB[pb0 + 1], tt("xt%d" % xi)], [tt("xt%d" % xi)])
                    dma(os.environ.get("MK_STQ", "sp"), out_d[tile * 128:(tile + 1) * 128, :], XT[:, xi, :], [tt("xt%d" % xi)], [])
        try:
            run_all()
        except _Stop:
            pass
        dump_aps = {"HT": HT, "K2": K2, "VX": VX, "RF": RF, "MISC": MISC, "GG": GG, "DMK": DMK, "QDEC": QDEC,
                    "YA": YA, "YR": YR, "MG": MG, "QT": QT, "ZA": ZA, "RKT": RKT, "RVG": RVG, "SST": SST,
                    "RQ": RQ, "QD": QD, "ZR": ZR, "XN": XN, "CS": CS}
        for nm in dumps:
            src = dump_aps[nm]
            shp = list(src.shape)
            flat = int(np.prod(shp[1:]))
            dd = nc.dram_tensor("dbg_" + nm, [128, flat], src.dtype, kind="ExternalOutput").ap()
            allT = list(tdict.values()) + TB
            sv = src[:] if len(shp) == 2 else src[:].rearrange("p a b -> p (a b)") if len(shp) == 3 else src[:].rearrange("p a b c -> p (a b c)")
            dma("sp", dd, sv, allT, [])
        import os
        if os.environ.get('MK_RESCHED', '1') == '1':
            S.reschedule()
        if not _EXP:
            S.emit(nc)
    return nc, S


_CACHE = {}


def kernel(x, c, w_ada, b_ada, g_pre, w_in, qn_g, kn_g, w_dec_f, w_dec_b, gn_g, w_pa, w_pr, w_out, g_post):
    x = np.asarray(x, np.float32)
    c = np.asarray(c, np.float32)
    f = lambda a: np.asarray(a, np.float32)[0]
    w_ada, b_ada, g_pre, w_in, qn_g, kn_g = f(w_ada), f(b_ada), f(g_pre), f(w_in), f(qn_g), f(kn_g)
    w_dec_f, w_dec_b, gn_g, w_pa, w_pr, w_out, g_post = f(w_dec_f), f(w_dec_b), f(gn_g), f(w_pa), f(w_pr), f(w_out), f(g_post)
    if "nc" not in _CACHE:
        _CACHE["nc"] = build_program()[0]
        _CACHE["consts"] = _host_consts()
    nc = _CACHE["nc"]
    rope, cst = _CACHE["consts"]
    blocks = _host_blocks(w_ada, w_in, w_pa, w_pr, w_out)
    col8 = lambda v: np.ascontiguousarray(v.reshape(8, 128).T)
    rows = np.ascontiguousarray(np.stack([b_ada[2048:3072], g_post], 0))
    wdec = np.concatenate([w_dec_f, w_dec_b])[None, :].astype(np.float32)
    in_maps = []
    for b in range(8):
        small = np.zeros((128, NSMALL), np.float32)
        small[:, 0:8] = col8(c[b])
        small[:, 8:16] = col8(g_pre)
        small[:, 16:24] = col8(b_ada[1024:2048])
        small[:, 24:32] = col8(b_ada[0:1024])
        small[:, 32] = np.concatenate([qn_g, qn_g])
        small[:, 33] = np.concatenate([kn_g, kn_g])
        small[:, 34:38] = gn_g.reshape(4, 128).T
        in_maps.append({"x": np.ascontiguousarray(x[b]), "wsrc": blocks, "rope": rope, "cst": cst,
                        "small": small, "rows": rows, "wdec": wdec})
    res = run_bass_kernel_spmd(nc, in_maps, core_ids=list(range(8)))
    return np.stack([np.asarray(r["out"], np.float32) for r in res.results], 0)
```

```python
import numpy as np
import concourse.bass as bass
import concourse.mybir as mybir
from concourse.bass_utils import run_bass_kernel_spmd

F32 = mybir.dt.float32
BF16 = mybir.dt.bfloat16
AF = mybir.ActivationFunctionType
ALU = mybir.AluOpType
AX = mybir.AxisListType


class T:
    __slots__ = ("name", "last_w", "readers", "excl")

    def __init__(self, name="", excl=False):
        self.name = name
        self.last_w = None
        self.readers = []
        self.excl = excl


class Op:
    __slots__ = ("eng", "fn", "deps", "signal", "token", "is_dma", "idx", "gidx", "preds", "succs", "dur", "lat", "cls", "prio", "t_start", "t_done", "npred", "tag")

    def __init__(self, eng, fn, is_dma):
        self.eng = eng
        self.fn = fn
        self.deps = []
        self.signal = False
        self.token = None
        self.is_dma = is_dma
        self.preds = []
        self.succs = []
        self.dur = 0.2
        self.lat = 0.0
        self.cls = None


class Sched:
    ENGS = ("pe", "act", "dve", "pool", "sp")

    def __init__(self, n_dma_sems=8):
        self.ops = {e: [] for e in self.ENGS}
        self.n_dma_sems = n_dma_sems
        self.dma_rr = {e: 0 for e in self.ENGS}
        self.dma_last = {}
        self.gcount = 0

    def add(self, eng, fn, reads=(), writes=(), dma=False, dur=0.2, lat=0.0, cls=None):
        op = Op(eng, fn, dma)
        op.dur, op.lat, op.cls = dur, lat, cls
        import sys as _sys
        op.tag = (_sys._getframe(2).f_lineno, _sys._getframe(3).f_lineno)
        op.gidx = self.gcount
        self.gcount += 1
        writes = list(writes) + [t for t in reads if t.excl]
        reads = [t for t in reads if not t.excl]
        raw = []
        war = []
        for t in reads:
            if t.last_w is not None:
                raw.append(t.last_w)
        for t in writes:
            if t.last_w is not None:
                raw.append(t.last_w)
            war.extend(t.readers)
        for t in reads:
            t.readers.append(op)
        for t in writes:
            t.last_w = op
            t.readers = []
        if dma:
            k = self.dma_rr[eng]
            self.dma_rr[eng] = (k + 1) % self.n_dma_sems
            prev = self.dma_last.get((eng, k))
            if prev is not None:
                raw.append(prev)
            self.dma_last[(eng, k)] = op
            op.token = (eng, k)
        seen = set()
        pseen = set()
        for lst, is_war in ((raw, False), (war, True)):
            for d in lst:
                if d is op or id(d) in seen:
                    continue
                if id(d) not in pseen:
                    pseen.add(id(d))
                    op.preds.append(d)
                if (not d.is_dma) and (not dma) and d.eng == eng:
                    if eng == "pe" or is_war:
                        continue
                seen.add(id(d))
                op.deps.append(d)
        op.idx = len(self.ops[eng])
        self.ops[eng].append(op)
        return op

    def reschedule(self, hop=0.15, act_switch=1.4):
        import heapq
        allops = []
        for e in self.ENGS:
            allops.extend(self.ops[e])
        allops.sort(key=lambda o: o.gidx)
        for o in allops:
            o.succs = []
        for o in allops:
            o.npred = len(o.preds)
            for p in o.preds:
                p.succs.append(o)
        for o in reversed(allops):
            m = 0.0
            for s_ in o.succs:
                if s_.prio > m:
                    m = s_.prio
            o.prio = m + o.dur + o.lat
        ready = {e: [] for e in self.ENGS}
        for o in allops:
            o.t_done = None
            if o.npred == 0:
                ready[o.eng].append(o)
        free_at = {e: 0.0 for e in self.ENGS}
        cur_cls = {e: None for e in self.ENGS}
        new_order = {e: [] for e in self.ENGS}
        nleft = len(allops)

        def rtime(o):
            t = 0.0
            for p in o.preds:
                tp = p.t_done + (hop if (p.eng != o.eng or p.is_dma) else 0.0)
                if tp > t:
                    t = tp
            return t

        rt_cache = {}
        while nleft:
            best = None
            for e in self.ENGS:
                lst = ready[e]
                if not lst:
                    continue
                fa = free_at[e]
                for o in lst:
                    r = rt_cache.get(id(o))
                    if r is None:
                        r = rtime(o)
                        rt_cache[id(o)] = r
                    st = r if r > fa else fa
                    if e == "act" and o.cls is not None and cur_cls[e] is not None and o.cls != cur_cls[e]:
                        st += act_switch
                    key = (st, -o.prio)
                    if best is None or key < best[0]:
                        best = (key, o)
            (st, _), o = best
            e = o.eng
            ready[e].remove(o)
            o.t_start = st
            free_at[e] = st + o.dur
            o.t_done = st + o.dur + o.lat
            if e == "act" and o.cls is not None:
                cur_cls[e] = o.cls
            new_order[e].append(o)
            nleft -= 1
            for s_ in o.succs:
                s_.npred -= 1
                if s_.npred == 0:
                    ready[s_.eng].append(s_)
        for e in self.ENGS:
            self.ops[e] = new_order[e]
        self.makespan = max(free_at.values())

    def emit(self, nc, block_ctx_extra=None):
        for e in self.ENGS:
            for op in self.ops[e]:
                for d in op.deps:
                    d.signal = True
        import contextlib
        with contextlib.ExitStack() as st:
            sems = {e: st.enter_context(nc.semaphore("s_" + e)) for e in ("pe", "act", "dve", "pool")}
            dsems = {}
            for e in self.ENGS:
                if any(o.is_dma for o in self.ops[e]):
                    for k in range(self.n_dma_sems):
                        dsems[(e, k)] = st.enter_context(nc.semaphore("d_%s_%d" % (e, k)))
            for e in self.ENGS:
                cnt = 0
                dcnt = {}
                for op in self.ops[e]:
                    if op.is_dma:
                        key = op.token
                        dcnt[key] = dcnt.get(key, 0) + 16
                        op.token = (dsems[key], dcnt[key], key)
                    elif op.signal:
                        cnt += 1
                        op.token = (sems[e], cnt, e)
            block = st.enter_context(nc.Block())
            handles = {"pe": block.tensor, "act": block.scalar, "dve": block.vector, "pool": block.gpsimd,
                       "sp": block.sync}
            nwaits = {e: 0 for e in self.ENGS}

            def make(e):
                ops = self.ops[e]

                def body(eng):
                    known = {}
                    for op in ops:
                        need = {}
                        for d in op.deps:
                            sem, val, key = d.token
                            if known.get(key, 0) >= val:
                                continue
                            if key not in need or need[key][1] < val:
                                need[key] = (sem, val)
                        for key, (sem, val) in need.items():
                            eng.wait_ge(sem, val)
                            known[key] = val
                            nwaits[e] += 1
                        inst = op.fn(eng)
                        if op.is_dma:
                            inst.then_inc(op.token[0], 16)
                        elif op.signal:
                            inst.then_inc(op.token[0], 1)
                    fin = {}
                    for op in ops:
                        if op.is_dma:
                            fin[op.token[2]] = (op.token[0], op.token[1])
                    for key, (sem, val) in fin.items():
                        if known.get(key, 0) < val:
                            eng.wait_ge(sem, val)
                return body

            for e in self.ENGS:
                if self.ops[e]:
                    handles[e](make(e))
            self.nwaits = nwaits

import contextlib

SEQ = 4096
DM = 1024
NG = 8
EPS = 1e-6

BLK_ADA = 0
BLK_P2KV = 6
BLK_RK = 7
BLK_RV = 8
BLK_Q = 9
BLK_ZA = 10
BLK_RQ = 11
BLK_ZR = 12
BLK_M = 13
BLK_WO = 21
NBLK = 23
NSMALL = 40


def _host_blocks(w_ada, w_in, w_pa, w_pr, w_out):
    blocks = np.zeros((NBLK, 128, 4096), np.float32)

    def kmaj(w):
        nc_ = w.shape[1]
        t = np.zeros((128, 8, 512), np.float32)
        t[:, :, :nc_] = w.reshape(8, 128, nc_).transpose(1, 0, 2)
        return t.reshape(128, 4096)

    wa = w_ada
    for j in range(6):
        blocks[BLK_ADA + j] = kmaj(wa[:, j * 512:(j + 1) * 512])
    o_q, o_k, o_v, o_za, o_rq, o_rk, o_rv, o_zr, o_gl = 0, 512, 640, 768, 1280, 1536, 1792, 2304, 2816
    k0 = w_in[:, o_k:o_k + 64]
    k1 = w_in[:, o_k + 64:o_k + 128]
    blocks[BLK_P2KV] = kmaj(np.concatenate([k0, k0, k1, k1, w_in[:, o_v:o_v + 128]], axis=1))
    blocks[BLK_RK] = kmaj(w_in[:, o_rk:o_rk + 256])
    blocks[BLK_RV] = kmaj(w_in[:, o_rv:o_rv + 512])
    blocks[BLK_Q] = kmaj(w_in[:, o_q:o_q + 512])
    blocks[BLK_ZA] = kmaj(w_in[:, o_za:o_za + 512])
    rq = [w_in[:, o_rq + 64 * h:o_rq + 64 * (h + 1)] for h in range(4)]
    blocks[BLK_RQ] = kmaj(np.concatenate([rq[0], rq[0], rq[1], rq[1], rq[2], rq[2], rq[3], rq[3]], axis=1))
    blocks[BLK_ZR] = kmaj(w_in[:, o_zr:o_zr + 512])
    for n in range(8):
        t = np.zeros((128, 4096), np.float32)
        gl = np.concatenate([w_in[:, o_gl + n * 128:o_gl + (n + 1) * 128],
                             w_in[:, o_gl + 1024 + n * 128:o_gl + 1024 + (n + 1) * 128]], axis=1)
        t[:, 0:2048] = gl.reshape(8, 128, 256).transpose(1, 0, 2).reshape(128, 2048)
        t[:, 2048:2560] = w_pa[:, n * 128:(n + 1) * 128].reshape(4, 128, 128).transpose(1, 0, 2).reshape(128, 512)
        t[:, 2560:3072] = w_pr[:, n * 128:(n + 1) * 128].reshape(4, 128, 128).transpose(1, 0, 2).reshape(128, 512)
        blocks[BLK_M + n] = t
    for hf in range(2):
        blocks[BLK_WO + hf] = kmaj(w_out[:, hf * 512:(hf + 1) * 512])
    return blocks


def _host_consts():
    t = np.arange(SEQ)
    row = (t // 64).astype(np.float32)
    col = (t % 64).astype(np.float32)
    inv = (10000.0 ** (-np.arange(0, 32, 2, dtype=np.float32) / 32.0)).astype(np.float32)
    ang_r = row[None, :] * inv[:, None]
    ang_c = col[None, :] * inv[:, None]
    cos64 = np.concatenate([np.cos(ang_r), np.cos(ang_r), np.cos(ang_c), np.cos(ang_c)], 0)
    sin64 = np.concatenate([np.sin(ang_r), np.sin(ang_r), np.sin(ang_c), np.sin(ang_c)], 0)
    rope = np.stack([np.concatenate([cos64, cos64], 0), np.concatenate([sin64, sin64], 0)], 0).astype(np.float32)
    ident = np.eye(128, dtype=np.float32)
    perm = np.zeros((128, 128), np.float32)
    for hb in range(2):
        for d in range(64):
            blk = d // 16
            if blk % 2 == 0:
                p, s = d + 16, -1.0
            else:
                p, s = d - 16, 1.0
            perm[hb * 64 + p, hb * 64 + d] = s
    ones = np.zeros((128, 128), np.float32)
    ones[:64, :64] = 1.0
    ones[64:, 64:] = 1.0
    return rope, np.stack([ident, perm, ones], 0)


class _Stop(Exception):
    pass


def build_program(stop=None, dumps=()):
    nc = bass.Bass("TRN2", target_bir_lowering=False)
    x_d = nc.dram_tensor("x", [SEQ, DM], F32, kind="ExternalInput").ap()
    wsrc = nc.dram_tensor("wsrc", [NBLK, 128, 4096], F32, kind="ExternalInput").ap()
    rope_d = nc.dram_tensor("rope", [2, 128, SEQ], F32, kind="ExternalInput").ap()
    cst_d = nc.dram_tensor("cst", [3, 128, 128], F32, kind="ExternalInput").ap()
    small_d = nc.dram_tensor("small", [128, NSMALL], F32, kind="ExternalInput").ap()
    rows_d = nc.dram_tensor("rows", [2, DM], F32, kind="ExternalInput").ap()
    wdec_d = nc.dram_tensor("wdec", [1, 8], F32, kind="ExternalInput").ap()
    out_d = nc.dram_tensor("out", [SEQ, DM], F32, kind="ExternalOutput").ap()
    wbf_d = nc.dram_tensor("wbf", [NBLK, 128, 4096], BF16, kind="Internal").ap()

    S = Sched(n_dma_sems=12)
    tdict = {}

    import os
    _alias = {"io0": "sa", "io1": "sa", "io2": "sa", "tmpd": "sa", "cbc": "mt1", "junk": "mt0"}
    if os.environ.get("MK_JUNK") == "1":
        del _alias["junk"]

    def tt(name):
        name = _alias.get(name, name)
        if name not in tdict:
            tdict[name] = T(name)
        return tdict[name]

    with contextlib.ExitStack() as st:
        def sb(name, shape, dt):
            return st.enter_context(nc.sbuf_tensor(name, shape, dt))

        PS = st.enter_context(nc.psum_tensor("ps", [128, 4096], F32))
        TB = [T("bank%d" % i, excl=True) for i in range(8)]

        def bank(i, n=1):
            return PS[:, i * 512:(i + n) * 512]

        import os
        _EXP = os.environ.get("MK_EXPLORE") == "1"
        _PDT = mybir.dt.int8 if _EXP else BF16
        K2 = sb("K2", [128, 2, SEQ], _PDT)
        VX = sb("VX", [128, 32, 320], _PDT)
        RF = sb("RF", [128, 16, 512], _PDT)
        NWB = int(os.environ.get('MK_NWB', '3'))
        WB = sb("WB", [128, NWB, 4096], BF16)
        NXT = int(os.environ.get('MK_NXT', '3'))
        XT = sb("XT", [128, NXT, DM], F32)
        XN = sb("XN", [128, 2, DM], BF16)
        HT = sb("HT", [128, 8, 512], BF16)
        HT2 = sb("HT2", [128, 8, 512], BF16)
        CS = sb("CS", [128, 2, 512], F32)
        QBF = sb("QBF", [128, 2, 512], BF16)
        SQ = sb("SQ", [128, 2, 512], BF16)
        T1 = sb("T1", [128, 2, 512], F32)
        T2 = sb("T2", [128, 2, 512], F32)
        RRS = sb("RRS", [128, 2, 512], F32)
        QT = sb("QT", [128, 4, 512], BF16)
        ZA = sb("ZA", [128, 4, 512], BF16)
        NPT = int(os.environ.get('MK_NPT', '3'))
        PT = sb("PT", [128, NPT, 1024], BF16)
        SA = sb("SA", [128, 512], F32)
        SB_ = sb("SB_", [128, 512], F32)
        DEN = sb("DEN", [128, 512], F32)
        YA = sb("YA", [128, 4, 512], BF16)
        YR = sb("YR", [128, 4, 512], BF16)
        RQ = sb("RQ", [128, 4, 512], BF16)
        QD = sb("QD", [128, 4, 512], BF16)
        ZR = sb("ZR", [128, 4, 512], BF16)
        RKT = sb("RKT", [128, 2, 512], BF16)
        RVG = sb("RVG", [128, 4, 512], BF16)
        KK = sb("KK", [128, 2, 512], BF16)
        SM = sb("SM", [128, 2, 512], BF16)
        RRO = sb("RRO", [128, 2, 512], BF16)
        ON = sb("ON", [128, 2, 512], BF16)
        SST = sb("SST", [128, 512], F32)
        NGT = int(os.environ.get("MK_NGT", "1"))
        GATE = sb("GATE", [128, NGT, 2, 512], BF16)
        MT = sb("MT", [128, 2, 512], F32)
        MG = sb("MG", [128, 8, 512], BF16)
        IDENT = sb("IDENT", [128, 128], BF16)
        PERM = sb("PERM", [128, 128], BF16)
        ONESB = sb("ONESB", [128, 128], BF16)
        DMK = sb("DMK", [128, 512], F32)
        QDEC = sb("QDEC", [128, 512], F32)
        GG = sb("GG", [128, DM], F32)
        SMALL = sb("SMALL", [128, NSMALL], F32)
        MISC = sb("MISC", [128, 128], F32)
        IO = SA[:, 0:384].rearrange("p (a b) -> p a b", a=3)
        TMPD = SA[:, 384:512]
        CBC = MT[:, 1, :].bitcast(BF16).rearrange("p (a b) -> p a b", a=8)
        JUNK = sb("JUNKB", [128, DM], BF16)[:] if os.environ.get("MK_JUNK") == "1" else MT[:, 0, :].bitcast(BF16)
        CBF = sb("CBF", [128, 8], BF16)
        STAT = sb("STAT", [128, 2, 4, 8], F32)

        _CACHE['sbuf_left'] = nc.sbuf_bytes_remaining
        M_ACOL, M_SHCOL, M_LG, M_DC, M_WD, M_E, M_KD8, M_KDV, M_IOP, M_IOPR, M_GK8 = 0, 8, 16, 24, 32, 40, 48, 56, 64, 65, 66
        M_SS, M_RSTD, M_CACT, M_TMP8 = 70, 74, 80, 88
        M_GNR, M_GNN = 96, 104
        S_C, S_GPRE, S_BSC, S_BSH, S_QG, S_KG, S_GN = 0, 8, 16, 24, 32, 33, 34

        def fsz(ap):
            n = 1
            for d in list(ap.shape)[1:]:
                n *= int(d)
            return n

        def mm(o, l, r, start, stop, rd, wr):
            n = fsz(o)
            k = int(l.shape[0])
            d = 0.06 + n * 0.00041
            if k <= 64 and n >= 256:
                d = 0.17
            S.add("pe", lambda e: e.matmul(o, l, r, start=start, stop=stop), reads=rd, writes=wr, dur=d)

        def trp(o, i, rd, wr):
            S.add("pe", lambda e: e.transpose(o, i, IDENT[:]), reads=rd + [tt("ident")], writes=wr, dur=0.115)

        _ACLS = {AF.Exp: "exp", AF.Ln: "exp", AF.Silu: "sig", AF.Sigmoid: "sig"}

        def act(o, i, f, rd, wr, **kw):
            S.add("act", lambda e: e.activation(out=o, in_=i, func=f, **kw), reads=rd, writes=wr,
                  dur=0.22 + fsz(i) * 0.00083 + (0.1 if "accum_out" in kw else 0.0), cls=_ACLS.get(f))

        def _vd(eng, n):
            return (0.1 + n * 0.00104) if eng == "dve" else (0.15 + n * 0.0021)

        def tten(eng, o, a, b, op, rd, wr):
            S.add(eng, lambda e: e.tensor_tensor(out=o, in0=a, in1=b, op=op), reads=rd, writes=wr, dur=_vd(eng, fsz(o)))

        def tsc(eng, o, a, s1, s2, op0, op1, rd, wr):
            if op1 is None:
                S.add(eng, lambda e: e.tensor_scalar(out=o, in0=a, scalar1=s1, scalar2=None, op0=op0), reads=rd, writes=wr, dur=_vd(eng, fsz(o)))
            else:
                S.add(eng, lambda e: e.tensor_scalar(out=o, in0=a, scalar1=s1, scalar2=s2, op0=op0, op1=op1), reads=rd, writes=wr, dur=_vd(eng, fsz(o)))

        def stt(eng, o, a, s, b, op0, op1, rd, wr):
            S.add(eng, lambda e: e.scalar_tensor_tensor(out=o, in0=a, scalar=s, in1=b, op0=op0, op1=op1), reads=rd, writes=wr, dur=_vd(eng, fsz(o)))

        def cpy(eng, o, i, rd, wr):
            S.add(eng, lambda e: e.tensor_copy(out=o, in_=i), reads=rd, writes=wr, dur=_vd(eng, fsz(o)))

        def dma(eng, o, i, rd, wr):
            nbytes = fsz(o) * int(o.shape[0]) * (2 if o.dtype == BF16 else 4)
            S.add(eng, lambda e: e.dma_start(out=o, in_=i), reads=rd, writes=wr, dma=True,
                  dur=(0.08 if eng == "sp" else 1.0), lat=2.0 + nbytes / 150000.0)

        MUL, ADD, MAX = ALU.mult, ALU.add, ALU.max

        def ckpt(name):
            if stop == name:
                raise _Stop()

        def bnstats(o, i, rd, wr):
            S.add("dve", lambda e: e.bn_stats(out=o, in_=i), reads=rd, writes=wr, dur=0.25)

        def bnaggr(o, i, rd, wr):
            S.add("dve", lambda e: e.bn_aggr(out=o, in_=i), reads=rd, writes=wr, dur=0.15)

        def recip(o, i, rd, wr):
            S.add("dve", lambda e: e.reciprocal(out=o, in_=i), reads=rd, writes=wr, dur=0.1 + fsz(o) * 0.00625)

        order = [BLK_ADA + j for j in range(4)]
        for tg in range(NG):
            order += [BLK_P2KV, BLK_RK, BLK_RV]
        order += [BLK_ADA + 4, BLK_ADA + 5]
        for tg in range(NG):
            order += [BLK_Q, BLK_ZA, BLK_RQ] + ([BLK_RK, BLK_RV] if tg > 0 else []) + [BLK_ZR] \
                + [BLK_M + n for n in range(8)] + [BLK_WO, BLK_WO + 1]
        conv_done = set()
        first_direct = set()
        ws_state = {"issued": 0, "cur": 0}

        def conv(blk):
            if blk in conv_done:
                return
            conv_done.add(blk)
            dma("pool", wbf_d[blk], wsrc[blk], [], [tt("wbf%d" % blk)])

        def ws_issue_upto(n):
            while ws_state["issued"] < min(n, len(order)):
                i = ws_state["issued"]
                blk = order[i]
                if BLK_ADA <= blk < BLK_ADA + 6 or (blk in (BLK_P2KV, BLK_RK, BLK_RV) and blk not in first_direct):
                    first_direct.add(blk)
                    dma("pool", WB[:, i % NWB, :], wsrc[blk], [], [tt("wb%d" % (i % NWB))])
                else:
                    conv(blk)
                    dma("sp", WB[:, i % NWB, :], wbf_d[blk], [tt("wbf%d" % blk)], [tt("wb%d" % (i % NWB))])
                ws_state["issued"] += 1

        def ws_get(expect, la=None):
            i = ws_state["cur"]
            assert order[i] == expect, (i, order[i], expect)
            ws_issue_upto(i + (NWB if la is None else la))
            ws_state["cur"] += 1
            return WB[:, i % NWB, :], tt("wb%d" % (i % NWB))

        def run_all():
            for blk in [BLK_P2KV, BLK_RK, BLK_RV]:
                conv(blk)
            dma("sp", SMALL[:], small_d, [], [tt("small")])
            dma("sp", GG[:], rows_d[0:1, :].partition_broadcast(128), [], [tt("gg")])
            dma("sp", MISC[:, M_WD:M_WD + 8], wdec_d.partition_broadcast(128), [], [tt("wd")])
            dma("pool", IDENT[:], cst_d[0], [], [tt("ident")])
            dma("pool", PERM[:], cst_d[1], [], [tt("perm")])
            dma("pool", ONESB[:], cst_d[2], [], [tt("onesb")])
            ws_issue_upto(NWB)

            act(MISC[:, M_CACT:M_CACT + 8], SMALL[:, S_C:S_C + 8], AF.Silu, [tt("small")], [tt("cact")])
            cpy("dve", CBF[:], MISC[:, M_CACT:M_CACT + 8], [tt("cact")], [tt("cbf")])
            cpy("dve", CBC, MISC[:, M_CACT:M_CACT + 8].unsqueeze(2).broadcast_to([128, 8, 128]), [tt("cact")], [tt("cbc")])
            act(MISC[:, M_E:M_E + 8], MISC[:, M_WD:M_WD + 8], AF.Exp, [tt("wd")], [tt("e")], scale=-1.0)
            act(MISC[:, M_E:M_E + 8], MISC[:, M_E:M_E + 8], AF.Ln, [tt("e")], [tt("e")], bias=1.0)
            tsc("dve", MISC[:, M_LG:M_LG + 8], MISC[:, M_E:M_E + 8], -1.0, None, MUL, None, [tt("e")], [tt("lg")])
            LGF = lambda h, lo=0, hi=128: MISC[lo:hi, M_LG + h:M_LG + h + 1]
            LGB = lambda h, lo=0, hi=128: MISC[lo:hi, M_LG + 4 + h:M_LG + 5 + h]
            S.add("pool", lambda e: e.iota(IO[:, 0, :], pattern=[[1, 128]], base=0, channel_multiplier=-1,
                                           allow_small_or_imprecise_dtypes=True), writes=[tt("io0")])
            S.add("pool", lambda e: e.iota(IO[:, 1, :], pattern=[[1, 128]], base=1, channel_multiplier=0,
                                           allow_small_or_imprecise_dtypes=True), writes=[tt("io1")])
            S.add("pool", lambda e: e.iota(IO[:, 2, :], pattern=[[-1, 128]], base=128, channel_multiplier=0,
                                           allow_small_or_imprecise_dtypes=True), writes=[tt("io2")])
            S.add("pool", lambda e: e.iota(MISC[:, M_IOP:M_IOP + 1], pattern=[[1, 1]], base=0, channel_multiplier=1,
                                           allow_small_or_imprecise_dtypes=True), writes=[tt("iop")])
            S.add("pool", lambda e: e.iota(MISC[:, M_IOPR:M_IOPR + 1], pattern=[[1, 1]], base=127, channel_multiplier=-1,
                                           allow_small_or_imprecise_dtypes=True), writes=[tt("iopr")])
            NEG = RRS[:, 0, 0:128]
            POS = RRS[:, 0, 128:256]
            tsc("dve", POS, IO[:, 0, :], 0.0, None, MAX, None, [tt("io0")], [tt("pos")])
            tten("dve", NEG, POS, IO[:, 0, :], ALU.subtract, [tt("pos"), tt("io0")], [tt("neg")])
            for h in range(4):
                tsc("dve", TMPD, POS, LGF(h), None, MUL, None, [tt("pos"), tt("lg")], [tt("tmpd")])
                stt("dve", TMPD, NEG, LGB(h), TMPD, MUL, ADD, [tt("neg"), tt("lg"), tt("tmpd")], [tt("tmpd")])
                act(DMK[:, h * 128:(h + 1) * 128], TMPD, AF.Exp, [tt("tmpd")], [tt("dmk")])
                tsc("dve", TMPD[0:64, :], IO[0:64, 1, :], LGF(h, 0, 64), None, MUL, None, [tt("io1"), tt("lg"), tt("dmk")], [tt("tmpd")])
                tsc("dve", TMPD[64:128, :], IO[64:128, 2, :], LGB(h, 64, 128), None, MUL, None, [tt("io2"), tt("lg"), tt("tmpd")], [tt("tmpd")])
                act(QDEC[:, h * 128:(h + 1) * 128], TMPD, AF.Exp, [tt("tmpd")], [tt("qdec")])
            KDV = MISC[:, M_KDV:M_KDV + 8].rearrange("p (h d) -> p h d", d=2)
            tsc("dve", MISC[:, M_KD8:M_KD8 + 4], MISC[:, M_LG:M_LG + 4], MISC[:, M_IOPR:M_IOPR + 1], None, MUL, None,
                [tt("lg"), tt("iopr")], [tt("kd8")])
            tsc("dve", MISC[:, M_KD8 + 4:M_KD8 + 8], MISC[:, M_LG + 4:M_LG + 8], MISC[:, M_IOP:M_IOP + 1], None, MUL, None,
                [tt("lg"), tt("iop"), tt("kd8")], [tt("kd8")])
            act(KDV[:, :, 0], MISC[:, M_KD8:M_KD8 + 4], AF.Exp, [tt("kd8")], [tt("kdv")])
            act(KDV[:, :, 1], MISC[:, M_KD8 + 4:M_KD8 + 8], AF.Exp, [tt("kd8"), tt("kdv")], [tt("kdv")])
            act(MISC[:, M_DC:M_DC + 8], MISC[:, M_LG:M_LG + 8], AF.Exp, [tt("lg")], [tt("dc")], scale=128.0)
            tsc("dve", MISC[:, M_GK8:M_GK8 + 1], SMALL[:, S_KG:S_KG + 1], 8.0, None, MUL, None, [tt("small")], [tt("gk8")])
            S.add("pool", lambda e: e.memset(SST[:], 0.0), writes=[tt("sst")])
            S.add("pool", lambda e: e.memset(VX[:], 1.0), writes=[tt("vx%d" % g) for g in range(NG)])

            ckpt('setup0')
            pmod = bank(6)[:, 0:16]
            for blk in range(4):
                wb, twb = ws_get(BLK_ADA + blk)
                wbv = wb.rearrange("p (k c) -> p k c", k=8)
                for c in range(4):
                    for kc in range(8):
                        mm(pmod[:, blk * 4 + c:blk * 4 + c + 1], wbv[:, kc, c * 128:(c + 1) * 128], CBF[:, kc:kc + 1],
                           kc == 0, kc == 7, [twb, tt("cbf")], [TB[6]])
            tten("dve", MISC[:, M_TMP8:M_TMP8 + 8], pmod[:, 8:16], SMALL[:, S_BSC:S_BSC + 8], ADD, [TB[6], tt("small")], [tt("tmp8")])
            stt("dve", MISC[:, M_ACOL:M_ACOL + 8], MISC[:, M_TMP8:M_TMP8 + 8], 1.0, SMALL[:, S_GPRE:S_GPRE + 8], ADD, MUL,
                [tt("tmp8"), tt("small")], [tt("acol")])
            tten("dve", MISC[:, M_SHCOL:M_SHCOL + 8], pmod[:, 0:8], SMALL[:, S_BSH:S_BSH + 8], ADD, [TB[6], tt("small")], [tt("shcol")])
            def gate_setup():
                dma("sp", XT[:, 0, :], rows_d[1:2, :].partition_broadcast(128), [], [tt("xt0")])
                for hf in range(2):
                    wb, twb = ws_get(BLK_ADA + 4 + hf)
                    wbv = wb.rearrange("p (k c) -> p k c", k=8)
                    for kc in range(8):
                        mm(bank(4 + hf), CBC[:, kc, :], wbv[:, kc, :], kc == 0, kc == 7, [twb, tt("cbc")], [TB[4 + hf]])
                    tten("dve", GG[:, hf * 512:(hf + 1) * 512], bank(4 + hf), GG[:, hf * 512:(hf + 1) * 512], ADD,
                         [TB[4 + hf], tt("gg")], [tt("gg")])
                    tten("dve", GG[:, hf * 512:(hf + 1) * 512], GG[:, hf * 512:(hf + 1) * 512], XT[:, 0, hf * 512:(hf + 1) * 512], MUL,
                         [tt("gg"), tt("xt0")], [tt("gg")])

            ckpt('ada')
            xt_rr = {"i": 0}

            def load_x(tile):
                i = xt_rr["i"] % NXT
                xt_rr["i"] += 1
                dma("sp", XT[:, i, :], x_d[tile * 128:(tile + 1) * 128, :], [], [tt("xt%d" % i)])
                return i

            cur = {"ht": HT, "htT": tt("ht")}

            hcfg = {"banks": [0, 1, 2, 3]}

            def _hreg(kc, jj, ntp):
                hb = hcfg["banks"]
                per = 8 // len(hb)
                bk = hb[kc // per]
                off = (kc % per) * (ntp * 128) + jj * 128
                return bk, off

            def hT_tile(tg, j):
                hb = hcfg["banks"]
                ntp = len(hb)
                tile = tg * 4 + j
                xi = load_x(tile)
                xb = j % 2
                act(JUNK, XT[:, xi, :], AF.Square, [tt("xt%d" % xi)], [tt("junk"), tt("ss")], accum_out=MISC[:, M_SS:M_SS + 1])
                act(MISC[:, M_RSTD:M_RSTD + 1], MISC[:, M_SS:M_SS + 1], AF.Ln, [tt("ss")], [tt("rstd")], scale=1.0 / DM, bias=EPS)
                act(MISC[:, M_RSTD:M_RSTD + 1], MISC[:, M_RSTD:M_RSTD + 1], AF.Exp, [tt("rstd")], [tt("rstd")], scale=-0.5)
                tsc("dve", XN[:, xb, :], XT[:, xi, :], MISC[:, M_RSTD:M_RSTD + 1], None, MUL, None,
                    [tt("xt%d" % xi), tt("rstd")], [tt("xn%d" % xb)])
                for kc in range(8):
                    bk, off = _hreg(kc, j % ntp, ntp)
                    trp(bank(bk).bitcast(BF16)[:, off:off + 128], XN[:, xb, kc * 128:(kc + 1) * 128], [tt("xn%d" % xb)], [TB[bk]])

            def hT_evac(dst, dstT, ps=0):
                hb = hcfg["banks"]
                ntp = len(hb)
                for kc in range(8):
                    bk, off = _hreg(kc, 0, ntp)
                    pb = bank(bk).bitcast(BF16)[:, off:off + ntp * 128]
                    o = dst[:, kc, ps * ntp * 128:(ps + 1) * ntp * 128]
                    if kc < 4:
                        tsc("dve", o, pb, MISC[:, M_ACOL + kc:M_ACOL + kc + 1], MISC[:, M_SHCOL + kc:M_SHCOL + kc + 1],
                            MUL, ADD, [TB[bk], tt("acol"), tt("shcol")], [dstT])
                    else:
                        act(o, pb, AF.Identity, [TB[bk], tt("acol"), tt("shcol")], [dstT],
                            scale=MISC[:, M_ACOL + kc:M_ACOL + kc + 1], bias=MISC[:, M_SHCOL + kc:M_SHCOL + kc + 1])

            def make_hT(tg, set_cur=True):
                if set_cur:
                    cur["ht"], cur["htT"] = bufs[tg % 2]
                if os.environ.get("MK_HT67", "1") == "1":
                    hcfg["banks"] = [6, 7]
                ntp = len(hcfg["banks"])
                for ps in range(4 // ntp):
                    for jj in range(ntp):
                        hT_tile(tg, ps * ntp + jj)
                    hT_evac(bufs[tg % 2][0], bufs[tg % 2][1], ps)

            def load_cs(tg):
                dma("sp", CS[:, 0, :], rope_d[0][:, tg * 512:(tg + 1) * 512], [], [tt("cs")])
                dma("sp", CS[:, 1, :], rope_d[1][:, tg * 512:(tg + 1) * 512], [], [tt("cs")])

            def proj_fm(wbv, twb, c0, bk):
                for kc in range(8):
                    mm(bank(bk), wbv[:, kc, c0:c0 + 128], cur["ht"][:, kc, :], kc == 0, kc == 7, [twb, cur["htT"]], [TB[bk]])

            def proj_tm(wbv, twb, j, c0, ncol, o, tbk):
                for kc in range(8):
                    mm(o, cur["ht"][:, kc, j * 128:(j + 1) * 128], wbv[:, kc, c0:c0 + ncol], kc == 0, kc == 7, [twb, cur["htT"]], [tbk])

            rope_ctr = {"i": 0, "alt": [(4, 5)]}

            def rope(bkA, gcol, gT, rms, out_ap, out_T):
                i = rope_ctr["i"] % 2
                rope_ctr["i"] += 1
                bS, bC = rope_ctr["alt"][i % len(rope_ctr["alt"])]
                if os.environ.get("MK_ROPE67", "1") == "1" and len(rope_ctr["alt"]) == 2:
                    bS = bC = 13 - bkA
                A = bank(bkA)
                rdg = [gT] if gT is not None else []
                gsc = gcol if gT is not None else float(gcol)
                tsc("dve", QBF[:, i, :], A, gsc, None, MUL, None, [TB[bkA]] + rdg, [tt("qbf%d" % i)])
                if rms:
                    act(SQ[:, i, :], A, AF.Square, [TB[bkA]], [tt("sq%d" % i)])
                mm(bank(bS), PERM[:], QBF[:, i, :], True, True, [tt("perm"), tt("qbf%d" % i)], [TB[bS]])
                stt("dve", T1[:, i, :], A, gsc, CS[:, 0, :], MUL, MUL, [TB[bkA], tt("cs")] + rdg, [tt("t1_%d" % i)])
                tten("dve", T2[:, i, :], bank(bS), CS[:, 1, :], MUL, [TB[bS], tt("cs")], [tt("t2_%d" % i)])
                if rms:
                    mm(bank(bC), ONESB[:], SQ[:, i, :], True, True, [tt("onesb"), tt("sq%d" % i)], [TB[bC]])
                    act(RRS[:, i, :], bank(bC), AF.Ln, [TB[bC]], [tt("rrs%d" % i)], bias=64.0 * EPS)
                    act(RRS[:, i, :], RRS[:, i, :], AF.Exp, [tt("rrs%d" % i)], [tt("rrs%d" % i)], scale=-0.5)
                    tten(os.environ.get("MK_RA", "pool"), T1[:, i, :], T1[:, i, :], T2[:, i, :], ADD, [tt("t1_%d" % i), tt("t2_%d" % i)], [tt("t1_%d" % i)])
                    tten(os.environ.get("MK_RM", "pool"), out_ap, T1[:, i, :], RRS[:, i, :], MUL, [tt("t1_%d" % i), tt("rrs%d" % i)], [out_T])
                else:
                    tten(os.environ.get("MK_RP", "pool"), out_ap, T1[:, i, :], T2[:, i, :], ADD, [tt("t1_%d" % i), tt("t2_%d" % i)], [out_T])

            def ret_rk(tg):
                wb, twb = ws_get(BLK_RK)
                wbv = wb.rearrange("p (k c) -> p k c", k=8)
                for pr in range(2):
                    proj_fm(wbv, twb, pr * 128, 6 + pr)
                    rope(6 + pr, 0.125, None, False, RKT[:, pr, :], tt("rkt"))

            def ret_rv(tg):
                wb, twb = ws_get(BLK_RV)
                wbv = wb.rearrange("p (k c) -> p k c", k=8)
                for j in range(4):
                    bk = 6 + (j % 2)
                    proj_tm(wbv, twb, j, 0, 512, bank(bk), TB[bk])
                    act(RVG[:, j, :], bank(bk), AF.Copy, [TB[bk]], [tt("rvg")])

            def ret_kv_inputs(tg):
                ret_rk(tg)
                ret_rv(tg)

            def kv_matmul(j):
                ptk = bank(6).bitcast(BF16)[:, 0:256]
                for pr in range(2):
                    trp(ptk[:, pr * 128:(pr + 1) * 128], RKT[:, pr, j * 128:(j + 1) * 128], [tt("rkt")], [TB[6]])
                kb = j % 2
                tten("dve", KK[:, kb, :].rearrange("p (h d c) -> p h d c", h=4, d=2),
                     ptk.rearrange("p (h c) -> p h c", h=4).unsqueeze(2).broadcast_to([128, 4, 2, 64]),
                     MISC[:, M_KDV:M_KDV + 8].rearrange("p (h d) -> p h d", d=2).unsqueeze(3).broadcast_to([128, 4, 2, 64]), MUL,
                     [TB[6], tt("kdv")], [tt("kk%d" % kb)])
                for h in range(4):
                    mm(bank(7)[:, h * 128:(h + 1) * 128], KK[:, kb, h * 128:(h + 1) * 128], RVG[:, j, h * 128:(h + 1) * 128],
                       True, True, [tt("kk%d" % kb), tt("rvg")], [TB[7]])

            bufs = [(HT, tt("ht")), (HT2, tt("ht2"))]
            for j in range(4):
                hT_tile(0, j)
            hT_evac(*bufs[0])
            for tg in range(NG):
                cur["ht"], cur["htT"] = bufs[tg % 2]
                nxt = bufs[(tg + 1) % 2] if tg + 1 < NG else None
                ckpt('ht%d' % tg)
                load_cs(tg)
                wb, twb = ws_get(BLK_P2KV)
                wbv = wb.rearrange("p (k c) -> p k c", k=8)
                for g in range(2):
                    proj_fm(wbv, twb, g * 128, 6 + g)
                    rope(6 + g, MISC[:, M_GK8:M_GK8 + 1], tt("gk8"), True, K2[:, g, tg * 512:(tg + 1) * 512], tt("k2_%d_%d" % (g, tg)))
                ckpt('k%d' % tg)
                if nxt:
                    hT_tile(tg + 1, 0)
                for j in range(4):
                    proj_tm(wbv, twb, j, 256, 128, bank(6)[:, j * 128:(j + 1) * 128], TB[6])
                cpy("dve", VX[:, tg * 4:(tg + 1) * 4, 64:320].rearrange("p j (g c) -> p j g c", g=2)[:, :, :, 0:64],
                    bank(6).rearrange("p (j g c) -> p j g c", j=4, g=2), [TB[6]], [tt("vx%d" % tg)])
                ckpt('v%d' % tg)
                if nxt:
                    hT_tile(tg + 1, 1)
                ret_rk(tg)
                if nxt:
                    hT_tile(tg + 1, 2)
                ret_rv(tg)
                ckpt('rkv%d' % tg)
                if nxt:
                    hT_tile(tg + 1, 3)
                    hT_evac(*nxt)
                for j in range(4):
                    ci = tg * 4 + j
                    kv_matmul(j)
                    act(RF[(ci % 2) * 64:(ci % 2) * 64 + 64, ci // 2, :], SST[0:64, :], AF.Copy, [tt("sst")], [tt("rf")])
                    tten("dve", SST[0:64, :].rearrange("p (h c) -> p h c", h=4), SST[0:64, :].rearrange("p (h c) -> p h c", h=4),
                         MISC[0:64, M_DC:M_DC + 4].unsqueeze(2).broadcast_to([64, 4, 128]), MUL, [tt("sst"), tt("dc")], [tt("sst")])
                    tten("dve", SST[0:64, :], SST[0:64, :], bank(7)[0:64, :], ADD, [tt("sst"), TB[7]], [tt("sst")])

            rope_ctr["alt"] = [(4, 5), (2, 3)]
            gate_setup()
            ckpt('p2')
            for tg in range(NG - 1, -1, -1):
                EARLY = os.environ.get("MK_EARLY", "3")
                if tg == NG - 1 or EARLY != "0":
                    cur["ht"], cur["htT"] = bufs[tg % 2]
                else:
                    make_hT(tg)
                load_cs(tg)
                wq, twq = ws_get(BLK_Q)
                wqv = wq.rearrange("p (k c) -> p k c", k=8)
                for p in range(4):
                    proj_fm(wqv, twq, p * 128, 6 + (p % 2))
                    rope(6 + (p % 2), SMALL[:, S_QG:S_QG + 1], tt("small"), True, QT[:, p, :], tt("qt%d" % p))
                wz, twz = ws_get(BLK_ZA)
                wzv = wz.rearrange("p (k c) -> p k c", k=8)
                for p in range(4):
                    bk = 6 + (p % 2)
                    proj_fm(wzv, twz, p * 128, bk)
                    act(ZA[:, p, :], bank(bk), AF.Silu, [TB[bk]], [tt("za%d" % p)])
                ckpt('q%d' % tg)
                for p in range(4):
                    g = p // 2

                    def qk(kc):
                        si = kc % 2
                        mm(bank(2 * si), K2[0:64, g, kc * 128:(kc + 1) * 128], QT[0:64, p, :], True, True,
                           [tt("k2_%d_%d" % (g, kc // 4)), tt("qt%d" % p)], [TB[2 * si]])
                        mm(bank(2 * si + 1), K2[64:128, g, kc * 128:(kc + 1) * 128], QT[64:128, p, :], True, True,
                           [tt("k2_%d_%d" % (g, kc // 4)), tt("qt%d" % p)], [TB[2 * si + 1]])

                    def pv(kc):
                        pi = kc % NPT
                        mm(bank(4), VX[:, kc, 64 + g * 128:192 + g * 128], PT[:, pi, 0:512], kc == 0, kc == 31,
                           [tt("vx%d" % (kc // 4)), tt("pt%d" % pi)], [TB[4]])
                        mm(bank(5), VX[:, kc, g * 128:g * 128 + 128], PT[:, pi, 512:1024], kc == 0, kc == 31,
                           [tt("vx%d" % (kc // 4)), tt("pt%d" % pi)], [TB[5]])

                    qk(0)
                    qk(1)
                    for kc in range(32):
                        si = kc % 2
                        act(PT[:, kc % NPT, :], bank(2 * si, 2), AF.Exp, [TB[2 * si], TB[2 * si + 1]], [tt("pt%d" % (kc % NPT))])
                        if kc + 2 < 32:
                            qk(kc + 2)
                        pv(kc)
                    cpy("dve", SA[:], bank(4), [TB[4]], [tt("sa")])
                    cpy("dve", SB_[:], bank(5), [TB[5]], [tt("sb")])
                    recip(DEN[0:64, :], SA[64:128, :], [tt("sa")], [tt("den")])
                    recip(DEN[64:128, :], SB_[0:64, :], [tt("sb")], [tt("den")])
                    tten("pool", DEN[0:64, :], SA[0:64, :], DEN[0:64, :], MUL, [tt("sa"), tt("den")], [tt("den")])
                    tten("pool", DEN[64:128, :], SB_[64:128, :], DEN[64:128, :], MUL, [tt("sb"), tt("den")], [tt("den")])
                    tten("pool", YA[:, p, :], DEN[:], ZA[:, p, :], MUL, [tt("den"), tt("za%d" % p)], [tt("ya")])
                ckpt('att%d' % tg)
                if EARLY == '1' and tg > 0:
                    make_hT(tg - 1, set_cur=False)
                wr, twr = ws_get(BLK_RQ)
                wrv = wr.rearrange("p (k c) -> p k c", k=8)
                for h in range(4):
                    proj_fm(wrv, twr, h * 128, 6 + (h % 2))
                    rope(6 + (h % 2), 1.0, None, False, RQ[:, h, :], tt("rq"))
                    tten("pool", QD[:, h, :].rearrange("p (j c) -> p j c", j=4), RQ[:, h, :].rearrange("p (j c) -> p j c", j=4),
                         QDEC[:, h * 128:(h + 1) * 128].unsqueeze(1).broadcast_to([128, 4, 128]), MUL, [tt("rq"), tt("qdec")], [tt("qd")])
                ckpt('rqd%d' % tg)
                if tg != NG - 1:
                    ret_kv_inputs(tg)
                wz, twz = ws_get(BLK_ZR)
                wzv = wz.rearrange("p (k c) -> p k c", k=8)
                for h in range(4):
                    bk = 6 + (h % 2)
                    proj_fm(wzv, twz, h * 128, bk)
                    act(ZR[:, h, :], bank(bk), AF.Silu, [TB[bk]], [tt("zr")])
                ckpt('rin%d' % tg)
                for j in range(3, -1, -1):
                    ci = tg * 4 + j
                    sb_i = j % 2
                    for h in range(4):
                        r0 = (h % 2) * 64
                        mm(bank(h % 2)[:, (h // 2) * 128:(h // 2 + 1) * 128], RKT[r0:r0 + 64, h // 2, j * 128:(j + 1) * 128],
                           RQ[r0:r0 + 64, h, j * 128:(j + 1) * 128], True, True, [tt("rkt"), tt("rq")], [TB[h % 2]])
                    for hb in range(2):
                        tten("dve", SM[:, sb_i, :].rearrange("p (a b c) -> p a b c", a=2, b=2)[:, :, hb, :],
                             bank(hb)[:, 0:256].rearrange("p (a c) -> p a c", a=2),
                             DMK[:].rearrange("p (a b c) -> p a b c", a=2, b=2)[:, :, hb, :], MUL, [TB[hb], tt("dmk")], [tt("sm%d" % sb_i)])
                    ckpt('rs%d' % ci)
                    act(RRO[0:64, sb_i, :], RF[(ci % 2) * 64:(ci % 2) * 64 + 64, ci // 2, :], AF.Copy, [tt("rf")], [tt("rro%d" % sb_i)])
                    cpy("pool", RRO[64:128, sb_i, :], SST[64:128, :], [tt("sst")], [tt("rro%d" % sb_i)])
                    for h in range(4):
                        mm(bank(2)[:, h * 128:(h + 1) * 128], SM[:, sb_i, h * 128:(h + 1) * 128], RVG[:, j, h * 128:(h + 1) * 128],
                           True, False, [tt("sm%d" % sb_i), tt("rvg")], [TB[2]])
                        mm(bank(2)[:, h * 128:(h + 1) * 128], QD[:, h, j * 128:(j + 1) * 128], RRO[:, sb_i, h * 128:(h + 1) * 128],
                           False, True, [tt("qd"), tt("rro%d" % sb_i)], [TB[2]])
                    ckpt('ro%d' % ci)
                    for h in range(4):
                        bnstats(STAT[:, sb_i, h, 0:6], bank(2)[:, h * 128:(h + 1) * 128], [TB[2]], [tt("stat%d" % sb_i)])
                        bnaggr(STAT[:, sb_i, h, 6:8], STAT[:, sb_i, h, 0:6], [tt("stat%d" % sb_i)], [tt("stat%d" % sb_i)])
                    gr = MISC[:, M_GNR + sb_i * 4:M_GNR + sb_i * 4 + 4]
                    gn = MISC[:, M_GNN + sb_i * 4:M_GNN + sb_i * 4 + 4]
                    act(gr, STAT[:, sb_i, :, 7], AF.Ln, [tt("stat%d" % sb_i)], [tt("gr%d" % sb_i)], bias=EPS)
                    act(gr, gr, AF.Exp, [tt("gr%d" % sb_i)], [tt("gr%d" % sb_i)], scale=-0.5)
                    stt("dve", gn, STAT[:, sb_i, :, 6], -1.0, gr, MUL, MUL, [tt("stat%d" % sb_i), tt("gr%d" % sb_i)], [tt("gn%d" % sb_i)])
                    for h in range(4):
                        tsc("dve", ON[:, sb_i, h * 128:(h + 1) * 128], bank(2)[:, h * 128:(h + 1) * 128],
                            MISC[:, M_GNR + sb_i * 4 + h:M_GNR + sb_i * 4 + h + 1], MISC[:, M_GNN + sb_i * 4 + h:M_GNN + sb_i * 4 + h + 1],
                            MUL, ADD, [TB[2], tt("gr%d" % sb_i), tt("gn%d" % sb_i)], [tt("on%d" % sb_i)])
                    ckpt('rgn%d' % ci)
                    ptn = bank(3).bitcast(BF16)[:, 0:512]
                    for h in range(4):
                        trp(ptn[:, h * 128:(h + 1) * 128], ON[:, sb_i, h * 128:(h + 1) * 128], [tt("on%d" % sb_i)], [TB[3]])
                    for h in range(4):
                        stt("dve", YR[:, h, j * 128:(j + 1) * 128], ptn[:, h * 128:(h + 1) * 128], SMALL[:, S_GN + h:S_GN + h + 1],
                            ZR[:, h, j * 128:(j + 1) * 128], MUL, MUL, [TB[3], tt("small"), tt("zr")], [tt("yr")])
                    ckpt('ryr%d' % ci)
                    kv_matmul(j)
                    tten("dve", SST[64:128, :].rearrange("p (h c) -> p h c", h=4), SST[64:128, :].rearrange("p (h c) -> p h c", h=4),
                         MISC[64:128, M_DC + 4:M_DC + 8].unsqueeze(2).broadcast_to([64, 4, 128]), MUL, [tt("sst"), tt("dc")], [tt("sst")])
                    tten("dve", SST[64:128, :], SST[64:128, :], bank(7)[64:128, :], ADD, [tt("sst"), TB[7]], [tt("sst")])
                ckpt('ret%d' % tg)
                if EARLY == '2' and tg > 0:
                    make_hT(tg - 1, set_cur=False)
                for n in range(8):
                    wm, twm = ws_get(BLK_M + n)
                    glv = wm[:, 0:2048].rearrange("p (k c) -> p k c", k=8)
                    pav = wm[:, 2048:2560].rearrange("p (k c) -> p k c", k=4)
                    prv = wm[:, 2560:3072].rearrange("p (k c) -> p k c", k=4)
                    gi = n % NGT
                    b0 = 4 * (n % 2)
                    for br in range(2):
                        bk = b0 + 2 + br
                        for kc in range(8):
                            mm(bank(bk), glv[:, kc, br * 128:(br + 1) * 128], cur["ht"][:, kc, :], kc == 0, kc == 7, [twm, cur["htT"]], [TB[bk]])
                        act(GATE[:, gi, br, :], bank(bk), AF.Sigmoid, [TB[bk]], [tt("gate%d_%d" % (gi, br))])
                    for cc in range(4):
                        mm(bank(b0), pav[:, cc, :], YA[:, cc, :], cc == 0, cc == 3, [twm, tt("ya")], [TB[b0]])
                    for cc in range(4):
                        mm(bank(b0 + 1), prv[:, cc, :], YR[:, cc, :], cc == 0, cc == 3, [twm, tt("yr")], [TB[b0 + 1]])
                    tten("dve", MT[:, 0, :], bank(b0), GATE[:, gi, 0, :], MUL, [TB[b0], tt("gate%d_0" % gi)], [tt("mt0")])
                    tten("dve", MT[:, 1, :], bank(b0 + 1), GATE[:, gi, 1, :], MUL, [TB[b0 + 1], tt("gate%d_1" % gi)], [tt("mt1")])
                    tten("pool", MG[:, n, :], MT[:, 0, :], MT[:, 1, :], ADD, [tt("mt0"), tt("mt1")], [tt("mg")])
                ckpt('mrg%d' % tg)
                if EARLY == '3' and tg > 0:
                    make_hT(tg - 1, set_cur=False)
                wo0, two0 = ws_get(BLK_WO)
                wo1, two1 = ws_get(BLK_WO + 1, la=NWB - 1)
                wov = [wo0.rearrange("p (k c) -> p k c", k=8), wo1.rearrange("p (k c) -> p k c", k=8)]
                twos = [two0, two1]
                for j in range(4):
                    tile = tg * 4 + j
                    pb0 = 2 * j
                    for hf in range(2):
                        for kc in range(8):
                            mm(bank(pb0 + hf), MG[:, kc, j * 128:(j + 1) * 128], wov[hf][:, kc, :], kc == 0, kc == 7,
                               [tt("mg"), twos[hf]], [TB[pb0 + hf]])
                    xi = load_x(tile)
                    act(JUNK, bank(pb0, 2), AF.Square, [TB[pb0], TB[pb0 + 1]], [tt("junk"), tt("ss2")], accum_out=MISC[:, M_SS + 1:M_SS + 2])
                    act(MISC[:, M_RSTD + 1:M_RSTD + 2], MISC[:, M_SS + 1:M_SS + 2], AF.Ln, [tt("ss2")], [tt("rstd2")], scale=1.0 / DM, bias=EPS)
                    act(MISC[:, M_RSTD + 1:M_RSTD + 2], MISC[:, M_RSTD + 1:M_RSTD + 2], AF.Exp, [tt("rstd2")], [tt("rstd2")], scale=-0.5)
                    stt("dve", bank(pb0, 2), bank(pb0, 2), MISC[:, M_RSTD + 1:M_RSTD + 2], GG[:], MUL, MUL,
                        [TB[pb0], TB[pb0 + 1], tt("rstd2"), tt("gg")], [TB[pb0], TB[pb0 + 1]])
                    tten("dve", XT[:, xi, :], bank(pb0, 2), XT[:, xi, :], ADD, [TB[pb0], TB[pb0 + 1], tt("xt%d" % xi)], [tt("xt%d" % xi)])
                    dma(os.environ.get("MK_STQ", "sp"), out_d[tile * 128:(tile + 1) * 128, :], XT[:, xi, :], [tt("xt%d" % xi)], [])
        try:
            run_all()
        except _Stop:
            pass
        dump_aps = {"HT": HT, "K2": K2, "VX": VX, "RF": RF, "MISC": MISC, "GG": GG, "DMK": DMK, "QDEC": QDEC,
                    "YA": YA, "YR": YR, "MG": MG, "QT": QT, "ZA": ZA, "RKT": RKT, "RVG": RVG, "SST": SST,
                    "RQ": RQ, "QD": QD, "ZR": ZR, "XN": XN, "CS": CS}
        for nm in dumps:
            src = dump_aps[nm]
            shp = list(src.shape)
            flat = int(np.prod(shp[1:]))
            dd = nc.dram_tensor("dbg_" + nm, [128, flat], src.dtype, kind="ExternalOutput").ap()
            allT = list(tdict.values()) + TB
            sv = src[:] if len(shp) == 2 else src[:].rearrange("p a b -> p (a b)") if len(shp) == 3 else src[:].rearrange("p a b c -> p (a b c)")
            dma("sp", dd, sv, allT, [])
        import os
        if os.environ.get('MK_RESCHED', '1') == '1':
            S.reschedule()
        if not _EXP:
            S.emit(nc)
    return nc, S


_CACHE = {}


def kernel(x, c, w_ada, b_ada, g_pre, w_in, qn_g, kn_g, w_dec_f, w_dec_b, gn_g, w_pa, w_pr, w_out, g_post):
    x = np.asarray(x, np.float32)
    c = np.asarray(c, np.float32)
    f = lambda a: np.asarray(a, np.float32)[0]
    w_ada, b_ada, g_pre, w_in, qn_g, kn_g = f(w_ada), f(b_ada), f(g_pre), f(w_in), f(qn_g), f(kn_g)
    w_dec_f, w_dec_b, gn_g, w_pa, w_pr, w_out, g_post = f(w_dec_f), f(w_dec_b), f(gn_g), f(w_pa), f(w_pr), f(w_out), f(g_post)
    if "nc" not in _CACHE:
        _CACHE["nc"] = build_program()[0]
        _CACHE["consts"] = _host_consts()
    nc = _CACHE["nc"]
    rope, cst = _CACHE["consts"]
    blocks = _host_blocks(w_ada, w_in, w_pa, w_pr, w_out)
    col8 = lambda v: np.ascontiguousarray(v.reshape(8, 128).T)
    rows = np.ascontiguousarray(np.stack([b_ada[2048:3072], g_post], 0))
    wdec = np.concatenate([w_dec_f, w_dec_b])[None, :].astype(np.float32)
    in_maps = []
    for b in range(8):
        small = np.zeros((128, NSMALL), np.float32)
        small[:, 0:8] = col8(c[b])
        small[:, 8:16] = col8(g_pre)
        small[:, 16:24] = col8(b_ada[1024:2048])
        small[:, 24:32] = col8(b_ada[0:1024])
        small[:, 32] = np.concatenate([qn_g, qn_g])
        small[:, 33] = np.concatenate([kn_g, kn_g])
        small[:, 34:38] = gn_g.reshape(4, 128).T
        in_maps.append({"x": np.ascontiguousarray(x[b]), "wsrc": blocks, "rope": rope, "cst": cst,
                        "small": small, "rows": rows, "wdec": wdec})
    res = run_bass_kernel_spmd(nc, in_maps, core_ids=list(range(8)))
    return np.stack([np.asarray(r["out"], np.float32) for r in res.results], 0)
```

```python
import numpy as np
import concourse.bass as bass
import concourse.mybir as mybir
from concourse.bass_utils import run_bass_kernel_spmd

F32 = mybir.dt.float32
BF16 = mybir.dt.bfloat16
AF = mybir.ActivationFunctionType
ALU = mybir.AluOpType
AX = mybir.AxisListType


class T:
    __slots__ = ("name", "last_w", "readers", "excl")

    def __init__(self, name="", excl=False):
        self.name = name
        self.last_w = None
        self.readers = []
        self.excl = excl


class Op:
    __slots__ = ("eng", "fn", "deps", "signal", "token", "is_dma", "idx", "gidx", "preds", "succs", "dur", "lat", "cls", "prio", "t_start", "t_done", "npred", "tag")

    def __init__(self, eng, fn, is_dma):
        self.eng = eng
        self.fn = fn
        self.deps = []
        self.signal = False
        self.token = None
        self.is_dma = is_dma
        self.preds = []
        self.succs = []
        self.dur = 0.2
        self.lat = 0.0
        self.cls = None


class Sched:
    ENGS = ("pe", "act", "dve", "pool", "sp")

    def __init__(self, n_dma_sems=8):
        self.ops = {e: [] for e in self.ENGS}
        self.n_dma_sems = n_dma_sems
        self.dma_rr = {e: 0 for e in self.ENGS}
        self.dma_last = {}
        self.gcount = 0

    def add(self, eng, fn, reads=(), writes=(), dma=False, dur=0.2, lat=0.0, cls=None):
        op = Op(eng, fn, dma)
        op.dur, op.lat, op.cls = dur, lat, cls
        import sys as _sys
        op.tag = (_sys._getframe(2).f_lineno, _sys._getframe(3).f_lineno)
        op.gidx = self.gcount
        self.gcount += 1
        writes = list(writes) + [t for t in reads if t.excl]
        reads = [t for t in reads if not t.excl]
        raw = []
        war = []
        for t in reads:
            if t.last_w is not None:
                raw.append(t.last_w)
        for t in writes:
            if t.last_w is not None:
                raw.append(t.last_w)
            war.extend(t.readers)
        for t in reads:
            t.readers.append(op)
        for t in writes:
            t.last_w = op
            t.readers = []
        if dma:
            k = self.dma_rr[eng]
            self.dma_rr[eng] = (k + 1) % self.n_dma_sems
            prev = self.dma_last.get((eng, k))
            if prev is not None:
                raw.append(prev)
            self.dma_last[(eng, k)] = op
            op.token = (eng, k)
        seen = set()
        pseen = set()
        for lst, is_war in ((raw, False), (war, True)):
            for d in lst:
                if d is op or id(d) in seen:
                    continue
                if id(d) not in pseen:
                    pseen.add(id(d))
                    op.preds.append(d)
                if (not d.is_dma) and (not dma) and d.eng == eng:
                    if eng == "pe" or is_war:
                        continue
                seen.add(id(d))
                op.deps.append(d)
        op.idx = len(self.ops[eng])
        self.ops[eng].append(op)
        return op

    def reschedule(self, hop=0.15, act_switch=1.4):
        import heapq
        allops = []
        for e in self.ENGS:
            allops.extend(self.ops[e])
        allops.sort(key=lambda o: o.gidx)
        for o in allops:
            o.succs = []
        for o in allops:
            o.npred = len(o.preds)
            for p in o.preds:
                p.succs.append(o)
        for o in reversed(allops):
            m = 0.0
            for s_ in o.succs:
                if s_.prio > m:
                    m = s_.prio
            o.prio = m + o.dur + o.lat
        ready = {e: [] for e in self.ENGS}
        for o in allops:
            o.t_done = None
            if o.npred == 0:
                ready[o.eng].append(o)
        free_at = {e: 0.0 for e in self.ENGS}
        cur_cls = {e: None for e in self.ENGS}
        new_order = {e: [] for e in self.ENGS}
        nleft = len(allops)

        def rtime(o):
            t = 0.0
            for p in o.preds:
                tp = p.t_done + (hop if (p.eng != o.eng or p.is_dma) else 0.0)
                if tp > t:
                    t = tp
            return t

        rt_cache = {}
        while nleft:
            best = None
            for e in self.ENGS:
                lst = ready[e]
                if not lst:
                    continue
                fa = free_at[e]
                for o in lst:
                    r = rt_cache.get(id(o))
                    if r is None:
                        r = rtime(o)
                        rt_cache[id(o)] = r
                    st = r if r > fa else fa
                    if e == "act" and o.cls is not None and cur_cls[e] is not None and o.cls != cur_cls[e]:
                        st += act_switch
                    key = (st, -o.prio)
                    if best is None or key < best[0]:
                        best = (key, o)
            (st, _), o = best
            e = o.eng
            ready[e].remove(o)
            o.t_start = st
            free_at[e] = st + o.dur
            o.t_done = st + o.dur + o.lat
            if e == "act" and o.cls is not None:
                cur_cls[e] = o.cls
            new_order[e].append(o)
            nleft -= 1
            for s_ in o.succs:
                s_.npred -= 1
                if s_.npred == 0:
                    ready[s_.eng].append(s_)
        for e in self.ENGS:
            self.ops[e] = new_order[e]
        self.makespan = max(free_at.values())

    def emit(self, nc, block_ctx_extra=None):
        for e in self.ENGS:
            for op in self.ops[e]:
                for d in op.deps:
                    d.signal = True
        import contextlib
        with contextlib.ExitStack() as st:
            sems = {e: st.enter_context(nc.semaphore("s_" + e)) for e in ("pe", "act", "dve", "pool")}
            dsems = {}
            for e in self.ENGS:
                if any(o.is_dma for o in self.ops[e]):
                    for k in range(self.n_dma_sems):
                        dsems[(e, k)] = st.enter_context(nc.semaphore("d_%s_%d" % (e, k)))
            for e in self.ENGS:
                cnt = 0
                dcnt = {}
                for op in self.ops[e]:
                    if op.is_dma:
                        key = op.token
                        dcnt[key] = dcnt.get(key, 0) + 16
                        op.token = (dsems[key], dcnt[key], key)
                    elif op.signal:
                        cnt += 1
                        op.token = (sems[e], cnt, e)
            block = st.enter_context(nc.Block())
            handles = {"pe": block.tensor, "act": block.scalar, "dve": block.vector, "pool": block.gpsimd,
                       "sp": block.sync}
            nwaits = {e: 0 for e in self.ENGS}

            def make(e):
                ops = self.ops[e]

                def body(eng):
                    known = {}
                    for op in ops:
                        need = {}
                        for d in op.deps:
                            sem, val, key = d.token
                            if known.get(key, 0) >= val:
                                continue
                            if key not in need or need[key][1] < val:
                                need[key] = (sem, val)
                        for key, (sem, val) in need.items():
                            eng.wait_ge(sem, val)
                            known[key] = val
                            nwaits[e] += 1
                        inst = op.fn(eng)
                        if op.is_dma:
                            inst.then_inc(op.token[0], 16)
                        elif op.signal:
                            inst.then_inc(op.token[0], 1)
                    fin = {}
                    for op in ops:
                        if op.is_dma:
                            fin[op.token[2]] = (op.token[0], op.token[1])
                    for key, (sem, val) in fin.items():
                        if known.get(key, 0) < val:
                            eng.wait_ge(sem, val)
                return body

            for e in self.ENGS:
                if self.ops[e]:
                    handles[e](make(e))
            self.nwaits = nwaits

import contextlib

SEQ = 4096
DM = 1024
NG = 8
EPS = 1e-6

BLK_ADA = 0
BLK_P2KV = 6
BLK_RK = 7
BLK_RV = 8
BLK_Q = 9
BLK_ZA = 10
BLK_RQ = 11
BLK_ZR = 12
BLK_M = 13
BLK_WO = 21
NBLK = 23
NSMALL = 40


def _host_blocks(w_ada, w_in, w_pa, w_pr, w_out):
    blocks = np.zeros((NBLK, 128, 4096), np.float32)

    def kmaj(w):
        nc_ = w.shape[1]
        t = np.zeros((128, 8, 512), np.float32)
        t[:, :, :nc_] = w.reshape(8, 128, nc_).transpose(1, 0, 2)
        return t.reshape(128, 4096)

    wa = w_ada
    for j in range(6):
        blocks[BLK_ADA + j] = kmaj(wa[:, j * 512:(j + 1) * 512])
    o_q, o_k, o_v, o_za, o_rq, o_rk, o_rv, o_zr, o_gl = 0, 512, 640, 768, 1280, 1536, 1792, 2304, 2816
    k0 = w_in[:, o_k:o_k + 64]
    k1 = w_in[:, o_k + 64:o_k + 128]
    blocks[BLK_P2KV] = kmaj(np.concatenate([k0, k0, k1, k1, w_in[:, o_v:o_v + 128]], axis=1))
    blocks[BLK_RK] = kmaj(w_in[:, o_rk:o_rk + 256])
    blocks[BLK_RV] = kmaj(w_in[:, o_rv:o_rv + 512])
    blocks[BLK_Q] = kmaj(w_in[:, o_q:o_q + 512])
    blocks[BLK_ZA] = kmaj(w_in[:, o_za:o_za + 512])
    rq = [w_in[:, o_rq + 64 * h:o_rq + 64 * (h + 1)] for h in range(4)]
    blocks[BLK_RQ] = kmaj(np.concatenate([rq[0], rq[0], rq[1], rq[1], rq[2], rq[2], rq[3], rq[3]], axis=1))
    blocks[BLK_ZR] = kmaj(w_in[:, o_zr:o_zr + 512])
    for n in range(8):
        t = np.zeros((128, 4096), np.float32)
        gl = np.concatenate([w_in[:, o_gl + n * 128:o_gl + (n + 1) * 128],
                             w_in[:, o_gl + 1024 + n * 128:o_gl + 1024 + (n + 1) * 128]], axis=1)
        t[:, 0:2048] = gl.reshape(8, 128, 256).transpose(1, 0, 2).reshape(128, 2048)
        t[:, 2048:2560] = w_pa[:, n * 128:(n + 1) * 128].reshape(4, 128, 128).transpose(1, 0, 2).reshape(128, 512)
        t[:, 2560:3072] = w_pr[:, n * 128:(n + 1) * 128].reshape(4, 128, 128).transpose(1, 0, 2).reshape(128, 512)
        blocks[BLK_M + n] = t
    for hf in range(2):
        blocks[BLK_WO + hf] = kmaj(w_out[:, hf * 512:(hf + 1) * 512])
    return blocks


def _host_consts():
    t = np.arange(SEQ)
    row = (t // 64).astype(np.float32)
    col = (t % 64).astype(np.float32)
    inv = (10000.0 ** (-np.arange(0, 32, 2, dtype=np.float32) / 32.0)).astype(np.float32)
    ang_r = row[None, :] * inv[:, None]
    ang_c = col[None, :] * inv[:, None]
    cos64 = np.concatenate([np.cos(ang_r), np.cos(ang_r), np.cos(ang_c), np.cos(ang_c)], 0)
    sin64 = np.concatenate([np.sin(ang_r), np.sin(ang_r), np.sin(ang_c), np.sin(ang_c)], 0)
    rope = np.stack([np.concatenate([cos64, cos64], 0), np.concatenate([sin64, sin64], 0)], 0).astype(np.float32)
    ident = np.eye(128, dtype=np.float32)
    perm = np.zeros((128, 128), np.float32)
    for hb in range(2):
        for d in range(64):
            blk = d // 16
            if blk % 2 == 0:
                p, s = d + 16, -1.0
            else:
                p, s = d - 16, 1.0
            perm[hb * 64 + p, hb * 64 + d] = s
    ones = np.zeros((128, 128), np.float32)
    ones[:64, :64] = 1.0
    ones[64:, 64:] = 1.0
    return rope, np.stack([ident, perm, ones], 0)


class _Stop(Exception):
    pass


def build_program(stop=None, dumps=()):
    nc = bass.Bass("TRN2", target_bir_lowering=False)
    x_d = nc.dram_tensor("x", [SEQ, DM], F32, kind="ExternalInput").ap()
    wsrc = nc.dram_tensor("wsrc", [NBLK, 128, 4096], F32, kind="ExternalInput").ap()
    rope_d = nc.dram_tensor("rope", [2, 128, SEQ], F32, kind="ExternalInput").ap()
    cst_d = nc.dram_tensor("cst", [3, 128, 128], F32, kind="ExternalInput").ap()
    small_d = nc.dram_tensor("small", [128, NSMALL], F32, kind="ExternalInput").ap()
    rows_d = nc.dram_tensor("rows", [2, DM], F32, kind="ExternalInput").ap()
    wdec_d = nc.dram_tensor("wdec", [1, 8], F32, kind="ExternalInput").ap()
    out_d = nc.dram_tensor("out", [SEQ, DM], F32, kind="ExternalOutput").ap()
    wbf_d = nc.dram_tensor("wbf", [NBLK, 128, 4096], BF16, kind="Internal").ap()

    S = Sched(n_dma_sems=12)
    tdict = {}

    import os
    _alias = {"io0": "sa", "io1": "sa", "io2": "sa", "tmpd": "sa", "cbc": "mt1", "junk": "mt0"}
    if os.environ.get("MK_JUNK") == "1":
        del _alias["junk"]

    def tt(name):
        name = _alias.get(name, name)
        if name not in tdict:
            tdict[name] = T(name)
        return tdict[name]

    with contextlib.ExitStack() as st:
        def sb(name, shape, dt):
            return st.enter_context(nc.sbuf_tensor(name, shape, dt))

        PS = st.enter_context(nc.psum_tensor("ps", [128, 4096], F32))
        TB = [T("bank%d" % i, excl=True) for i in range(8)]

        def bank(i, n=1):
            return PS[:, i * 512:(i + n) * 512]

        import os
        _EXP = os.environ.get("MK_EXPLORE") == "1"
        _PDT = mybir.dt.int8 if _EXP else BF16
        K2 = sb("K2", [128, 2, SEQ], _PDT)
        VX = sb("VX", [128, 32, 320], _PDT)
        RF = sb("RF", [128, 16, 512], _PDT)
        NWB = int(os.environ.get('MK_NWB', '3'))
        WB = sb("WB", [128, NWB, 4096], BF16)
        NXT = int(os.environ.get('MK_NXT', '3'))
        XT = sb("XT", [128, NXT, DM], F32)
        XN = sb("XN", [128, 2, DM], BF16)
        HT = sb("HT", [128, 8, 512], BF16)
        HT2 = sb("HT2", [128, 8, 512], BF16)
        CS = sb("CS", [128, 2, 512], F32)
        QBF = sb("QBF", [128, 2, 512], BF16)
        SQ = sb("SQ", [128, 2, 512], BF16)
        T1 = sb("T1", [128, 2, 512], F32)
        T2 = sb("T2", [128, 2, 512], F32)
        RRS = sb("RRS", [128, 2, 512], F32)
        QT = sb("QT", [128, 4, 512], BF16)
        ZA = sb("ZA", [128, 4, 512], BF16)
        NPT = int(os.environ.get('MK_NPT', '3'))
        PT = sb("PT", [128, NPT, 1024], BF16)
        SA = sb("SA", [128, 512], F32)
        SB_ = sb("SB_", [128, 512], F32)
        DEN = sb("DEN", [128, 512], F32)
        YA = sb("YA", [128, 4, 512], BF16)
        YR = sb("YR", [128, 4, 512], BF16)
        RQ = sb("RQ", [128, 4, 512], BF16)
        QD = sb("QD", [128, 4, 512], BF16)
        ZR = sb("ZR", [128, 4, 512], BF16)
        RKT = sb("RKT", [128, 2, 512], BF16)
        RVG = sb("RVG", [128, 4, 512], BF16)
        KK = sb("KK", [128, 2, 512], BF16)
        SM = sb("SM", [128, 2, 512], BF16)
        RRO = sb("RRO", [128, 2, 512], BF16)
        ON = sb("ON", [128, 2, 512], BF16)
        SST = sb("SST", [128, 512], F32)
        NGT = int(os.environ.get("MK_NGT", "1"))
        GATE = sb("GATE", [128, NGT, 2, 512], BF16)
        MT = sb("MT", [128, 2, 512], F32)
        MG = sb("MG", [128, 8, 512], BF16)
        IDENT = sb("IDENT", [128, 128], BF16)
        PERM = sb("PERM", [128, 128], BF16)
        ONESB = sb("ONESB", [128, 128], BF16)
        DMK = sb("DMK", [128, 512], F32)
        QDEC = sb("QDEC", [128, 512], F32)
        GG = sb("GG", [128, DM], F32)
        SMALL = sb("SMALL", [128, NSMALL], F32)
        MISC = sb("MISC", [128, 128], F32)
        IO = SA[:, 0:384].rearrange("p (a b) -> p a b", a=3)
        TMPD = SA[:, 384:512]
        CBC = MT[:, 1, :].bitcast(BF16).rearrange("p (a b) -> p a b", a=8)
        JUNK = sb("JUNKB", [128, DM], BF16)[:] if os.environ.get("MK_JUNK") == "1" else MT[:, 0, :].bitcast(BF16)
        CBF = sb("CBF", [128, 8], BF16)
        STAT = sb("STAT", [128, 2, 4, 8], F32)

        _CACHE['sbuf_left'] = nc.sbuf_bytes_remaining
        M_ACOL, M_SHCOL, M_LG, M_DC, M_WD, M_E, M_KD8, M_KDV, M_IOP, M_IOPR, M_GK8 = 0, 8, 16, 24, 32, 40, 48, 56, 64, 65, 66
        M_SS, M_RSTD, M_CACT, M_TMP8 = 70, 74, 80, 88
        M_GNR, M_GNN = 96, 104
        S_C, S_GPRE, S_BSC, S_BSH, S_QG, S_KG, S_GN = 0, 8, 16, 24, 32, 33, 34

        def fsz(ap):
            n = 1
            for d in list(ap.shape)[1:]:
                n *= int(d)
            return n

        def mm(o, l, r, start, stop, rd, wr):
            n = fsz(o)
            k = int(l.shape[0])
            d = 0.06 + n * 0.00041
            if k <= 64 and n >= 256:
                d = 0.17
            S.add("pe", lambda e: e.matmul(o, l, r, start=start, stop=stop), reads=rd, writes=wr, dur=d)

        def trp(o, i, rd, wr):
            S.add("pe", lambda e: e.transpose(o, i, IDENT[:]), reads=rd + [tt("ident")], writes=wr, dur=0.115)

        _ACLS = {AF.Exp: "exp", AF.Ln: "exp", AF.Silu: "sig", AF.Sigmoid: "sig"}

        def act(o, i, f, rd, wr, **kw):
            S.add("act", lambda e: e.activation(out=o, in_=i, func=f, **kw), reads=rd, writes=wr,
                  dur=0.22 + fsz(i) * 0.00083 + (0.1 if "accum_out" in kw else 0.0), cls=_ACLS.get(f))

        def _vd(eng, n):
            return (0.1 + n * 0.00104) if eng == "dve" else (0.15 + n * 0.0021)

        def tten(eng, o, a, b, op, rd, wr):
            S.add(eng, lambda e: e.tensor_tensor(out=o, in0=a, in1=b, op=op), reads=rd, writes=wr, dur=_vd(eng, fsz(o)))

        def tsc(eng, o, a, s1, s2, op0, op1, rd, wr):
            if op1 is None:
                S.add(eng, lambda e: e.tensor_scalar(out=o, in0=a, scalar1=s1, scalar2=None, op0=op0), reads=rd, writes=wr, dur=_vd(eng, fsz(o)))
            else:
                S.add(eng, lambda e: e.tensor_scalar(out=o, in0=a, scalar1=s1, scalar2=s2, op0=op0, op1=op1), reads=rd, writes=wr, dur=_vd(eng, fsz(o)))

        def stt(eng, o, a, s, b, op0, op1, rd, wr):
            S.add(eng, lambda e: e.scalar_tensor_tensor(out=o, in0=a, scalar=s, in1=b, op0=op0, op1=op1), reads=rd, writes=wr, dur=_vd(eng, fsz(o)))

        def cpy(eng, o, i, rd, wr):
            S.add(eng, lambda e: e.tensor_copy(out=o, in_=i), reads=rd, writes=wr, dur=_vd(eng, fsz(o)))

        def dma(eng, o, i, rd, wr):
            nbytes = fsz(o) * int(o.shape[0]) * (2 if o.dtype == BF16 else 4)
            S.add(eng, lambda e: e.dma_start(out=o, in_=i), reads=rd, writes=wr, dma=True,
                  dur=(0.08 if eng == "sp" else 1.0), lat=2.0 + nbytes / 150000.0)

        MUL, ADD, MAX = ALU.mult, ALU.add, ALU.max

        def ckpt(name):
            if stop == name:
                raise _Stop()

        def bnstats(o, i, rd, wr):
            S.add("dve", lambda e: e.bn_stats(out=o, in_=i), reads=rd, writes=wr, dur=0.25)

        def bnaggr(o, i, rd, wr):
            S.add("dve", lambda e: e.bn_aggr(out=o, in_=i), reads=rd, writes=wr, dur=0.15)

        def recip(o, i, rd, wr):
            S.add("dve", lambda e: e.reciprocal(out=o, in_=i), reads=rd, writes=wr, dur=0.1 + fsz(o) * 0.00625)

        order = [BLK_ADA + j for j in range(4)]
        for tg in range(NG):
            order += [BLK_P2KV, BLK_RK, BLK_RV]
        order += [BLK_ADA + 4, BLK_ADA + 5]
        for tg in range(NG):
            order += [BLK_Q, BLK_ZA, BLK_RQ] + ([BLK_RK, BLK_RV] if tg > 0 else []) + [BLK_ZR] \
                + [BLK_M + n for n in range(8)] + [BLK_WO, BLK_WO + 1]
        conv_done = set()
        first_direct = set()
        ws_state = {"issued": 0, "cur": 0}

        def conv(blk):
            if blk in conv_done:
                return
            conv_done.add(blk)
            dma("pool", wbf_d[blk], wsrc[blk], [], [tt("wbf%d" % blk)])

        def ws_issue_upto(n):
            while ws_state["issued"] < min(n, len(order)):
                i = ws_state["issued"]
                blk = order[i]
                if BLK_ADA <= blk < BLK_ADA + 6 or (blk in (BLK_P2KV, BLK_RK, BLK_RV) and blk not in first_direct):
                    first_direct.add(blk)
                    dma("pool", WB[:, i % NWB, :], wsrc[blk], [], [tt("wb%d" % (i % NWB))])
                else:
                    conv(blk)
                    dma("sp", WB[:, i % NWB, :], wbf_d[blk], [tt("wbf%d" % blk)], [tt("wb%d" % (i % NWB))])
                ws_state["issued"] += 1

        def ws_get(expect, la=None):
            i = ws_state["cur"]
            assert order[i] == expect, (i, order[i], expect)
            ws_issue_upto(i + (NWB if la is None else la))
            ws_state["cur"] += 1
            return WB[:, i % NWB, :], tt("wb%d" % (i % NWB))

        def run_all():
            for blk in [BLK_P2KV, BLK_RK, BLK_RV]:
                conv(blk)
            dma("sp", SMALL[:], small_d, [], [tt("small")])
            dma("sp", GG[:], rows_d[0:1, :].partition_broadcast(128), [], [tt("gg")])
            dma("sp", MISC[:, M_WD:M_WD + 8], wdec_d.partition_broadcast(128), [], [tt("wd")])
            dma("pool", IDENT[:], cst_d[0], [], [tt("ident")])
            dma("pool", PERM[:], cst_d[1], [], [tt("perm")])
            dma("pool", ONESB[:], cst_d[2], [], [tt("onesb")])
            ws_issue_upto(NWB)

            act(MISC[:, M_CACT:M_CACT + 8], SMALL[:, S_C:S_C + 8], AF.Silu, [tt("small")], [tt("cact")])
            cpy("dve", CBF[:], MISC[:, M_CACT:M_CACT + 8], [tt("cact")], [tt("cbf")])
            cpy("dve", CBC, MISC[:, M_CACT:M_CACT + 8].unsqueeze(2).broadcast_to([128, 8, 128]), [tt("cact")], [tt("cbc")])
            act(MISC[:, M_E:M_E + 8], MISC[:, M_WD:M_WD + 8], AF.Exp, [tt("wd")], [tt("e")], scale=-1.0)
            act(MISC[:, M_E:M_E + 8], MISC[:, M_E:M_E + 8], AF.Ln, [tt("e")], [tt("e")], bias=1.0)
            tsc("dve", MISC[:, M_LG:M_LG + 8], MISC[:, M_E:M_E + 8], -1.0, None, MUL, None, [tt("e")], [tt("lg")])
            LGF = lambda h, lo=0, hi=128: MISC[lo:hi, M_LG + h:M_LG + h + 1]
            LGB = lambda h, lo=0, hi=128: MISC[lo:hi, M_LG + 4 + h:M_LG + 5 + h]
            S.add("pool", lambda e: e.iota(IO[:, 0, :], pattern=[[1, 128]], base=0, channel_multiplier=-1,
                                           allow_small_or_imprecise_dtypes=True), writes=[tt("io0")])
            S.add("pool", lambda e: e.iota(IO[:, 1, :], pattern=[[1, 128]], base=1, channel_multiplier=0,
                                           allow_small_or_imprecise_dtypes=True), writes=[tt("io1")])
            S.add("pool", lambda e: e.iota(IO[:, 2, :], pattern=[[-1, 128]], base=128, channel_multiplier=0,
                                           allow_small_or_imprecise_dtypes=True), writes=[tt("io2")])
            S.add("pool", lambda e: e.iota(MISC[:, M_IOP:M_IOP + 1], pattern=[[1, 1]], base=0, channel_multiplier=1,
                                           allow_small_or_imprecise_dtypes=True), writes=[tt("iop")])
            S.add("pool", lambda e: e.iota(MISC[:, M_IOPR:M_IOPR + 1], pattern=[[1, 1]], base=127, channel_multiplier=-1,
                                           allow_small_or_imprecise_dtypes=True), writes=[tt("iopr")])
            NEG = RRS[:, 0, 0:128]
            POS = RRS[:, 0, 128:256]
            tsc("dve", POS, IO[:, 0, :], 0.0, None, MAX, None, [tt("io0")], [tt("pos")])
            tten("dve", NEG, POS, IO[:, 0, :], ALU.subtract, [tt("pos"), tt("io0")], [tt("neg")])
            for h in range(4):
                tsc("dve", TMPD, POS, LGF(h), None, MUL, None, [tt("pos"), tt("lg")], [tt("tmpd")])
                stt("dve", TMPD, NEG, LGB(h), TMPD, MUL, ADD, [tt("neg"), tt("lg"), tt("tmpd")], [tt("tmpd")])
                act(DMK[:, h * 128:(h + 1) * 128], TMPD, AF.Exp, [tt("tmpd")], [tt("dmk")])
                tsc("dve", TMPD[0:64, :], IO[0:64, 1, :], LGF(h, 0, 64), None, MUL, None, [tt("io1"), tt("lg"), tt("dmk")], [tt("tmpd")])
                tsc("dve", TMPD[64:128, :], IO[64:128, 2, :], LGB(h, 64, 128), None, MUL, None, [tt("io2"), tt("lg"), tt("tmpd")], [tt("tmpd")])
                act(QDEC[:, h * 128:(h + 1) * 128], TMPD, AF.Exp, [tt("tmpd")], [tt("qdec")])
            KDV = MISC[:, M_KDV:M_KDV + 8].rearrange("p (h d) -> p h d", d=2)
            tsc("dve", MISC[:, M_KD8:M_KD8 + 4], MISC[:, M_LG:M_LG + 4], MISC[:, M_IOPR:M_IOPR + 1], None, MUL, None,
                [tt("lg"), tt("iopr")], [tt("kd8")])
            tsc("dve", MISC[:, M_KD8 + 4:M_KD8 + 8], MISC[:, M_LG + 4:M_LG + 8], MISC[:, M_IOP:M_IOP + 1], None, MUL, None,
                [tt("lg"), tt("iop"), tt("kd8")], [tt("kd8")])
            act(KDV[:, :, 0], MISC[:, M_KD8:M_KD8 + 4], AF.Exp, [tt("kd8")], [tt("kdv")])
            act(KDV[:, :, 1], MISC[:, M_KD8 + 4:M_KD8 + 8], AF.Exp, [tt("kd8"), tt("kdv")], [tt("kdv")])
            act(MISC[:, M_DC:M_DC + 8], MISC[:, M_LG:M_LG + 8], AF.Exp, [tt("lg")], [tt("dc")], scale=128.0)
            tsc("dve", MISC[:, M_GK8:M_GK8 + 1], SMALL[:, S_KG:S_KG + 1], 8.0, None, MUL, None, [tt("small")], [tt("gk8")])
            S.add("pool", lambda e: e.memset(SST[:], 0.0), writes=[tt("sst")])
            S.add("dve", lambda e: e.memset(VX[:], 1.0), writes=[tt("vx%d" % g) for g in range(NG)], dur=6.0)

            ckpt('setup0')
            pmod = bank(6)[:, 0:16]
            for blk in range(4):
                wb, twb = ws_get(BLK_ADA + blk)
                wbv = wb.rearrange("p (k c) -> p k c", k=8)
                for c in range(4):
                    for kc in range(8):
                        mm(pmod[:, blk * 4 + c:blk * 4 + c + 1], wbv[:, kc, c * 128:(c + 1) * 128], CBF[:, kc:kc + 1],
                           kc == 0, kc == 7, [twb, tt("cbf")], [TB[6]])
            tten("dve", MISC[:, M_TMP8:M_TMP8 + 8], pmod[:, 8:16], SMALL[:, S_BSC:S_BSC + 8], ADD, [TB[6], tt("small")], [tt("tmp8")])
            stt("dve", MISC[:, M_ACOL:M_ACOL + 8], MISC[:, M_TMP8:M_TMP8 + 8], 1.0, SMALL[:, S_GPRE:S_GPRE + 8], ADD, MUL,
                [tt("tmp8"), tt("small")], [tt("acol")])
            tten("dve", MISC[:, M_SHCOL:M_SHCOL + 8], pmod[:, 0:8], SMALL[:, S_BSH:S_BSH + 8], ADD, [TB[6], tt("small")], [tt("shcol")])
            def gate_setup():
                dma("sp", XT[:, 0, :], rows_d[1:2, :].partition_broadcast(128), [], [tt("xt0")])
                for hf in range(2):
                    wb, twb = ws_get(BLK_ADA + 4 + hf)
                    wbv = wb.rearrange("p (k c) -> p k c", k=8)
                    for kc in range(8):
                        mm(bank(4 + hf), CBC[:, kc, :], wbv[:, kc, :], kc == 0, kc == 7, [twb, tt("cbc")], [TB[4 + hf]])
                    tten("dve", GG[:, hf * 512:(hf + 1) * 512], bank(4 + hf), GG[:, hf * 512:(hf + 1) * 512], ADD,
                         [TB[4 + hf], tt("gg")], [tt("gg")])
                    tten("dve", GG[:, hf * 512:(hf + 1) * 512], GG[:, hf * 512:(hf + 1) * 512], XT[:, 0, hf * 512:(hf + 1) * 512], MUL,
                         [tt("gg"), tt("xt0")], [tt("gg")])

            ckpt('ada')
            xt_rr = {"i": 0}

            def load_x(tile):
                i = xt_rr["i"] % NXT
                xt_rr["i"] += 1
                dma("sp", XT[:, i, :], x_d[tile * 128:(tile + 1) * 128, :], [], [tt("xt%d" % i)])
                return i

            cur = {"ht": HT, "htT": tt("ht")}

            hcfg = {"banks": [0, 1, 2, 3]}

            def _hreg(kc, jj, ntp):
                hb = hcfg["banks"]
                per = 8 // len(hb)
                bk = hb[kc // per]
                off = (kc % per) * (ntp * 128) + jj * 128
                return bk, off

            def hT_tile(tg, j):
                hb = hcfg["banks"]
                ntp = len(hb)
                tile = tg * 4 + j
                xi = load_x(tile)
                xb = j % 2
                act(JUNK, XT[:, xi, :], AF.Square, [tt("xt%d" % xi)], [tt("junk"), tt("ss")], accum_out=MISC[:, M_SS:M_SS + 1])
                act(MISC[:, M_RSTD:M_RSTD + 1], MISC[:, M_SS:M_SS + 1], AF.Ln, [tt("ss")], [tt("rstd")], scale=1.0 / DM, bias=EPS)
                act(MISC[:, M_RSTD:M_RSTD + 1], MISC[:, M_RSTD:M_RSTD + 1], AF.Exp, [tt("rstd")], [tt("rstd")], scale=-0.5)
                tsc("dve", XN[:, xb, :], XT[:, xi, :], MISC[:, M_RSTD:M_RSTD + 1], None, MUL, None,
                    [tt("xt%d" % xi), tt("rstd")], [tt("xn%d" % xb)])
                for kc in range(8):
                    bk, off = _hreg(kc, j % ntp, ntp)
                    trp(bank(bk).bitcast(BF16)[:, off:off + 128], XN[:, xb, kc * 128:(kc + 1) * 128], [tt("xn%d" % xb)], [TB[bk]])

            def hT_evac(dst, dstT, ps=0):
                hb = hcfg["banks"]
                ntp = len(hb)
                for kc in range(8):
                    bk, off = _hreg(kc, 0, ntp)
                    pb = bank(bk).bitcast(BF16)[:, off:off + ntp * 128]
                    o = dst[:, kc, ps * ntp * 128:(ps + 1) * ntp * 128]
                    if kc < 4:
                        tsc("dve", o, pb, MISC[:, M_ACOL + kc:M_ACOL + kc + 1], MISC[:, M_SHCOL + kc:M_SHCOL + kc + 1],
                            MUL, ADD, [TB[bk], tt("acol"), tt("shcol")], [dstT])
                    else:
                        act(o, pb, AF.Identity, [TB[bk], tt("acol"), tt("shcol")], [dstT],
                            scale=MISC[:, M_ACOL + kc:M_ACOL + kc + 1], bias=MISC[:, M_SHCOL + kc:M_SHCOL + kc + 1])

            def make_hT(tg, set_cur=True):
                if set_cur:
                    cur["ht"], cur["htT"] = bufs[tg % 2]
                if os.environ.get("MK_HT67", "1") == "1":
                    hcfg["banks"] = [6, 7]
                ntp = len(hcfg["banks"])
                for ps in range(4 // ntp):
                    for jj in range(ntp):
                        hT_tile(tg, ps * ntp + jj)
                    hT_evac(bufs[tg % 2][0], bufs[tg % 2][1], ps)

            def load_cs(tg):
                dma("sp", CS[:, 0, :], rope_d[0][:, tg * 512:(tg + 1) * 512], [], [tt("cs")])
                dma("sp", CS[:, 1, :], rope_d[1][:, tg * 512:(tg + 1) * 512], [], [tt("cs")])

            def proj_fm(wbv, twb, c0, bk):
                for kc in range(8):
                    mm(bank(bk), wbv[:, kc, c0:c0 + 128], cur["ht"][:, kc, :], kc == 0, kc == 7, [twb, cur["htT"]], [TB[bk]])

            def proj_tm(wbv, twb, j, c0, ncol, o, tbk):
                for kc in range(8):
                    mm(o, cur["ht"][:, kc, j * 128:(j + 1) * 128], wbv[:, kc, c0:c0 + ncol], kc == 0, kc == 7, [twb, cur["htT"]], [tbk])

            rope_ctr = {"i": 0, "alt": [(4, 5)]}

            def rope(bkA, gcol, gT, rms, out_ap, out_T):
                i = rope_ctr["i"] % 2
                rope_ctr["i"] += 1
                bS, bC = rope_ctr["alt"][i % len(rope_ctr["alt"])]
                if os.environ.get("MK_ROPE67", "1") == "1" and len(rope_ctr["alt"]) == 2:
                    bS = bC = 13 - bkA
                A = bank(bkA)
                rdg = [gT] if gT is not None else []
                gsc = gcol if gT is not None else float(gcol)
                tsc("dve", QBF[:, i, :], A, gsc, None, MUL, None, [TB[bkA]] + rdg, [tt("qbf%d" % i)])
                if rms:
                    act(SQ[:, i, :], A, AF.Square, [TB[bkA]], [tt("sq%d" % i)])
                mm(bank(bS), PERM[:], QBF[:, i, :], True, True, [tt("perm"), tt("qbf%d" % i)], [TB[bS]])
                stt("dve", T1[:, i, :], A, gsc, CS[:, 0, :], MUL, MUL, [TB[bkA], tt("cs")] + rdg, [tt("t1_%d" % i)])
                tten("dve", T2[:, i, :], bank(bS), CS[:, 1, :], MUL, [TB[bS], tt("cs")], [tt("t2_%d" % i)])
                if rms:
                    mm(bank(bC), ONESB[:], SQ[:, i, :], True, True, [tt("onesb"), tt("sq%d" % i)], [TB[bC]])
                    act(RRS[:, i, :], bank(bC), AF.Ln, [TB[bC]], [tt("rrs%d" % i)], bias=64.0 * EPS)
                    act(RRS[:, i, :], RRS[:, i, :], AF.Exp, [tt("rrs%d" % i)], [tt("rrs%d" % i)], scale=-0.5)
                    tten(os.environ.get("MK_RA", "pool"), T1[:, i, :], T1[:, i, :], T2[:, i, :], ADD, [tt("t1_%d" % i), tt("t2_%d" % i)], [tt("t1_%d" % i)])
                    tten(os.environ.get("MK_RM", "pool"), out_ap, T1[:, i, :], RRS[:, i, :], MUL, [tt("t1_%d" % i), tt("rrs%d" % i)], [out_T])
                else:
                    tten(os.environ.get("MK_RP", "pool"), out_ap, T1[:, i, :], T2[:, i, :], ADD, [tt("t1_%d" % i), tt("t2_%d" % i)], [out_T])

            def ret_rk(tg):
                wb, twb = ws_get(BLK_RK)
                wbv = wb.rearrange("p (k c) -> p k c", k=8)
                for pr in range(2):
                    proj_fm(wbv, twb, pr * 128, 6 + pr)
                    rope(6 + pr, 0.125, None, False, RKT[:, pr, :], tt("rkt"))

            def ret_rv(tg):
                wb, twb = ws_get(BLK_RV)
                wbv = wb.rearrange("p (k c) -> p k c", k=8)
                for j in range(4):
                    bk = 6 + (j % 2)
                    proj_tm(wbv, twb, j, 0, 512, bank(bk), TB[bk])
                    act(RVG[:, j, :], bank(bk), AF.Copy, [TB[bk]], [tt("rvg")])

            def ret_kv_inputs(tg):
                ret_rk(tg)
                ret_rv(tg)

            def kv_matmul(j):
                ptk = bank(6).bitcast(BF16)[:, 0:256]
                for pr in range(2):
                    trp(ptk[:, pr * 128:(pr + 1) * 128], RKT[:, pr, j * 128:(j + 1) * 128], [tt("rkt")], [TB[6]])
                kb = j % 2
                tten("dve", KK[:, kb, :].rearrange("p (h d c) -> p h d c", h=4, d=2),
                     ptk.rearrange("p (h c) -> p h c", h=4).unsqueeze(2).broadcast_to([128, 4, 2, 64]),
                     MISC[:, M_KDV:M_KDV + 8].rearrange("p (h d) -> p h d", d=2).unsqueeze(3).broadcast_to([128, 4, 2, 64]), MUL,
                     [TB[6], tt("kdv")], [tt("kk%d" % kb)])
                for h in range(4):
                    mm(bank(7)[:, h * 128:(h + 1) * 128], KK[:, kb, h * 128:(h + 1) * 128], RVG[:, j, h * 128:(h + 1) * 128],
                       True, True, [tt("kk%d" % kb), tt("rvg")], [TB[7]])

            bufs = [(HT, tt("ht")), (HT2, tt("ht2"))]
            for j in range(4):
                hT_tile(0, j)
            hT_evac(*bufs[0])
            for tg in range(NG):
                cur["ht"], cur["htT"] = bufs[tg % 2]
                nxt = bufs[(tg + 1) % 2] if tg + 1 < NG else None
                ckpt('ht%d' % tg)
                load_cs(tg)
                wb, twb = ws_get(BLK_P2KV)
                wbv = wb.rearrange("p (k c) -> p k c", k=8)
                for g in range(2):
                    proj_fm(wbv, twb, g * 128, 6 + g)
                    rope(6 + g, MISC[:, M_GK8:M_GK8 + 1], tt("gk8"), True, K2[:, g, tg * 512:(tg + 1) * 512], tt("k2_%d_%d" % (g, tg)))
                ckpt('k%d' % tg)
                if nxt:
                    hT_tile(tg + 1, 0)
                for j in range(4):
                    proj_tm(wbv, twb, j, 256, 128, bank(6)[:, j * 128:(j + 1) * 128], TB[6])
                cpy("dve", VX[:, tg * 4:(tg + 1) * 4, 64:320].rearrange("p j (g c) -> p j g c", g=2)[:, :, :, 0:64],
                    bank(6).rearrange("p (j g c) -> p j g c", j=4, g=2), [TB[6]], [tt("vx%d" % tg)])
                ckpt('v%d' % tg)
                if nxt:
                    hT_tile(tg + 1, 1)
                ret_rk(tg)
                if nxt:
                    hT_tile(tg + 1, 2)
                ret_rv(tg)
                ckpt('rkv%d' % tg)
                if nxt:
                    hT_tile(tg + 1, 3)
                    hT_evac(*nxt)
                for j in range(4):
                    ci = tg * 4 + j
                    kv_matmul(j)
                    act(RF[(ci % 2) * 64:(ci % 2) * 64 + 64, ci // 2, :], SST[0:64, :], AF.Copy, [tt("sst")], [tt("rf")])
                    tten("dve", SST[0:64, :].rearrange("p (h c) -> p h c", h=4), SST[0:64, :].rearrange("p (h c) -> p h c", h=4),
                         MISC[0:64, M_DC:M_DC + 4].unsqueeze(2).broadcast_to([64, 4, 128]), MUL, [tt("sst"), tt("dc")], [tt("sst")])
                    tten("dve", SST[0:64, :], SST[0:64, :], bank(7)[0:64, :], ADD, [tt("sst"), TB[7]], [tt("sst")])

            rope_ctr["alt"] = [(4, 5), (2, 3)]
            gate_setup()
            ckpt('p2')
            for tg in range(NG - 1, -1, -1):
                EARLY = os.environ.get("MK_EARLY", "3")
                if tg == NG - 1 or EARLY != "0":
                    cur["ht"], cur["htT"] = bufs[tg % 2]
                else:
                    make_hT(tg)
                load_cs(tg)
                wq, twq = ws_get(BLK_Q)
                wqv = wq.rearrange("p (k c) -> p k c", k=8)
                for p in range(4):
                    proj_fm(wqv, twq, p * 128, 6 + (p % 2))
                    rope(6 + (p % 2), SMALL[:, S_QG:S_QG + 1], tt("small"), True, QT[:, p, :], tt("qt%d" % p))
                wz, twz = ws_get(BLK_ZA)
                wzv = wz.rearrange("p (k c) -> p k c", k=8)
                for p in range(4):
                    bk = 6 + (p % 2)
                    proj_fm(wzv, twz, p * 128, bk)
                    act(ZA[:, p, :], bank(bk), AF.Silu, [TB[bk]], [tt("za%d" % p)])
                ckpt('q%d' % tg)
                for p in range(4):
                    g = p // 2

                    def qk(kc):
                        si = kc % 2
                        mm(bank(2 * si), K2[0:64, g, kc * 128:(kc + 1) * 128], QT[0:64, p, :], True, True,
                           [tt("k2_%d_%d" % (g, kc // 4)), tt("qt%d" % p)], [TB[2 * si]])
                        mm(bank(2 * si + 1), K2[64:128, g, kc * 128:(kc + 1) * 128], QT[64:128, p, :], True, True,
                           [tt("k2_%d_%d" % (g, kc // 4)), tt("qt%d" % p)], [TB[2 * si + 1]])

                    def pv(kc):
                        pi = kc % NPT
                        mm(bank(4), VX[:, kc, 64 + g * 128:192 + g * 128], PT[:, pi, 0:512], kc == 0, kc == 31,
                           [tt("vx%d" % (kc // 4)), tt("pt%d" % pi)], [TB[4]])
                        mm(bank(5), VX[:, kc, g * 128:g * 128 + 128], PT[:, pi, 512:1024], kc == 0, kc == 31,
                           [tt("vx%d" % (kc // 4)), tt("pt%d" % pi)], [TB[5]])

                    qk(0)
                    qk(1)
                    for kc in range(32):
                        si = kc % 2
                        act(PT[:, kc % NPT, :], bank(2 * si, 2), AF.Exp, [TB[2 * si], TB[2 * si + 1]], [tt("pt%d" % (kc % NPT))])
                        if kc + 2 < 32:
                            qk(kc + 2)
                        pv(kc)
                    cpy("dve", SA[:], bank(4), [TB[4]], [tt("sa")])
                    cpy("dve", SB_[:], bank(5), [TB[5]], [tt("sb")])
                    recip(DEN[0:64, :], SA[64:128, :], [tt("sa")], [tt("den")])
                    recip(DEN[64:128, :], SB_[0:64, :], [tt("sb")], [tt("den")])
                    tten("pool", DEN[0:64, :], SA[0:64, :], DEN[0:64, :], MUL, [tt("sa"), tt("den")], [tt("den")])
                    tten("pool", DEN[64:128, :], SB_[64:128, :], DEN[64:128, :], MUL, [tt("sb"), tt("den")], [tt("den")])
                    tten("pool", YA[:, p, :], DEN[:], ZA[:, p, :], MUL, [tt("den"), tt("za%d" % p)], [tt("ya")])
                ckpt('att%d' % tg)
                if EARLY == '1' and tg > 0:
                    make_hT(tg - 1, set_cur=False)
                wr, twr = ws_get(BLK_RQ)
                wrv = wr.rearrange("p (k c) -> p k c", k=8)
                for h in range(4):
                    proj_fm(wrv, twr, h * 128, 6 + (h % 2))
                    rope(6 + (h % 2), 1.0, None, False, RQ[:, h, :], tt("rq"))
                    tten("pool", QD[:, h, :].rearrange("p (j c) -> p j c", j=4), RQ[:, h, :].rearrange("p (j c) -> p j c", j=4),
                         QDEC[:, h * 128:(h + 1) * 128].unsqueeze(1).broadcast_to([128, 4, 128]), MUL, [tt("rq"), tt("qdec")], [tt("qd")])
                ckpt('rqd%d' % tg)
                if tg != NG - 1:
                    ret_kv_inputs(tg)
                wz, twz = ws_get(BLK_ZR)
                wzv = wz.rearrange("p (k c) -> p k c", k=8)
                for h in range(4):
                    bk = 6 + (h % 2)
                    proj_fm(wzv, twz, h * 128, bk)
                    act(ZR[:, h, :], bank(bk), AF.Silu, [TB[bk]], [tt("zr")])
                ckpt('rin%d' % tg)
                for j in range(3, -1, -1):
                    ci = tg * 4 + j
                    sb_i = j % 2
                    for h in range(4):
                        r0 = (h % 2) * 64
                        mm(bank(h % 2)[:, (h // 2) * 128:(h // 2 + 1) * 128], RKT[r0:r0 + 64, h // 2, j * 128:(j + 1) * 128],
                           RQ[r0:r0 + 64, h, j * 128:(j + 1) * 128], True, True, [tt("rkt"), tt("rq")], [TB[h % 2]])
                    for hb in range(2):
                        tten("dve", SM[:, sb_i, :].rearrange("p (a b c) -> p a b c", a=2, b=2)[:, :, hb, :],
                             bank(hb)[:, 0:256].rearrange("p (a c) -> p a c", a=2),
                             DMK[:].rearrange("p (a b c) -> p a b c", a=2, b=2)[:, :, hb, :], MUL, [TB[hb], tt("dmk")], [tt("sm%d" % sb_i)])
                    ckpt('rs%d' % ci)
                    act(RRO[0:64, sb_i, :], RF[(ci % 2) * 64:(ci % 2) * 64 + 64, ci // 2, :], AF.Copy, [tt("rf")], [tt("rro%d" % sb_i)])
                    cpy("pool", RRO[64:128, sb_i, :], SST[64:128, :], [tt("sst")], [tt("rro%d" % sb_i)])
                    for h in range(4):
                        mm(bank(2)[:, h * 128:(h + 1) * 128], SM[:, sb_i, h * 128:(h + 1) * 128], RVG[:, j, h * 128:(h + 1) * 128],
                           True, False, [tt("sm%d" % sb_i), tt("rvg")], [TB[2]])
                        mm(bank(2)[:, h * 128:(h + 1) * 128], QD[:, h, j * 128:(j + 1) * 128], RRO[:, sb_i, h * 128:(h + 1) * 128],
                           False, True, [tt("qd"), tt("rro%d" % sb_i)], [TB[2]])
                    ckpt('ro%d' % ci)
                    for h in range(4):
                        bnstats(STAT[:, sb_i, h, 0:6], bank(2)[:, h * 128:(h + 1) * 128], [TB[2]], [tt("stat%d" % sb_i)])
                        bnaggr(STAT[:, sb_i, h, 6:8], STAT[:, sb_i, h, 0:6], [tt("stat%d" % sb_i)], [tt("stat%d" % sb_i)])
                    gr = MISC[:, M_GNR + sb_i * 4:M_GNR + sb_i * 4 + 4]
                    gn = MISC[:, M_GNN + sb_i * 4:M_GNN + sb_i * 4 + 4]
                    act(gr, STAT[:, sb_i, :, 7], AF.Ln, [tt("stat%d" % sb_i)], [tt("gr%d" % sb_i)], bias=EPS)
                    act(gr, gr, AF.Exp, [tt("gr%d" % sb_i)], [tt("gr%d" % sb_i)], scale=-0.5)
                    stt("dve", gn, STAT[:, sb_i, :, 6], -1.0, gr, MUL, MUL, [tt("stat%d" % sb_i), tt("gr%d" % sb_i)], [tt("gn%d" % sb_i)])
                    for h in range(4):
                        tsc("dve", ON[:, sb_i, h * 128:(h + 1) * 128], bank(2)[:, h * 128:(h + 1) * 128],
                            MISC[:, M_GNR + sb_i * 4 + h:M_GNR + sb_i * 4 + h + 1], MISC[:, M_GNN + sb_i * 4 + h:M_GNN + sb_i * 4 + h + 1],
                            MUL, ADD, [TB[2], tt("gr%d" % sb_i), tt("gn%d" % sb_i)], [tt("on%d" % sb_i)])
                    ckpt('rgn%d' % ci)
                    ptn = bank(3).bitcast(BF16)[:, 0:512]
                    for h in range(4):
                        trp(ptn[:, h * 128:(h + 1) * 128], ON[:, sb_i, h * 128:(h + 1) * 128], [tt("on%d" % sb_i)], [TB[3]])
                    for h in range(4):
                        stt("dve", YR[:, h, j * 128:(j + 1) * 128], ptn[:, h * 128:(h + 1) * 128], SMALL[:, S_GN + h:S_GN + h + 1],
                            ZR[:, h, j * 128:(j + 1) * 128], MUL, MUL, [TB[3], tt("small"), tt("zr")], [tt("yr")])
                    ckpt('ryr%d' % ci)
                    kv_matmul(j)
                    tten("dve", SST[64:128, :].rearrange("p (h c) -> p h c", h=4), SST[64:128, :].rearrange("p (h c) -> p h c", h=4),
                         MISC[64:128, M_DC + 4:M_DC + 8].unsqueeze(2).broadcast_to([64, 4, 128]), MUL, [tt("sst"), tt("dc")], [tt("sst")])
                    tten("dve", SST[64:128, :], SST[64:128, :], bank(7)[64:128, :], ADD, [tt("sst"), TB[7]], [tt("sst")])
                ckpt('ret%d' % tg)
                if EARLY == '2' and tg > 0:
                    make_hT(tg - 1, set_cur=False)
                for n in range(8):
                    wm, twm = ws_get(BLK_M + n)
                    glv = wm[:, 0:2048].rearrange("p (k c) -> p k c", k=8)
                    pav = wm[:, 2048:2560].rearrange("p (k c) -> p k c", k=4)
                    prv = wm[:, 2560:3072].rearrange("p (k c) -> p k c", k=4)
                    gi = n % NGT
                    b0 = 4 * (n % 2)
                    for br in range(2):
                        bk = b0 + 2 + br
                        for kc in range(8):
                            mm(bank(bk), glv[:, kc, br * 128:(br + 1) * 128], cur["ht"][:, kc, :], kc == 0, kc == 7, [twm, cur["htT"]], [TB[bk]])
                        act(GATE[:, gi, br, :], bank(bk), AF.Sigmoid, [TB[bk]], [tt("gate%d_%d" % (gi, br))])
                    for cc in range(4):
                        mm(bank(b0), pav[:, cc, :], YA[:, cc, :], cc == 0, cc == 3, [twm, tt("ya")], [TB[b0]])
                    for cc in range(4):
                        mm(bank(b0 + 1), prv[:, cc, :], YR[:, cc, :], cc == 0, cc == 3, [twm, tt("yr")], [TB[b0 + 1]])
                    tten("dve", MT[:, 0, :], bank(b0), GATE[:, gi, 0, :], MUL, [TB[b0], tt("gate%d_0" % gi)], [tt("mt0")])
                    tten("dve", MT[:, 1, :], bank(b0 + 1), GATE[:, gi, 1, :], MUL, [TB[b0 + 1], tt("gate%d_1" % gi)], [tt("mt1")])
                    tten("pool", MG[:, n, :], MT[:, 0, :], MT[:, 1, :], ADD, [tt("mt0"), tt("mt1")], [tt("mg")])
                ckpt('mrg%d' % tg)
                if EARLY == '3' and tg > 0:
                    make_hT(tg - 1, set_cur=False)
                wo0, two0 = ws_get(BLK_WO)
                wo1, two1 = ws_get(BLK_WO + 1, la=NWB - 1)
                wov = [wo0.rearrange("p (k c) -> p k c", k=8), wo1.rearrange("p (k c) -> p k c", k=8)]
                twos = [two0, two1]
                for j in range(4):
                    tile = tg * 4 + j
                    pb0 = 2 * j
                    for hf in range(2):
                        for kc in range(8):
                            mm(bank(pb0 + hf), MG[:, kc, j * 128:(j + 1) * 128], wov[hf][:, kc, :], kc == 0, kc == 7,
                               [tt("mg"), twos[hf]], [TB[pb0 + hf]])
                    xi = load_x(tile)
                    act(JUNK, bank(pb0, 2), AF.Square, [TB[pb0], TB[pb0 + 1]], [tt("junk"), tt("ss2")], accum_out=MISC[:, M_SS + 1:M_SS + 2])
                    act(MISC[:, M_RSTD + 1:M_RSTD + 2], MISC[:, M_SS + 1:M_SS + 2], AF.Ln, [tt("ss2")], [tt("rstd2")], scale=1.0 / DM, bias=EPS)
                    act(MISC[:, M_RSTD + 1:M_RSTD + 2], MISC[:, M_RSTD + 1:M_RSTD + 2], AF.Exp, [tt("rstd2")], [tt("rstd2")], scale=-0.5)
                    stt("dve", bank(pb0, 2), bank(pb0, 2), MISC[:, M_RSTD + 1:M_RSTD + 2], GG[:], MUL, MUL,
                        [TB[pb0], TB[pb0 + 1], tt("rstd2"), tt("gg")], [TB[pb0], TB[pb0 + 1]])
                    tten("dve", XT[:, xi, :], bank(pb0, 2), XT[:, xi, :], ADD, [TB[pb0], TB[pb0 + 1], tt("xt%d" % xi)], [tt("xt%d" % xi)])
                    dma(os.environ.get("MK_STQ", "sp"), out_d[tile * 128:(tile + 1) * 128, :], XT[:, xi, :], [tt("xt%d" % xi)], [])
        try:
            run_all()
        except _Stop:
            pass
        dump_aps = {"HT": HT, "K2": K2, "VX": VX, "RF": RF, "MISC": MISC, "GG": GG, "DMK": DMK, "QDEC": QDEC,
                    "YA": YA, "YR": YR, "MG": MG, "QT": QT, "ZA": ZA, "RKT": RKT, "RVG": RVG, "SST": SST,
                    "RQ": RQ, "QD": QD, "ZR": ZR, "XN": XN, "CS": CS}
        for nm in dumps:
            src = dump_aps[nm]
            shp = list(src.shape)
            flat = int(np.prod(shp[1:]))
            dd = nc.dram_tensor("dbg_" + nm, [128, flat], src.dtype, kind="ExternalOutput").ap()
            allT = list(tdict.values()) + TB
            sv = src[:] if len(shp) == 2 else src[:].rearrange("p a b -> p (a b)") if len(shp) == 3 else src[:].rearrange("p a b c -> p (a b c)")
            dma("sp", dd, sv, allT, [])
        import os
        if os.environ.get('MK_RESCHED', '1') == '1':
            S.reschedule()
        if not _EXP:
            S.emit(nc)
    return nc, S


_CACHE = {}


def kernel(x, c, w_ada, b_ada, g_pre, w_in, qn_g, kn_g, w_dec_f, w_dec_b, gn_g, w_pa, w_pr, w_out, g_post):
    x = np.asarray(x, np.float32)
    c = np.asarray(c, np.float32)
    f = lambda a: np.asarray(a, np.float32)[0]
    w_ada, b_ada, g_pre, w_in, qn_g, kn_g = f(w_ada), f(b_ada), f(g_pre), f(w_in), f(qn_g), f(kn_g)
    w_dec_f, w_dec_b, gn_g, w_pa, w_pr, w_out, g_post = f(w_dec_f), f(w_dec_b), f(gn_g), f(w_pa), f(w_pr), f(w_out), f(g_post)
    if "nc" not in _CACHE:
        _CACHE["nc"] = build_program()[0]
        _CACHE["consts"] = _host_consts()
    nc = _CACHE["nc"]
    rope, cst = _CACHE["consts"]
    blocks = _host_blocks(w_ada, w_in, w_pa, w_pr, w_out)
    col8 = lambda v: np.ascontiguousarray(v.reshape(8, 128).T)
    rows = np.ascontiguousarray(np.stack([b_ada[2048:3072], g_post], 0))
    wdec = np.concatenate([w_dec_f, w_dec_b])[None, :].astype(np.float32)
    in_maps = []
    for b in range(8):
        small = np.zeros((128, NSMALL), np.float32)
        small[:, 0:8] = col8(c[b])
        small[:, 8:16] = col8(g_pre)
        small[:, 16:24] = col8(b_ada[1024:2048])
        small[:, 24:32] = col8(b_ada[0:1024])
        small[:, 32] = np.concatenate([qn_g, qn_g])
        small[:, 33] = np.concatenate([kn_g, kn_g])
        small[:, 34:38] = gn_g.reshape(4, 128).T
        in_maps.append({"x": np.ascontiguousarray(x[b]), "wsrc": blocks, "rope": rope, "cst": cst,
                        "small": small, "rows": rows, "wdec": wdec})
    res = run_bass_kernel_spmd(nc, in_maps, core_ids=list(range(8)))
    return np.stack([np.asarray(r["out"], np.float32) for r in res.results], 0)
```
